# Optimizing a Trainium2 kernel written in Bass

```python
import jax
import jax.numpy as jnp
from jax import lax
import numpy as np

D_MODEL = 1024
BATCH = 4
SEQ = 8192
DEPTH = 4
DEC_BATCH = 16
DEC_SEQ = 4096
PAST_LEN = 128

GRID_W = 64
CHUNK = 128
N_BRANCH = 4
BRANCH_W = D_MODEL // N_BRANCH
HEAD_DIM = 64
A_GROUPS = BRANCH_W // HEAD_DIM
NA_HEADS = BRANCH_W // HEAD_DIM
NA_WIN_R = 8
NA_WIN_C = 16
ML_HEADS = BRANCH_W // HEAD_DIM
N_GATES = 4 * ML_HEADS
CONV_W = 31
D_FF = ((8 * D_MODEL + 3 * 256 - 1) // (3 * 256)) * 256
EPS = 1e-6
IN_SIZES = (BRANCH_W, BRANCH_W,
            BRANCH_W, BRANCH_W, BRANCH_W,
            BRANCH_W, BRANCH_W, BRANCH_W, BRANCH_W, N_GATES,
            BRANCH_W, BRANCH_W,
            N_BRANCH * D_MODEL)
N_IN = 11 * BRANCH_W + N_GATES + N_BRANCH * D_MODEL

kernel_name = "hybrid_gated_branch_encoder"


def _rmsnorm(x, g):
    x32 = x.astype(jnp.float32)
    y = x32 * lax.rsqrt(jnp.mean(x32 * x32, axis=-1, keepdims=True) + EPS)
    return (y * g.astype(jnp.float32)).astype(x.dtype)


def _layernorm(x, g, b):
    x32 = x.astype(jnp.float32)
    mu = jnp.mean(x32, axis=-1, keepdims=True)
    var = jnp.mean(jnp.square(x32 - mu), axis=-1, keepdims=True)
    y = (x32 - mu) * lax.rsqrt(var + EPS)
    return (y * g.astype(jnp.float32) + b.astype(jnp.float32)).astype(x.dtype)


def _spatial_gating(u, v, ln_g, ln_b, w_sp, b_sp):
    bsz, seq, _ = u.shape
    n_chunks = seq // CHUNK
    u = jax.nn.gelu(u)
    v = _layernorm(jax.nn.gelu(v), ln_g, ln_b)
    vg = v.reshape(bsz, n_chunks, CHUNK, A_GROUPS, HEAD_DIM)
    s = jnp.einsum("gts,bnsgc->bntgc", w_sp, vg) + b_sp.T[None, None, :, :, None]
    return u * s.reshape(bsz, seq, BRANCH_W)


def _neighbourhood_attention(q, k, v, rpb):
    bsz, seq, _ = q.shape
    rows = seq // GRID_W
    kr = min(NA_WIN_R, rows)

    def grid(t):
        return t.reshape(bsz, rows, GRID_W, NA_HEADS, HEAD_DIM).transpose(0, 3, 1, 2, 4)

    qg = grid(q) * (HEAD_DIM ** -0.5)
    kg, vg = grid(k), grid(v)
    r_ar = np.arange(rows)
    c_ar = np.arange(GRID_W)
    row_idx = np.clip(r_ar - kr // 2, 0, rows - kr)[:, None] + np.arange(kr)[None, :]
    col_idx = np.clip(c_ar - NA_WIN_C // 2, 0, GRID_W - NA_WIN_C)[:, None] + np.arange(NA_WIN_C)[None, :]
    k_rows = kg[:, :, row_idx]
    v_rows = vg[:, :, row_idx]
    band = jnp.einsum("bhrqd,bhrakd->bhraqk", qg, k_rows)
    q_sel = c_ar[:, None]
    scores = band[..., q_sel, col_idx].astype(jnp.float32)
    dr = row_idx - r_ar[:, None] + (NA_WIN_R - 1)
    dc = col_idx - c_ar[:, None] + (NA_WIN_C - 1)
    bias = rpb[:, dr[:, :, None, None], dc[None, None, :, :]]
    probs = jax.nn.softmax(scores + bias.astype(jnp.float32)[None], axis=(3, 5)).astype(v.dtype)
    p_band = jnp.zeros(band.shape, v.dtype).at[..., q_sel, col_idx].set(probs)
    out = jnp.einsum("bhraqk,bhrakd->bhrqd", p_band, v_rows)
    return out.transpose(0, 2, 3, 1, 4).reshape(bsz, seq, BRANCH_W)


def _mlstm_direction(q, k, v, log_i, log_f):
    bsz, seq, _ = q.shape
    n = seq // CHUNK

    def heads(t):
        return t.reshape(bsz, n, CHUNK, ML_HEADS, HEAD_DIM).transpose(0, 3, 1, 2, 4)

    def gate(t):
        return t.reshape(bsz, n, CHUNK, ML_HEADS).transpose(0, 3, 1, 2)

    qh, kh, vh = heads(q), heads(k) * (HEAD_DIM ** -0.5), heads(v)
    li, lf = gate(log_i), gate(log_f)
    b = jnp.cumsum(lf, axis=-1)
    b_last = b[..., -1]
    a = b_last[..., None] - b + li
    m_loc = jnp.max(a, axis=-1)
    w = jnp.exp(a - m_loc[..., None])
    s_c = jnp.einsum("bhnsv,bhnsk->bhnvk", vh * w[..., None], kh)
    s_n = jnp.einsum("bhns,bhnsk->bhnk", w, kh)

    def step(carry, xs):
        c_st, n_st, m_st = carry
        bl, ml, sc, sn = xs
        m_new = jnp.maximum(bl + m_st, ml)
        f_old = jnp.exp(bl + m_st - m_new)
        f_loc = jnp.exp(ml - m_new)
        c_new = f_old[..., None, None] * c_st + f_loc[..., None, None] * sc
        n_new = f_old[..., None] * n_st + f_loc[..., None] * sn
        return (c_new, n_new, m_new), (c_st, n_st, m_st)

    init = (jnp.zeros((bsz, ML_HEADS, HEAD_DIM, HEAD_DIM), jnp.float32),
            jnp.zeros((bsz, ML_HEADS, HEAD_DIM), jnp.float32),
            jnp.zeros((bsz, ML_HEADS), jnp.float32))
    xs = (jnp.moveaxis(b_last, 2, 0), jnp.moveaxis(m_loc, 2, 0),
          jnp.moveaxis(s_c, 2, 0), jnp.moveaxis(s_n, 2, 0))
    _, (c_prev, n_prev, m_prev) = lax.scan(step, init, xs)
    c_prev = jnp.moveaxis(c_prev, 0, 2)
    n_prev = jnp.moveaxis(n_prev, 0, 2)
    m_prev = jnp.moveaxis(m_prev, 0, 2)

    causal = np.tril(np.ones((CHUNK, CHUNK), dtype=bool))
    d_log = jnp.where(causal, b[..., :, None] - b[..., None, :] + li[..., None, :], -jnp.inf)
    m_inter = b + m_prev[..., None]
    m_t = jnp.maximum(m_inter, jnp.max(d_log, axis=-1))
    d_mat = jnp.exp(d_log - m_t[..., None])
    w_ts = jnp.einsum("bhntd,bhnsd->bhnts", qh, kh) * d_mat
    scale_inter = jnp.exp(m_inter - m_t)
    num = (jnp.einsum("bhnts,bhnsd->bhntd", w_ts, vh)
           + scale_inter[..., None] * jnp.einsum("bhnvk,bhntk->bhntv", c_prev, qh))
    den = jnp.sum(w_ts, axis=-1) + scale_inter * jnp.einsum("bhnk,bhntk->bhnt", n_prev, qh)
    h = num / jnp.maximum(jnp.abs(den), jnp.exp(-m_t))[..., None]
    return h.transpose(0, 2, 3, 1, 4).reshape(bsz, seq, ML_HEADS * HEAD_DIM)


def _mlstm_branch(q, k, v, o, gates, gate_b, hnorm_g):
    dt = q.dtype
    f32 = jnp.float32
    q32, k32, v32 = q.astype(f32), k.astype(f32), v.astype(f32)
    bsz, seq, _ = q.shape
    g = (gates.astype(f32) + gate_b.astype(f32)).reshape(bsz, seq, 4, ML_HEADS)
    li_f, lf_f = g[:, :, 0], jax.nn.log_sigmoid(g[:, :, 1])
    li_b, lf_b = g[:, :, 2], jax.nn.log_sigmoid(g[:, :, 3])
    h_fwd = _mlstm_direction(q32, k32, v32, li_f, lf_f)
    h_bwd = jnp.flip(_mlstm_direction(jnp.flip(q32, 1), jnp.flip(k32, 1), jnp.flip(v32, 1),
                                      jnp.flip(li_b, 1), jnp.flip(lf_b, 1)), 1)
    hh = (h_fwd + h_bwd).reshape(bsz, seq, ML_HEADS, HEAD_DIM)
    hh = hh * lax.rsqrt(jnp.mean(hh * hh, axis=-1, keepdims=True) + EPS)
    h = hh.reshape(bsz, seq, BRANCH_W) * hnorm_g.astype(f32)
    return (jax.nn.sigmoid(o.astype(f32)) * h).astype(dt)


def _conv_module(a, g, conv_w, conv_b, ln_g, ln_b):
    y = a * jax.nn.sigmoid(g)
    y = lax.conv_general_dilated(
        y, conv_w[:, None, :].astype(y.dtype), window_strides=(1,),
        padding=[(CONV_W // 2, CONV_W // 2)],
        dimension_numbers=("NWC", "WIO", "NWC"),
        feature_group_count=BRANCH_W) + conv_b
    return jax.nn.silu(_layernorm(y, ln_g, ln_b))


def _encoder_layer(x, c, w_ada, b_ada, g_mix, g_ffn, w_in, a_ln_g, a_ln_b, a_w_sp, a_b_sp,
                   b_rpb, c_gate_b, c_hnorm_g, d_conv_w, d_conv_b, d_ln_g, d_ln_b,
                   w_branch, w_out, w_ffn_in, w_ffn_out):
    bsz, seq, _ = x.shape
    mod = jax.nn.silu(c) @ w_ada + b_ada
    sh1, sc1, gt1, sh2, sc2, gt2 = [m[:, None, :] for m in jnp.split(mod, 6, axis=-1)]
    h = _rmsnorm(x, g_mix) * (1 + sc1) + sh1
    z = h @ w_in
    (a_u, a_v, b_q, b_k, b_v, c_q, c_k, c_v, c_o, c_g, d_a, d_g, br_gate) = jnp.split(
        z, np.cumsum(IN_SIZES)[:-1].tolist(), axis=-1)
    ys = (_spatial_gating(a_u, a_v, a_ln_g, a_ln_b, a_w_sp, a_b_sp),
          _neighbourhood_attention(b_q, b_k, b_v, b_rpb),
          _mlstm_branch(c_q, c_k, c_v, c_o, c_g, c_gate_b, c_hnorm_g),
          _conv_module(d_a, d_g, d_conv_w, d_conv_b, d_ln_g, d_ln_b))
    gates = jax.nn.sigmoid(br_gate.reshape(bsz, seq, N_BRANCH, D_MODEL))
    merged = gates[:, :, 0] * (ys[0] @ w_branch[0])
    for i in range(1, N_BRANCH):
        merged = merged + gates[:, :, i] * (ys[i] @ w_branch[i])
    x = x + gt1 * (merged @ w_out)
    h = _rmsnorm(x, g_ffn) * (1 + sc2) + sh2
    gp, up = jnp.split(h @ w_ffn_in, 2, axis=-1)
    x = x + gt2 * ((jax.nn.silu(gp) * up) @ w_ffn_out)
    return x


def setup_inputs(seed: int = 0) -> dict:
    key = jax.random.key(seed)
    ks = jax.random.split(key, 32)
    f32 = jnp.float32

    def nrm(k, shape, s):
        return jax.random.normal(k, shape, f32) * s

    L = DEPTH
    i_bias = nrm(ks[10], (L, 2, 1, ML_HEADS), 0.1)
    f_bias = 3.0 + 3.0 * jax.random.uniform(ks[11], (L, 2, 1, ML_HEADS), f32)
    c_gate_b = jnp.concatenate([i_bias, f_bias], axis=2).reshape(L, N_GATES)
    return {
        "x_prompt": nrm(ks[0], (BATCH, SEQ, D_MODEL), 1.0),
        "x_sample": nrm(ks[1], (DEC_BATCH, DEC_SEQ, D_MODEL), 1.0),
        "c_prompt": nrm(ks[2], (BATCH, D_MODEL), 1.0),
        "c_sample": nrm(ks[3], (DEC_BATCH, D_MODEL), 1.0),
        "w_ada": nrm(ks[4], (L, D_MODEL, 6 * D_MODEL), 0.5 * D_MODEL ** -0.5),
        "b_ada": nrm(ks[5], (L, 6 * D_MODEL), 0.01),
        "g_norm_mix": 1.0 + nrm(ks[6], (L, D_MODEL), 0.02),
        "g_norm_ffn": 1.0 + nrm(ks[7], (L, D_MODEL), 0.02),
        "w_in": nrm(ks[8], (L, D_MODEL, N_IN), D_MODEL ** -0.5),
        "a_ln_g": 1.0 + nrm(ks[9], (L, BRANCH_W), 0.02),
        "a_ln_b": nrm(ks[12], (L, BRANCH_W), 0.02),
        "a_w_sp": nrm(ks[13], (L, A_GROUPS, CHUNK, CHUNK), CHUNK ** -0.5),
        "a_b_sp": 1.0 + nrm(ks[14], (L, A_GROUPS, CHUNK), 0.02),
        "b_rpb": nrm(ks[15], (L, NA_HEADS, 2 * NA_WIN_R - 1, 2 * NA_WIN_C - 1), 0.1),
        "c_gate_b": c_gate_b,
        "c_hnorm_g": 1.0 + nrm(ks[16], (L, BRANCH_W), 0.02),
        "d_conv_w": nrm(ks[17], (L, CONV_W, BRANCH_W), CONV_W ** -0.5),
        "d_conv_b": nrm(ks[18], (L, BRANCH_W), 0.02),
        "d_ln_g": 1.0 + nrm(ks[19], (L, BRANCH_W), 0.02),
        "d_ln_b": nrm(ks[20], (L, BRANCH_W), 0.02),
        "w_branch": nrm(ks[21], (L, N_BRANCH, BRANCH_W, D_MODEL), BRANCH_W ** -0.5),
        "w_out": nrm(ks[22], (L, D_MODEL, D_MODEL), D_MODEL ** -0.5),
        "w_ffn_in": nrm(ks[23], (L, D_MODEL, 2 * D_FF), D_MODEL ** -0.5),
        "w_ffn_out": nrm(ks[24], (L, D_FF, D_MODEL), D_FF ** -0.5),
        "g_final": 1.0 + nrm(ks[25], (D_MODEL,), 0.02),
    }


def reference(x_prompt, x_sample, c_prompt, c_sample, w_ada, b_ada, g_norm_mix, g_norm_ffn,
              w_in, a_ln_g, a_ln_b, a_w_sp, a_b_sp, b_rpb, c_gate_b, c_hnorm_g,
              d_conv_w, d_conv_b, d_ln_g, d_ln_b, w_branch, w_out, w_ffn_in, w_ffn_out,
              g_final):
    def trunk(x, c):
        for l in range(DEPTH):
            x = _encoder_layer(x, c, w_ada[l], b_ada[l], g_norm_mix[l], g_norm_ffn[l], w_in[l],
                               a_ln_g[l], a_ln_b[l], a_w_sp[l], a_b_sp[l], b_rpb[l],
                               c_gate_b[l], c_hnorm_g[l], d_conv_w[l], d_conv_b[l],
                               d_ln_g[l], d_ln_b[l], w_branch[l], w_out[l],
                               w_ffn_in[l], w_ffn_out[l])
        return _rmsnorm(x, g_final)

    y_prompt = trunk(x_prompt, c_prompt)
    y_sample = trunk(x_sample, c_sample)
    return (y_prompt, y_sample)
```

```python
import os
import numpy as np
from contextlib import ExitStack
import concourse.bass as bass
import concourse.mybir as mybir
from concourse.bass_utils import run_bass_kernel_spmd

F32 = mybir.dt.float32
BF16 = mybir.dt.bfloat16
AF = mybir.ActivationFunctionType
ALU = mybir.AluOpType
AX = mybir.AxisListType

D = 1024
NIN = 6928
DFF = 2816
NEG = -30000.0
EPS = 1e-6
NSLOT = 43
CLS = {
    "INT": (0, [-2, -1, 0, 1, 2]),
    "TOP0": (5, [0, 1, 2, 3]),
    "TOP1": (9, [-1, 0, 1, 2]),
    "BOT1": (13, [-2, -1, 0, 1]),
    "BOT0": (17, [-3, -2, -1, 0]),
    "JA1": (21, [-2, -1, 0, 1, 2]),
    "JA0": (26, [-3, -2, -1, 0, 1, 2]),
    "JB0": (32, [-2, -1, 0, 1, 2, 3]),
    "JB1": (38, [-2, -1, 0, 1, 2]),
}


class Res:
    __slots__ = ("name", "w", "r", "dsem", "multi", "excl")

    def __init__(self, name, multi=False, excl=False):
        self.name = name
        self.multi = multi
        self.excl = excl
        self.w = []
        self.r = {}
        self.dsem = None


class Sem:
    __slots__ = ("h", "total", "is_dma", "name")

    def __init__(self, h, is_dma, name):
        self.h = h
        self.total = 0
        self.is_dma = is_dma
        self.name = name


class Eng:
    def __init__(self, name, h, sem):
        self.name = name
        self.h = h
        self.sem = sem
        self.waited = {}


class FW:
    def __init__(self, nc, stack):
        self.nc = nc
        self.stack = stack
        self.eng = {}
        self.sems = []
        for name, h in (("pe", nc.tensor), ("dve", nc.vector), ("act", nc.scalar),
                        ("pool", nc.gpsimd), ("sp", nc.sync)):
            s = Sem(stack.enter_context(nc.semaphore("s_" + name)), False, name)
            self.sems.append(s)
            self.eng[name] = Eng(name, h, s)
        self.ninst = 0
        self.uid = 0
        self.free_dsems = []
        self.phase_dsems = None

    def sbuf(self, name, shape, dt, stack=None):
        self.uid += 1
        return (stack or self.stack).enter_context(
            self.nc.sbuf_tensor("%s_%d" % (name, self.uid), list(shape), dt))

    def psum(self, name, shape, dt=F32, stack=None):
        self.uid += 1
        esz = 4 if dt == F32 else 2
        n = int(np.prod(shape[1:]))
        be = 2048 // esz
        nb = -(-n // be)
        t = (stack or self.stack).enter_context(
            self.nc.psum_tensor("%s_%d" % (name, self.uid), [128, nb * be], dt))
        v = t[:, 0:n]
        if len(shape) == 3:
            v = v.rearrange("p (a b) -> p a b", b=shape[2])
        return v

    def res(self, name, dma=False, multi=False, excl=False):
        r = Res(name, multi, excl)
        if dma:
            if not self.free_dsems:
                self.uid += 1
                h = self.stack.enter_context(self.nc.semaphore("d_%d" % self.uid))
                sm = Sem(h, True, name)
                self.sems.append(sm)
                self.free_dsems.append(sm)
            r.dsem = self.free_dsems.pop()
            if self.phase_dsems is not None:
                self.phase_dsems.append(r.dsem)
        return r

    def phase_begin(self):
        self.phase_dsems = []

    def phase_end(self):
        try:
            print("sbuf remaining", self.nc.sbuf_bytes_remaining, "ninst", self.ninst, flush=True)
        except Exception as ex:
            print("sbuf remaining ?", ex)
        self.barrier()
        self.free_dsems.extend(self.phase_dsems)
        self.phase_dsems = None

    def _need(self, e, deps):
        best = {}
        for s, v in deps:
            if s is e.sem:
                if e.name == "pe" or e.name == "sp":
                    continue
                if e.sem.total - v >= 2:
                    continue
            if s.is_dma:
                v = s.total
            if v > best.get(s, 0):
                best[s] = v
        for s, v in best.items():
            if e.waited.get(s, 0) >= v:
                continue
            e.h.wait_ge(s.h, v)
            e.waited[s] = v
            self.ninst += 1

    def _collect(self, reads, writes):
        deps = []
        for r in reads:
            deps.extend(r.w)
        for w in writes:
            if not w.multi:
                deps.extend(w.w)
            deps.extend(w.r.items())
        return deps

    def _record(self, sem, reads, writes):
        key = (sem, sem.total)
        for r in reads:
            r.r[sem] = sem.total
        for w in writes:
            if w.multi:
                w.w = [k for k in w.w if k[0] is not sem] + [key]
            else:
                w.w = [key]
                w.r = {}

    def op(self, ename, fn, reads=(), writes=()):
        e = self.eng[ename]
        xr = [r for r in reads if r.excl]
        if xr:
            reads = [r for r in reads if not r.excl]
            writes = list(writes) + xr
        self._need(e, self._collect(reads, writes))
        ins = fn(e.h)
        e.sem.total += 1
        ins.then_inc(e.sem.h, 1)
        self.ninst += 1
        self._record(e.sem, reads, writes)
        return ins

    def dma(self, qname, out, in_, reads=(), writes=(), dres=None, **kw):
        e = self.eng[qname]
        self._need(e, self._collect(reads, writes))
        ds = dres.dsem
        ins = e.h.dma_start(out=out, in_=in_, **kw)
        ds.total += 16
        ins.then_inc(ds.h, 16)
        self.ninst += 1
        self._record(ds, reads, writes)
        return ins

    def barrier(self):
        for e in self.eng.values():
            for s in self.sems:
                if s is e.sem or s.total == 0:
                    continue
                if e.waited.get(s, 0) >= s.total:
                    continue
                e.h.wait_ge(s.h, s.total)
                e.waited[s] = s.total
                self.ninst += 1


class Builder:
    def __init__(self, UL, L, last_is_final=True):
        self.UL = UL
        self.L = L
        self.NT = 3 * UL
        self.NBK = UL // 512
        self.NCH = UL // 128
        self.NB2 = UL // 128
        assert self.NB2 >= 8

    def declare(self, nc):
        L, NT = self.L, self.NT
        di = lambda n, s, dt=F32: nc.dram_tensor(n, list(s), dt, kind="ExternalInput").ap()
        dbg = getattr(self, "debug", False)
        dx = lambda n, s, dt=F32: nc.dram_tensor(n, list(s), dt, kind=("ExternalOutput" if dbg else "Internal")).ap()
        I = {}
        I["xT"] = di("xT", [D, NT])
        I["cT"] = di("cT", [128, 8, 3])
        I["link"] = di("link", [128, 1])
        I["w_ada"] = di("w_ada", [L, D, 6 * D])
        I["b_adaT"] = di("b_adaT", [128, L, 48])
        I["gvec"] = di("gvec", [128, L, 2, 8])
        I["w_in"] = di("w_in", [L, D, NIN])
        I["a_ln"] = di("a_ln", [L, 512])
        I["a_w_spT"] = di("a_w_spT", [L, 128, 4, 128])
        I["a_b_sp"] = di("a_b_sp", [128, L, 4])
        I["rpbt"] = di("rpbt", [L, 128, NSLOT * 4, 128])
        I["c_gate_b"] = di("c_gate_b", [L, 16])
        I["c_hnorm"] = di("c_hnorm", [L, 256])
        I["d_conv_wT"] = di("d_conv_wT", [128, L, 2, 31])
        I["d_vec"] = di("d_vec", [128, L, 3, 2])
        I["w_branch"] = di("w_branch", [L, 1024, D])
        I["w_out"] = di("w_out", [L, D, D])
        I["w_ffn_in"] = di("w_ffn_in", [L, D, 2 * DFF])
        I["w_ffn_out"] = di("w_ffn_out", [L, DFF, D])
        I["g_finalT"] = di("g_finalT", [128, 8])
        I["c_ident"] = di("c_ident", [128, 128])
        I["c_triu"] = di("c_triu", [128, 128])
        I["c_tril"] = di("c_tril", [128, 128])
        I["c_sel"] = di("c_sel", [128, 2, 128])
        self.I = I
        self.yT = nc.dram_tensor("yT", [D, NT], F32, kind="ExternalOutput").ap()
        S = {}
        S["xm"] = dx("xm", [D, NT])
        S["xn"] = dx("xn", [D, NT])
        S["yall"] = dx("yall", [D, NT], BF16)
        S["bq"] = dx("bq", [256, NT], BF16)
        S["bk"] = dx("bk", [256, NT], BF16)
        S["bv"] = dx("bv", [NT, 256], BF16)
        S["cq"] = dx("cq", [256, NT], BF16)
        S["ck"] = dx("ck", [256, NT], BF16)
        S["ckt"] = dx("ckt", [NT, 256], BF16)
        S["cvt"] = dx("cvt", [NT, 256])
        S["co"] = dx("co", [NT, 256])
        S["cg"] = dx("cg", [NT, 16])
        S["chf"] = dx("chf", [NT, 256])
        S["dy"] = dx("dy", [256, NT])
        self.S = S

    def build(self):
        nc = bass.Bass("TRN2", target_bir_lowering=False)
        self.nc = nc
        self.declare(nc)
        with ExitStack() as st:
            fw = FW(nc, st)
            self.fw = fw
            self.R = {k: fw.res(k, multi=True) for k in list(self.S) + ["xT", "yT"]}
            self.setup_consts(st)
            self.ensure_eps()
            self.phase_mod()
            import os
            ph = os.environ.get("KPHASES", "p1,attn,mlstm,conv,p3a,p3b").split(",")
            for l in range(self.L):
                for p in ("p1", "attn", "mlstm", "conv", "p3a", "p3b"):
                    if p in ph:
                        getattr(self, "phase_" + p)(l)
            fw.barrier()
            self.ninst = fw.ninst
        return nc

    def ld(self, tile_ap, dram_ap, res, dram_res=None, q="sp"):
        reads = [dram_res] if dram_res is not None else []
        self.fw.dma(q, tile_ap, dram_ap, reads=reads, writes=[res], dres=res)

    def stt(self, dram_ap, tile_ap, res, dram_res, q=None):
        import os
        q = q or os.environ.get("KSTQ", "sp")
        self.fw.dma(q, dram_ap, tile_ap, reads=[res], writes=[dram_res], dres=res)

    def setup_consts(self, st):
        fw, I = self.fw, self.I
        L = self.L
        C = {}

        def cload(name, shape, src, dt=F32):
            t = fw.sbuf(name, shape, dt)
            r = fw.res(name, dma=True)
            self.ld(t[:], src, r)
            C[name] = (t, r)
            return t, r

        cload("identf", [128, 128], I["c_ident"])
        cload("triu", [128, 128], I["c_triu"])
        cload("tril", [128, 128], I["c_tril"])
        cload("sel", [128, 2, 128], I["c_sel"])
        cload("link", [128, 1], I["link"])
        cload("cT", [128, 8, 3], I["cT"])
        cload("b_adaT", [128, L, 48], I["b_adaT"])
        cload("gvec", [128, L, 2, 8], I["gvec"])
        cload("a_b_sp", [128, L, 4], I["a_b_sp"])
        cload("d_conv_wT", [128, L, 2, 31], I["d_conv_wT"])
        cload("d_vec", [128, L, 3, 2], I["d_vec"])
        cload("g_finalT", [128, 8], I["g_finalT"])
        identb = fw.sbuf("identb", [128, 128], BF16)
        ridb = fw.res("identb")
        fw.op("dve", lambda e: e.tensor_copy(identb[:], C["identf"][0][:]), reads=[C["identf"][1]], writes=[ridb])
        C["identb"] = (identb, ridb)
        onesf = fw.sbuf("onesf", [128, 128], F32)
        ronesf = fw.res("onesf")
        fw.op("dve", lambda e: e.memset(onesf[:], 1.0), writes=[ronesf])
        C["onesf"] = (onesf, ronesf)
        onesb = fw.sbuf("onesb", [128, 128], BF16)
        ronesb = fw.res("onesb")
        fw.op("dve", lambda e: e.memset(onesb[:], 1.0), writes=[ronesb])
        C["onesb"] = (onesb, ronesb)
        self.C = C
        self.mod = fw.sbuf("mod", [128, L, 6, 8, 3], F32)
        self.rmod = fw.res("mod")
        self.gm = fw.sbuf("gm", [128, L, 2, 8, 3], F32)
        self.rgm = fw.res("gm")

    def phase_mod(self):
        fw, I, C = self.fw, self.I, self.C
        L = self.L
        fw.phase_begin()
        with ExitStack() as ps:
            sc = fw.sbuf("silu_c", [128, 8, 3], F32, ps)
            rsc = fw.res("silu_c")
            cT, rcT = C["cT"]
            fw.op("act", lambda e: e.activation(sc[:], cT[:], AF.Silu), reads=[rcT], writes=[rsc])
            wst = [(fw.sbuf("wada", [128, 8, 1024], F32, ps), fw.res("wada", dma=True)) for _ in range(2)]
            pm = [(fw.psum("pmod", [128, 8, 4], F32, ps), fw.res("pmod", excl=True)) for _ in range(2)]
            it = 0
            badaT, rbada = C["b_adaT"]
            for l in range(L):
                for m in range(6):
                    wt, rw = wst[it % 2]
                    pt, rp = pm[it % 2]
                    it += 1
                    src = I["w_ada"][l, :, m * 1024:(m + 1) * 1024].rearrange("(k p) n -> p k n", p=128)
                    for k2 in range(2):
                        self.ld(wt[:, 4 * k2:4 * k2 + 4, :], src[:, 4 * k2:4 * k2 + 4, :], rw)
                    for f in range(8):
                        for k in range(8):
                            fw.op("pe", lambda e, f=f, k=k: e.matmul(pt[:, f, 0:3], wt[:, k, f * 128:(f + 1) * 128], sc[:, k, :],
                                                                     start=(k == 0), stop=(k == 7)),
                                  reads=[rw, rsc], writes=[rp])
                    for f in range(8):
                        fw.op("dve", lambda e, f=f: e.tensor_scalar(self.mod[:, l, m, f, :], pt[:, f, 0:3],
                                                                    badaT[:, l, m * 8 + f:m * 8 + f + 1], None, ALU.add),
                              reads=[rp, rbada], writes=[self.rmod])
            gvec, rg = C["gvec"]
            for l in range(L):
                for j, m in ((0, 1), (1, 4)):
                    for u in range(3):
                        fw.op("dve", lambda e, l=l, j=j, m=m, u=u: e.scalar_tensor_tensor(
                            self.gm[:, l, j, :, u], self.mod[:, l, m, :, u], 1.0, gvec[:, l, j, :], ALU.add, ALU.mult),
                            reads=[self.rmod, rg], writes=[self.rgm])
        fw.phase_end()

    def load_cast(self, dst, rdst, k0, nk, src_rows, ncols, stg, eng_cycle, CW):
        fw = self.fw
        for k in range(nk):
            for c0 in range(0, ncols, CW):
                cw = min(CW, ncols - c0)
                st_t, st_r = stg[self.stg_i % len(stg)]
                self.stg_i += 1
                self.ld(st_t[:, 0:cw], src_rows[k * 128:(k + 1) * 128, c0:c0 + cw], st_r)
                en = eng_cycle[self.stg_i % len(eng_cycle)]
                if en == "act":
                    fw.op("act", lambda e: e.copy(dst[:, k0 + k, c0:c0 + cw], st_t[:, 0:cw]), reads=[st_r], writes=[rdst])
                else:
                    fw.op(en, lambda e: e.tensor_copy(dst[:, k0 + k, c0:c0 + cw], st_t[:, 0:cw]), reads=[st_r], writes=[rdst])

    def norm_mod(self, xb, rxb, hT, rhT, sq, rsq, pst, rpst, rstd, rrstd, l, j, u, tmp, rtmp):
        fw = self.fw
        onesf, ronesf = self.C["onesf"]
        for k in range(8):
            sqk, rsqk = sq[k % 2], rsq[k % 2]
            fw.op("act", lambda e, k=k: e.activation(sqk[:], xb[:, k, :], AF.Square), reads=[rxb], writes=[rsqk])
            fw.op("pe", lambda e, k=k: e.matmul(pst[:], onesf[:], sqk[:], start=(k == 0), stop=(k == 7)),
                  reads=[rsqk, ronesf], writes=[rpst])
        fw.op("act", lambda e: e.activation(rstd[:], pst[:], AF.Sqrt, scale=1.0 / D, bias=self.eps_ap()),
              reads=[rpst, self.reps], writes=[rrstd])
        fw.op("dve", lambda e: e.reciprocal(rstd[:], rstd[:]), reads=[rrstd], writes=[rrstd])
        m_sh = 0 if j == 0 else 3
        for k in range(8):
            fw.op("dve", lambda e, k=k: e.tensor_tensor(tmp[:], xb[:, k, :], rstd[:], ALU.mult),
                  reads=[rxb, rrstd], writes=[rtmp])
            fw.op("dve", lambda e, k=k: e.tensor_scalar(hT[:, k, :], tmp[:], self.gm[:, l, j, k, u:u + 1],
                                                        self.mod[:, l, m_sh, k, u:u + 1], ALU.mult, ALU.add),
                  reads=[rtmp, self.rgm, self.rmod], writes=[rhT])

    def eps_ap(self):
        return self.epst[:, 0:1]

    def ensure_eps(self):
        if getattr(self, "epst", None) is None:
            fw = self.fw
            self.epst = fw.sbuf("epst", [128, 2], F32)
            self.reps = fw.res("epst")
            fw.op("dve", lambda e: e.memset(self.epst[:, 0:1], EPS), writes=[self.reps])
            fw.op("dve", lambda e: e.memset(self.epst[:, 1:2], 1.0), writes=[self.reps])

    def phase_p1(self, l):
        fw, I, S, R, C = self.fw, self.I, self.S, self.R, self.C
        self.ensure_eps()
        xsrc, rxsrc = (I["xT"], R["xT"]) if l == 0 else (S["xn"], R["xn"])
        fw.phase_begin()
        with ExitStack() as ps:
            sb = lambda n, s, dt=F32: fw.sbuf(n, s, dt, ps)
            W = sb("w1", [128, 8, 2832], BF16)
            rW = fw.res("w1")
            stg = [(sb("stg", [128, 1416], F32), fw.res("stg", dma=True)) for _ in range(3)]
            self.stg_i = 0
            self.load_cast(W, rW, 0, 8, I["w_in"][l], 2832, stg, ["pool", "dve", "act"], 1416)
            wsp = sb("wsp", [128, 4, 128], BF16)
            rwsp = fw.res("wsp")
            wspf = sb("wspf", [128, 4, 128], F32)
            rwspf = fw.res("wspf", dma=True)
            self.ld(wspf[:], I["a_w_spT"][l], rwspf)
            fw.op("pool", lambda e: e.tensor_copy(wsp[:], wspf[:]), reads=[rwspf], writes=[rwsp])
            aln = sb("aln", [128, 512], F32)
            raln = fw.res("aln", dma=True)
            self.ld(aln[:], I["a_ln"][l].partition_broadcast(128), raln)
            absp, rabsp = C["a_b_sp"]
            identb, ridb = C["identb"]
            xb = [(sb("xb", [128, 8, 512]), fw.res("xb", dma=True)) for _ in range(2)]
            hT = sb("hT", [128, 8, 512], BF16); rhT = fw.res("hT")
            sq = [sb("sq", [128, 512]) for _ in range(2)]; rsq = [fw.res("sq") for _ in range(2)]
            rstd = sb("rstd", [128, 512]); rrstd = fw.res("rstd")
            tmp = sb("tmp", [128, 512]); rtmp = fw.res("tmp")
            pst = fw.psum("pst", [128, 512], F32, ps); rpst = fw.res("pst", excl=True)
            pfm = [(fw.psum("pfm", [128, 512], F32, ps), fw.res("pfm", excl=True)) for _ in range(2)]
            ptm = [(fw.psum("ptm", [128, 512], F32, ps), fw.res("ptm", excl=True)) for _ in range(2)]
            psA = fw.psum("psA", [128, 256], F32, ps); rpsA = fw.res("psA", excl=True)
            ptr = fw.psum("ptr", [128, 2, 128], BF16, ps); rptr = fw.res("ptr", excl=True)
            ofm = [(sb("ofm", [128, 2, 512], BF16), fw.res("ofm", dma=True)) for _ in range(3)]
            ofd = [(sb("ofd", [128, 2, 512], F32), fw.res("ofd", dma=True)) for _ in range(2)]
            sgd = sb("sgd", [128, 512]); rsgd = fw.res("sgd")
            otm = [(sb("otm", [128, 4, 256], BF16), fw.res("otm", dma=True)) for _ in range(4)]
            oco = [(sb("oco", [128, 4, 256], F32), fw.res("oco", dma=True)) for _ in range(2)]
            ocv = [(sb("ocv", [128, 4, 256], F32), fw.res("ocv", dma=True)) for _ in range(2)]
            ocg = [(sb("ocg", [128, 4, 16], F32), fw.res("ocg", dma=True)) for _ in range(2)]
            yA = [(sb("yA", [128, 2, 512], BF16), fw.res("yA", dma=True)) for _ in range(2)]
            g1 = sb("g1", [128, 512]); rg1 = fw.res("g1")
            g2 = sb("g2", [128, 512]); rg2 = fw.res("g2")
            gu = sb("gu", [128, 512]); rgu = fw.res("gu")
            vc = sb("vc", [128, 256]); rvc = fw.res("vc")
            vn = sb("vn", [128, 256], BF16); rvn = fw.res("vn")
            st1 = sb("st1", [128, 4]); rst1 = fw.res("st1")
            junk = sb("junk", [128, 256]); rjunk = fw.res("junk")
            yAt = sb("yAt", [128, 256], BF16); ryAt = fw.res("yAt")
            nfm = ntm = nofm = 0
            for u in range(3):
                for b in range(self.NBK):
                    t0 = u * self.UL + b * 512
                    gi = u * self.NBK + b
                    xt, rx = xb[gi % 2]
                    self.ld(xt[:], xsrc[:, t0:t0 + 512].rearrange("(k p) t -> p k t", p=128), rx, rxsrc)
                    import os
                    cut = int(os.environ.get("KCUT", "99"))
                    if cut < 1:
                        continue
                    self.norm_mod(xt, rx, hT, rhT, sq, rsq, pst, rpst, rstd, rrstd, l, 0, u, tmp, rtmp)
                    if cut < 2:
                        continue
                    fm_jobs = [(512, "bq"), (768, "bk"), (1280, "cq"), (1536, "ck")]
                    for col0, name in fm_jobs:
                        ot, ro = ofm[nofm % 3]
                        nofm += 1
                        for c in range(2):
                            pt, rp = pfm[nfm % 2]
                            nfm += 1
                            for k in range(8):
                                fw.op("pe", lambda e, k=k, c=c: e.matmul(pt[:], W[:, k, col0 + c * 128:col0 + (c + 1) * 128], hT[:, k, :],
                                                                          start=(k == 0), stop=(k == 7)),
                                      reads=[rW, rhT], writes=[rp])
                            scale = 0.125 if name in ("bq", "cq") else 1.0
                            fw.op("act", lambda e, c=c: e.activation(ot[:, c, :], pt[:], AF.Identity, scale=scale),
                                  reads=[rp], writes=[ro])
                        self.stt(S[name][:, t0:t0 + 512].rearrange("(c p) t -> p c t", p=128), ot[:], ro, R[name])
                    if cut < 3:
                        continue
                    od, rod = ofd[gi % 2]
                    for c in range(2):
                        pa, rpa = pfm[nfm % 2]
                        nfm += 1
                        pg, rpg = pfm[nfm % 2]
                        nfm += 1
                        for k in range(8):
                            fw.op("pe", lambda e, k=k, c=c: e.matmul(pa[:], W[:, k, 2320 + c * 128:2320 + (c + 1) * 128], hT[:, k, :],
                                                                      start=(k == 0), stop=(k == 7)), reads=[rW, rhT], writes=[rpa])
                        for k in range(8):
                            fw.op("pe", lambda e, k=k, c=c: e.matmul(pg[:], W[:, k, 2576 + c * 128:2576 + (c + 1) * 128], hT[:, k, :],
                                                                      start=(k == 0), stop=(k == 7)), reads=[rW, rhT], writes=[rpg])
                        fw.op("act", lambda e: e.activation(sgd[:], pg[:], AF.Sigmoid), reads=[rpg], writes=[rsgd])
                        fw.op("dve", lambda e, c=c: e.tensor_tensor(od[:, c, :], pa[:], sgd[:], ALU.mult),
                              reads=[rpa, rsgd], writes=[rod])
                    self.stt(S["dy"][:, t0:t0 + 512].rearrange("(c p) t -> p c t", p=128), od[:], rod, R["dy"])
                    if cut < 4:
                        continue
                    obv, robv = otm[(2 * gi) % 4]
                    okt, rokt = otm[(2 * gi + 1) % 4]
                    ovt, rovt = ocv[gi % 2]
                    oo, roo = oco[gi % 2]
                    og, rog = ocg[gi % 2]
                    ya, rya = yA[gi % 2]
                    for ch in range(4):
                        ts_ = slice(ch * 128, (ch + 1) * 128)

                        def tm_mm(col0, ncol):
                            nonlocal ntm
                            pt, rp = ptm[ntm % 2]
                            ntm += 1
                            for k in range(8):
                                fw.op("pe", lambda e, k=k: e.matmul(pt[:, 0:ncol], hT[:, k, ts_], W[:, k, col0:col0 + ncol],
                                                                    start=(k == 0), stop=(k == 7)), reads=[rW, rhT], writes=[rp])
                            return pt, rp
                        skip = os.environ.get("KSKIP", "").split(",")
                        if "tm" in skip:
                            continue
                        if "bv" not in skip:
                            pt, rp = tm_mm(1024, 256)
                            if "bve" not in skip:
                                fw.op("act", lambda e: e.copy(obv[:, ch, :], pt[:, 0:256]), reads=[rp], writes=[robv])
                        if "ckv" not in skip:
                            pt, rp = tm_mm(1536, 512)
                            if "e1" not in skip:
                                fw.op("act", lambda e: e.copy(okt[:, ch, :], pt[:, 0:256]), reads=[rp], writes=[rokt])
                            if "e2" not in skip:
                                fw.op("dve", lambda e: e.tensor_copy(ovt[:, ch, :], pt[:, 256:512]), reads=[rp], writes=[rovt])
                        if "og" in skip:
                            continue
                        pt, rp = tm_mm(2048, 272)
                        fw.op("act", lambda e: e.activation(oo[:, ch, :], pt[:, 0:256], AF.Sigmoid), reads=[rp], writes=[roo])
                        fw.op("dve", lambda e: e.tensor_copy(og[:, ch, :], pt[:, 256:272]), reads=[rp], writes=[rog])
                        if cut < 5:
                            continue
                        pt, rp = tm_mm(0, 512)
                        fw.op("act", lambda e: e.activation(g1[:], pt[:], AF.Square), reads=[rp], writes=[rg1])
                        fw.op("dve", lambda e: e.tensor_scalar(g1[:], g1[:], 0.044715, 1.0, ALU.mult, ALU.add), reads=[rg1], writes=[rg1])
                        fw.op("dve", lambda e: e.tensor_tensor(g2[:], g1[:], pt[:], ALU.mult), reads=[rg1, rp], writes=[rg2])
                        fw.op("act", lambda e: e.activation(g2[:], g2[:], AF.Sigmoid, scale=1.5957691216), reads=[rg2], writes=[rg2])
                        fw.op("dve", lambda e: e.tensor_tensor(gu[:], g2[:], pt[:], ALU.mult), reads=[rg2, rp], writes=[rgu])
                        fw.op("dve", lambda e: e.reduce_sum(st1[:, 0:1], gu[:, 256:512], AX.X), reads=[rgu], writes=[rst1])
                        fw.op("dve", lambda e: e.tensor_scalar(st1[:, 1:2], st1[:, 0:1], 1.0 / 256, None, ALU.mult), reads=[rst1], writes=[rst1])
                        fw.op("dve", lambda e: e.tensor_scalar(vc[:], gu[:, 256:512], st1[:, 1:2], None, ALU.subtract),
                              reads=[rgu, rst1], writes=[rvc])
                        fw.op("act", lambda e: e.activation(junk[:], vc[:], AF.Square, accum_out=st1[:, 2:3]),
                              reads=[rvc], writes=[rjunk, rst1])
                        fw.op("act", lambda e: e.activation(st1[:, 3:4], st1[:, 2:3], AF.Sqrt, scale=1.0 / 256, bias=self.eps_ap()),
                              reads=[rst1, self.reps], writes=[rst1])
                        fw.op("dve", lambda e: e.reciprocal(st1[:, 3:4], st1[:, 3:4]), reads=[rst1], writes=[rst1])
                        fw.op("dve", lambda e: e.scalar_tensor_tensor(vc[:], vc[:], st1[:, 3:4], aln[:, 0:256], ALU.mult, ALU.mult),
                              reads=[rvc, rst1, raln], writes=[rvc])
                        fw.op("dve", lambda e: e.tensor_tensor(vn[:], vc[:], aln[:, 256:512], ALU.add), reads=[rvc, raln], writes=[rvn])
                        for g in range(4):
                            fw.op("pe", lambda e, g=g: e.matmul(psA[:, g * 64:(g + 1) * 64], wsp[:, g, :], vn[:, g * 64:(g + 1) * 64],
                                                                start=True, stop=True), reads=[rwsp, rvn], writes=[rpsA])
                        for g in range(4):
                            fw.op("dve", lambda e, g=g: e.scalar_tensor_tensor(yAt[:, g * 64:(g + 1) * 64], psA[:, g * 64:(g + 1) * 64],
                                                                               absp[:, l, g:g + 1], gu[:, g * 64:(g + 1) * 64],
                                                                               ALU.add, ALU.mult),
                                  reads=[rpsA, rabsp, rgu], writes=[ryAt])
                        for c in range(2):
                            fw.op("pe", lambda e, c=c: e.transpose(ptr[:, c, :], yAt[:, c * 128:(c + 1) * 128], identb[:]),
                                  reads=[ryAt, ridb], writes=[rptr])
                        fw.op("act", lambda e: e.copy(ya[:, :, ts_], ptr[:]), reads=[rptr], writes=[rya])
                    tsl = slice(t0, t0 + 512)
                    if "st" in os.environ.get("KSKIP", "").split(","):
                        continue
                    self.stt(S["bv"][tsl, :].rearrange("(c p) f -> p c f", p=128), obv[:], robv, R["bv"])
                    self.stt(S["ckt"][tsl, :].rearrange("(c p) f -> p c f", p=128), okt[:], rokt, R["ckt"])
                    self.stt(S["cvt"][tsl, :].rearrange("(c p) f -> p c f", p=128), ovt[:], rovt, R["cvt"])
                    self.stt(S["co"][tsl, :].rearrange("(c p) f -> p c f", p=128), oo[:], roo, R["co"])
                    self.stt(S["cg"][tsl, :].rearrange("(c p) f -> p c f", p=128), og[:], rog, R["cg"])
                    self.stt(S["yall"][0:256, tsl].rearrange("(c p) t -> p c t", p=128), ya[:], rya, R["yall"])
            fw.phase_end()

    def attn_plan(self):
        NB2 = self.NB2
        plan = []
        for u in range(3):
            for b in range(NB2):
                if u == 2 or (u == 0 and b < NB2 - 2) or (u == 1 and b >= 2):
                    if b == 0 and u != 1:
                        cls = "TOP0"
                    elif b == 1 and u != 1:
                        cls = "TOP1"
                    elif b == NB2 - 2 and u != 0:
                        cls = "BOT1"
                    elif b == NB2 - 1 and u != 0:
                        cls = "BOT0"
                    else:
                        cls = "INT"
                elif u == 0:
                    cls = "JA1" if b == NB2 - 2 else "JA0"
                else:
                    cls = "JB0" if b == 0 else "JB1"
                s0, offs = CLS[cls]
                ents = []
                for i, o in enumerate(offs):
                    kp = b + o
                    ku = u
                    if kp >= NB2:
                        ku, kp = u + 1, kp - NB2
                    elif kp < 0:
                        ku, kp = u - 1, kp + NB2
                    assert 0 <= ku <= 2 and (ku == u or (u, ku) in ((0, 1), (1, 0)))
                    ents.append((ku, kp, s0 + i))
                plan.append((u, b, ents))
        return plan

    def phase_attn(self, l):
        fw, I, S, R, C = self.fw, self.I, self.S, self.R, self.C
        UL, NB2 = self.UL, self.NB2
        identb, ridb = C["identb"]
        fw.phase_begin()
        with ExitStack() as ps:
            sb = lambda n, s, dt=F32: fw.sbuf(n, s, dt, ps)
            bt = sb("bt", [128, NSLOT * 4, 128], BF16); rbt = fw.res("bt")
            stg = [(sb("bstg", [128, 8, 128], F32), fw.res("bstg", dma=True)) for _ in range(2)]
            n = 0
            for s0 in range(0, NSLOT * 4, 8):
                s1 = min(s0 + 8, NSLOT * 4)
                stt_, rs = stg[n % 2]
                n += 1
                self.ld(stt_[:, 0:s1 - s0, :], I["rpbt"][l, :, s0:s1, :], rs)
                fw.op("pool", lambda e, s0=s0, s1=s1, stt_=stt_: e.tensor_copy(bt[:, s0:s1, :], stt_[:, 0:s1 - s0, :]),
                      reads=[rs], writes=[rbt])
            kT = sb("kT", [128, 2, 2 * UL], BF16); rkT = fw.res("kT", dma=True)
            V = sb("V", [128, 2 * self.NCH, 4, 65], BF16); rV = fw.res("V", dma=True)
            fw.op("pool", lambda e: e.memset(V[:], 1.0), writes=[rV])
            qT = [(sb("qT", [128, 2, 512], BF16), fw.res("qT", dma=True)) for _ in range(2)]
            pS = [(fw.psum("pS", [128, 8, 128], F32, ps), fw.res("pS", excl=True)) for _ in range(2)]
            pO = [(fw.psum("pO", [128, 4, 65], F32, ps), fw.res("pO", excl=True)) for _ in range(2)]
            ptr = fw.psum("ptrb", [128, 2, 128], BF16, ps); rptr = fw.res("ptrb", excl=True)
            PT = [(sb("PT", [128, 8, 128], BF16), fw.res("PT")) for _ in range(3)]
            rc = sb("rc", [128, 4]); rrc = fw.res("rc")
            yB = sb("yB", [128, 4, 64], BF16); ryB = fw.res("yB")
            yo = [(sb("yBo", [128, 2, 512], BF16), fw.res("yBo", dma=True)) for _ in range(2)]
            plan = self.attn_plan()
            nps = npt = nq = 0
            for grp in (0, 2):
                units = (0, 1) if grp == 0 else (2,)
                base = grp * UL
                ntok = len(units) * UL
                self.ld(kT[:, :, 0:ntok], S["bk"][:, base:base + ntok].rearrange("(c p) t -> p c t", p=128), rkT, R["bk"])
                for h in range(4):
                    self.ld(V[:, 0:ntok // 128, h, 0:64],
                            S["bv"][base:base + ntok, h * 64:(h + 1) * 64].rearrange("(c p) d -> p c d", p=128),
                            rV, R["bv"])
                for (u, b, ents) in plan:
                    if u not in units:
                        continue
                    if b % 4 == 0:
                        qt, rq = qT[nq % 2]
                        yot, ryo = yo[nq % 2]
                        nq += 1
                        tq0 = u * UL + b * 128
                        self.ld(qt[:], S["bq"][:, tq0:tq0 + 512].rearrange("(c p) t -> p c t", p=128), rq, R["bq"])
                    qs = slice((b % 4) * 128, (b % 4 + 1) * 128)
                    po, rpo = pO[(nps // 4) % 2]
                    ne = len(ents)
                    for h in range(4):
                        hp = slice((h % 2) * 64, (h % 2) * 64 + 64)
                        hc = h // 2
                        pst_, rps = pS[nps % 2]
                        nps += 1
                        for i, (ku, kp, slot) in enumerate(ents):
                            k0 = (ku * UL - base) + kp * 128
                            fw.op("pe", lambda e, i=i, k0=k0: e.matmul(pst_[:, i, :], kT[hp, hc, k0:k0 + 128], qt[hp, hc, qs],
                                                                        start=True, stop=False), reads=[rkT, rq], writes=[rps])
                            fw.op("pe", lambda e, i=i, slot=slot: e.matmul(pst_[:, i, :], identb[:], bt[:, slot * 4 + h, :],
                                                                            start=False, stop=True), reads=[ridb, rbt], writes=[rps])
                        pt_, rpt = PT[npt % 3]
                        npt += 1
                        n1 = min(ne, 4)
                        fw.op("act", lambda e: e.activation(pt_[:, 0:n1, :], pst_[:, 0:n1, :], AF.Exp), reads=[rps], writes=[rpt])
                        if ne > 4:
                            fw.op("act", lambda e: e.activation(pt_[:, 4:ne, :], pst_[:, 4:ne, :], AF.Exp), reads=[rps], writes=[rpt])
                        for i, (ku, kp, slot) in enumerate(ents):
                            vc_ = (ku * UL - base) // 128 + kp
                            fw.op("pe", lambda e, i=i, vc_=vc_: e.matmul(po[:, h, :], pt_[:, i, :], V[:, vc_, h, :],
                                                                          start=(i == 0), stop=(i == ne - 1)),
                                  reads=[rpt, rV], writes=[rpo])
                    fw.op("dve", lambda e: e.reciprocal(rc[:], po[:, :, 64]), reads=[rpo], writes=[rrc])
                    fw.op("dve", lambda e: e.tensor_tensor(yB[:], po[:, :, 0:64], rc[:].unsqueeze(2).to_broadcast([128, 4, 64]), ALU.mult),
                          reads=[rpo, rrc], writes=[ryB])
                    for c in range(2):
                        fw.op("pe", lambda e, c=c: e.transpose(ptr[:, c, :], yB[:, 2 * c:2 * c + 2, :], identb[:]),
                              reads=[ryB, ridb], writes=[rptr])
                    fw.op("act", lambda e: e.copy(yot[:, :, qs], ptr[:]), reads=[rptr], writes=[ryo])
                    if b % 4 == 3:
                        self.stt(S["yall"][256:512, tq0:tq0 + 512].rearrange("(c p) t -> p c t", p=128), yot[:], ryo, R["yall"])
            fw.phase_end()

    def phase_mlstm(self, l):
        fw, I, S, R, C = self.fw, self.I, self.S, self.R, self.C
        UL, NCH = self.UL, self.NCH
        identb, ridb = C["identb"]
        onesf, ronesf = C["onesf"]
        triu, rtriu = C["triu"]
        tril, rtril = C["tril"]
        sel, rsel = C["sel"]
        link, rlink = C["link"]
        fw.phase_begin()
        with ExitStack() as ps:
            sb = lambda n, s, dt=F32: fw.sbuf(n, s, dt, ps)
            gb = sb("gb", [128, 16]); rgb = fw.res("gb", dma=True)
            self.ld(gb[:], I["c_gate_b"][l].partition_broadcast(128), rgb)
            hng = sb("hng", [128, 256]); rhng = fw.res("hng", dma=True)
            self.ld(hng[:], I["c_hnorm"][l].partition_broadcast(128), rhng)
            NBUF = 2
            qTb = [(sb("cqT", [128, 2, 512], BF16), fw.res("cqT", dma=True)) for _ in range(NBUF)]
            kTb = [(sb("ckT", [128, 2, 512], BF16), fw.res("ckT", dma=True)) for _ in range(NBUF)]
            ktb = [(sb("ckt", [128, 4, 256], BF16), fw.res("ckt", dma=True)) for _ in range(NBUF)]
            vtb = [(sb("cvt", [128, 4, 4, 65], F32), fw.res("cvt", dma=True)) for _ in range(NBUF)]
            for vt_, rv_ in vtb:
                fw.op("pool", lambda e, vt_=vt_: e.memset(vt_[:], 1.0), writes=[rv_])
            gtb = [(sb("cgt", [128, 4, 16]), fw.res("cgt", dma=True)) for _ in range(NBUF)]
            otb = [(sb("cot", [128, 4, 256]), fw.res("cot", dma=True)) for _ in range(NBUF)]
            hfb = [(sb("chf", [128, 4, 256]), fw.res("chf", dma=True)) for _ in range(NBUF)]
            yCo = [(sb("yCo", [128, 2, 512], BF16), fw.res("yCo", dma=True)) for _ in range(2)]
            CT = sb("CT", [128, 2, 65]); rCT = fw.res("CT")
            CTb = sb("CTb", [128, 2, 65], BF16); rCTb = fw.res("CTb")
            klo = sb("klo", [128, 2, 128], BF16); rklo = fw.res("klo")
            khi = sb("khi", [128, 2, 128], BF16); rkhi = fw.res("khi")
            fw.op("pool", lambda e: e.memset(klo[:], 0.0), writes=[rklo])
            fw.op("pool", lambda e: e.memset(khi[:], 0.0), writes=[rkhi])
            g = sb("g", [128, 8]); rg = fw.res("g")
            lf = sb("lf", [128, 4]); rlf = fw.res("lf")
            ex = sb("ex", [128, 12]); rex = fw.res("ex")
            Fm = sb("Fm", [128, 2]); rFm = fw.res("Fm")
            gsb = sb("gsb", [128, 8]); rgsb = fw.res("gsb")
            wm = sb("wm", [128, 4, 128], BF16); rwm = fw.res("wm")
            va = sb("va", [128, 4, 65], BF16); rva = fw.res("va")
            tn = sb("tn", [128, 4, 65]); rtn = fw.res("tn")
            dd = sb("dd", [128, 4]); rdd = fw.res("dd")
            hd = sb("hd", [128, 256]); rhd = fw.res("hd")
            sqh = sb("sqh", [128, 256]); rsqh = fw.res("sqh")
            ss = sb("ss", [128, 4]); rss = fw.res("ss")
            yCt = sb("yCt", [128, 256], BF16); ryCt = fw.res("yCt")
            pG = fw.psum("pG", [128, 16], F32, ps); rpG = fw.res("pG", excl=True)
            pQK = fw.psum("pQK", [128, 2, 128], F32, ps); rpQK = fw.res("pQK", excl=True)
            pQK2 = fw.psum("pQK2", [128, 2, 128], F32, ps); rpQK2 = fw.res("pQK2", excl=True)
            pN = fw.psum("pN", [128, 4, 65], F32, ps); rpN = fw.res("pN", excl=True)
            pC = fw.psum("pC", [128, 2, 65], F32, ps); rpC = fw.res("pC", excl=True)
            ptr = fw.psum("ptrc", [128, 2, 128], BF16, ps); rptr = fw.res("ptrc", excl=True)
            nld = 0
            for d in (0, 1):
                go = 8 * d
                tri, rtri = (triu, rtriu) if d == 0 else (tril, rtril)
                unit_order = (0, 1, 2) if d == 0 else (1, 0, 2)
                for ui, u in enumerate(unit_order):
                    if ui == 1:
                        fw.op("dve", lambda e: e.tensor_scalar(CT[:], CT[:], link[:, 0:1], None, ALU.mult),
                              reads=[rCT, rlink], writes=[rCT])
                    else:
                        fw.op("dve", lambda e: e.memset(CT[:], 0.0), writes=[rCT])
                    blocks = range(self.NBK) if d == 0 else range(self.NBK - 1, -1, -1)
                    for b in blocks:
                        t0 = u * UL + b * 512
                        tsl = slice(t0, t0 + 512)
                        bi = nld % NBUF
                        nld += 1
                        qt, rq = qTb[bi]; kt, rk = kTb[bi]; ktm, rktm = ktb[bi]; vtm, rvtm = vtb[bi]
                        gt, rgt = gtb[bi]; ot, rot = otb[bi]; hf, rhf = hfb[bi]
                        self.ld(qt[:], S["cq"][:, tsl].rearrange("(c p) t -> p c t", p=128), rq, R["cq"])
                        self.ld(kt[:], S["ck"][:, tsl].rearrange("(c p) t -> p c t", p=128), rk, R["ck"])
                        self.ld(ktm[:], S["ckt"][tsl, :].rearrange("(c p) f -> p c f", p=128), rktm, R["ckt"])
                        for h in range(4):
                            self.ld(vtm[:, :, h, 0:64], S["cvt"][tsl, h * 64:(h + 1) * 64].rearrange("(c p) d -> p c d", p=128), rvtm, R["cvt"])
                        self.ld(gt[:], S["cg"][tsl, :].rearrange("(c p) f -> p c f", p=128), rgt, R["cg"])
                        if d == 1:
                            self.ld(ot[:], S["co"][tsl, :].rearrange("(c p) f -> p c f", p=128), rot, R["co"])
                            self.ld(hf[:], S["chf"][tsl, :].rearrange("(c p) f -> p c f", p=128), rhf, R["chf"])
                            yco, ryco = yCo[nld % 2]
                        chunks = range(4) if d == 0 else range(3, -1, -1)
                        for ch in chunks:
                            cs = slice(ch * 128, (ch + 1) * 128)
                            fw.op("dve", lambda e: e.tensor_tensor(g[:], gt[:, ch, go:go + 8], gb[:, go:go + 8], ALU.add),
                                  reads=[rgt, rgb], writes=[rg])
                            if int(os.environ.get('KCUTM', '99')) < 1:
                                continue
                            fw.op("act", lambda e: e.activation(lf[:], g[:, 4:8], AF.Exp, scale=-1.0), reads=[rg], writes=[rlf])
                            fw.op("act", lambda e: e.activation(lf[:], lf[:], AF.Ln, scale=1.0, bias=self.epst[:, 1:2]),
                                  reads=[rlf, self.reps], writes=[rlf])
                            fw.op("dve", lambda e: e.tensor_scalar(lf[:], lf[:], -1.0, None, ALU.mult), reads=[rlf], writes=[rlf])
                            if int(os.environ.get('KCUTM', '99')) < 2:
                                continue
                            fw.op("pe", lambda e: e.matmul(pG[:, 0:4], tri[:], lf[:], start=True, stop=True), reads=[rtri, rlf], writes=[rpG])
                            fw.op("pe", lambda e: e.matmul(pG[:, 4:8], onesf[:], lf[:], start=True, stop=True), reads=[ronesf, rlf], writes=[rpG])
                            fw.op("pe", lambda e: e.matmul(pG[:, 8:10], sel[:, 0, :], lf[:, 0:4:2], start=True, stop=False),
                                  reads=[rsel, rlf], writes=[rpG])
                            fw.op("pe", lambda e: e.matmul(pG[:, 8:10], sel[:, 1, :], lf[:, 1:4:2], start=False, stop=True),
                                  reads=[rsel, rlf], writes=[rpG])
                            if int(os.environ.get('KCUTM', '99')) < 3:
                                continue
                            fw.op("act", lambda e: e.copy(gsb[:], pG[:, 0:8]), reads=[rpG], writes=[rgsb])
                            fw.op("dve", lambda e: e.tensor_tensor(ex[:, 4:8], gsb[:, 0:4], gsb[:, 4:8], ALU.subtract), reads=[rgsb], writes=[rex])
                            fw.op("dve", lambda e: e.tensor_tensor(ex[:, 0:4], g[:, 0:4], ex[:, 4:8], ALU.subtract), reads=[rg, rex], writes=[rex])
                            fw.op("act", lambda e: e.activation(ex[:, 0:8], ex[:, 0:8], AF.Exp), reads=[rex], writes=[rex])
                            fw.op("act", lambda e: e.activation(Fm[:], pG[:, 8:10], AF.Exp), reads=[rpG], writes=[rFm])
                            if int(os.environ.get('KCUTM', '99')) < 4:
                                continue
                            fw.op("dve", lambda e: e.tensor_tensor(va[:], vtm[:, ch, :, :],
                                                                   ex[:, 0:4].unsqueeze(2).to_broadcast([128, 4, 65]), ALU.mult),
                                  reads=[rvtm, rex], writes=[rva])
                            if int(os.environ.get('KCUTM', '99')) < 5:
                                continue
                            for h in range(4):
                                hp = slice((h % 2) * 64, (h % 2) * 64 + 64)
                                pq_, rpq_ = (pQK, rpQK) if h % 2 == 0 else (pQK2, rpQK2)
                                fw.op("pe", lambda e, h=h, hp=hp: e.matmul(pq_[:, h // 2, :], kt[hp, h // 2, cs], qt[hp, h // 2, cs],
                                                                            start=True, stop=True), reads=[rk, rq], writes=[rpq_])
                            wm4 = wm[:].rearrange("p (j q) t -> p j q t", q=2)
                            fw.op("dve", lambda e: e.tensor_tensor(wm4[:, :, 0, :], pQK[:], tri[:].unsqueeze(1).to_broadcast([128, 2, 128]), ALU.mult),
                                  reads=[rpQK, rtri], writes=[rwm])
                            fw.op("dve", lambda e: e.tensor_tensor(wm4[:, :, 1, :], pQK2[:], tri[:].unsqueeze(1).to_broadcast([128, 2, 128]), ALU.mult),
                                  reads=[rpQK2, rtri], writes=[rwm])
                            if int(os.environ.get('KCUTM', '99')) < 6:
                                continue
                            fw.op("dve", lambda e: e.tensor_tensor(CTb[:], CT[:], Fm[:].unsqueeze(2).to_broadcast([128, 2, 65]), ALU.mult),
                                  reads=[rCT, rFm], writes=[rCTb])
                            for h in range(4):
                                hp = slice((h % 2) * 64, (h % 2) * 64 + 64)
                                fw.op("pe", lambda e, h=h: e.matmul(pN[:, h, :], wm[:, h, :], va[:, h, :], start=True, stop=False),
                                      reads=[rwm, rva], writes=[rpN])
                                fw.op("pe", lambda e, h=h, hp=hp: e.matmul(pN[:, h, :], qt[hp, h // 2, cs], CTb[hp, h // 2, :], start=False, stop=True),
                                      reads=[rq, rCTb], writes=[rpN])
                            if int(os.environ.get('KCUTM', '99')) < 7:
                                continue
                            fw.op("pool", lambda e: e.tensor_copy(klo[:, :, 0:64], ktm[:, ch, :].rearrange("p (a b) -> p a b", b=128)[:, :, 0:64]),
                                  reads=[rktm], writes=[rklo])
                            fw.op("pool", lambda e: e.tensor_copy(khi[:, :, 64:128], ktm[:, ch, :].rearrange("p (a b) -> p a b", b=128)[:, :, 64:128]),
                                  reads=[rktm], writes=[rkhi])
                            for p in range(2):
                                fw.op("pe", lambda e, p=p: e.matmul(pC[:, p, :], klo[:, p, :], va[:, 2 * p, :], start=True, stop=False),
                                      reads=[rklo, rva], writes=[rpC])
                                fw.op("pe", lambda e, p=p: e.matmul(pC[:, p, :], khi[:, p, :], va[:, 2 * p + 1, :], start=False, stop=True),
                                      reads=[rkhi, rva], writes=[rpC])
                            for p in range(2):
                                fw.op("dve", lambda e, p=p: e.scalar_tensor_tensor(CT[:, p, :], CT[:, p, :], Fm[:, p:p + 1], pC[:, p, :],
                                                                                   ALU.mult, ALU.add),
                                      reads=[rCT, rFm, rpC], writes=[rCT])
                            if int(os.environ.get('KCUTM', '99')) < 8:
                                continue
                            fw.op("dve", lambda e: e.tensor_tensor(tn[:], pN[:], ex[:, 4:8].unsqueeze(2).to_broadcast([128, 4, 65]), ALU.mult),
                                  reads=[rpN, rex], writes=[rtn])
                            fw.op("act", lambda e: e.activation(dd[:], tn[:, :, 64], AF.Abs), reads=[rtn], writes=[rdd])
                            fw.op("dve", lambda e: e.tensor_scalar(dd[:], dd[:], 1.0, None, ALU.max), reads=[rdd], writes=[rdd])
                            fw.op("dve", lambda e: e.reciprocal(dd[:], dd[:]), reads=[rdd], writes=[rdd])
                            if d == 0:
                                fw.op("dve", lambda e: e.tensor_tensor(hf[:, ch, :].rearrange("p (h d) -> p h d", d=64), tn[:, :, 0:64],
                                                                       dd[:].unsqueeze(2).to_broadcast([128, 4, 64]), ALU.mult),
                                      reads=[rtn, rdd], writes=[rhf])
                            else:
                                hd3 = hd[:].rearrange("p (h d) -> p h d", d=64)
                                fw.op("dve", lambda e: e.tensor_tensor(hd3, tn[:, :, 0:64],
                                                                       dd[:].unsqueeze(2).to_broadcast([128, 4, 64]), ALU.mult),
                                      reads=[rtn, rdd], writes=[rhd])
                                fw.op("dve", lambda e: e.tensor_tensor(hd[:], hd[:], hf[:, ch, :], ALU.add), reads=[rhd, rhf], writes=[rhd])
                                fw.op("act", lambda e: e.activation(sqh[:], hd[:], AF.Square), reads=[rhd], writes=[rsqh])
                                fw.op("dve", lambda e: e.reduce_sum(ss[:], sqh[:].rearrange("p (h d) -> p h d", d=64), AX.X), reads=[rsqh], writes=[rss])
                                fw.op("act", lambda e: e.activation(ss[:], ss[:], AF.Sqrt, scale=1.0 / 64, bias=self.eps_ap()),
                                      reads=[rss, self.reps], writes=[rss])
                                fw.op("dve", lambda e: e.reciprocal(ss[:], ss[:]), reads=[rss], writes=[rss])
                                fw.op("dve", lambda e: e.tensor_tensor(hd3, hd3, ss[:].unsqueeze(2).to_broadcast([128, 4, 64]), ALU.mult),
                                      reads=[rhd, rss], writes=[rhd])
                                fw.op("dve", lambda e: e.tensor_tensor(hd[:], hd[:], hng[:], ALU.mult), reads=[rhd, rhng], writes=[rhd])
                                fw.op("dve", lambda e: e.tensor_tensor(yCt[:], hd[:], ot[:, ch, :], ALU.mult), reads=[rhd, rot], writes=[ryCt])
                                for c in range(2):
                                    fw.op("pe", lambda e, c=c: e.transpose(ptr[:, c, :], yCt[:, c * 128:(c + 1) * 128], identb[:]),
                                          reads=[ryCt, ridb], writes=[rptr])
                                fw.op("act", lambda e: e.copy(yco[:, :, cs], ptr[:]), reads=[rptr], writes=[ryco])
                        if d == 0:
                            self.stt(S["chf"][tsl, :].rearrange("(c p) f -> p c f", p=128), hf[:], rhf, R["chf"])
                        else:
                            self.stt(S["yall"][512:768, tsl].rearrange("(c p) t -> p c t", p=128), yco[:], ryco, R["yall"])
            fw.phase_end()

    def phase_conv(self, l):
        fw, I, S, R, C = self.fw, self.I, self.S, self.R, self.C
        UL = self.UL
        onesb, ronesb = C["onesb"]
        link, rlink = C["link"]
        cw, rcw = C["d_conv_wT"]
        dv, rdv = C["d_vec"]
        fw.phase_begin()
        with ExitStack() as ps:
            sb = lambda n, s, dt=F32: fw.sbuf(n, s, dt, ps)
            yp = [(sb("yp", [128, 2, UL + 30]), fw.res("yp", dma=True)) for _ in range(2)]
            acc = sb("acc", [128, 2, 512]); racc = fw.res("acc")
            accb = sb("accb", [128, 2, 512], BF16); raccb = fw.res("accb")
            sqb = sb("sqb", [128, 2, 512], BF16); rsqb = fw.res("sqb")
            m2 = sb("m2", [128, 512]); rm2 = fw.res("m2")
            rs = sb("rs", [128, 512]); rrs = fw.res("rs")
            tt = sb("tt", [128, 512]); rtt = fw.res("tt")
            yo = [(sb("yDo", [128, 2, 512], BF16), fw.res("yDo", dma=True)) for _ in range(2)]
            pM = fw.psum("pM", [128, 512], F32, ps); rpM = fw.res("pM", excl=True)
            pQ = fw.psum("pQ", [128, 512], F32, ps); rpQ = fw.res("pQ", excl=True)
            no = 0
            for u in range(3):
                ypt, ryp = yp[u % 2]
                fw.op("pool", lambda e: e.memset(ypt[:, :, 0:15], 0.0), writes=[ryp])
                fw.op("pool", lambda e: e.memset(ypt[:, :, UL + 15:UL + 30], 0.0), writes=[ryp])
                src = S["dy"].rearrange("(c p) t -> p c t", p=128)
                self.ld(ypt[:, :, 15:15 + UL], src[:, :, u * UL:(u + 1) * UL], ryp, R["dy"])
                if u == 0:
                    self.ld(ypt[:, :, UL + 15:UL + 30], src[:, :, UL:UL + 15], ryp, R["dy"])
                    fw.op("pool", lambda e: e.tensor_scalar(ypt[:, :, UL + 15:UL + 30], ypt[:, :, UL + 15:UL + 30], link[:, 0:1], None, ALU.mult),
                          reads=[ryp, rlink], writes=[ryp])
                elif u == 1:
                    self.ld(ypt[:, :, 0:15], src[:, :, UL - 15:UL], ryp, R["dy"])
                    fw.op("pool", lambda e: e.tensor_scalar(ypt[:, :, 0:15], ypt[:, :, 0:15], link[:, 0:1], None, ALU.mult),
                          reads=[ryp, rlink], writes=[ryp])
                for b in range(self.NBK):
                    t0 = b * 512
                    for c in range(2):
                        fw.op("dve", lambda e, c=c: e.tensor_scalar(acc[:, c, :], ypt[:, c, t0:t0 + 512], cw[:, l, c, 0:1], dv[:, l, 0, c:c + 1],
                                                                    ALU.mult, ALU.add), reads=[ryp, rcw, rdv], writes=[racc])
                        for k in range(1, 31):
                            fw.op("dve", lambda e, c=c, k=k: e.scalar_tensor_tensor(acc[:, c, :], ypt[:, c, t0 + k:t0 + k + 512], cw[:, l, c, k:k + 1],
                                                                                    acc[:, c, :], ALU.mult, ALU.add),
                                  reads=[ryp, rcw, racc], writes=[racc])
                    for c in range(2):
                        fw.op("act", lambda e, c=c: e.activation(sqb[:, c, :], acc[:, c, :], AF.Square), reads=[racc], writes=[rsqb])
                        fw.op("pool", lambda e, c=c: e.tensor_copy(accb[:, c, :], acc[:, c, :]), reads=[racc], writes=[raccb])
                    for c in range(2):
                        fw.op("pe", lambda e, c=c: e.matmul(pM[:], onesb[:], accb[:, c, :], start=(c == 0), stop=(c == 1)),
                              reads=[ronesb, raccb], writes=[rpM])
                    for c in range(2):
                        fw.op("pe", lambda e, c=c: e.matmul(pQ[:], onesb[:], sqb[:, c, :], start=(c == 0), stop=(c == 1)),
                              reads=[ronesb, rsqb], writes=[rpQ])
                    fw.op("act", lambda e: e.activation(m2[:], pM[:], AF.Square, scale=1.0 / 256), reads=[rpM], writes=[rm2])
                    fw.op("dve", lambda e: e.scalar_tensor_tensor(rs[:], pQ[:], 1.0 / 256, m2[:], ALU.mult, ALU.subtract),
                          reads=[rpQ, rm2], writes=[rrs])
                    fw.op("dve", lambda e: e.tensor_scalar(rs[:], rs[:], 0.0, None, ALU.max), reads=[rrs], writes=[rrs])
                    fw.op("act", lambda e: e.activation(rs[:], rs[:], AF.Sqrt, scale=1.0, bias=self.eps_ap()), reads=[rrs, self.reps], writes=[rrs])
                    fw.op("dve", lambda e: e.reciprocal(rs[:], rs[:]), reads=[rrs], writes=[rrs])
                    yot, ryo = yo[no % 2]
                    no += 1
                    for c in range(2):
                        fw.op("dve", lambda e, c=c: e.scalar_tensor_tensor(tt[:], pM[:], -1.0 / 256, acc[:, c, :], ALU.mult, ALU.add),
                              reads=[rpM, racc], writes=[rtt])
                        fw.op("dve", lambda e: e.tensor_tensor(tt[:], tt[:], rs[:], ALU.mult), reads=[rtt, rrs], writes=[rtt])
                        fw.op("act", lambda e, c=c: e.activation(yot[:, c, :], tt[:], AF.Silu, scale=dv[:, l, 1, c:c + 1], bias=dv[:, l, 2, c:c + 1]),
                              reads=[rtt, rdv], writes=[ryo])
                    tg = u * UL + t0
                    self.stt(S["yall"][768:1024, tg:tg + 512].rearrange("(c p) t -> p c t", p=128), yot[:], ryo, R["yall"])
            fw.phase_end()

    def phase_p3a(self, l):
        fw, I, S, R, C = self.fw, self.I, self.S, self.R, self.C
        BT = 256
        xsrc, rxsrc = (I["xT"], R["xT"]) if l == 0 else (S["xn"], R["xn"])
        fw.phase_begin()
        with ExitStack() as ps:
            sb = lambda n, s, dt=F32: fw.sbuf(n, s, dt, ps)
            Wg = sb("wg", [128, 8, 4096], BF16); rWg = fw.res("wg")
            Wb = sb("wb", [128, 8, 1024], BF16); rWb = fw.res("wb")
            Wo = sb("wo", [128, 8, 1024], BF16); rWo = fw.res("wo")
            stg = [(sb("stg3", [128, 1024], F32), fw.res("stg3", dma=True)) for _ in range(3)]
            self.stg_i = 0
            self.load_cast(Wg, rWg, 0, 8, I["w_in"][l][:, 2832:6928], 4096, stg, ["pool", "dve", "act"], 1024)
            self.load_cast(Wb, rWb, 0, 8, I["w_branch"][l], 1024, stg, ["pool", "dve", "act"], 1024)
            self.load_cast(Wo, rWo, 0, 8, I["w_out"][l], 1024, stg, ["pool", "dve", "act"], 1024)
            xb = [(sb("xb3", [128, 8, BT]), fw.res("xb3", dma=True)) for _ in range(2)]
            yb = [(sb("yb3", [128, 8, BT], BF16), fw.res("yb3", dma=True)) for _ in range(2)]
            hT = sb("hT3", [128, 8, BT], BF16); rhT = fw.res("hT3")
            sq = [sb("sq3", [128, BT]) for _ in range(2)]; rsq = [fw.res("sq3") for _ in range(2)]
            rstd = sb("rstd3", [128, BT]); rrstd = fw.res("rstd3")
            tmp = sb("tmp3", [128, BT]); rtmp = fw.res("tmp3")
            sg = [(sb("sg3", [128, BT]), fw.res("sg3")) for _ in range(2)]
            t2 = [(sb("t23", [128, BT]), fw.res("t23")) for _ in range(2)]
            acc = [(sb("acc3", [128, BT]), fw.res("acc3")) for _ in range(2)]
            mg = sb("mg3", [128, 8, BT], BF16); rmg = fw.res("mg3")
            pst = fw.psum("pst3", [128, BT], F32, ps); rpst = fw.res("pst3", excl=True)
            pg = [(fw.psum("pg3", [128, BT], F32, ps), fw.res("pg3", excl=True)) for _ in range(2)]
            pp = [(fw.psum("pp3", [128, BT], F32, ps), fw.res("pp3", excl=True)) for _ in range(2)]
            po = [(fw.psum("po3", [128, BT], F32, ps), fw.res("po3", excl=True)) for _ in range(2)]
            n = no = nt2 = 0
            for gi in range(self.NT // BT):
                t0 = gi * BT
                u = t0 // self.UL
                xt, rx = xb[gi % 2]
                yt, ry = yb[gi % 2]
                self.ld(xt[:], xsrc[:, t0:t0 + BT].rearrange("(k p) t -> p k t", p=128), rx, rxsrc)
                self.ld(yt[:], S["yall"][:, t0:t0 + BT].rearrange("(k p) t -> p k t", p=128), ry, R["yall"])
                self.norm_mod(xt, rx, hT, rhT, sq, rsq, pst, rpst, rstd, rrstd, l, 0, u, tmp, rtmp)
                for f in range(8):
                    acct, racc = acc[f % 2]
                    for br in range(4):
                        pgt, rpg = pg[n % 2]
                        ppt, rpp = pp[n % 2]
                        sgt, rsg = sg[n % 2]
                        n += 1
                        col = br * 1024 + f * 128
                        for k in range(8):
                            fw.op("pe", lambda e, k=k: e.matmul(pgt[:], Wg[:, k, col:col + 128], hT[:, k, :], start=(k == 0), stop=(k == 7)),
                                  reads=[rWg, rhT], writes=[rpg])
                        for k in range(2):
                            fw.op("pe", lambda e, k=k: e.matmul(ppt[:], Wb[:, br * 2 + k, f * 128:(f + 1) * 128], yt[:, br * 2 + k, :],
                                                                start=(k == 0), stop=(k == 1)), reads=[rWb, ry], writes=[rpp])
                        fw.op("act", lambda e: e.activation(sgt[:], pgt[:], AF.Sigmoid), reads=[rpg], writes=[rsg])
                        if br == 0:
                            fw.op("dve", lambda e: e.tensor_tensor(acct[:], sgt[:], ppt[:], ALU.mult), reads=[rsg, rpp], writes=[racc])
                        else:
                            t2t, rt2 = t2[nt2 % 2]
                            nt2 += 1
                            fw.op("dve", lambda e: e.tensor_tensor(t2t[:], sgt[:], ppt[:], ALU.mult), reads=[rsg, rpp], writes=[rt2])
                            if br < 3:
                                fw.op("pool", lambda e: e.tensor_tensor(acct[:], acct[:], t2t[:], ALU.add), reads=[racc, rt2], writes=[racc])
                            else:
                                fw.op("pool", lambda e, f=f: e.tensor_tensor(mg[:, f, :], acct[:], t2t[:], ALU.add), reads=[racc, rt2], writes=[rmg])
                for f in range(8):
                    pot, rpo = po[no % 2]
                    no += 1
                    for k in range(8):
                        fw.op("pe", lambda e, k=k, f=f: e.matmul(pot[:], Wo[:, k, f * 128:(f + 1) * 128], mg[:, k, :], start=(k == 0), stop=(k == 7)),
                              reads=[rWo, rmg], writes=[rpo])
                    fw.op("dve", lambda e, f=f: e.scalar_tensor_tensor(xt[:, f, :], pot[:], self.mod[:, l, 2, f, u:u + 1], xt[:, f, :],
                                                                       ALU.mult, ALU.add), reads=[rpo, self.rmod, rx], writes=[rx])
                self.stt(S["xm"][:, t0:t0 + BT].rearrange("(k p) t -> p k t", p=128), xt[:], rx, R["xm"])
        fw.phase_end()

    def phase_p3b(self, l):
        fw, I, S, R, C = self.fw, self.I, self.S, self.R, self.C
        BT = 256
        last = (l == self.L - 1)
        fw.phase_begin()
        with ExitStack() as ps:
            sb = lambda n, s, dt=F32: fw.sbuf(n, s, dt, ps)
            W1 = sb("wf1", [128, 8, 2 * DFF], BF16); rW1 = fw.res("wf1")
            W2 = sb("wf2", [128, 22, 1024], BF16); rW2 = fw.res("wf2")
            stg = [(sb("stg4", [128, 1408], F32), fw.res("stg4", dma=True)) for _ in range(2)]
            self.stg_i = 0
            self.load_cast(W1, rW1, 0, 8, I["w_ffn_in"][l], 2 * DFF, stg, ["pool", "dve", "act"], 1408)
            self.load_cast(W2, rW2, 0, 22, I["w_ffn_out"][l], 1024, stg, ["pool", "dve", "act"], 1408)
            xb = [(sb("xb4", [128, 8, BT]), fw.res("xb4", dma=True)) for _ in range(2)]
            hT = sb("hT4", [128, 8, BT], BF16); rhT = fw.res("hT4")
            sq = [sb("sq4", [128, BT]) for _ in range(2)]; rsq = [fw.res("sq4") for _ in range(2)]
            rstd = sb("rstd4", [128, BT]); rrstd = fw.res("rstd4")
            tmp = sb("tmp4", [128, BT]); rtmp = fw.res("tmp4")
            sg = [(sb("sg4", [128, BT]), fw.res("sg4")) for _ in range(2)]
            hid = sb("hid4", [128, 22, BT], BF16); rhid = fw.res("hid4")
            pst = fw.psum("pst4", [128, BT], F32, ps); rpst = fw.res("pst4", excl=True)
            pg = [(fw.psum("pg4", [128, BT], F32, ps), fw.res("pg4", excl=True)) for _ in range(2)]
            pu = [(fw.psum("pu4", [128, BT], F32, ps), fw.res("pu4", excl=True)) for _ in range(2)]
            po = [(fw.psum("po4", [128, BT], F32, ps), fw.res("po4", excl=True)) for _ in range(2)]
            gfin, rgfin = C["g_finalT"]
            onesf, ronesf = C["onesf"]
            n = no = 0
            for gi in range(self.NT // BT):
                t0 = gi * BT
                u = t0 // self.UL
                xt, rx = xb[gi % 2]
                self.ld(xt[:], S["xm"][:, t0:t0 + BT].rearrange("(k p) t -> p k t", p=128), rx, R["xm"])
                self.norm_mod(xt, rx, hT, rhT, sq, rsq, pst, rpst, rstd, rrstd, l, 1, u, tmp, rtmp)
                for j in range(22):
                    pgt, rpg = pg[n % 2]
                    put, rpu = pu[n % 2]
                    sgt, rsg = sg[n % 2]
                    n += 1
                    for k in range(8):
                        fw.op("pe", lambda e, k=k, j=j: e.matmul(pgt[:], W1[:, k, j * 128:(j + 1) * 128], hT[:, k, :], start=(k == 0), stop=(k == 7)),
                              reads=[rW1, rhT], writes=[rpg])
                    for k in range(8):
                        fw.op("pe", lambda e, k=k, j=j: e.matmul(put[:], W1[:, k, DFF + j * 128:DFF + (j + 1) * 128], hT[:, k, :],
                                                                  start=(k == 0), stop=(k == 7)), reads=[rW1, rhT], writes=[rpu])
                    fw.op("act", lambda e: e.activation(sgt[:], pgt[:], AF.Silu), reads=[rpg], writes=[rsg])
                    fw.op("dve", lambda e, j=j: e.tensor_tensor(hid[:, j, :], sgt[:], put[:], ALU.mult), reads=[rsg, rpu], writes=[rhid])
                for f in range(8):
                    pot, rpo = po[no % 2]
                    no += 1
                    for k in range(22):
                        fw.op("pe", lambda e, k=k, f=f: e.matmul(pot[:], W2[:, k, f * 128:(f + 1) * 128], hid[:, k, :], start=(k == 0), stop=(k == 21)),
                              reads=[rW2, rhid], writes=[rpo])
                    fw.op("dve", lambda e, f=f: e.scalar_tensor_tensor(xt[:, f, :], pot[:], self.mod[:, l, 5, f, u:u + 1], xt[:, f, :],
                                                                       ALU.mult, ALU.add), reads=[rpo, self.rmod, rx], writes=[rx])
                if not last:
                    self.stt(S["xn"][:, t0:t0 + BT].rearrange("(k p) t -> p k t", p=128), xt[:], rx, R["xn"])
                else:
                    for k in range(8):
                        sqk, rsqk = sq[k % 2], rsq[k % 2]
                        fw.op("act", lambda e, k=k: e.activation(sqk[:], xt[:, k, :], AF.Square), reads=[rx], writes=[rsqk])
                        fw.op("pe", lambda e, k=k: e.matmul(pst[:], onesf[:], sqk[:], start=(k == 0), stop=(k == 7)),
                              reads=[rsqk, ronesf], writes=[rpst])
                    fw.op("act", lambda e: e.activation(rstd[:], pst[:], AF.Sqrt, scale=1.0 / D, bias=self.eps_ap()),
                          reads=[rpst, self.reps], writes=[rrstd])
                    fw.op("dve", lambda e: e.reciprocal(rstd[:], rstd[:]), reads=[rrstd], writes=[rrstd])
                    for k in range(8):
                        fw.op("dve", lambda e, k=k: e.scalar_tensor_tensor(xt[:, k, :], xt[:, k, :], gfin[:, k:k + 1], rstd[:], ALU.mult, ALU.mult),
                              reads=[rx, rgfin, rrstd], writes=[rx])
                    self.stt(self.yT[:, t0:t0 + BT].rearrange("(k p) t -> p k t", p=128), xt[:], rx, R["yT"])
        fw.phase_end()


def _bias_tile_idx(rows_total, qr0, kr0, q_valid_rows, k_valid_rows):
    kk = np.arange(128)
    qq = np.arange(128)
    krow = kr0 + kk // 64
    kcol = kk % 64
    qrow = qr0 + qq // 64
    qcol = qq % 64
    kr = min(8, rows_total)
    wlo = np.clip(qrow - kr // 2, 0, rows_total - kr)
    clo = np.clip(qcol - 8, 0, 64 - 16)
    vr = (krow[:, None] >= wlo[None, :]) & (krow[:, None] < wlo[None, :] + kr)
    vcol = (kcol[:, None] >= clo[None, :]) & (kcol[:, None] < clo[None, :] + 16)
    valid = vr & vcol
    valid &= (krow[:, None] >= 0) & (krow[:, None] < rows_total) & (qrow[None, :] >= 0) & (qrow[None, :] < rows_total)
    dr = np.clip(krow[:, None] - qrow[None, :] + 7, 0, 14)
    dc = np.clip(kcol[:, None] - qcol[None, :] + 15, 0, 30)
    return valid, dr, dc


def _make_rpbt(rpb_l, UL, link):
    Rr = UL // 64
    NB2 = UL // 128
    out = np.full((128, NSLOT * 4, 128), NEG, np.float32)

    def fill(slot, rows_total, qr0, kr0):
        valid, dr, dc = _bias_tile_idx(rows_total, qr0, kr0, None, None)
        for h in range(4):
            vals = rpb_l[h][dr, dc]
            out[:, slot * 4 + h, :] = np.where(valid, vals, np.float32(NEG))

    big = 64 if Rr >= 16 else Rr
    Rg = max(Rr, 16)
    for cls, bsel in (("INT", 4), ("TOP0", 0), ("TOP1", 1), ("BOT1", Rg // 2 - 2), ("BOT0", Rg // 2 - 1)):
        s0, offs = CLS[cls]
        for i, o in enumerate(offs):
            fill(s0 + i, Rg, 2 * bsel, 2 * (bsel + o))
    for cls, u, b in (("JA1", 0, NB2 - 2), ("JA0", 0, NB2 - 1), ("JB0", 1, 0), ("JB1", 1, 1)):
        s0, offs = CLS[cls]
        for i, o in enumerate(offs):
            if link:
                fill(s0 + i, 2 * Rr, 2 * (u * NB2 + b), 2 * (u * NB2 + b + o))
            else:
                kp = b + o
                if kp < 0 or kp >= NB2:
                    continue
                fill(s0 + i, Rr, 2 * b, 2 * kp)
    return out


def _host_prep(inp, UL, L, units_per_core):
    f32 = np.float32
    shared = {}
    shared["w_ada"] = np.ascontiguousarray(inp["w_ada"][:L])
    shared["b_adaT"] = np.ascontiguousarray(inp["b_ada"][:L].reshape(L, 48, 128).transpose(2, 0, 1))
    gv = np.stack([inp["g_norm_mix"][:L], inp["g_norm_ffn"][:L]], 1)
    shared["gvec"] = np.ascontiguousarray(gv.reshape(L, 2, 8, 128).transpose(3, 0, 1, 2))
    shared["w_in"] = np.ascontiguousarray(inp["w_in"][:L])
    shared["a_ln"] = np.ascontiguousarray(np.concatenate([inp["a_ln_g"][:L], inp["a_ln_b"][:L]], 1))
    shared["a_w_spT"] = np.ascontiguousarray(inp["a_w_sp"][:L].transpose(0, 3, 1, 2))
    shared["a_b_sp"] = np.ascontiguousarray(inp["a_b_sp"][:L].transpose(2, 0, 1))
    shared["c_gate_b"] = np.ascontiguousarray(inp["c_gate_b"][:L])
    shared["c_hnorm"] = np.ascontiguousarray(inp["c_hnorm_g"][:L])
    shared["d_conv_wT"] = np.ascontiguousarray(inp["d_conv_w"][:L].reshape(L, 31, 2, 128).transpose(3, 0, 2, 1))
    dv = np.stack([inp["d_conv_b"][:L], inp["d_ln_g"][:L], inp["d_ln_b"][:L]], 1)
    shared["d_vec"] = np.ascontiguousarray(dv.reshape(L, 3, 2, 128).transpose(3, 0, 1, 2))
    shared["w_branch"] = np.ascontiguousarray(inp["w_branch"][:L].reshape(L, 1024, D))
    shared["w_out"] = np.ascontiguousarray(inp["w_out"][:L])
    shared["w_ffn_in"] = np.ascontiguousarray(inp["w_ffn_in"][:L])
    shared["w_ffn_out"] = np.ascontiguousarray(inp["w_ffn_out"][:L])
    shared["g_finalT"] = np.ascontiguousarray(inp["g_final"].reshape(8, 128).T)
    shared["c_ident"] = np.eye(128, dtype=f32)
    shared["c_triu"] = np.triu(np.ones((128, 128), f32))
    shared["c_tril"] = np.tril(np.ones((128, 128), f32))
    sel = np.zeros((128, 2, 128), f32)
    sel[:, 0, 0:64] = 1.0
    sel[:, 1, 64:128] = 1.0
    shared["c_sel"] = sel
    rp = {}
    for link in (0, 1):
        rp[link] = np.stack([_make_rpbt(inp["b_rpb"][l], UL, link) for l in range(L)], 0)
    in_maps = []
    for link, units in units_per_core:
        m = dict(shared)
        xs, cs = [], []
        for which, si, t0 in units:
            x = inp["x_prompt"] if which == "p" else inp["x_sample"]
            c = inp["c_prompt"] if which == "p" else inp["c_sample"]
            xs.append(x[si, t0:t0 + UL, :])
            cs.append(c[si])
        m["xT"] = np.ascontiguousarray(np.concatenate(xs, 0).T)
        cc = np.stack(cs, 0)
        m["cT"] = np.ascontiguousarray(cc.reshape(3, 8, 128).transpose(2, 1, 0))
        m["link"] = np.full((128, 1), float(link), f32)
        m["rpbt"] = rp[link]
        in_maps.append(m)
    return in_maps


_NC_CACHE = {}


def run_config(inp, UL, L, units_per_core, debug=False):
    key = (UL, L, debug)
    if key not in _NC_CACHE:
        b = Builder(UL, L)
        b.debug = debug
        _NC_CACHE[key] = (b.build(), b)
    nc, b = _NC_CACHE[key]
    in_maps = _host_prep(inp, UL, L, units_per_core)
    res = run_bass_kernel_spmd(nc, in_maps, core_ids=list(range(len(in_maps))))
    if debug:
        return res.results
    return [np.asarray(r["yT"]) for r in res.results]


def kernel(**inputs):
    inp = {k: np.asarray(v) for k, v in inputs.items()}
    UL = 4096
    L = 4
    units = []
    for c in range(4):
        units.append((1, [("p", c, 0), ("p", c, UL), ("s", c, 0)]))
    for c in range(4):
        units.append((0, [("s", 4 + 3 * c + j, 0) for j in range(3)]))
    outs = run_config(inp, UL, L, units)
    yp = np.empty(inp["x_prompt"].shape, np.float32)
    ys = np.empty(inp["x_sample"].shape, np.float32)
    for c, (link, us) in enumerate(units):
        yT = outs[c]
        for j, (which, si, t0) in enumerate(us):
            blk = yT[:, j * UL:(j + 1) * UL].T
            if which == "p":
                yp[si, t0:t0 + UL, :] = blk
            else:
                ys[si, 0:UL, :] = blk
    return (yp, ys)
```

```python
import os
import numpy as np
from contextlib import ExitStack
import concourse.bass as bass
import concourse.mybir as mybir
from concourse.bass_utils import run_bass_kernel_spmd

F32 = mybir.dt.float32
BF16 = mybir.dt.bfloat16
AF = mybir.ActivationFunctionType
ALU = mybir.AluOpType
AX = mybir.AxisListType

D = 1024
NIN = 6928
DFF = 2816
NEG = -30000.0
EPS = 1e-6
NSLOT = 43
CLS = {
    "INT": (0, [-2, -1, 0, 1, 2]),
    "TOP0": (5, [0, 1, 2, 3]),
    "TOP1": (9, [-1, 0, 1, 2]),
    "BOT1": (13, [-2, -1, 0, 1]),
    "BOT0": (17, [-3, -2, -1, 0]),
    "JA1": (21, [-2, -1, 0, 1, 2]),
    "JA0": (26, [-3, -2, -1, 0, 1, 2]),
    "JB0": (32, [-2, -1, 0, 1, 2, 3]),
    "JB1": (38, [-2, -1, 0, 1, 2]),
}


class Res:
    __slots__ = ("name", "w", "r", "dsem", "multi", "excl")

    def __init__(self, name, multi=False, excl=False):
        self.name = name
        self.multi = multi
        self.excl = excl
        self.w = []
        self.r = {}
        self.dsem = None


class Sem:
    __slots__ = ("h", "total", "is_dma", "name")

    def __init__(self, h, is_dma, name):
        self.h = h
        self.total = 0
        self.is_dma = is_dma
        self.name = name


class Eng:
    def __init__(self, name, h, sem):
        self.name = name
        self.h = h
        self.sem = sem
        self.waited = {}


class FW:
    def __init__(self, nc, stack):
        self.nc = nc
        self.stack = stack
        self.eng = {}
        self.sems = []
        for name, h in (("pe", nc.tensor), ("dve", nc.vector), ("act", nc.scalar),
                        ("pool", nc.gpsimd), ("sp", nc.sync)):
            s = Sem(stack.enter_context(nc.semaphore("s_" + name)), False, name)
            self.sems.append(s)
            self.eng[name] = Eng(name, h, s)
        self.ninst = 0
        self.uid = 0
        self.free_dsems = []
        self.phase_dsems = None

    def sbuf(self, name, shape, dt, stack=None):
        self.uid += 1
        return (stack or self.stack).enter_context(
            self.nc.sbuf_tensor("%s_%d" % (name, self.uid), list(shape), dt))

    def psum(self, name, shape, dt=F32, stack=None):
        self.uid += 1
        esz = 4 if dt == F32 else 2
        n = int(np.prod(shape[1:]))
        be = 2048 // esz
        nb = -(-n // be)
        t = (stack or self.stack).enter_context(
            self.nc.psum_tensor("%s_%d" % (name, self.uid), [128, nb * be], dt))
        v = t[:, 0:n]
        if len(shape) == 3:
            v = v.rearrange("p (a b) -> p a b", b=shape[2])
        return v

    def res(self, name, dma=False, multi=False, excl=False):
        r = Res(name, multi, excl)
        if dma:
            if not self.free_dsems:
                self.uid += 1
                h = self.stack.enter_context(self.nc.semaphore("d_%d" % self.uid))
                sm = Sem(h, True, name)
                self.sems.append(sm)
                self.free_dsems.append(sm)
            r.dsem = self.free_dsems.pop()
            if self.phase_dsems is not None:
                self.phase_dsems.append(r.dsem)
        return r

    def phase_begin(self):
        self.phase_dsems = []

    def phase_end(self):
        try:
            print("sbuf remaining", self.nc.sbuf_bytes_remaining, "ninst", self.ninst, flush=True)
        except Exception as ex:
            print("sbuf remaining ?", ex)
        self.barrier()
        self.free_dsems.extend(self.phase_dsems)
        self.phase_dsems = None

    def _need(self, e, deps):
        best = {}
        for s, v in deps:
            if s is e.sem:
                if e.name == "pe" or e.name == "sp":
                    continue
                if e.sem.total - v >= 2:
                    continue
            if s.is_dma:
                v = s.total
            if v > best.get(s, 0):
                best[s] = v
        for s, v in best.items():
            if e.waited.get(s, 0) >= v:
                continue
            e.h.wait_ge(s.h, v)
            e.waited[s] = v
            self.ninst += 1

    def _collect(self, reads, writes):
        deps = []
        for r in reads:
            deps.extend(r.w)
        for w in writes:
            if not w.multi:
                deps.extend(w.w)
            deps.extend(w.r.items())
        return deps

    def _record(self, sem, reads, writes):
        key = (sem, sem.total)
        for r in reads:
            r.r[sem] = sem.total
        for w in writes:
            if w.multi:
                w.w = [k for k in w.w if k[0] is not sem] + [key]
            else:
                w.w = [key]
                w.r = {}

    def op(self, ename, fn, reads=(), writes=()):
        e = self.eng[ename]
        xr = [r for r in reads if r.excl]
        if xr:
            reads = [r for r in reads if not r.excl]
            writes = list(writes) + xr
        self._need(e, self._collect(reads, writes))
        ins = fn(e.h)
        e.sem.total += 1
        ins.then_inc(e.sem.h, 1)
        self.ninst += 1
        self._record(e.sem, reads, writes)
        return ins

    def dma(self, qname, out, in_, reads=(), writes=(), dres=None, **kw):
        e = self.eng[qname]
        self._need(e, self._collect(reads, writes))
        ds = dres.dsem
        ins = e.h.dma_start(out=out, in_=in_, **kw)
        ds.total += 16
        ins.then_inc(ds.h, 16)
        self.ninst += 1
        self._record(ds, reads, writes)
        return ins

    def barrier(self):
        for e in self.eng.values():
            for s in self.sems:
                if s is e.sem or s.total == 0:
                    continue
                if e.waited.get(s, 0) >= s.total:
                    continue
                e.h.wait_ge(s.h, s.total)
                e.waited[s] = s.total
                self.ninst += 1


class Builder:
    def __init__(self, UL, L, last_is_final=True):
        self.UL = UL
        self.L = L
        self.NT = 3 * UL
        self.NBK = UL // 512
        self.NCH = UL // 128
        self.NB2 = UL // 128
        assert self.NB2 >= 8

    def declare(self, nc):
        L, NT = self.L, self.NT
        di = lambda n, s, dt=F32: nc.dram_tensor(n, list(s), dt, kind="ExternalInput").ap()
        dbg = getattr(self, "debug", False)
        dx = lambda n, s, dt=F32: nc.dram_tensor(n, list(s), dt, kind=("ExternalOutput" if dbg else "Internal")).ap()
        I = {}
        I["xT"] = di("xT", [D, NT])
        I["cT"] = di("cT", [128, 8, 3])
        I["link"] = di("link", [128, 1])
        I["w_ada"] = di("w_ada", [L, D, 6 * D])
        I["b_adaT"] = di("b_adaT", [128, L, 48])
        I["gvec"] = di("gvec", [128, L, 2, 8])
        I["w_in"] = di("w_in", [L, D, NIN])
        I["a_ln"] = di("a_ln", [L, 512])
        I["a_w_spT"] = di("a_w_spT", [L, 128, 4, 128])
        I["a_b_sp"] = di("a_b_sp", [128, L, 4])
        I["rpbt"] = di("rpbt", [L, 128, NSLOT * 4, 128])
        I["c_gate_b"] = di("c_gate_b", [L, 16])
        I["c_hnorm"] = di("c_hnorm", [L, 256])
        I["d_conv_wT"] = di("d_conv_wT", [128, L, 2, 31])
        I["d_vec"] = di("d_vec", [128, L, 3, 2])
        I["w_branch"] = di("w_branch", [L, 1024, D])
        I["w_out"] = di("w_out", [L, D, D])
        I["w_ffn_in"] = di("w_ffn_in", [L, D, 2 * DFF])
        I["w_ffn_out"] = di("w_ffn_out", [L, DFF, D])
        I["g_finalT"] = di("g_finalT", [128, 8])
        I["c_ident"] = di("c_ident", [128, 128])
        I["c_triu"] = di("c_triu", [128, 128])
        I["c_tril"] = di("c_tril", [128, 128])
        I["c_sel"] = di("c_sel", [128, 2, 128])
        self.I = I
        self.yT = nc.dram_tensor("yT", [D, NT], F32, kind="ExternalOutput").ap()
        S = {}
        S["xm"] = dx("xm", [D, NT])
        S["xn"] = dx("xn", [D, NT])
        S["yall"] = dx("yall", [D, NT], BF16)
        S["bq"] = dx("bq", [256, NT], BF16)
        S["bk"] = dx("bk", [256, NT], BF16)
        S["bv"] = dx("bv", [NT, 256], BF16)
        S["cq"] = dx("cq", [256, NT], BF16)
        S["ck"] = dx("ck", [256, NT], BF16)
        S["ckt"] = dx("ckt", [NT, 256], BF16)
        S["cvt"] = dx("cvt", [NT, 256])
        S["co"] = dx("co", [NT, 256])
        S["cg"] = dx("cg", [NT, 16])
        S["chf"] = dx("chf", [NT, 256])
        S["dy"] = dx("dy", [256, NT])
        self.S = S

    def build(self):
        nc = bass.Bass("TRN2", target_bir_lowering=False)
        self.nc = nc
        self.declare(nc)
        with ExitStack() as st:
            fw = FW(nc, st)
            self.fw = fw
            self.R = {k: fw.res(k, multi=True) for k in list(self.S) + ["xT", "yT"]}
            self.setup_consts(st)
            self.ensure_eps()
            self.phase_mod()
            import os
            ph = os.environ.get("KPHASES", "p1,bc,conv,p3a,p3b").split(",")
            for l in range(self.L):
                for p in ("p1", "bc", "conv", "p3a", "p3b"):
                    if p in ph:
                        getattr(self, "phase_" + p)(l)
            fw.barrier()
            self.ninst = fw.ninst
        return nc

    def ld(self, tile_ap, dram_ap, res, dram_res=None, q="sp"):
        reads = [dram_res] if dram_res is not None else []
        self.fw.dma(q, tile_ap, dram_ap, reads=reads, writes=[res], dres=res)

    def stt(self, dram_ap, tile_ap, res, dram_res, q=None):
        import os
        q = q or os.environ.get("KSTQ", "sp")
        self.fw.dma(q, dram_ap, tile_ap, reads=[res], writes=[dram_res], dres=res)

    def setup_consts(self, st):
        fw, I = self.fw, self.I
        L = self.L
        C = {}

        def cload(name, shape, src, dt=F32):
            t = fw.sbuf(name, shape, dt)
            r = fw.res(name, dma=True)
            self.ld(t[:], src, r)
            C[name] = (t, r)
            return t, r

        cload("identf", [128, 128], I["c_ident"])
        cload("triu", [128, 128], I["c_triu"])
        cload("tril", [128, 128], I["c_tril"])
        cload("sel", [128, 2, 128], I["c_sel"])
        cload("link", [128, 1], I["link"])
        cload("cT", [128, 8, 3], I["cT"])
        cload("b_adaT", [128, L, 48], I["b_adaT"])
        cload("gvec", [128, L, 2, 8], I["gvec"])
        cload("a_b_sp", [128, L, 4], I["a_b_sp"])
        cload("d_conv_wT", [128, L, 2, 31], I["d_conv_wT"])
        cload("d_vec", [128, L, 3, 2], I["d_vec"])
        cload("g_finalT", [128, 8], I["g_finalT"])
        identb = fw.sbuf("identb", [128, 128], BF16)
        ridb = fw.res("identb")
        fw.op("dve", lambda e: e.tensor_copy(identb[:], C["identf"][0][:]), reads=[C["identf"][1]], writes=[ridb])
        C["identb"] = (identb, ridb)
        onesf = fw.sbuf("onesf", [128, 128], F32)
        ronesf = fw.res("onesf")
        fw.op("dve", lambda e: e.memset(onesf[:], 1.0), writes=[ronesf])
        C["onesf"] = (onesf, ronesf)
        onesb = fw.sbuf("onesb", [128, 128], BF16)
        ronesb = fw.res("onesb")
        fw.op("dve", lambda e: e.memset(onesb[:], 1.0), writes=[ronesb])
        C["onesb"] = (onesb, ronesb)
        self.C = C
        self.mod = fw.sbuf("mod", [128, L, 6, 8, 3], F32)
        self.rmod = fw.res("mod")
        self.gm = fw.sbuf("gm", [128, L, 2, 8, 3], F32)
        self.rgm = fw.res("gm")

    def phase_mod(self):
        fw, I, C = self.fw, self.I, self.C
        L = self.L
        fw.phase_begin()
        with ExitStack() as ps:
            sc = fw.sbuf("silu_c", [128, 8, 3], F32, ps)
            rsc = fw.res("silu_c")
            cT, rcT = C["cT"]
            fw.op("act", lambda e: e.activation(sc[:], cT[:], AF.Silu), reads=[rcT], writes=[rsc])
            wst = [(fw.sbuf("wada", [128, 8, 1024], F32, ps), fw.res("wada", dma=True)) for _ in range(2)]
            pm = [(fw.psum("pmod", [128, 8, 4], F32, ps), fw.res("pmod", excl=True)) for _ in range(2)]
            it = 0
            badaT, rbada = C["b_adaT"]
            for l in range(L):
                for m in range(6):
                    wt, rw = wst[it % 2]
                    pt, rp = pm[it % 2]
                    it += 1
                    src = I["w_ada"][l, :, m * 1024:(m + 1) * 1024].rearrange("(k p) n -> p k n", p=128)
                    for k2 in range(2):
                        self.ld(wt[:, 4 * k2:4 * k2 + 4, :], src[:, 4 * k2:4 * k2 + 4, :], rw)
                    for f in range(8):
                        for k in range(8):
                            fw.op("pe", lambda e, f=f, k=k: e.matmul(pt[:, f, 0:3], wt[:, k, f * 128:(f + 1) * 128], sc[:, k, :],
                                                                     start=(k == 0), stop=(k == 7)),
                                  reads=[rw, rsc], writes=[rp])
                    for f in range(8):
                        fw.op("dve", lambda e, f=f: e.tensor_scalar(self.mod[:, l, m, f, :], pt[:, f, 0:3],
                                                                    badaT[:, l, m * 8 + f:m * 8 + f + 1], None, ALU.add),
                              reads=[rp, rbada], writes=[self.rmod])
            gvec, rg = C["gvec"]
            for l in range(L):
                for j, m in ((0, 1), (1, 4)):
                    for u in range(3):
                        fw.op("dve", lambda e, l=l, j=j, m=m, u=u: e.scalar_tensor_tensor(
                            self.gm[:, l, j, :, u], self.mod[:, l, m, :, u], 1.0, gvec[:, l, j, :], ALU.add, ALU.mult),
                            reads=[self.rmod, rg], writes=[self.rgm])
        fw.phase_end()

    def load_cast(self, dst, rdst, k0, nk, src_rows, ncols, stg, eng_cycle, CW):
        fw = self.fw
        for k in range(nk):
            for c0 in range(0, ncols, CW):
                cw = min(CW, ncols - c0)
                st_t, st_r = stg[self.stg_i % len(stg)]
                self.stg_i += 1
                self.ld(st_t[:, 0:cw], src_rows[k * 128:(k + 1) * 128, c0:c0 + cw], st_r)
                en = eng_cycle[self.stg_i % len(eng_cycle)]
                if en == "act":
                    fw.op("act", lambda e: e.copy(dst[:, k0 + k, c0:c0 + cw], st_t[:, 0:cw]), reads=[st_r], writes=[rdst])
                else:
                    fw.op(en, lambda e: e.tensor_copy(dst[:, k0 + k, c0:c0 + cw], st_t[:, 0:cw]), reads=[st_r], writes=[rdst])

    def norm_mod(self, xb, rxb, hT, rhT, sq, rsq, pst, rpst, rstd, rrstd, l, j, u, tmp, rtmp):
        fw = self.fw
        onesf, ronesf = self.C["onesf"]
        for k in range(8):
            sqk, rsqk = sq[k % 2], rsq[k % 2]
            fw.op("act", lambda e, k=k: e.activation(sqk[:], xb[:, k, :], AF.Square), reads=[rxb], writes=[rsqk])
            fw.op("pe", lambda e, k=k: e.matmul(pst[:], onesf[:], sqk[:], start=(k == 0), stop=(k == 7)),
                  reads=[rsqk, ronesf], writes=[rpst])
        fw.op("act", lambda e: e.activation(rstd[:], pst[:], AF.Sqrt, scale=1.0 / D, bias=self.eps_ap()),
              reads=[rpst, self.reps], writes=[rrstd])
        fw.op("dve", lambda e: e.reciprocal(rstd[:], rstd[:]), reads=[rrstd], writes=[rrstd])
        m_sh = 0 if j == 0 else 3
        for k in range(8):
            fw.op("dve", lambda e, k=k: e.tensor_tensor(tmp[:], xb[:, k, :], rstd[:], ALU.mult),
                  reads=[rxb, rrstd], writes=[rtmp])
            fw.op("dve", lambda e, k=k: e.tensor_scalar(hT[:, k, :], tmp[:], self.gm[:, l, j, k, u:u + 1],
                                                        self.mod[:, l, m_sh, k, u:u + 1], ALU.mult, ALU.add),
                  reads=[rtmp, self.rgm, self.rmod], writes=[rhT])

    def eps_ap(self):
        return self.epst[:, 0:1]

    def ensure_eps(self):
        if getattr(self, "epst", None) is None:
            fw = self.fw
            self.epst = fw.sbuf("epst", [128, 2], F32)
            self.reps = fw.res("epst")
            fw.op("dve", lambda e: e.memset(self.epst[:, 0:1], EPS), writes=[self.reps])
            fw.op("dve", lambda e: e.memset(self.epst[:, 1:2], 1.0), writes=[self.reps])

    def phase_p1(self, l):
        fw, I, S, R, C = self.fw, self.I, self.S, self.R, self.C
        self.ensure_eps()
        xsrc, rxsrc = (I["xT"], R["xT"]) if l == 0 else (S["xn"], R["xn"])
        fw.phase_begin()
        with ExitStack() as ps:
            sb = lambda n, s, dt=F32: fw.sbuf(n, s, dt, ps)
            W = sb("w1", [128, 8, 2832], BF16)
            rW = fw.res("w1")
            stg = [(sb("stg", [128, 1416], F32), fw.res("stg", dma=True)) for _ in range(3)]
            self.stg_i = 0
            self.load_cast(W, rW, 0, 8, I["w_in"][l], 2832, stg, ["pool", "dve", "act"], 1416)
            wsp = sb("wsp", [128, 4, 128], BF16)
            rwsp = fw.res("wsp")
            wspf = sb("wspf", [128, 4, 128], F32)
            rwspf = fw.res("wspf", dma=True)
            self.ld(wspf[:], I["a_w_spT"][l], rwspf)
            fw.op("pool", lambda e: e.tensor_copy(wsp[:], wspf[:]), reads=[rwspf], writes=[rwsp])
            aln = sb("aln", [128, 512], F32)
            raln = fw.res("aln", dma=True)
            self.ld(aln[:], I["a_ln"][l].partition_broadcast(128), raln)
            absp, rabsp = C["a_b_sp"]
            identb, ridb = C["identb"]
            xb = [(sb("xb", [128, 8, 512]), fw.res("xb", dma=True)) for _ in range(2)]
            hT = sb("hT", [128, 8, 512], BF16); rhT = fw.res("hT")
            sq = [sb("sq", [128, 512]) for _ in range(2)]; rsq = [fw.res("sq") for _ in range(2)]
            rstd = sb("rstd", [128, 512]); rrstd = fw.res("rstd")
            tmp = sb("tmp", [128, 512]); rtmp = fw.res("tmp")
            pst = fw.psum("pst", [128, 512], F32, ps); rpst = fw.res("pst", excl=True)
            pfm = [(fw.psum("pfm", [128, 512], F32, ps), fw.res("pfm", excl=True)) for _ in range(2)]
            ptm = [(fw.psum("ptm", [128, 512], F32, ps), fw.res("ptm", excl=True)) for _ in range(2)]
            psA = fw.psum("psA", [128, 256], F32, ps); rpsA = fw.res("psA", excl=True)
            ptr = fw.psum("ptr", [128, 2, 128], BF16, ps); rptr = fw.res("ptr", excl=True)
            ofm = [(sb("ofm", [128, 2, 512], BF16), fw.res("ofm", dma=True)) for _ in range(3)]
            ofd = [(sb("ofd", [128, 2, 512], F32), fw.res("ofd", dma=True)) for _ in range(2)]
            sgd = sb("sgd", [128, 512]); rsgd = fw.res("sgd")
            otm = [(sb("otm", [128, 4, 256], BF16), fw.res("otm", dma=True)) for _ in range(4)]
            oco = [(sb("oco", [128, 4, 256], F32), fw.res("oco", dma=True)) for _ in range(2)]
            ocv = [(sb("ocv", [128, 4, 256], F32), fw.res("ocv", dma=True)) for _ in range(2)]
            ocg = [(sb("ocg", [128, 4, 16], F32), fw.res("ocg", dma=True)) for _ in range(2)]
            yA = [(sb("yA", [128, 2, 512], BF16), fw.res("yA", dma=True)) for _ in range(2)]
            g1 = sb("g1", [128, 512]); rg1 = fw.res("g1")
            g2 = sb("g2", [128, 512]); rg2 = fw.res("g2")
            gu = sb("gu", [128, 512]); rgu = fw.res("gu")
            vc = sb("vc", [128, 256]); rvc = fw.res("vc")
            vn = sb("vn", [128, 256], BF16); rvn = fw.res("vn")
            st1 = sb("st1", [128, 4]); rst1 = fw.res("st1")
            junk = sb("junk", [128, 256]); rjunk = fw.res("junk")
            yAt = sb("yAt", [128, 256], BF16); ryAt = fw.res("yAt")
            nfm = ntm = nofm = 0
            for u in range(3):
                for b in range(self.NBK):
                    t0 = u * self.UL + b * 512
                    gi = u * self.NBK + b
                    xt, rx = xb[gi % 2]
                    self.ld(xt[:], xsrc[:, t0:t0 + 512].rearrange("(k p) t -> p k t", p=128), rx, rxsrc)
                    import os
                    cut = int(os.environ.get("KCUT", "99"))
                    if cut < 1:
                        continue
                    self.norm_mod(xt, rx, hT, rhT, sq, rsq, pst, rpst, rstd, rrstd, l, 0, u, tmp, rtmp)
                    if cut < 2:
                        continue
                    fm_jobs = [(512, "bq"), (768, "bk"), (1280, "cq"), (1536, "ck")]
                    for col0, name in fm_jobs:
                        ot, ro = ofm[nofm % 3]
                        nofm += 1
                        for c in range(2):
                            pt, rp = pfm[nfm % 2]
                            nfm += 1
                            for k in range(8):
                                fw.op("pe", lambda e, k=k, c=c: e.matmul(pt[:], W[:, k, col0 + c * 128:col0 + (c + 1) * 128], hT[:, k, :],
                                                                          start=(k == 0), stop=(k == 7)),
                                      reads=[rW, rhT], writes=[rp])
                            scale = 0.125 if name in ("bq", "cq") else 1.0
                            fw.op("act", lambda e, c=c: e.activation(ot[:, c, :], pt[:], AF.Identity, scale=scale),
                                  reads=[rp], writes=[ro])
                        self.stt(S[name][:, t0:t0 + 512].rearrange("(c p) t -> p c t", p=128), ot[:], ro, R[name])
                    if cut < 3:
                        continue
                    od, rod = ofd[gi % 2]
                    for c in range(2):
                        pa, rpa = pfm[nfm % 2]
                        nfm += 1
                        pg, rpg = pfm[nfm % 2]
                        nfm += 1
                        for k in range(8):
                            fw.op("pe", lambda e, k=k, c=c: e.matmul(pa[:], W[:, k, 2320 + c * 128:2320 + (c + 1) * 128], hT[:, k, :],
                                                                      start=(k == 0), stop=(k == 7)), reads=[rW, rhT], writes=[rpa])
                        for k in range(8):
                            fw.op("pe", lambda e, k=k, c=c: e.matmul(pg[:], W[:, k, 2576 + c * 128:2576 + (c + 1) * 128], hT[:, k, :],
                                                                      start=(k == 0), stop=(k == 7)), reads=[rW, rhT], writes=[rpg])
                        fw.op("act", lambda e: e.activation(sgd[:], pg[:], AF.Sigmoid), reads=[rpg], writes=[rsgd])
                        fw.op("dve", lambda e, c=c: e.tensor_tensor(od[:, c, :], pa[:], sgd[:], ALU.mult),
                              reads=[rpa, rsgd], writes=[rod])
                    self.stt(S["dy"][:, t0:t0 + 512].rearrange("(c p) t -> p c t", p=128), od[:], rod, R["dy"])
                    if cut < 4:
                        continue
                    obv, robv = otm[(2 * gi) % 4]
                    okt, rokt = otm[(2 * gi + 1) % 4]
                    ovt, rovt = ocv[gi % 2]
                    oo, roo = oco[gi % 2]
                    og, rog = ocg[gi % 2]
                    ya, rya = yA[gi % 2]
                    for ch in range(4):
                        ts_ = slice(ch * 128, (ch + 1) * 128)

                        def tm_mm(col0, ncol):
                            nonlocal ntm
                            pt, rp = ptm[ntm % 2]
                            ntm += 1
                            for k in range(8):
                                fw.op("pe", lambda e, k=k: e.matmul(pt[:, 0:ncol], hT[:, k, ts_], W[:, k, col0:col0 + ncol],
                                                                    start=(k == 0), stop=(k == 7)), reads=[rW, rhT], writes=[rp])
                            return pt, rp
                        skip = os.environ.get("KSKIP", "").split(",")
                        if "tm" in skip:
                            continue
                        if "bv" not in skip:
                            pt, rp = tm_mm(1024, 256)
                            if "bve" not in skip:
                                fw.op("act", lambda e: e.copy(obv[:, ch, :], pt[:, 0:256]), reads=[rp], writes=[robv])
                        if "ckv" not in skip:
                            pt, rp = tm_mm(1536, 512)
                            if "e1" not in skip:
                                fw.op("act", lambda e: e.copy(okt[:, ch, :], pt[:, 0:256]), reads=[rp], writes=[rokt])
                            if "e2" not in skip:
                                fw.op("dve", lambda e: e.tensor_copy(ovt[:, ch, :], pt[:, 256:512]), reads=[rp], writes=[rovt])
                        if "og" in skip:
                            continue
                        pt, rp = tm_mm(2048, 272)
                        fw.op("act", lambda e: e.activation(oo[:, ch, :], pt[:, 0:256], AF.Sigmoid), reads=[rp], writes=[roo])
                        fw.op("dve", lambda e: e.tensor_copy(og[:, ch, :], pt[:, 256:272]), reads=[rp], writes=[rog])
                        if cut < 5:
                            continue
                        pt, rp = tm_mm(0, 512)
                        fw.op("act", lambda e: e.activation(g1[:], pt[:], AF.Square), reads=[rp], writes=[rg1])
                        fw.op("dve", lambda e: e.tensor_scalar(g1[:], g1[:], 0.044715, 1.0, ALU.mult, ALU.add), reads=[rg1], writes=[rg1])
                        fw.op("dve", lambda e: e.tensor_tensor(g2[:], g1[:], pt[:], ALU.mult), reads=[rg1, rp], writes=[rg2])
                        fw.op("act", lambda e: e.activation(g2[:], g2[:], AF.Sigmoid, scale=1.5957691216), reads=[rg2], writes=[rg2])
                        fw.op("dve", lambda e: e.tensor_tensor(gu[:], g2[:], pt[:], ALU.mult), reads=[rg2, rp], writes=[rgu])
                        fw.op("dve", lambda e: e.reduce_sum(st1[:, 0:1], gu[:, 256:512], AX.X), reads=[rgu], writes=[rst1])
                        fw.op("dve", lambda e: e.tensor_scalar(st1[:, 1:2], st1[:, 0:1], 1.0 / 256, None, ALU.mult), reads=[rst1], writes=[rst1])
                        fw.op("dve", lambda e: e.tensor_scalar(vc[:], gu[:, 256:512], st1[:, 1:2], None, ALU.subtract),
                              reads=[rgu, rst1], writes=[rvc])
                        fw.op("act", lambda e: e.activation(junk[:], vc[:], AF.Square, accum_out=st1[:, 2:3]),
                              reads=[rvc], writes=[rjunk, rst1])
                        fw.op("act", lambda e: e.activation(st1[:, 3:4], st1[:, 2:3], AF.Sqrt, scale=1.0 / 256, bias=self.eps_ap()),
                              reads=[rst1, self.reps], writes=[rst1])
                        fw.op("dve", lambda e: e.reciprocal(st1[:, 3:4], st1[:, 3:4]), reads=[rst1], writes=[rst1])
                        fw.op("dve", lambda e: e.scalar_tensor_tensor(vc[:], vc[:], st1[:, 3:4], aln[:, 0:256], ALU.mult, ALU.mult),
                              reads=[rvc, rst1, raln], writes=[rvc])
                        fw.op("dve", lambda e: e.tensor_tensor(vn[:], vc[:], aln[:, 256:512], ALU.add), reads=[rvc, raln], writes=[rvn])
                        for g in range(4):
                            fw.op("pe", lambda e, g=g: e.matmul(psA[:, g * 64:(g + 1) * 64], wsp[:, g, :], vn[:, g * 64:(g + 1) * 64],
                                                                start=True, stop=True), reads=[rwsp, rvn], writes=[rpsA])
                        for g in range(4):
                            fw.op("dve", lambda e, g=g: e.scalar_tensor_tensor(yAt[:, g * 64:(g + 1) * 64], psA[:, g * 64:(g + 1) * 64],
                                                                               absp[:, l, g:g + 1], gu[:, g * 64:(g + 1) * 64],
                                                                               ALU.add, ALU.mult),
                                  reads=[rpsA, rabsp, rgu], writes=[ryAt])
                        for c in range(2):
                            fw.op("pe", lambda e, c=c: e.transpose(ptr[:, c, :], yAt[:, c * 128:(c + 1) * 128], identb[:]),
                                  reads=[ryAt, ridb], writes=[rptr])
                        fw.op("act", lambda e: e.copy(ya[:, :, ts_], ptr[:]), reads=[rptr], writes=[rya])
                    tsl = slice(t0, t0 + 512)
                    if "st" in os.environ.get("KSKIP", "").split(","):
                        continue
                    self.stt(S["bv"][tsl, :].rearrange("(c p) f -> p c f", p=128), obv[:], robv, R["bv"])
                    self.stt(S["ckt"][tsl, :].rearrange("(c p) f -> p c f", p=128), okt[:], rokt, R["ckt"])
                    self.stt(S["cvt"][tsl, :].rearrange("(c p) f -> p c f", p=128), ovt[:], rovt, R["cvt"])
                    self.stt(S["co"][tsl, :].rearrange("(c p) f -> p c f", p=128), oo[:], roo, R["co"])
                    self.stt(S["cg"][tsl, :].rearrange("(c p) f -> p c f", p=128), og[:], rog, R["cg"])
                    self.stt(S["yall"][0:256, tsl].rearrange("(c p) t -> p c t", p=128), ya[:], rya, R["yall"])
            fw.phase_end()

    def attn_plan(self):
        NB2 = self.NB2
        plan = []
        for u in range(3):
            for b in range(NB2):
                if u == 2 or (u == 0 and b < NB2 - 2) or (u == 1 and b >= 2):
                    if b == 0 and u != 1:
                        cls = "TOP0"
                    elif b == 1 and u != 1:
                        cls = "TOP1"
                    elif b == NB2 - 2 and u != 0:
                        cls = "BOT1"
                    elif b == NB2 - 1 and u != 0:
                        cls = "BOT0"
                    else:
                        cls = "INT"
                elif u == 0:
                    cls = "JA1" if b == NB2 - 2 else "JA0"
                else:
                    cls = "JB0" if b == 0 else "JB1"
                s0, offs = CLS[cls]
                ents = []
                for i, o in enumerate(offs):
                    kp = b + o
                    ku = u
                    if kp >= NB2:
                        ku, kp = u + 1, kp - NB2
                    elif kp < 0:
                        ku, kp = u - 1, kp + NB2
                    assert 0 <= ku <= 2 and (ku == u or (u, ku) in ((0, 1), (1, 0)))
                    ents.append((ku, kp, s0 + i))
                plan.append((u, b, ents))
        return plan

    def gen_attn(self, l, ptr, rptr, ps):
        fw, I, S, R, C = self.fw, self.I, self.S, self.R, self.C
        UL, NB2 = self.UL, self.NB2
        identb, ridb = C["identb"]
        if True:
            sb = lambda n, s, dt=F32: fw.sbuf(n, s, dt, ps)
            bt = sb("bt", [128, NSLOT * 4, 128], BF16); rbt = fw.res("bt")
            stg = [(sb("bstg", [128, 8, 128], F32), fw.res("bstg", dma=True)) for _ in range(2)]
            n = 0
            for s0 in range(0, NSLOT * 4, 8):
                s1 = min(s0 + 8, NSLOT * 4)
                stt_, rs = stg[n % 2]
                n += 1
                self.ld(stt_[:, 0:s1 - s0, :], I["rpbt"][l, :, s0:s1, :], rs)
                fw.op("pool", lambda e, s0=s0, s1=s1, stt_=stt_: e.tensor_copy(bt[:, s0:s1, :], stt_[:, 0:s1 - s0, :]),
                      reads=[rs], writes=[rbt])
            kT = sb("kT", [128, 2, 2 * UL], BF16); rkT = fw.res("kT", dma=True)
            V = sb("V", [128, 2 * self.NCH, 4, 65], BF16); rV = fw.res("V", dma=True)
            fw.op("pool", lambda e: e.memset(V[:], 1.0), writes=[rV])
            qT = [(sb("qT", [128, 2, 512], BF16), fw.res("qT", dma=True)) for _ in range(2)]
            pS = [(fw.psum("pS", [128, 8, 128], F32, ps), fw.res("pS", excl=True)) for _ in range(1)]
            pO = [(fw.psum("pO", [128, 4, 65], F32, ps), fw.res("pO", excl=True)) for _ in range(1)]
            PT = [(sb("PT", [128, 8, 128], BF16), fw.res("PT")) for _ in range(3)]
            rc = sb("rc", [128, 4]); rrc = fw.res("rc")
            yB = sb("yB", [128, 4, 64], BF16); ryB = fw.res("yB")
            yo = [(sb("yBo", [128, 2, 512], BF16), fw.res("yBo", dma=True)) for _ in range(2)]
            plan = self.attn_plan()
            nps = npt = nq = 0
            for grp in (0, 2):
                units = (0, 1) if grp == 0 else (2,)
                base = grp * UL
                ntok = len(units) * UL
                self.ld(kT[:, :, 0:ntok], S["bk"][:, base:base + ntok].rearrange("(c p) t -> p c t", p=128), rkT, R["bk"])
                for h in range(4):
                    self.ld(V[:, 0:ntok // 128, h, 0:64],
                            S["bv"][base:base + ntok, h * 64:(h + 1) * 64].rearrange("(c p) d -> p c d", p=128),
                            rV, R["bv"])
                for (u, b, ents) in plan:
                    if u not in units:
                        continue
                    if b % 4 == 0:
                        qt, rq = qT[nq % 2]
                        yot, ryo = yo[nq % 2]
                        nq += 1
                        tq0 = u * UL + b * 128
                        self.ld(qt[:], S["bq"][:, tq0:tq0 + 512].rearrange("(c p) t -> p c t", p=128), rq, R["bq"])
                    qs = slice((b % 4) * 128, (b % 4 + 1) * 128)
                    po, rpo = pO[0]
                    ne = len(ents)
                    for h in range(4):
                        hp = slice((h % 2) * 64, (h % 2) * 64 + 64)
                        hc = h // 2
                        pst_, rps = pS[0]
                        nps += 1
                        for i, (ku, kp, slot) in enumerate(ents):
                            k0 = (ku * UL - base) + kp * 128
                            fw.op("pe", lambda e, i=i, k0=k0: e.matmul(pst_[:, i, :], kT[hp, hc, k0:k0 + 128], qt[hp, hc, qs],
                                                                        start=True, stop=False), reads=[rkT, rq], writes=[rps])
                            fw.op("pe", lambda e, i=i, slot=slot: e.matmul(pst_[:, i, :], identb[:], bt[:, slot * 4 + h, :],
                                                                            start=False, stop=True), reads=[ridb, rbt], writes=[rps])
                        pt_, rpt = PT[npt % 3]
                        npt += 1
                        n1 = min(ne, 4)
                        fw.op("act", lambda e: e.activation(pt_[:, 0:n1, :], pst_[:, 0:n1, :], AF.Exp), reads=[rps], writes=[rpt])
                        if ne > 4:
                            fw.op("act", lambda e: e.activation(pt_[:, 4:ne, :], pst_[:, 4:ne, :], AF.Exp), reads=[rps], writes=[rpt])
                        for i, (ku, kp, slot) in enumerate(ents):
                            vc_ = (ku * UL - base) // 128 + kp
                            fw.op("pe", lambda e, i=i, vc_=vc_: e.matmul(po[:, h, :], pt_[:, i, :], V[:, vc_, h, :],
                                                                          start=(i == 0), stop=(i == ne - 1)),
                                  reads=[rpt, rV], writes=[rpo])
                    fw.op("dve", lambda e: e.reciprocal(rc[:], po[:, :, 64]), reads=[rpo], writes=[rrc])
                    fw.op("dve", lambda e: e.tensor_tensor(yB[:], po[:, :, 0:64], rc[:].unsqueeze(2).to_broadcast([128, 4, 64]), ALU.mult),
                          reads=[rpo, rrc], writes=[ryB])
                    for c in range(2):
                        fw.op("pe", lambda e, c=c: e.transpose(ptr[:, c, :], yB[:, 2 * c:2 * c + 2, :], identb[:]),
                              reads=[ryB, ridb], writes=[rptr])
                    fw.op("act", lambda e: e.copy(yot[:, :, qs], ptr[:]), reads=[rptr], writes=[ryo])
                    if b % 4 == 3:
                        self.stt(S["yall"][256:512, tq0:tq0 + 512].rearrange("(c p) t -> p c t", p=128), yot[:], ryo, R["yall"])
                    yield

    def gen_mlstm(self, l, ptr, rptr, ps):
        fw, I, S, R, C = self.fw, self.I, self.S, self.R, self.C
        UL, NCH = self.UL, self.NCH
        identb, ridb = C["identb"]
        onesf, ronesf = C["onesf"]
        triu, rtriu = C["triu"]
        tril, rtril = C["tril"]
        sel, rsel = C["sel"]
        link, rlink = C["link"]
        if True:
            sb = lambda n, s, dt=F32: fw.sbuf(n, s, dt, ps)
            gb = sb("gb", [128, 16]); rgb = fw.res("gb", dma=True)
            self.ld(gb[:], I["c_gate_b"][l].partition_broadcast(128), rgb)
            hng = sb("hng", [128, 256]); rhng = fw.res("hng", dma=True)
            self.ld(hng[:], I["c_hnorm"][l].partition_broadcast(128), rhng)
            NBUF = 2
            qTb = [(sb("cqT", [128, 2, 512], BF16), fw.res("cqT", dma=True)) for _ in range(NBUF)]
            kTb = [(sb("ckT", [128, 2, 512], BF16), fw.res("ckT", dma=True)) for _ in range(NBUF)]
            ktb = [(sb("ckt", [128, 4, 256], BF16), fw.res("ckt", dma=True)) for _ in range(NBUF)]
            vtb = [(sb("cvt", [128, 4, 4, 65], F32), fw.res("cvt", dma=True)) for _ in range(NBUF)]
            for vt_, rv_ in vtb:
                fw.op("pool", lambda e, vt_=vt_: e.memset(vt_[:], 1.0), writes=[rv_])
            gtb = [(sb("cgt", [128, 4, 16]), fw.res("cgt", dma=True)) for _ in range(NBUF)]
            otb = [(sb("cot", [128, 4, 256]), fw.res("cot", dma=True)) for _ in range(NBUF)]
            hfb = [(sb("chf", [128, 4, 256]), fw.res("chf", dma=True)) for _ in range(NBUF)]
            yCo = [(sb("yCo", [128, 2, 512], BF16), fw.res("yCo", dma=True)) for _ in range(2)]
            CT = sb("CT", [128, 2, 65]); rCT = fw.res("CT")
            CTb = sb("CTb", [128, 2, 65], BF16); rCTb = fw.res("CTb")
            klo = sb("klo", [128, 2, 128], BF16); rklo = fw.res("klo")
            khi = sb("khi", [128, 2, 128], BF16); rkhi = fw.res("khi")
            fw.op("pool", lambda e: e.memset(klo[:], 0.0), writes=[rklo])
            fw.op("pool", lambda e: e.memset(khi[:], 0.0), writes=[rkhi])
            g = sb("g", [128, 8]); rg = fw.res("g")
            lf = sb("lf", [128, 4]); rlf = fw.res("lf")
            ex = sb("ex", [128, 12]); rex = fw.res("ex")
            Fm = sb("Fm", [128, 2]); rFm = fw.res("Fm")
            gsb = sb("gsb", [128, 8]); rgsb = fw.res("gsb")
            wm = sb("wm", [128, 4, 128], BF16); rwm = fw.res("wm")
            va = sb("va", [128, 4, 65], BF16); rva = fw.res("va")
            tn = sb("tn", [128, 4, 65]); rtn = fw.res("tn")
            dd = sb("dd", [128, 4]); rdd = fw.res("dd")
            hd = sb("hd", [128, 256]); rhd = fw.res("hd")
            sqh = sb("sqh", [128, 256]); rsqh = fw.res("sqh")
            ss = sb("ss", [128, 4]); rss = fw.res("ss")
            yCt = sb("yCt", [128, 256], BF16); ryCt = fw.res("yCt")
            pGC = fw.psum("pGC", [128, 512], F32, ps); rpG = fw.res("pGC", excl=True)
            pG = pGC[:, 0:16]
            pC = pGC[:, 32:162].rearrange("p (a b) -> p a b", b=65); rpC = rpG
            pQK = fw.psum("pQK", [128, 2, 128], F32, ps); rpQK = fw.res("pQK", excl=True)
            pQK2 = fw.psum("pQK2", [128, 2, 128], F32, ps); rpQK2 = fw.res("pQK2", excl=True)
            pN = fw.psum("pN", [128, 4, 65], F32, ps); rpN = fw.res("pN", excl=True)
            nld = 0
            for d in (0, 1):
                go = 8 * d
                tri, rtri = (triu, rtriu) if d == 0 else (tril, rtril)
                unit_order = (0, 1, 2) if d == 0 else (1, 0, 2)
                for ui, u in enumerate(unit_order):
                    if ui == 1:
                        fw.op("dve", lambda e: e.tensor_scalar(CT[:], CT[:], link[:, 0:1], None, ALU.mult),
                              reads=[rCT, rlink], writes=[rCT])
                    else:
                        fw.op("dve", lambda e: e.memset(CT[:], 0.0), writes=[rCT])
                    blocks = range(self.NBK) if d == 0 else range(self.NBK - 1, -1, -1)
                    for b in blocks:
                        t0 = u * UL + b * 512
                        tsl = slice(t0, t0 + 512)
                        bi = nld % NBUF
                        nld += 1
                        qt, rq = qTb[bi]; kt, rk = kTb[bi]; ktm, rktm = ktb[bi]; vtm, rvtm = vtb[bi]
                        gt, rgt = gtb[bi]; ot, rot = otb[bi]; hf, rhf = hfb[bi]
                        self.ld(qt[:], S["cq"][:, tsl].rearrange("(c p) t -> p c t", p=128), rq, R["cq"])
                        self.ld(kt[:], S["ck"][:, tsl].rearrange("(c p) t -> p c t", p=128), rk, R["ck"])
                        self.ld(ktm[:], S["ckt"][tsl, :].rearrange("(c p) f -> p c f", p=128), rktm, R["ckt"])
                        for h in range(4):
                            self.ld(vtm[:, :, h, 0:64], S["cvt"][tsl, h * 64:(h + 1) * 64].rearrange("(c p) d -> p c d", p=128), rvtm, R["cvt"])
                        self.ld(gt[:], S["cg"][tsl, :].rearrange("(c p) f -> p c f", p=128), rgt, R["cg"])
                        if d == 1:
                            self.ld(ot[:], S["co"][tsl, :].rearrange("(c p) f -> p c f", p=128), rot, R["co"])
                            self.ld(hf[:], S["chf"][tsl, :].rearrange("(c p) f -> p c f", p=128), rhf, R["chf"])
                            yco, ryco = yCo[nld % 2]
                        chunks = range(4) if d == 0 else range(3, -1, -1)
                        for ch in chunks:
                            cs = slice(ch * 128, (ch + 1) * 128)
                            fw.op("dve", lambda e: e.tensor_tensor(g[:], gt[:, ch, go:go + 8], gb[:, go:go + 8], ALU.add),
                                  reads=[rgt, rgb], writes=[rg])
                            if int(os.environ.get('KCUTM', '99')) < 1:
                                continue
                            fw.op("act", lambda e: e.activation(lf[:], g[:, 4:8], AF.Exp, scale=-1.0), reads=[rg], writes=[rlf])
                            fw.op("act", lambda e: e.activation(lf[:], lf[:], AF.Ln, scale=1.0, bias=self.epst[:, 1:2]),
                                  reads=[rlf, self.reps], writes=[rlf])
                            fw.op("dve", lambda e: e.tensor_scalar(lf[:], lf[:], -1.0, None, ALU.mult), reads=[rlf], writes=[rlf])
                            if int(os.environ.get('KCUTM', '99')) < 2:
                                continue
                            fw.op("pe", lambda e: e.matmul(pG[:, 0:4], tri[:], lf[:], start=True, stop=True), reads=[rtri, rlf], writes=[rpG])
                            fw.op("pe", lambda e: e.matmul(pG[:, 4:8], onesf[:], lf[:], start=True, stop=True), reads=[ronesf, rlf], writes=[rpG])
                            fw.op("pe", lambda e: e.matmul(pG[:, 8:10], sel[:, 0, :], lf[:, 0:4:2], start=True, stop=False),
                                  reads=[rsel, rlf], writes=[rpG])
                            fw.op("pe", lambda e: e.matmul(pG[:, 8:10], sel[:, 1, :], lf[:, 1:4:2], start=False, stop=True),
                                  reads=[rsel, rlf], writes=[rpG])
                            if int(os.environ.get('KCUTM', '99')) < 3:
                                continue
                            fw.op("act", lambda e: e.copy(gsb[:], pG[:, 0:8]), reads=[rpG], writes=[rgsb])
                            fw.op("dve", lambda e: e.tensor_tensor(ex[:, 4:8], gsb[:, 0:4], gsb[:, 4:8], ALU.subtract), reads=[rgsb], writes=[rex])
                            fw.op("dve", lambda e: e.tensor_tensor(ex[:, 0:4], g[:, 0:4], ex[:, 4:8], ALU.subtract), reads=[rg, rex], writes=[rex])
                            fw.op("act", lambda e: e.activation(ex[:, 0:8], ex[:, 0:8], AF.Exp), reads=[rex], writes=[rex])
                            fw.op("act", lambda e: e.activation(Fm[:], pG[:, 8:10], AF.Exp), reads=[rpG], writes=[rFm])
                            if int(os.environ.get('KCUTM', '99')) < 4:
                                continue
                            fw.op("dve", lambda e: e.tensor_tensor(va[:], vtm[:, ch, :, :],
                                                                   ex[:, 0:4].unsqueeze(2).to_broadcast([128, 4, 65]), ALU.mult),
                                  reads=[rvtm, rex], writes=[rva])
                            if int(os.environ.get('KCUTM', '99')) < 5:
                                continue
                            for h in range(4):
                                hp = slice((h % 2) * 64, (h % 2) * 64 + 64)
                                pq_, rpq_ = (pQK, rpQK) if h % 2 == 0 else (pQK2, rpQK2)
                                fw.op("pe", lambda e, h=h, hp=hp: e.matmul(pq_[:, h // 2, :], kt[hp, h // 2, cs], qt[hp, h // 2, cs],
                                                                            start=True, stop=True), reads=[rk, rq], writes=[rpq_])
                            wm4 = wm[:].rearrange("p (j q) t -> p j q t", q=2)
                            fw.op("dve", lambda e: e.tensor_tensor(wm4[:, :, 0, :], pQK[:], tri[:].unsqueeze(1).to_broadcast([128, 2, 128]), ALU.mult),
                                  reads=[rpQK, rtri], writes=[rwm])
                            fw.op("dve", lambda e: e.tensor_tensor(wm4[:, :, 1, :], pQK2[:], tri[:].unsqueeze(1).to_broadcast([128, 2, 128]), ALU.mult),
                                  reads=[rpQK2, rtri], writes=[rwm])
                            if int(os.environ.get('KCUTM', '99')) < 6:
                                continue
                            fw.op("dve", lambda e: e.tensor_tensor(CTb[:], CT[:], Fm[:].unsqueeze(2).to_broadcast([128, 2, 65]), ALU.mult),
                                  reads=[rCT, rFm], writes=[rCTb])
                            for h in range(4):
                                hp = slice((h % 2) * 64, (h % 2) * 64 + 64)
                                fw.op("pe", lambda e, h=h: e.matmul(pN[:, h, :], wm[:, h, :], va[:, h, :], start=True, stop=False),
                                      reads=[rwm, rva], writes=[rpN])
                                fw.op("pe", lambda e, h=h, hp=hp: e.matmul(pN[:, h, :], qt[hp, h // 2, cs], CTb[hp, h // 2, :], start=False, stop=True),
                                      reads=[rq, rCTb], writes=[rpN])
                            if int(os.environ.get('KCUTM', '99')) < 7:
                                continue
                            fw.op("pool", lambda e: e.tensor_copy(klo[:, :, 0:64], ktm[:, ch, :].rearrange("p (a b) -> p a b", b=128)[:, :, 0:64]),
                                  reads=[rktm], writes=[rklo])
                            fw.op("pool", lambda e: e.tensor_copy(khi[:, :, 64:128], ktm[:, ch, :].rearrange("p (a b) -> p a b", b=128)[:, :, 64:128]),
                                  reads=[rktm], writes=[rkhi])
                            for p in range(2):
                                fw.op("pe", lambda e, p=p: e.matmul(pC[:, p, :], klo[:, p, :], va[:, 2 * p, :], start=True, stop=False),
                                      reads=[rklo, rva], writes=[rpC])
                                fw.op("pe", lambda e, p=p: e.matmul(pC[:, p, :], khi[:, p, :], va[:, 2 * p + 1, :], start=False, stop=True),
                                      reads=[rkhi, rva], writes=[rpC])
                            for p in range(2):
                                fw.op("dve", lambda e, p=p: e.scalar_tensor_tensor(CT[:, p, :], CT[:, p, :], Fm[:, p:p + 1], pC[:, p, :],
                                                                                   ALU.mult, ALU.add),
                                      reads=[rCT, rFm, rpC], writes=[rCT])
                            if int(os.environ.get('KCUTM', '99')) < 8:
                                continue
                            fw.op("dve", lambda e: e.tensor_tensor(tn[:], pN[:], ex[:, 4:8].unsqueeze(2).to_broadcast([128, 4, 65]), ALU.mult),
                                  reads=[rpN, rex], writes=[rtn])
                            fw.op("dve", lambda e: e.scalar_tensor_tensor(dd[:], tn[:, :, 64], -1.0, tn[:, :, 64], ALU.mult, ALU.max), reads=[rtn], writes=[rdd])
                            fw.op("dve", lambda e: e.tensor_scalar(dd[:], dd[:], 1.0, None, ALU.max), reads=[rdd], writes=[rdd])
                            fw.op("dve", lambda e: e.reciprocal(dd[:], dd[:]), reads=[rdd], writes=[rdd])
                            if d == 0:
                                fw.op("dve", lambda e: e.tensor_tensor(hf[:, ch, :].rearrange("p (h d) -> p h d", d=64), tn[:, :, 0:64],
                                                                       dd[:].unsqueeze(2).to_broadcast([128, 4, 64]), ALU.mult),
                                      reads=[rtn, rdd], writes=[rhf])
                            else:
                                hd3 = hd[:].rearrange("p (h d) -> p h d", d=64)
                                fw.op("dve", lambda e: e.tensor_tensor(hd3, tn[:, :, 0:64],
                                                                       dd[:].unsqueeze(2).to_broadcast([128, 4, 64]), ALU.mult),
                                      reads=[rtn, rdd], writes=[rhd])
                                fw.op("dve", lambda e: e.tensor_tensor(hd[:], hd[:], hf[:, ch, :], ALU.add), reads=[rhd, rhf], writes=[rhd])
                                fw.op("dve", lambda e: e.tensor_tensor(sqh[:], hd[:], hd[:], ALU.mult), reads=[rhd], writes=[rsqh])
                                fw.op("dve", lambda e: e.reduce_sum(ss[:], sqh[:].rearrange("p (h d) -> p h d", d=64), AX.X), reads=[rsqh], writes=[rss])
                                fw.op("act", lambda e: e.activation(ss[:], ss[:], AF.Ln, scale=1.0 / 64, bias=self.eps_ap()),
                                      reads=[rss, self.reps], writes=[rss])
                                fw.op("act", lambda e: e.activation(ss[:], ss[:], AF.Exp, scale=-0.5), reads=[rss], writes=[rss])
                                fw.op("dve", lambda e: e.tensor_tensor(hd3, hd3, ss[:].unsqueeze(2).to_broadcast([128, 4, 64]), ALU.mult),
                                      reads=[rhd, rss], writes=[rhd])
                                fw.op("dve", lambda e: e.tensor_tensor(hd[:], hd[:], hng[:], ALU.mult), reads=[rhd, rhng], writes=[rhd])
                                fw.op("dve", lambda e: e.tensor_tensor(yCt[:], hd[:], ot[:, ch, :], ALU.mult), reads=[rhd, rot], writes=[ryCt])
                                for c in range(2):
                                    fw.op("pe", lambda e, c=c: e.transpose(ptr[:, c, :], yCt[:, c * 128:(c + 1) * 128], identb[:]),
                                          reads=[ryCt, ridb], writes=[rptr])
                                fw.op("act", lambda e: e.copy(yco[:, :, cs], ptr[:]), reads=[rptr], writes=[ryco])
                            yield
                        if d == 0:
                            self.stt(S["chf"][tsl, :].rearrange("(c p) f -> p c f", p=128), hf[:], rhf, R["chf"])
                        else:
                            self.stt(S["yall"][512:768, tsl].rearrange("(c p) t -> p c t", p=128), yco[:], ryco, R["yall"])

    def phase_bc(self, l):
        fw = self.fw
        fw.phase_begin()
        with ExitStack() as ps:
            ptr = fw.psum("ptrbc", [128, 2, 128], BF16, ps)
            rptr = fw.res("ptrbc", excl=True)
            ga = self.gen_attn(l, ptr, rptr, ps)
            gm = self.gen_mlstm(l, ptr, rptr, ps)
            live = [[ga, 1], [gm, 2]]
            while live:
                for ent in list(live):
                    g, r = ent
                    try:
                        for _ in range(r):
                            next(g)
                    except StopIteration:
                        live.remove(ent)
            fw.phase_end()

    def phase_conv(self, l):
        fw, I, S, R, C = self.fw, self.I, self.S, self.R, self.C
        UL = self.UL
        onesb, ronesb = C["onesb"]
        link, rlink = C["link"]
        cw, rcw = C["d_conv_wT"]
        dv, rdv = C["d_vec"]
        fw.phase_begin()
        with ExitStack() as ps:
            sb = lambda n, s, dt=F32: fw.sbuf(n, s, dt, ps)
            yp = [(sb("yp", [128, 2, UL + 30]), fw.res("yp", dma=True)) for _ in range(2)]
            acc = sb("acc", [128, 2, 512]); racc2 = [fw.res("acc0"), fw.res("acc1")]
            identf, ridf = C["identf"]
            dg = sb("dg", [128, 2, 31, 128], BF16); rdg = fw.res("dg")
            for cc in range(2):
                for k in range(31):
                    en = "dve" if (k % 2 == 0) else "pool"
                    fw.op(en, lambda e, cc=cc, k=k: e.tensor_scalar(dg[:, cc, k, :], identf[:], cw[:, l, cc, k:k + 1], None, ALU.mult),
                          reads=[ridf, rcw], writes=[rdg])
            ypb = [(sb("ypb", [128, 2, UL + 30], BF16), fw.res("ypb")) for _ in range(2)]
            pcv = [(fw.psum("pcv", [128, 512], F32, ps), fw.res("pcv", excl=True)) for _ in range(2)]
            accb = sb("accb", [128, 2, 512], BF16); raccb = fw.res("accb")
            sqb = sb("sqb", [128, 2, 512], BF16); rsqb = fw.res("sqb")
            m2 = sb("m2", [128, 512]); rm2 = fw.res("m2")
            rs = sb("rs", [128, 512]); rrs = fw.res("rs")
            tt = sb("tt", [128, 512]); rtt = fw.res("tt")
            yo = [(sb("yDo", [128, 2, 512], BF16), fw.res("yDo", dma=True)) for _ in range(2)]
            pM = fw.psum("pM", [128, 512], F32, ps); rpM = fw.res("pM", excl=True)
            pQ = fw.psum("pQ", [128, 512], F32, ps); rpQ = fw.res("pQ", excl=True)
            no = 0
            for u in range(3):
                ypt, ryp = yp[u % 2]
                fw.op("pool", lambda e: e.memset(ypt[:, :, 0:15], 0.0), writes=[ryp])
                fw.op("pool", lambda e: e.memset(ypt[:, :, UL + 15:UL + 30], 0.0), writes=[ryp])
                src = S["dy"].rearrange("(c p) t -> p c t", p=128)
                self.ld(ypt[:, :, 15:15 + UL], src[:, :, u * UL:(u + 1) * UL], ryp, R["dy"])
                if u == 0:
                    self.ld(ypt[:, :, UL + 15:UL + 30], src[:, :, UL:UL + 15], ryp, R["dy"])
                    fw.op("pool", lambda e: e.tensor_scalar(ypt[:, :, UL + 15:UL + 30], ypt[:, :, UL + 15:UL + 30], link[:, 0:1], None, ALU.mult),
                          reads=[ryp, rlink], writes=[ryp])
                elif u == 1:
                    self.ld(ypt[:, :, 0:15], src[:, :, UL - 15:UL], ryp, R["dy"])
                    fw.op("pool", lambda e: e.tensor_scalar(ypt[:, :, 0:15], ypt[:, :, 0:15], link[:, 0:1], None, ALU.mult),
                          reads=[ryp, rlink], writes=[ryp])
                ypbt, rypb = ypb[u % 2]
                fw.op("act", lambda e: e.copy(ypbt[:, 0, :], ypt[:, 0, :]), reads=[ryp], writes=[rypb])
                fw.op("pool", lambda e: e.tensor_copy(ypbt[:, 1, :], ypt[:, 1, :]), reads=[ryp], writes=[rypb])
                for b in range(self.NBK):
                    t0 = b * 512
                    for c in range(2):
                        pct, rpc = pcv[c]
                        for k in range(31):
                            fw.op("pe", lambda e, c=c, k=k: e.matmul(pct[:], dg[:, c, k, :], ypbt[:, c, t0 + k:t0 + k + 512],
                                                                      start=(k == 0), stop=(k == 30)), reads=[rdg, rypb], writes=[rpc])
                        fw.op("act", lambda e, c=c: e.activation(acc[:, c, :], pct[:], AF.Identity, scale=1.0, bias=dv[:, l, 0, c:c + 1]),
                              reads=[rpc, rdv], writes=[racc2[c]])
                    for c in range(2):
                        fw.op("act", lambda e, c=c: e.activation(sqb[:, c, :], acc[:, c, :], AF.Square), reads=[racc2[c]], writes=[rsqb])
                        fw.op("pool", lambda e, c=c: e.tensor_copy(accb[:, c, :], acc[:, c, :]), reads=[racc2[c]], writes=[raccb])
                    for c in range(2):
                        fw.op("pe", lambda e, c=c: e.matmul(pM[:], onesb[:], accb[:, c, :], start=(c == 0), stop=(c == 1)),
                              reads=[ronesb, raccb], writes=[rpM])
                    for c in range(2):
                        fw.op("pe", lambda e, c=c: e.matmul(pQ[:], onesb[:], sqb[:, c, :], start=(c == 0), stop=(c == 1)),
                              reads=[ronesb, rsqb], writes=[rpQ])
                    fw.op("act", lambda e: e.activation(m2[:], pM[:], AF.Square, scale=1.0 / 256), reads=[rpM], writes=[rm2])
                    fw.op("dve", lambda e: e.scalar_tensor_tensor(rs[:], pQ[:], 1.0 / 256, m2[:], ALU.mult, ALU.subtract),
                          reads=[rpQ, rm2], writes=[rrs])
                    fw.op("dve", lambda e: e.tensor_scalar(rs[:], rs[:], 0.0, None, ALU.max), reads=[rrs], writes=[rrs])
                    fw.op("act", lambda e: e.activation(rs[:], rs[:], AF.Sqrt, scale=1.0, bias=self.eps_ap()), reads=[rrs, self.reps], writes=[rrs])
                    fw.op("dve", lambda e: e.reciprocal(rs[:], rs[:]), reads=[rrs], writes=[rrs])
                    yot, ryo = yo[no % 2]
                    no += 1
                    for c in range(2):
                        fw.op("dve", lambda e, c=c: e.scalar_tensor_tensor(tt[:], pM[:], -1.0 / 256, acc[:, c, :], ALU.mult, ALU.add),
                              reads=[rpM, racc2[c]], writes=[rtt])
                        fw.op("dve", lambda e: e.tensor_tensor(tt[:], tt[:], rs[:], ALU.mult), reads=[rtt, rrs], writes=[rtt])
                        fw.op("act", lambda e, c=c: e.activation(yot[:, c, :], tt[:], AF.Silu, scale=dv[:, l, 1, c:c + 1], bias=dv[:, l, 2, c:c + 1]),
                              reads=[rtt, rdv], writes=[ryo])
                    tg = u * UL + t0
                    self.stt(S["yall"][768:1024, tg:tg + 512].rearrange("(c p) t -> p c t", p=128), yot[:], ryo, R["yall"])
            fw.phase_end()

    def phase_p3a(self, l):
        fw, I, S, R, C = self.fw, self.I, self.S, self.R, self.C
        BT = 256
        xsrc, rxsrc = (I["xT"], R["xT"]) if l == 0 else (S["xn"], R["xn"])
        fw.phase_begin()
        with ExitStack() as ps:
            sb = lambda n, s, dt=F32: fw.sbuf(n, s, dt, ps)
            Wg = sb("wg", [128, 8, 4096], BF16); rWg = fw.res("wg")
            Wb = sb("wb", [128, 8, 1024], BF16); rWb = fw.res("wb")
            Wo = sb("wo", [128, 8, 1024], BF16); rWo = fw.res("wo")
            stg = [(sb("stg3", [128, 1024], F32), fw.res("stg3", dma=True)) for _ in range(3)]
            self.stg_i = 0
            self.load_cast(Wg, rWg, 0, 8, I["w_in"][l][:, 2832:6928], 4096, stg, ["pool", "dve", "act"], 1024)
            self.load_cast(Wb, rWb, 0, 8, I["w_branch"][l], 1024, stg, ["pool", "dve", "act"], 1024)
            self.load_cast(Wo, rWo, 0, 8, I["w_out"][l], 1024, stg, ["pool", "dve", "act"], 1024)
            xb = [(sb("xb3", [128, 8, BT]), fw.res("xb3", dma=True)) for _ in range(2)]
            yb = [(sb("yb3", [128, 8, BT], BF16), fw.res("yb3", dma=True)) for _ in range(2)]
            hT = sb("hT3", [128, 8, BT], BF16); rhT = fw.res("hT3")
            hT2 = sb("hT3b", [128, 8, BT], BF16); rhT2 = fw.res("hT3b")
            sq = [sb("sq3", [128, BT]) for _ in range(2)]; rsq = [fw.res("sq3") for _ in range(2)]
            rstd = sb("rstd3", [128, BT]); rrstd = fw.res("rstd3")
            tmp = sb("tmp3", [128, BT]); rtmp = fw.res("tmp3")
            sg = [(sb("sg3", [128, BT]), fw.res("sg3")) for _ in range(3)]
            t2 = [(sb("t23", [128, BT]), fw.res("t23")) for _ in range(2)]
            acc = [(sb("acc3", [128, BT]), fw.res("acc3")) for _ in range(2)]
            mg = sb("mg3", [128, 8, BT], BF16); rmg = fw.res("mg3")
            pg = [(fw.psum("pg3", [128, BT], F32, ps), fw.res("pg3", excl=True)) for _ in range(3)]
            pp = [(fw.psum("pp3", [128, BT], F32, ps), fw.res("pp3", excl=True)) for _ in range(3)]
            po = [(fw.psum("po3", [128, BT], F32, ps), fw.res("po3", excl=True)) for _ in range(2)]
            pst, rpst = po[1]
            n = no = nt2 = 0
            NBLK = min(self.NT // BT, int(os.environ.get('KBLK', '9999')))
            hTs = [(hT, rhT), (hT2, rhT2)]

            def prep(gi):
                t0 = gi * BT
                xt, rx = xb[gi % 2]
                yt, ry = yb[gi % 2]
                self.ld(xt[:], xsrc[:, t0:t0 + BT].rearrange("(k p) t -> p k t", p=128), rx, rxsrc)
                self.ld(yt[:], S["yall"][:, t0:t0 + BT].rearrange("(k p) t -> p k t", p=128), ry, R["yall"])

            def nm(gi):
                xt, rx = xb[gi % 2]
                h_, rh_ = hTs[gi % 2]
                self.norm_mod(xt, rx, h_, rh_, sq, rsq, pst, rpst, rstd, rrstd, l, 0, (gi * BT) // self.UL, tmp, rtmp)

            prep(0)
            nm(0)
            for gi in range(NBLK):
                t0 = gi * BT
                u = t0 // self.UL
                xt, rx = xb[gi % 2]
                yt, ry = yb[gi % 2]
                hT, rhT = hTs[gi % 2]
                if gi + 1 < NBLK:
                    prep(gi + 1)
                for f in range(8):
                    acct, racc = acc[f % 2]
                    for br in range(4):
                        pgt, rpg = pg[n % 3]
                        ppt, rpp = pp[n % 3]
                        sgt, rsg = sg[n % 3]
                        n += 1
                        col = br * 1024 + f * 128
                        for k in range(8):
                            fw.op("pe", lambda e, k=k: e.matmul(pgt[:], Wg[:, k, col:col + 128], hT[:, k, :], start=(k == 0), stop=(k == 7)),
                                  reads=[rWg, rhT], writes=[rpg])
                        for k in range(2):
                            fw.op("pe", lambda e, k=k: e.matmul(ppt[:], Wb[:, br * 2 + k, f * 128:(f + 1) * 128], yt[:, br * 2 + k, :],
                                                                start=(k == 0), stop=(k == 1)), reads=[rWb, ry], writes=[rpp])
                        fw.op("act", lambda e: e.activation(sgt[:], pgt[:], AF.Sigmoid), reads=[rpg], writes=[rsg])
                        if br == 0:
                            fw.op("dve", lambda e: e.tensor_tensor(acct[:], sgt[:], ppt[:], ALU.mult), reads=[rsg, rpp], writes=[racc])
                        else:
                            t2t, rt2 = t2[nt2 % 2]
                            nt2 += 1
                            fw.op("dve", lambda e: e.tensor_tensor(t2t[:], sgt[:], ppt[:], ALU.mult), reads=[rsg, rpp], writes=[rt2])
                            if br < 3:
                                fw.op("pool", lambda e: e.tensor_tensor(acct[:], acct[:], t2t[:], ALU.add), reads=[racc, rt2], writes=[racc])
                            else:
                                fw.op("pool", lambda e, f=f: e.tensor_tensor(mg[:, f, :], acct[:], t2t[:], ALU.add), reads=[racc, rt2], writes=[rmg])
                if gi + 1 < NBLK:
                    nm(gi + 1)
                for f in range(8):
                    pot, rpo = po[no % 2]
                    no += 1
                    for k in range(8):
                        fw.op("pe", lambda e, k=k, f=f: e.matmul(pot[:], Wo[:, k, f * 128:(f + 1) * 128], mg[:, k, :], start=(k == 0), stop=(k == 7)),
                              reads=[rWo, rmg], writes=[rpo])
                    fw.op("dve", lambda e, f=f: e.scalar_tensor_tensor(xt[:, f, :], pot[:], self.mod[:, l, 2, f, u:u + 1], xt[:, f, :],
                                                                       ALU.mult, ALU.add), reads=[rpo, self.rmod, rx], writes=[rx])
                self.stt(S["xm"][:, t0:t0 + BT].rearrange("(k p) t -> p k t", p=128), xt[:], rx, R["xm"])
        fw.phase_end()

    def phase_p3b(self, l):
        fw, I, S, R, C = self.fw, self.I, self.S, self.R, self.C
        BT = 256
        last = (l == self.L - 1)
        fw.phase_begin()
        with ExitStack() as ps:
            sb = lambda n, s, dt=F32: fw.sbuf(n, s, dt, ps)
            W1 = sb("wf1", [128, 8, 2 * DFF], BF16); rW1 = fw.res("wf1")
            W2 = sb("wf2", [128, 22, 1024], BF16); rW2 = fw.res("wf2")
            stg = [(sb("stg4", [128, 1408], F32), fw.res("stg4", dma=True)) for _ in range(2)]
            self.stg_i = 0
            self.load_cast(W1, rW1, 0, 8, I["w_ffn_in"][l], 2 * DFF, stg, ["pool", "dve", "act"], 1408)
            self.load_cast(W2, rW2, 0, 22, I["w_ffn_out"][l], 1024, stg, ["pool", "dve", "act"], 1408)
            xb = [(sb("xb4", [128, 8, BT]), fw.res("xb4", dma=True)) for _ in range(2)]
            hT = sb("hT4", [128, 8, BT], BF16); rhT = fw.res("hT4")
            hT2 = sb("hT4b", [128, 8, BT], BF16); rhT2 = fw.res("hT4b")
            sq = [sb("sq4", [128, BT]) for _ in range(2)]; rsq = [fw.res("sq4") for _ in range(2)]
            rstd = sb("rstd4", [128, BT]); rrstd = fw.res("rstd4")
            tmp = sb("tmp4", [128, BT]); rtmp = fw.res("tmp4")
            sg = [(sb("sg4", [128, BT]), fw.res("sg4")) for _ in range(3)]
            hid = sb("hid4", [128, 22, BT], BF16); rhid = fw.res("hid4")
            pg = [(fw.psum("pg4", [128, BT], F32, ps), fw.res("pg4", excl=True)) for _ in range(3)]
            pu = [(fw.psum("pu4", [128, BT], F32, ps), fw.res("pu4", excl=True)) for _ in range(3)]
            po = [(fw.psum("po4", [128, BT], F32, ps), fw.res("po4", excl=True)) for _ in range(2)]
            pst, rpst = po[1]
            gfin, rgfin = C["g_finalT"]
            onesf, ronesf = C["onesf"]
            n = no = 0
            NBLK = min(self.NT // BT, int(os.environ.get('KBLK', '9999')))
            hTs = [(hT, rhT), (hT2, rhT2)]

            def prep(gi):
                t0 = gi * BT
                xt, rx = xb[gi % 2]
                self.ld(xt[:], S["xm"][:, t0:t0 + BT].rearrange("(k p) t -> p k t", p=128), rx, R["xm"])

            def nm(gi):
                xt, rx = xb[gi % 2]
                h_, rh_ = hTs[gi % 2]
                self.norm_mod(xt, rx, h_, rh_, sq, rsq, pst, rpst, rstd, rrstd, l, 1, (gi * BT) // self.UL, tmp, rtmp)

            prep(0)
            nm(0)
            for gi in range(NBLK):
                t0 = gi * BT
                u = t0 // self.UL
                xt, rx = xb[gi % 2]
                hT, rhT = hTs[gi % 2]
                if gi + 1 < NBLK:
                    prep(gi + 1)
                for j in range(22):
                    pgt, rpg = pg[n % 3]
                    put, rpu = pu[n % 3]
                    sgt, rsg = sg[n % 3]
                    n += 1
                    for k in range(8):
                        fw.op("pe", lambda e, k=k, j=j: e.matmul(pgt[:], W1[:, k, j * 128:(j + 1) * 128], hT[:, k, :], start=(k == 0), stop=(k == 7)),
                              reads=[rW1, rhT], writes=[rpg])
                    for k in range(8):
                        fw.op("pe", lambda e, k=k, j=j: e.matmul(put[:], W1[:, k, DFF + j * 128:DFF + (j + 1) * 128], hT[:, k, :],
                                                                  start=(k == 0), stop=(k == 7)), reads=[rW1, rhT], writes=[rpu])
                    fw.op("act", lambda e: e.activation(sgt[:], pgt[:], AF.Silu), reads=[rpg], writes=[rsg])
                    fw.op("dve", lambda e, j=j: e.tensor_tensor(hid[:, j, :], sgt[:], put[:], ALU.mult), reads=[rsg, rpu], writes=[rhid])
                if gi + 1 < NBLK:
                    nm(gi + 1)
                for f in range(8):
                    pot, rpo = po[no % 2]
                    no += 1
                    for k in range(22):
                        fw.op("pe", lambda e, k=k, f=f: e.matmul(pot[:], W2[:, k, f * 128:(f + 1) * 128], hid[:, k, :], start=(k == 0), stop=(k == 21)),
                              reads=[rW2, rhid], writes=[rpo])
                    fw.op("dve", lambda e, f=f: e.scalar_tensor_tensor(xt[:, f, :], pot[:], self.mod[:, l, 5, f, u:u + 1], xt[:, f, :],
                                                                       ALU.mult, ALU.add), reads=[rpo, self.rmod, rx], writes=[rx])
                if not last:
                    self.stt(S["xn"][:, t0:t0 + BT].rearrange("(k p) t -> p k t", p=128), xt[:], rx, R["xn"])
                else:
                    for k in range(8):
                        sqk, rsqk = sq[k % 2], rsq[k % 2]
                        fw.op("act", lambda e, k=k: e.activation(sqk[:], xt[:, k, :], AF.Square), reads=[rx], writes=[rsqk])
                        fw.op("pe", lambda e, k=k: e.matmul(pst[:], onesf[:], sqk[:], start=(k == 0), stop=(k == 7)),
                              reads=[rsqk, ronesf], writes=[rpst])
                    fw.op("act", lambda e: e.activation(rstd[:], pst[:], AF.Sqrt, scale=1.0 / D, bias=self.eps_ap()),
                          reads=[rpst, self.reps], writes=[rrstd])
                    fw.op("dve", lambda e: e.reciprocal(rstd[:], rstd[:]), reads=[rrstd], writes=[rrstd])
                    for k in range(8):
                        fw.op("dve", lambda e, k=k: e.scalar_tensor_tensor(xt[:, k, :], xt[:, k, :], gfin[:, k:k + 1], rstd[:], ALU.mult, ALU.mult),
                              reads=[rx, rgfin, rrstd], writes=[rx])
                    self.stt(self.yT[:, t0:t0 + BT].rearrange("(k p) t -> p k t", p=128), xt[:], rx, R["yT"])
        fw.phase_end()


def _bias_tile_idx(rows_total, qr0, kr0, q_valid_rows, k_valid_rows):
    kk = np.arange(128)
    qq = np.arange(128)
    krow = kr0 + kk // 64
    kcol = kk % 64
    qrow = qr0 + qq // 64
    qcol = qq % 64
    kr = min(8, rows_total)
    wlo = np.clip(qrow - kr // 2, 0, rows_total - kr)
    clo = np.clip(qcol - 8, 0, 64 - 16)
    vr = (krow[:, None] >= wlo[None, :]) & (krow[:, None] < wlo[None, :] + kr)
    vcol = (kcol[:, None] >= clo[None, :]) & (kcol[:, None] < clo[None, :] + 16)
    valid = vr & vcol
    valid &= (krow[:, None] >= 0) & (krow[:, None] < rows_total) & (qrow[None, :] >= 0) & (qrow[None, :] < rows_total)
    dr = np.clip(krow[:, None] - qrow[None, :] + 7, 0, 14)
    dc = np.clip(kcol[:, None] - qcol[None, :] + 15, 0, 30)
    return valid, dr, dc


def _make_rpbt(rpb_l, UL, link):
    Rr = UL // 64
    NB2 = UL // 128
    out = np.full((128, NSLOT * 4, 128), NEG, np.float32)

    def fill(slot, rows_total, qr0, kr0):
        valid, dr, dc = _bias_tile_idx(rows_total, qr0, kr0, None, None)
        for h in range(4):
            vals = rpb_l[h][dr, dc]
            out[:, slot * 4 + h, :] = np.where(valid, vals, np.float32(NEG))

    big = 64 if Rr >= 16 else Rr
    Rg = max(Rr, 16)
    for cls, bsel in (("INT", 4), ("TOP0", 0), ("TOP1", 1), ("BOT1", Rg // 2 - 2), ("BOT0", Rg // 2 - 1)):
        s0, offs = CLS[cls]
        for i, o in enumerate(offs):
            fill(s0 + i, Rg, 2 * bsel, 2 * (bsel + o))
    for cls, u, b in (("JA1", 0, NB2 - 2), ("JA0", 0, NB2 - 1), ("JB0", 1, 0), ("JB1", 1, 1)):
        s0, offs = CLS[cls]
        for i, o in enumerate(offs):
            if link:
                fill(s0 + i, 2 * Rr, 2 * (u * NB2 + b), 2 * (u * NB2 + b + o))
            else:
                kp = b + o
                if kp < 0 or kp >= NB2:
                    continue
                fill(s0 + i, Rr, 2 * b, 2 * kp)
    return out


def _host_prep(inp, UL, L, units_per_core):
    f32 = np.float32
    shared = {}
    shared["w_ada"] = np.ascontiguousarray(inp["w_ada"][:L])
    shared["b_adaT"] = np.ascontiguousarray(inp["b_ada"][:L].reshape(L, 48, 128).transpose(2, 0, 1))
    gv = np.stack([inp["g_norm_mix"][:L], inp["g_norm_ffn"][:L]], 1)
    shared["gvec"] = np.ascontiguousarray(gv.reshape(L, 2, 8, 128).transpose(3, 0, 1, 2))
    shared["w_in"] = np.ascontiguousarray(inp["w_in"][:L])
    shared["a_ln"] = np.ascontiguousarray(np.concatenate([inp["a_ln_g"][:L], inp["a_ln_b"][:L]], 1))
    shared["a_w_spT"] = np.ascontiguousarray(inp["a_w_sp"][:L].transpose(0, 3, 1, 2))
    shared["a_b_sp"] = np.ascontiguousarray(inp["a_b_sp"][:L].transpose(2, 0, 1))
    shared["c_gate_b"] = np.ascontiguousarray(inp["c_gate_b"][:L])
    shared["c_hnorm"] = np.ascontiguousarray(inp["c_hnorm_g"][:L])
    shared["d_conv_wT"] = np.ascontiguousarray(inp["d_conv_w"][:L].reshape(L, 31, 2, 128).transpose(3, 0, 2, 1))
    dv = np.stack([inp["d_conv_b"][:L], inp["d_ln_g"][:L], inp["d_ln_b"][:L]], 1)
    shared["d_vec"] = np.ascontiguousarray(dv.reshape(L, 3, 2, 128).transpose(3, 0, 1, 2))
    shared["w_branch"] = np.ascontiguousarray(inp["w_branch"][:L].reshape(L, 1024, D))
    shared["w_out"] = np.ascontiguousarray(inp["w_out"][:L])
    shared["w_ffn_in"] = np.ascontiguousarray(inp["w_ffn_in"][:L])
    shared["w_ffn_out"] = np.ascontiguousarray(inp["w_ffn_out"][:L])
    shared["g_finalT"] = np.ascontiguousarray(inp["g_final"].reshape(8, 128).T)
    shared["c_ident"] = np.eye(128, dtype=f32)
    shared["c_triu"] = np.triu(np.ones((128, 128), f32))
    shared["c_tril"] = np.tril(np.ones((128, 128), f32))
    sel = np.zeros((128, 2, 128), f32)
    sel[:, 0, 0:64] = 1.0
    sel[:, 1, 64:128] = 1.0
    shared["c_sel"] = sel
    rp = {}
    for link in (0, 1):
        rp[link] = np.stack([_make_rpbt(inp["b_rpb"][l], UL, link) for l in range(L)], 0)
    in_maps = []
    for link, units in units_per_core:
        m = dict(shared)
        xs, cs = [], []
        for which, si, t0 in units:
            x = inp["x_prompt"] if which == "p" else inp["x_sample"]
            c = inp["c_prompt"] if which == "p" else inp["c_sample"]
            xs.append(x[si, t0:t0 + UL, :])
            cs.append(c[si])
        m["xT"] = np.ascontiguousarray(np.concatenate(xs, 0).T)
        cc = np.stack(cs, 0)
        m["cT"] = np.ascontiguousarray(cc.reshape(3, 8, 128).transpose(2, 1, 0))
        m["link"] = np.full((128, 1), float(link), f32)
        m["rpbt"] = rp[link]
        in_maps.append(m)
    return in_maps


_NC_CACHE = {}


def run_config(inp, UL, L, units_per_core, debug=False):
    key = (UL, L, debug)
    if key not in _NC_CACHE:
        b = Builder(UL, L)
        b.debug = debug
        _NC_CACHE[key] = (b.build(), b)
    nc, b = _NC_CACHE[key]
    in_maps = _host_prep(inp, UL, L, units_per_core)
    res = run_bass_kernel_spmd(nc, in_maps, core_ids=list(range(len(in_maps))))
    if debug:
        return res.results
    return [np.asarray(r["yT"]) for r in res.results]


def kernel(**inputs):
    inp = {k: np.asarray(v) for k, v in inputs.items()}
    UL = 4096
    L = 4
    units = []
    for c in range(4):
        units.append((1, [("p", c, 0), ("p", c, UL), ("s", c, 0)]))
    for c in range(4):
        units.append((0, [("s", 4 + 3 * c + j, 0) for j in range(3)]))
    outs = run_config(inp, UL, L, units)
    yp = np.empty(inp["x_prompt"].shape, np.float32)
    ys = np.empty(inp["x_sample"].shape, np.float32)
    for c, (link, us) in enumerate(units):
        yT = outs[c]
        for j, (which, si, t0) in enumerate(us):
            blk = yT[:, j * UL:(j + 1) * UL].T
            if which == "p":
                yp[si, t0:t0 + UL, :] = blk
            else:
                ys[si, 0:UL, :] = blk
    return (yp, ys)
```

```python
import os
import numpy as np
from contextlib import ExitStack
import concourse.bass as bass
import concourse.mybir as mybir
from concourse.bass_utils import run_bass_kernel_spmd

F32 = mybir.dt.float32
BF16 = mybir.dt.bfloat16
AF = mybir.ActivationFunctionType
ALU = mybir.AluOpType
AX = mybir.AxisListType

D = 1024
NIN = 6928
DFF = 2816
NEG = -30000.0
EPS = 1e-6
NSLOT = 43
CLS = {
    "INT": (0, [-2, -1, 0, 1, 2]),
    "TOP0": (5, [0, 1, 2, 3]),
    "TOP1": (9, [-1, 0, 1, 2]),
    "BOT1": (13, [-2, -1, 0, 1]),
    "BOT0": (17, [-3, -2, -1, 0]),
    "JA1": (21, [-2, -1, 0, 1, 2]),
    "JA0": (26, [-3, -2, -1, 0, 1, 2]),
    "JB0": (32, [-2, -1, 0, 1, 2, 3]),
    "JB1": (38, [-2, -1, 0, 1, 2]),
}


class Res:
    __slots__ = ("name", "w", "r", "dsem", "multi", "excl")

    def __init__(self, name, multi=False, excl=False):
        self.name = name
        self.multi = multi
        self.excl = excl
        self.w = []
        self.r = {}
        self.dsem = None


class Sem:
    __slots__ = ("h", "total", "is_dma", "name")

    def __init__(self, h, is_dma, name):
        self.h = h
        self.total = 0
        self.is_dma = is_dma
        self.name = name


class Eng:
    def __init__(self, name, h, sem):
        self.name = name
        self.h = h
        self.sem = sem
        self.waited = {}


class FW:
    def __init__(self, nc, stack):
        self.nc = nc
        self.stack = stack
        self.eng = {}
        self.sems = []
        for name, h in (("pe", nc.tensor), ("dve", nc.vector), ("act", nc.scalar),
                        ("pool", nc.gpsimd), ("sp", nc.sync)):
            s = Sem(stack.enter_context(nc.semaphore("s_" + name)), False, name)
            self.sems.append(s)
            self.eng[name] = Eng(name, h, s)
        self.ninst = 0
        self.uid = 0
        self.free_dsems = []
        self.phase_dsems = None

    def sbuf(self, name, shape, dt, stack=None):
        self.uid += 1
        return (stack or self.stack).enter_context(
            self.nc.sbuf_tensor("%s_%d" % (name, self.uid), list(shape), dt))

    def psum(self, name, shape, dt=F32, stack=None):
        self.uid += 1
        esz = 4 if dt == F32 else 2
        n = int(np.prod(shape[1:]))
        be = 2048 // esz
        nb = -(-n // be)
        t = (stack or self.stack).enter_context(
            self.nc.psum_tensor("%s_%d" % (name, self.uid), [128, nb * be], dt))
        v = t[:, 0:n]
        if len(shape) == 3:
            v = v.rearrange("p (a b) -> p a b", b=shape[2])
        return v

    def res(self, name, dma=False, multi=False, excl=False):
        r = Res(name, multi, excl)
        if dma:
            if not self.free_dsems:
                self.uid += 1
                h = self.stack.enter_context(self.nc.semaphore("d_%d" % self.uid))
                sm = Sem(h, True, name)
                self.sems.append(sm)
                self.free_dsems.append(sm)
            r.dsem = self.free_dsems.pop()
            if self.phase_dsems is not None:
                self.phase_dsems.append(r.dsem)
        return r

    def phase_begin(self):
        self.phase_dsems = []

    def phase_end(self):
        try:
            print("sbuf remaining", self.nc.sbuf_bytes_remaining, "ninst", self.ninst, flush=True)
        except Exception as ex:
            print("sbuf remaining ?", ex)
        self.barrier()
        self.free_dsems.extend(self.phase_dsems)
        self.phase_dsems = None

    def _need(self, e, deps):
        best = {}
        for s, v in deps:
            if s is e.sem:
                if e.name == "pe" or e.name == "sp":
                    continue
                if e.sem.total - v >= 2:
                    continue
            if s.is_dma:
                v = s.total
            if v > best.get(s, 0):
                best[s] = v
        for s, v in best.items():
            if e.waited.get(s, 0) >= v:
                continue
            e.h.wait_ge(s.h, v)
            e.waited[s] = v
            self.ninst += 1

    def _collect(self, reads, writes):
        deps = []
        for r in reads:
            deps.extend(r.w)
        for w in writes:
            if not w.multi:
                deps.extend(w.w)
            deps.extend(w.r.items())
        return deps

    def _record(self, sem, reads, writes):
        key = (sem, sem.total)
        for r in reads:
            r.r[sem] = sem.total
        for w in writes:
            if w.multi:
                w.w = [k for k in w.w if k[0] is not sem] + [key]
            else:
                w.w = [key]
                w.r = {}

    def op(self, ename, fn, reads=(), writes=()):
        e = self.eng[ename]
        xr = [r for r in reads if r.excl]
        if xr:
            reads = [r for r in reads if not r.excl]
            writes = list(writes) + xr
        self._need(e, self._collect(reads, writes))
        ins = fn(e.h)
        e.sem.total += 1
        ins.then_inc(e.sem.h, 1)
        self.ninst += 1
        self._record(e.sem, reads, writes)
        return ins

    def dma(self, qname, out, in_, reads=(), writes=(), dres=None, **kw):
        e = self.eng[qname]
        self._need(e, self._collect(reads, writes))
        ds = dres.dsem
        ins = e.h.dma_start(out=out, in_=in_, **kw)
        ds.total += 16
        ins.then_inc(ds.h, 16)
        self.ninst += 1
        self._record(ds, reads, writes)
        return ins

    def barrier(self):
        for e in self.eng.values():
            for s in self.sems:
                if s is e.sem or s.total == 0:
                    continue
                if e.waited.get(s, 0) >= s.total:
                    continue
                e.h.wait_ge(s.h, s.total)
                e.waited[s] = s.total
                self.ninst += 1


class Builder:
    def __init__(self, UL, L, last_is_final=True):
        self.UL = UL
        self.L = L
        self.NT = 3 * UL
        self.NBK = UL // 512
        self.NCH = UL // 128
        self.NB2 = UL // 128
        assert self.NB2 >= 8

    def declare(self, nc):
        L, NT = self.L, self.NT
        di = lambda n, s, dt=F32: nc.dram_tensor(n, list(s), dt, kind="ExternalInput").ap()
        dbg = getattr(self, "debug", False)
        dx = lambda n, s, dt=F32: nc.dram_tensor(n, list(s), dt, kind=("ExternalOutput" if dbg else "Internal")).ap()
        I = {}
        I["xT"] = di("xT", [D, NT])
        I["cT"] = di("cT", [128, 8, 3])
        I["link"] = di("link", [128, 1])
        I["w_ada"] = di("w_ada", [L, D, 6 * D])
        I["b_adaT"] = di("b_adaT", [128, L, 48])
        I["gvec"] = di("gvec", [128, L, 2, 8])
        I["w_in"] = di("w_in", [L, D, NIN])
        I["a_ln"] = di("a_ln", [L, 512])
        I["a_w_spT"] = di("a_w_spT", [L, 128, 4, 128])
        I["a_b_sp"] = di("a_b_sp", [128, L, 4])
        I["rpbt"] = di("rpbt", [L, 128, NSLOT * 4, 128])
        I["c_gate_b"] = di("c_gate_b", [L, 16])
        I["c_hnorm"] = di("c_hnorm", [L, 256])
        I["d_conv_wT"] = di("d_conv_wT", [128, L, 2, 31])
        I["d_vec"] = di("d_vec", [128, L, 3, 2])
        I["w_branch"] = di("w_branch", [L, 1024, D])
        I["w_out"] = di("w_out", [L, D, D])
        I["w_ffn_in"] = di("w_ffn_in", [L, D, 2 * DFF])
        I["w_ffn_out"] = di("w_ffn_out", [L, DFF, D])
        I["g_finalT"] = di("g_finalT", [128, 8])
        I["c_ident"] = di("c_ident", [128, 128])
        I["c_triu"] = di("c_triu", [128, 128])
        I["c_tril"] = di("c_tril", [128, 128])
        I["c_sel"] = di("c_sel", [128, 2, 128])
        self.I = I
        self.yT = nc.dram_tensor("yT", [D, NT], F32, kind="ExternalOutput").ap()
        S = {}
        S["xm"] = dx("xm", [D, NT])
        S["xn"] = dx("xn", [D, NT])
        S["yall"] = dx("yall", [D, NT], BF16)
        S["bq"] = dx("bq", [256, NT], BF16)
        S["bk"] = dx("bk", [256, NT], BF16)
        S["bv"] = dx("bv", [NT, 256], BF16)
        S["cq"] = dx("cq", [256, NT], BF16)
        S["ck"] = dx("ck", [256, NT], BF16)
        S["ckt"] = dx("ckt", [NT, 256], BF16)
        S["cvt"] = dx("cvt", [NT, 256])
        S["co"] = dx("co", [NT, 256])
        S["cg"] = dx("cg", [NT, 16])
        S["chf"] = dx("chf", [NT, 256])
        S["dy"] = dx("dy", [256, NT])
        self.S = S

    def build(self):
        nc = bass.Bass("TRN2", target_bir_lowering=False)
        self.nc = nc
        self.declare(nc)
        with ExitStack() as st:
            fw = FW(nc, st)
            self.fw = fw
            self.R = {k: fw.res(k, multi=True) for k in list(self.S) + ["xT", "yT"]}
            self.setup_consts(st)
            self.ensure_eps()
            self.phase_mod()
            import os
            ph = os.environ.get("KPHASES", "p1,bc,conv,p3a,p3b").split(",")
            for l in range(self.L):
                for p in ("p1", "bc", "conv", "p3a", "p3b"):
                    if p in ph:
                        getattr(self, "phase_" + p)(l)
            fw.barrier()
            self.ninst = fw.ninst
        return nc

    def ld(self, tile_ap, dram_ap, res, dram_res=None, q="sp"):
        reads = [dram_res] if dram_res is not None else []
        self.fw.dma(q, tile_ap, dram_ap, reads=reads, writes=[res], dres=res)

    def stt(self, dram_ap, tile_ap, res, dram_res, q=None):
        import os
        q = q or os.environ.get("KSTQ", "pool")
        self.fw.dma(q, dram_ap, tile_ap, reads=[res], writes=[dram_res], dres=res)

    def setup_consts(self, st):
        fw, I = self.fw, self.I
        L = self.L
        C = {}

        def cload(name, shape, src, dt=F32):
            t = fw.sbuf(name, shape, dt)
            r = fw.res(name, dma=True)
            self.ld(t[:], src, r)
            C[name] = (t, r)
            return t, r

        cload("identf", [128, 128], I["c_ident"])
        cload("triu", [128, 128], I["c_triu"])
        cload("tril", [128, 128], I["c_tril"])
        cload("sel", [128, 2, 128], I["c_sel"])
        cload("link", [128, 1], I["link"])
        cload("cT", [128, 8, 3], I["cT"])
        cload("b_adaT", [128, L, 48], I["b_adaT"])
        cload("gvec", [128, L, 2, 8], I["gvec"])
        cload("a_b_sp", [128, L, 4], I["a_b_sp"])
        cload("d_conv_wT", [128, L, 2, 31], I["d_conv_wT"])
        cload("d_vec", [128, L, 3, 2], I["d_vec"])
        cload("g_finalT", [128, 8], I["g_finalT"])
        identb = fw.sbuf("identb", [128, 128], BF16)
        ridb = fw.res("identb")
        fw.op("dve", lambda e: e.tensor_copy(identb[:], C["identf"][0][:]), reads=[C["identf"][1]], writes=[ridb])
        C["identb"] = (identb, ridb)
        onesf = fw.sbuf("onesf", [128, 128], F32)
        ronesf = fw.res("onesf")
        fw.op("dve", lambda e: e.memset(onesf[:], 1.0), writes=[ronesf])
        C["onesf"] = (onesf, ronesf)
        onesb = fw.sbuf("onesb", [128, 128], BF16)
        ronesb = fw.res("onesb")
        fw.op("dve", lambda e: e.memset(onesb[:], 1.0), writes=[ronesb])
        C["onesb"] = (onesb, ronesb)
        self.C = C
        self.mod = fw.sbuf("mod", [128, L, 6, 8, 3], F32)
        self.rmod = fw.res("mod")
        self.gm = fw.sbuf("gm", [128, L, 2, 8, 3], F32)
        self.rgm = fw.res("gm")

    def phase_mod(self):
        fw, I, C = self.fw, self.I, self.C
        L = self.L
        fw.phase_begin()
        with ExitStack() as ps:
            sc = fw.sbuf("silu_c", [128, 8, 3], F32, ps)
            rsc = fw.res("silu_c")
            cT, rcT = C["cT"]
            fw.op("act", lambda e: e.activation(sc[:], cT[:], AF.Silu), reads=[rcT], writes=[rsc])
            wst = [(fw.sbuf("wada", [128, 8, 1024], F32, ps), fw.res("wada", dma=True)) for _ in range(2)]
            pm = [(fw.psum("pmod", [128, 8, 4], F32, ps), fw.res("pmod", excl=True)) for _ in range(2)]
            it = 0
            badaT, rbada = C["b_adaT"]
            for l in range(L):
                for m in range(6):
                    wt, rw = wst[it % 2]
                    pt, rp = pm[it % 2]
                    it += 1
                    src = I["w_ada"][l, :, m * 1024:(m + 1) * 1024].rearrange("(k p) n -> p k n", p=128)
                    for k2 in range(2):
                        self.ld(wt[:, 4 * k2:4 * k2 + 4, :], src[:, 4 * k2:4 * k2 + 4, :], rw)
                    for f in range(8):
                        for k in range(8):
                            fw.op("pe", lambda e, f=f, k=k: e.matmul(pt[:, f, 0:3], wt[:, k, f * 128:(f + 1) * 128], sc[:, k, :],
                                                                     start=(k == 0), stop=(k == 7)),
                                  reads=[rw, rsc], writes=[rp])
                    for f in range(8):
                        fw.op("dve", lambda e, f=f: e.tensor_scalar(self.mod[:, l, m, f, :], pt[:, f, 0:3],
                                                                    badaT[:, l, m * 8 + f:m * 8 + f + 1], None, ALU.add),
                              reads=[rp, rbada], writes=[self.rmod])
            gvec, rg = C["gvec"]
            for l in range(L):
                for j, m in ((0, 1), (1, 4)):
                    for u in range(3):
                        fw.op("dve", lambda e, l=l, j=j, m=m, u=u: e.scalar_tensor_tensor(
                            self.gm[:, l, j, :, u], self.mod[:, l, m, :, u], 1.0, gvec[:, l, j, :], ALU.add, ALU.mult),
                            reads=[self.rmod, rg], writes=[self.rgm])
        fw.phase_end()

    def load_cast(self, dst, rdst, k0, nk, src_rows, ncols, stg, eng_cycle, CW):
        fw = self.fw
        for k in range(nk):
            for c0 in range(0, ncols, CW):
                cw = min(CW, ncols - c0)
                st_t, st_r = stg[self.stg_i % len(stg)]
                self.stg_i += 1
                self.ld(st_t[:, 0:cw], src_rows[k * 128:(k + 1) * 128, c0:c0 + cw], st_r)
                en = eng_cycle[self.stg_i % len(eng_cycle)]
                if en == "act":
                    fw.op("act", lambda e: e.copy(dst[:, k0 + k, c0:c0 + cw], st_t[:, 0:cw]), reads=[st_r], writes=[rdst])
                else:
                    fw.op(en, lambda e: e.tensor_copy(dst[:, k0 + k, c0:c0 + cw], st_t[:, 0:cw]), reads=[st_r], writes=[rdst])

    def norm_mod(self, xb, rxb, hT, rhT, sq, rsq, pst, rpst, rstd, rrstd, l, j, u, tmp, rtmp):
        fw = self.fw
        onesf, ronesf = self.C["onesf"]
        for k in range(8):
            sqk, rsqk = sq[k % 2], rsq[k % 2]
            fw.op("act", lambda e, k=k: e.activation(sqk[:], xb[:, k, :], AF.Square), reads=[rxb], writes=[rsqk])
            fw.op("pe", lambda e, k=k: e.matmul(pst[:], onesf[:], sqk[:], start=(k == 0), stop=(k == 7)),
                  reads=[rsqk, ronesf], writes=[rpst])
        fw.op("act", lambda e: e.activation(rstd[:], pst[:], AF.Sqrt, scale=1.0 / D, bias=self.eps_ap()),
              reads=[rpst, self.reps], writes=[rrstd])
        fw.op("dve", lambda e: e.reciprocal(rstd[:], rstd[:]), reads=[rrstd], writes=[rrstd])
        m_sh = 0 if j == 0 else 3
        for k in range(8):
            fw.op("dve", lambda e, k=k: e.tensor_tensor(tmp[:], xb[:, k, :], rstd[:], ALU.mult),
                  reads=[rxb, rrstd], writes=[rtmp])
            fw.op("dve", lambda e, k=k: e.tensor_scalar(hT[:, k, :], tmp[:], self.gm[:, l, j, k, u:u + 1],
                                                        self.mod[:, l, m_sh, k, u:u + 1], ALU.mult, ALU.add),
                  reads=[rtmp, self.rgm, self.rmod], writes=[rhT])

    def eps_ap(self):
        return self.epst[:, 0:1]

    def ensure_eps(self):
        if getattr(self, "epst", None) is None:
            fw = self.fw
            self.epst = fw.sbuf("epst", [128, 2], F32)
            self.reps = fw.res("epst")
            fw.op("dve", lambda e: e.memset(self.epst[:, 0:1], EPS), writes=[self.reps])
            fw.op("dve", lambda e: e.memset(self.epst[:, 1:2], 1.0), writes=[self.reps])

    def phase_p1(self, l):
        fw, I, S, R, C = self.fw, self.I, self.S, self.R, self.C
        self.ensure_eps()
        xsrc, rxsrc = (I["xT"], R["xT"]) if l == 0 else (S["xn"], R["xn"])
        fw.phase_begin()
        with ExitStack() as ps:
            sb = lambda n, s, dt=F32: fw.sbuf(n, s, dt, ps)
            W = sb("w1", [128, 8, 2832], BF16)
            rW = fw.res("w1")
            stg = [(sb("stg", [128, 1416], F32), fw.res("stg", dma=True)) for _ in range(3)]
            self.stg_i = 0
            self.load_cast(W, rW, 0, 8, I["w_in"][l], 2832, stg, ["pool", "dve", "act"], 1416)
            wsp = sb("wsp", [128, 4, 128], BF16)
            rwsp = fw.res("wsp")
            wspf = sb("wspf", [128, 4, 128], F32)
            rwspf = fw.res("wspf", dma=True)
            self.ld(wspf[:], I["a_w_spT"][l], rwspf)
            fw.op("pool", lambda e: e.tensor_copy(wsp[:], wspf[:]), reads=[rwspf], writes=[rwsp])
            aln = sb("aln", [128, 512], F32)
            raln = fw.res("aln", dma=True)
            self.ld(aln[:], I["a_ln"][l].partition_broadcast(128), raln)
            absp, rabsp = C["a_b_sp"]
            identb, ridb = C["identb"]
            xb = [(sb("xb", [128, 8, 512]), fw.res("xb", dma=True)) for _ in range(2)]
            hT = sb("hT", [128, 8, 512], BF16); rhT = fw.res("hT")
            sq = [sb("sq", [128, 512]) for _ in range(2)]; rsq = [fw.res("sq") for _ in range(2)]
            rstd = sb("rstd", [128, 512]); rrstd = fw.res("rstd")
            tmp = sb("tmp", [128, 512]); rtmp = fw.res("tmp")
            pst = fw.psum("pst", [128, 512], F32, ps); rpst = fw.res("pst", excl=True)
            pbig = fw.psum("pbig", [128, 4, 512], F32, ps)
            rbig = [fw.res("pbig%d" % i, excl=True) for i in range(4)]
            pfm = [(pbig[:, 0, :], rbig[0]), (pbig[:, 1, :], rbig[1])]
            ptm = [(pbig[:, 2, :], rbig[2]), (pbig[:, 3, :], rbig[3])]
            psA = fw.psum("psA", [128, 4, 256], F32, ps); rpsA = fw.res("psA", excl=True)
            ptr = fw.psum("ptr", [128, 8, 128], BF16, ps); rptr = fw.res("ptr", excl=True)
            ofm = [(sb("ofm", [128, 2, 512], BF16), fw.res("ofm", dma=True)) for _ in range(3)]
            ofd = [(sb("ofd", [128, 2, 512], F32), fw.res("ofd", dma=True)) for _ in range(2)]
            sgd = sb("sgd", [128, 512]); rsgd = fw.res("sgd")
            otm = [(sb("otm", [128, 4, 256], BF16), fw.res("otm", dma=True)) for _ in range(4)]
            oco = [(sb("oco", [128, 4, 256], F32), fw.res("oco", dma=True)) for _ in range(2)]
            ocv = [(sb("ocv", [128, 4, 256], F32), fw.res("ocv", dma=True)) for _ in range(2)]
            ocg = [(sb("ocg", [128, 4, 16], F32), fw.res("ocg", dma=True)) for _ in range(2)]
            yA = [(sb("yA", [128, 2, 512], BF16), fw.res("yA", dma=True)) for _ in range(2)]
            g1 = sb("g1", [128, 4, 512]); rg1 = fw.res("g1")
            gu = sb("gu", [128, 4, 512]); rgu = fw.res("gu")
            vc = sb("vc", [128, 4, 256]); rvc = fw.res("vc")
            vn = sb("vn", [128, 4, 256], BF16); rvn = fw.res("vn")
            st4 = sb("st4", [128, 16]); rst4 = fw.res("st4")
            yAt = sb("yAt", [128, 4, 256], BF16); ryAt = fw.res("yAt")
            nfm = ntm = nofm = 0
            for u in range(3):
                for b in range(self.NBK):
                    t0 = u * self.UL + b * 512
                    gi = u * self.NBK + b
                    xt, rx = xb[gi % 2]
                    self.ld(xt[:], xsrc[:, t0:t0 + 512].rearrange("(k p) t -> p k t", p=128), rx, rxsrc)
                    import os
                    cut = int(os.environ.get("KCUT", "99"))
                    if cut < 1:
                        continue
                    self.norm_mod(xt, rx, hT, rhT, sq, rsq, pst, rpst, rstd, rrstd, l, 0, u, tmp, rtmp)
                    if cut < 2:
                        continue
                    fm_jobs = [(512, "bq"), (768, "bk"), (1280, "cq"), (1536, "ck")]
                    for col0, name in fm_jobs:
                        ot, ro = ofm[nofm % 3]
                        nofm += 1
                        for c in range(2):
                            pt, rp = pfm[nfm % 2]
                            nfm += 1
                            for k in range(8):
                                fw.op("pe", lambda e, k=k, c=c: e.matmul(pt[:], W[:, k, col0 + c * 128:col0 + (c + 1) * 128], hT[:, k, :],
                                                                          start=(k == 0), stop=(k == 7)),
                                      reads=[rW, rhT], writes=[rp])
                            scale = 0.125 if name in ("bq", "cq") else 1.0
                            fw.op("act", lambda e, c=c: e.activation(ot[:, c, :], pt[:], AF.Identity, scale=scale),
                                  reads=[rp], writes=[ro])
                        self.stt(S[name][:, t0:t0 + 512].rearrange("(c p) t -> p c t", p=128), ot[:], ro, R[name])
                    if cut < 3:
                        continue
                    od, rod = ofd[gi % 2]
                    for c in range(2):
                        pa, rpa = pfm[nfm % 2]
                        nfm += 1
                        pg, rpg = pfm[nfm % 2]
                        nfm += 1
                        for k in range(8):
                            fw.op("pe", lambda e, k=k, c=c: e.matmul(pa[:], W[:, k, 2320 + c * 128:2320 + (c + 1) * 128], hT[:, k, :],
                                                                      start=(k == 0), stop=(k == 7)), reads=[rW, rhT], writes=[rpa])
                        for k in range(8):
                            fw.op("pe", lambda e, k=k, c=c: e.matmul(pg[:], W[:, k, 2576 + c * 128:2576 + (c + 1) * 128], hT[:, k, :],
                                                                      start=(k == 0), stop=(k == 7)), reads=[rW, rhT], writes=[rpg])
                        fw.op("act", lambda e: e.activation(sgd[:], pg[:], AF.Sigmoid), reads=[rpg], writes=[rsgd])
                        fw.op("dve", lambda e, c=c: e.tensor_tensor(od[:, c, :], pa[:], sgd[:], ALU.mult),
                              reads=[rpa, rsgd], writes=[rod])
                    self.stt(S["dy"][:, t0:t0 + 512].rearrange("(c p) t -> p c t", p=128), od[:], rod, R["dy"])
                    if cut < 4:
                        continue
                    obv, robv = otm[(2 * gi) % 4]
                    okt, rokt = otm[(2 * gi + 1) % 4]
                    ovt, rovt = ocv[gi % 2]
                    oo, roo = oco[gi % 2]
                    og, rog = ocg[gi % 2]
                    ya, rya = yA[gi % 2]
                    for ch in range(4):
                        ts_ = slice(ch * 128, (ch + 1) * 128)

                        def tm_mm(col0, ncol):
                            nonlocal ntm
                            pt, rp = ptm[ntm % 2]
                            ntm += 1
                            for k in range(8):
                                fw.op("pe", lambda e, k=k: e.matmul(pt[:, 0:ncol], hT[:, k, ts_], W[:, k, col0:col0 + ncol],
                                                                    start=(k == 0), stop=(k == 7)), reads=[rW, rhT], writes=[rp])
                            return pt, rp
                        skip = os.environ.get("KSKIP", "").split(",")
                        if "tm" in skip:
                            continue
                        if "bv" not in skip:
                            pt, rp = tm_mm(1024, 256)
                            if "bve" not in skip:
                                fw.op("act", lambda e: e.copy(obv[:, ch, :], pt[:, 0:256]), reads=[rp], writes=[robv])
                        if "ckv" not in skip:
                            pt, rp = tm_mm(1536, 512)
                            if "e1" not in skip:
                                fw.op("act", lambda e: e.copy(okt[:, ch, :], pt[:, 0:256]), reads=[rp], writes=[rokt])
                            if "e2" not in skip:
                                fw.op("dve", lambda e: e.tensor_copy(ovt[:, ch, :], pt[:, 256:512]), reads=[rp], writes=[rovt])
                        if "og" in skip:
                            continue
                        pt, rp = tm_mm(2048, 272)
                        fw.op("act", lambda e: e.activation(oo[:, ch, :], pt[:, 0:256], AF.Sigmoid), reads=[rp], writes=[roo])
                        fw.op("dve", lambda e: e.tensor_copy(og[:, ch, :], pt[:, 256:272]), reads=[rp], writes=[rog])
                    for ch in range(4):
                        for k in range(8):
                            fw.op("pe", lambda e, k=k, ch=ch: e.matmul(pbig[:, ch, :], hT[:, k, ch * 128:(ch + 1) * 128], W[:, k, 0:512],
                                                                        start=(k == 0), stop=(k == 7)), reads=[rW, rhT], writes=[rbig[ch]])
                    R4 = list(rbig)
                    fw.op("act", lambda e: e.activation(g1[:], pbig[:], AF.Square), reads=R4, writes=[rg1])
                    fw.op("dve", lambda e: e.tensor_scalar(g1[:], g1[:], 0.044715, 1.0, ALU.mult, ALU.add), reads=[rg1], writes=[rg1])
                    fw.op("dve", lambda e: e.tensor_tensor(g1[:], g1[:], pbig[:], ALU.mult), reads=[rg1] + R4, writes=[rg1])
                    fw.op("act", lambda e: e.activation(g1[:], g1[:], AF.Sigmoid, scale=1.5957691216), reads=[rg1], writes=[rg1])
                    fw.op("dve", lambda e: e.tensor_tensor(gu[:], g1[:], pbig[:], ALU.mult), reads=[rg1] + R4, writes=[rgu])
                    guv = gu[:, :, 256:512]
                    g1v = g1[:, :, 0:256]
                    bc4 = lambda ap: ap.unsqueeze(2).to_broadcast([128, 4, 256])
                    fw.op("dve", lambda e: e.reduce_sum(st4[:, 0:4], guv, AX.X), reads=[rgu], writes=[rst4])
                    fw.op("dve", lambda e: e.tensor_scalar(st4[:, 4:8], st4[:, 0:4], 1.0 / 256, None, ALU.mult), reads=[rst4], writes=[rst4])
                    fw.op("dve", lambda e: e.tensor_tensor(vc[:], guv, bc4(st4[:, 4:8]), ALU.subtract), reads=[rgu, rst4], writes=[rvc])
                    fw.op("dve", lambda e: e.tensor_tensor(g1v, vc[:], vc[:], ALU.mult), reads=[rvc], writes=[rg1])
                    fw.op("dve", lambda e: e.reduce_sum(st4[:, 8:12], g1v, AX.X), reads=[rg1], writes=[rst4])
                    fw.op("act", lambda e: e.activation(st4[:, 12:16], st4[:, 8:12], AF.Sqrt, scale=1.0 / 256, bias=self.eps_ap()),
                          reads=[rst4, self.reps], writes=[rst4])
                    fw.op("dve", lambda e: e.reciprocal(st4[:, 12:16], st4[:, 12:16]), reads=[rst4], writes=[rst4])
                    fw.op("dve", lambda e: e.tensor_tensor(vc[:], vc[:], bc4(st4[:, 12:16]), ALU.mult), reads=[rvc, rst4], writes=[rvc])
                    fw.op("dve", lambda e: e.tensor_tensor(vc[:], vc[:], aln[:, 0:256].unsqueeze(1).to_broadcast([128, 4, 256]), ALU.mult),
                          reads=[rvc, raln], writes=[rvc])
                    fw.op("dve", lambda e: e.tensor_tensor(vn[:], vc[:], aln[:, 256:512].unsqueeze(1).to_broadcast([128, 4, 256]), ALU.add),
                          reads=[rvc, raln], writes=[rvn])
                    for ch in range(4):
                        for g in range(4):
                            fw.op("pe", lambda e, g=g, ch=ch: e.matmul(psA[:, ch, g * 64:(g + 1) * 64], wsp[:, g, :], vn[:, ch, g * 64:(g + 1) * 64],
                                                                        start=True, stop=True), reads=[rwsp, rvn], writes=[rpsA])
                    for g in range(4):
                        fw.op("dve", lambda e, g=g: e.scalar_tensor_tensor(yAt[:, :, g * 64:(g + 1) * 64], psA[:, :, g * 64:(g + 1) * 64],
                                                                           absp[:, l, g:g + 1], gu[:, :, g * 64:(g + 1) * 64],
                                                                           ALU.add, ALU.mult),
                              reads=[rpsA, rabsp, rgu], writes=[ryAt])
                    for ch in range(4):
                        for c in range(2):
                            fw.op("pe", lambda e, c=c, ch=ch: e.transpose(ptr[:, ch * 2 + c, :], yAt[:, ch, c * 128:(c + 1) * 128], identb[:]),
                                  reads=[ryAt, ridb], writes=[rptr])
                    fw.op("act", lambda e: e.copy(ya[:].rearrange("p c (h t) -> p h c t", t=128),
                                                  ptr[:].rearrange("p (h c) t -> p h c t", c=2)), reads=[rptr], writes=[rya])
                    tsl = slice(t0, t0 + 512)
                    if "st" in os.environ.get("KSKIP", "").split(","):
                        continue
                    self.stt(S["bv"][tsl, :].rearrange("(c p) f -> p c f", p=128), obv[:], robv, R["bv"])
                    self.stt(S["ckt"][tsl, :].rearrange("(c p) f -> p c f", p=128), okt[:], rokt, R["ckt"])
                    self.stt(S["cvt"][tsl, :].rearrange("(c p) f -> p c f", p=128), ovt[:], rovt, R["cvt"])
                    self.stt(S["co"][tsl, :].rearrange("(c p) f -> p c f", p=128), oo[:], roo, R["co"])
                    self.stt(S["cg"][tsl, :].rearrange("(c p) f -> p c f", p=128), og[:], rog, R["cg"])
                    self.stt(S["yall"][0:256, tsl].rearrange("(c p) t -> p c t", p=128), ya[:], rya, R["yall"])
            fw.phase_end()

    def attn_plan(self):
        NB2 = self.NB2
        plan = []
        for u in range(3):
            for b in range(NB2):
                if u == 2 or (u == 0 and b < NB2 - 2) or (u == 1 and b >= 2):
                    if b == 0 and u != 1:
                        cls = "TOP0"
                    elif b == 1 and u != 1:
                        cls = "TOP1"
                    elif b == NB2 - 2 and u != 0:
                        cls = "BOT1"
                    elif b == NB2 - 1 and u != 0:
                        cls = "BOT0"
                    else:
                        cls = "INT"
                elif u == 0:
                    cls = "JA1" if b == NB2 - 2 else "JA0"
                else:
                    cls = "JB0" if b == 0 else "JB1"
                s0, offs = CLS[cls]
                ents = []
                for i, o in enumerate(offs):
                    kp = b + o
                    ku = u
                    if kp >= NB2:
                        ku, kp = u + 1, kp - NB2
                    elif kp < 0:
                        ku, kp = u - 1, kp + NB2
                    assert 0 <= ku <= 2 and (ku == u or (u, ku) in ((0, 1), (1, 0)))
                    ents.append((ku, kp, s0 + i))
                plan.append((u, b, ents))
        return plan

    def gen_attn(self, l, ptr, rptr, ps):
        fw, I, S, R, C = self.fw, self.I, self.S, self.R, self.C
        UL, NB2 = self.UL, self.NB2
        identb, ridb = C["identb"]
        if True:
            sb = lambda n, s, dt=F32: fw.sbuf(n, s, dt, ps)
            bt = sb("bt", [128, NSLOT * 4, 128], BF16); rbt = fw.res("bt")
            stg = [(sb("bstg", [128, 8, 128], F32), fw.res("bstg", dma=True)) for _ in range(2)]
            n = 0
            for s0 in range(0, NSLOT * 4, 8):
                s1 = min(s0 + 8, NSLOT * 4)
                stt_, rs = stg[n % 2]
                n += 1
                self.ld(stt_[:, 0:s1 - s0, :], I["rpbt"][l, :, s0:s1, :], rs)
                fw.op("pool", lambda e, s0=s0, s1=s1, stt_=stt_: e.tensor_copy(bt[:, s0:s1, :], stt_[:, 0:s1 - s0, :]),
                      reads=[rs], writes=[rbt])
            kT = sb("kT", [128, 2, 2 * UL], BF16); rkT = fw.res("kT", dma=True)
            V = sb("V", [128, 2 * self.NCH, 4, 65], BF16); rV = fw.res("V", dma=True)
            fw.op("pool", lambda e: e.memset(V[:], 1.0), writes=[rV])
            qT = [(sb("qT", [128, 2, 512], BF16), fw.res("qT", dma=True)) for _ in range(2)]
            pS = [(fw.psum("pS", [128, 8, 128], F32, ps), fw.res("pS", excl=True)) for _ in range(1)]
            pO = [(fw.psum("pO", [128, 4, 65], F32, ps), fw.res("pO", excl=True)) for _ in range(1)]
            PT = [(sb("PT", [128, 8, 128], BF16), fw.res("PT")) for _ in range(3)]
            rc = sb("rc", [128, 4]); rrc = fw.res("rc")
            yB = sb("yB", [128, 4, 64], BF16); ryB = fw.res("yB")
            yo = [(sb("yBo", [128, 2, 512], BF16), fw.res("yBo", dma=True)) for _ in range(2)]
            plan = self.attn_plan()
            nps = npt = nq = 0
            for grp in (0, 2):
                units = (0, 1) if grp == 0 else (2,)
                base = grp * UL
                ntok = len(units) * UL
                self.ld(kT[:, :, 0:ntok], S["bk"][:, base:base + ntok].rearrange("(c p) t -> p c t", p=128), rkT, R["bk"])
                for h in range(4):
                    self.ld(V[:, 0:ntok // 128, h, 0:64],
                            S["bv"][base:base + ntok, h * 64:(h + 1) * 64].rearrange("(c p) d -> p c d", p=128),
                            rV, R["bv"])
                for (u, b, ents) in plan:
                    if u not in units:
                        continue
                    if b % 4 == 0:
                        qt, rq = qT[nq % 2]
                        yot, ryo = yo[nq % 2]
                        nq += 1
                        tq0 = u * UL + b * 128
                        self.ld(qt[:], S["bq"][:, tq0:tq0 + 512].rearrange("(c p) t -> p c t", p=128), rq, R["bq"])
                    qs = slice((b % 4) * 128, (b % 4 + 1) * 128)
                    po, rpo = pO[0]
                    ne = len(ents)
                    for h in range(4):
                        hp = slice((h % 2) * 64, (h % 2) * 64 + 64)
                        hc = h // 2
                        pst_, rps = pS[0]
                        nps += 1
                        for i, (ku, kp, slot) in enumerate(ents):
                            k0 = (ku * UL - base) + kp * 128
                            fw.op("pe", lambda e, i=i, k0=k0: e.matmul(pst_[:, i, :], kT[hp, hc, k0:k0 + 128], qt[hp, hc, qs],
                                                                        start=True, stop=False), reads=[rkT, rq], writes=[rps])
                            fw.op("pe", lambda e, i=i, slot=slot: e.matmul(pst_[:, i, :], identb[:], bt[:, slot * 4 + h, :],
                                                                            start=False, stop=True), reads=[ridb, rbt], writes=[rps])
                        pt_, rpt = PT[npt % 3]
                        npt += 1
                        n1 = min(ne, 4)
                        fw.op("act", lambda e: e.activation(pt_[:, 0:n1, :], pst_[:, 0:n1, :], AF.Exp), reads=[rps], writes=[rpt])
                        if ne > 4:
                            fw.op("act", lambda e: e.activation(pt_[:, 4:ne, :], pst_[:, 4:ne, :], AF.Exp), reads=[rps], writes=[rpt])
                        for i, (ku, kp, slot) in enumerate(ents):
                            vc_ = (ku * UL - base) // 128 + kp
                            fw.op("pe", lambda e, i=i, vc_=vc_: e.matmul(po[:, h, :], pt_[:, i, :], V[:, vc_, h, :],
                                                                          start=(i == 0), stop=(i == ne - 1)),
                                  reads=[rpt, rV], writes=[rpo])
                    fw.op("dve", lambda e: e.reciprocal(rc[:], po[:, :, 64]), reads=[rpo], writes=[rrc])
                    fw.op("dve", lambda e: e.tensor_tensor(yB[:], po[:, :, 0:64], rc[:].unsqueeze(2).to_broadcast([128, 4, 64]), ALU.mult),
                          reads=[rpo, rrc], writes=[ryB])
                    for c in range(2):
                        fw.op("pe", lambda e, c=c: e.transpose(ptr[:, c, :], yB[:, 2 * c:2 * c + 2, :], identb[:]),
                              reads=[ryB, ridb], writes=[rptr])
                    fw.op("act", lambda e: e.copy(yot[:, :, qs], ptr[:]), reads=[rptr], writes=[ryo])
                    if b % 4 == 3:
                        self.stt(S["yall"][256:512, tq0:tq0 + 512].rearrange("(c p) t -> p c t", p=128), yot[:], ryo, R["yall"])
                    yield

    def gen_mlstm(self, l, ptr, rptr, ps):
        fw, I, S, R, C = self.fw, self.I, self.S, self.R, self.C
        UL, NCH = self.UL, self.NCH
        identb, ridb = C["identb"]
        onesf, ronesf = C["onesf"]
        triu, rtriu = C["triu"]
        tril, rtril = C["tril"]
        sel, rsel = C["sel"]
        link, rlink = C["link"]
        if True:
            sb = lambda n, s, dt=F32: fw.sbuf(n, s, dt, ps)
            gb = sb("gb", [128, 16]); rgb = fw.res("gb", dma=True)
            self.ld(gb[:], I["c_gate_b"][l].partition_broadcast(128), rgb)
            hng = sb("hng", [128, 256]); rhng = fw.res("hng", dma=True)
            self.ld(hng[:], I["c_hnorm"][l].partition_broadcast(128), rhng)
            NBUF = 2
            qTb = [(sb("cqT", [128, 2, 512], BF16), fw.res("cqT", dma=True)) for _ in range(NBUF)]
            kTb = [(sb("ckT", [128, 2, 512], BF16), fw.res("ckT", dma=True)) for _ in range(NBUF)]
            ktb = [(sb("ckt", [128, 4, 256], BF16), fw.res("ckt", dma=True)) for _ in range(NBUF)]
            vtb = [(sb("cvt", [128, 4, 4, 65], F32), fw.res("cvt", dma=True)) for _ in range(NBUF)]
            for vt_, rv_ in vtb:
                fw.op("pool", lambda e, vt_=vt_: e.memset(vt_[:], 1.0), writes=[rv_])
            gtb = [(sb("cgt", [128, 4, 16]), fw.res("cgt", dma=True)) for _ in range(NBUF)]
            otb = [(sb("cot", [128, 4, 256]), fw.res("cot", dma=True)) for _ in range(NBUF)]
            hfb = [(sb("chf", [128, 4, 256]), fw.res("chf", dma=True)) for _ in range(NBUF)]
            yCo = [(sb("yCo", [128, 2, 512], BF16), fw.res("yCo", dma=True)) for _ in range(2)]
            CT = sb("CT", [128, 2, 65]); rCT = fw.res("CT")
            CTb = sb("CTb", [128, 2, 65], BF16); rCTb = fw.res("CTb")
            klo = sb("klo", [128, 2, 128], BF16); rklo = fw.res("klo")
            khi = sb("khi", [128, 2, 128], BF16); rkhi = fw.res("khi")
            fw.op("pool", lambda e: e.memset(klo[:], 0.0), writes=[rklo])
            fw.op("pool", lambda e: e.memset(khi[:], 0.0), writes=[rkhi])
            g = sb("g", [128, 8]); rg = fw.res("g")
            lf = sb("lf", [128, 4]); rlf = fw.res("lf")
            ex = sb("ex", [128, 12]); rex = fw.res("ex")
            Fm = sb("Fm", [128, 2]); rFm = fw.res("Fm")
            gsb = sb("gsb", [128, 8]); rgsb = fw.res("gsb")
            wm = sb("wm", [128, 4, 128], BF16); rwm = fw.res("wm")
            va = sb("va", [128, 4, 65], BF16); rva = fw.res("va")
            tn = sb("tn", [128, 4, 65]); rtn = fw.res("tn")
            dd = sb("dd", [128, 4]); rdd = fw.res("dd")
            hd = sb("hd", [128, 256]); rhd = fw.res("hd")
            sqh = sb("sqh", [128, 256]); rsqh = fw.res("sqh")
            ss = sb("ss", [128, 4]); rss = fw.res("ss")
            yCt = sb("yCt", [128, 256], BF16); ryCt = fw.res("yCt")
            pGC = fw.psum("pGC", [128, 512], F32, ps); rpG = fw.res("pGC", excl=True)
            pG = pGC[:, 0:16]
            pC = pGC[:, 32:162].rearrange("p (a b) -> p a b", b=65); rpC = rpG
            pQK = fw.psum("pQK", [128, 2, 128], F32, ps); rpQK = fw.res("pQK", excl=True)
            pQK2 = fw.psum("pQK2", [128, 2, 128], F32, ps); rpQK2 = fw.res("pQK2", excl=True)
            pN = fw.psum("pN", [128, 4, 65], F32, ps); rpN = fw.res("pN", excl=True)
            nld = 0
            for d in (0, 1):
                go = 8 * d
                tri, rtri = (triu, rtriu) if d == 0 else (tril, rtril)
                unit_order = (0, 1, 2) if d == 0 else (1, 0, 2)
                for ui, u in enumerate(unit_order):
                    if ui == 1:
                        fw.op("dve", lambda e: e.tensor_scalar(CT[:], CT[:], link[:, 0:1], None, ALU.mult),
                              reads=[rCT, rlink], writes=[rCT])
                    else:
                        fw.op("dve", lambda e: e.memset(CT[:], 0.0), writes=[rCT])
                    blocks = range(self.NBK) if d == 0 else range(self.NBK - 1, -1, -1)
                    for b in blocks:
                        t0 = u * UL + b * 512
                        tsl = slice(t0, t0 + 512)
                        bi = nld % NBUF
                        nld += 1
                        qt, rq = qTb[bi]; kt, rk = kTb[bi]; ktm, rktm = ktb[bi]; vtm, rvtm = vtb[bi]
                        gt, rgt = gtb[bi]; ot, rot = otb[bi]; hf, rhf = hfb[bi]
                        self.ld(qt[:], S["cq"][:, tsl].rearrange("(c p) t -> p c t", p=128), rq, R["cq"])
                        self.ld(kt[:], S["ck"][:, tsl].rearrange("(c p) t -> p c t", p=128), rk, R["ck"])
                        self.ld(ktm[:], S["ckt"][tsl, :].rearrange("(c p) f -> p c f", p=128), rktm, R["ckt"])
                        for h in range(4):
                            self.ld(vtm[:, :, h, 0:64], S["cvt"][tsl, h * 64:(h + 1) * 64].rearrange("(c p) d -> p c d", p=128), rvtm, R["cvt"])
                        self.ld(gt[:], S["cg"][tsl, :].rearrange("(c p) f -> p c f", p=128), rgt, R["cg"])
                        if d == 1:
                            self.ld(ot[:], S["co"][tsl, :].rearrange("(c p) f -> p c f", p=128), rot, R["co"])
                            self.ld(hf[:], S["chf"][tsl, :].rearrange("(c p) f -> p c f", p=128), rhf, R["chf"])
                            yco, ryco = yCo[nld % 2]
                        chunks = range(4) if d == 0 else range(3, -1, -1)
                        for ch in chunks:
                            cs = slice(ch * 128, (ch + 1) * 128)
                            fw.op("dve", lambda e: e.tensor_tensor(g[:], gt[:, ch, go:go + 8], gb[:, go:go + 8], ALU.add),
                                  reads=[rgt, rgb], writes=[rg])
                            if int(os.environ.get('KCUTM', '99')) < 1:
                                continue
                            fw.op("act", lambda e: e.activation(lf[:], g[:, 4:8], AF.Exp, scale=-1.0), reads=[rg], writes=[rlf])
                            fw.op("act", lambda e: e.activation(lf[:], lf[:], AF.Ln, scale=1.0, bias=self.epst[:, 1:2]),
                                  reads=[rlf, self.reps], writes=[rlf])
                            fw.op("dve", lambda e: e.tensor_scalar(lf[:], lf[:], -1.0, None, ALU.mult), reads=[rlf], writes=[rlf])
                            if int(os.environ.get('KCUTM', '99')) < 2:
                                continue
                            fw.op("pe", lambda e: e.matmul(pG[:, 0:4], tri[:], lf[:], start=True, stop=True), reads=[rtri, rlf], writes=[rpG])
                            fw.op("pe", lambda e: e.matmul(pG[:, 4:8], onesf[:], lf[:], start=True, stop=True), reads=[ronesf, rlf], writes=[rpG])
                            fw.op("pe", lambda e: e.matmul(pG[:, 8:10], sel[:, 0, :], lf[:, 0:4:2], start=True, stop=False),
                                  reads=[rsel, rlf], writes=[rpG])
                            fw.op("pe", lambda e: e.matmul(pG[:, 8:10], sel[:, 1, :], lf[:, 1:4:2], start=False, stop=True),
                                  reads=[rsel, rlf], writes=[rpG])
                            if int(os.environ.get('KCUTM', '99')) < 3:
                                continue
                            fw.op("act", lambda e: e.copy(gsb[:], pG[:, 0:8]), reads=[rpG], writes=[rgsb])
                            fw.op("dve", lambda e: e.tensor_tensor(ex[:, 4:8], gsb[:, 0:4], gsb[:, 4:8], ALU.subtract), reads=[rgsb], writes=[rex])
                            fw.op("dve", lambda e: e.tensor_tensor(ex[:, 0:4], g[:, 0:4], ex[:, 4:8], ALU.subtract), reads=[rg, rex], writes=[rex])
                            fw.op("act", lambda e: e.activation(ex[:, 0:8], ex[:, 0:8], AF.Exp), reads=[rex], writes=[rex])
                            fw.op("act", lambda e: e.activation(Fm[:], pG[:, 8:10], AF.Exp), reads=[rpG], writes=[rFm])
                            if int(os.environ.get('KCUTM', '99')) < 4:
                                continue
                            fw.op("dve", lambda e: e.tensor_tensor(va[:], vtm[:, ch, :, :],
                                                                   ex[:, 0:4].unsqueeze(2).to_broadcast([128, 4, 65]), ALU.mult),
                                  reads=[rvtm, rex], writes=[rva])
                            if int(os.environ.get('KCUTM', '99')) < 5:
                                continue
                            for h in range(4):
                                hp = slice((h % 2) * 64, (h % 2) * 64 + 64)
                                pq_, rpq_ = (pQK, rpQK) if h % 2 == 0 else (pQK2, rpQK2)
                                fw.op("pe", lambda e, h=h, hp=hp: e.matmul(pq_[:, h // 2, :], kt[hp, h // 2, cs], qt[hp, h // 2, cs],
                                                                            start=True, stop=True), reads=[rk, rq], writes=[rpq_])
                            wm4 = wm[:].rearrange("p (j q) t -> p j q t", q=2)
                            fw.op("dve", lambda e: e.tensor_tensor(wm4[:, :, 0, :], pQK[:], tri[:].unsqueeze(1).to_broadcast([128, 2, 128]), ALU.mult),
                                  reads=[rpQK, rtri], writes=[rwm])
                            fw.op("dve", lambda e: e.tensor_tensor(wm4[:, :, 1, :], pQK2[:], tri[:].unsqueeze(1).to_broadcast([128, 2, 128]), ALU.mult),
                                  reads=[rpQK2, rtri], writes=[rwm])
                            if int(os.environ.get('KCUTM', '99')) < 6:
                                continue
                            fw.op("dve", lambda e: e.tensor_tensor(CTb[:], CT[:], Fm[:].unsqueeze(2).to_broadcast([128, 2, 65]), ALU.mult),
                                  reads=[rCT, rFm], writes=[rCTb])
                            for h in range(4):
                                hp = slice((h % 2) * 64, (h % 2) * 64 + 64)
                                fw.op("pe", lambda e, h=h: e.matmul(pN[:, h, :], wm[:, h, :], va[:, h, :], start=True, stop=False),
                                      reads=[rwm, rva], writes=[rpN])
                                fw.op("pe", lambda e, h=h, hp=hp: e.matmul(pN[:, h, :], qt[hp, h // 2, cs], CTb[hp, h // 2, :], start=False, stop=True),
                                      reads=[rq, rCTb], writes=[rpN])
                            if int(os.environ.get('KCUTM', '99')) < 7:
                                continue
                            fw.op("pool", lambda e: e.tensor_copy(klo[:, :, 0:64], ktm[:, ch, :].rearrange("p (a b) -> p a b", b=128)[:, :, 0:64]),
                                  reads=[rktm], writes=[rklo])
                            fw.op("pool", lambda e: e.tensor_copy(khi[:, :, 64:128], ktm[:, ch, :].rearrange("p (a b) -> p a b", b=128)[:, :, 64:128]),
                                  reads=[rktm], writes=[rkhi])
                            for p in range(2):
                                fw.op("pe", lambda e, p=p: e.matmul(pC[:, p, :], klo[:, p, :], va[:, 2 * p, :], start=True, stop=False),
                                      reads=[rklo, rva], writes=[rpC])
                                fw.op("pe", lambda e, p=p: e.matmul(pC[:, p, :], khi[:, p, :], va[:, 2 * p + 1, :], start=False, stop=True),
                                      reads=[rkhi, rva], writes=[rpC])
                            for p in range(2):
                                fw.op("dve", lambda e, p=p: e.scalar_tensor_tensor(CT[:, p, :], CT[:, p, :], Fm[:, p:p + 1], pC[:, p, :],
                                                                                   ALU.mult, ALU.add),
                                      reads=[rCT, rFm, rpC], writes=[rCT])
                            if int(os.environ.get('KCUTM', '99')) < 8:
                                continue
                            fw.op("dve", lambda e: e.tensor_tensor(tn[:], pN[:], ex[:, 4:8].unsqueeze(2).to_broadcast([128, 4, 65]), ALU.mult),
                                  reads=[rpN, rex], writes=[rtn])
                            fw.op("dve", lambda e: e.scalar_tensor_tensor(dd[:], tn[:, :, 64], -1.0, tn[:, :, 64], ALU.mult, ALU.max), reads=[rtn], writes=[rdd])
                            fw.op("dve", lambda e: e.tensor_scalar(dd[:], dd[:], 1.0, None, ALU.max), reads=[rdd], writes=[rdd])
                            fw.op("dve", lambda e: e.reciprocal(dd[:], dd[:]), reads=[rdd], writes=[rdd])
                            if d == 0:
                                fw.op("dve", lambda e: e.tensor_tensor(hf[:, ch, :].rearrange("p (h d) -> p h d", d=64), tn[:, :, 0:64],
                                                                       dd[:].unsqueeze(2).to_broadcast([128, 4, 64]), ALU.mult),
                                      reads=[rtn, rdd], writes=[rhf])
                            else:
                                hd3 = hd[:].rearrange("p (h d) -> p h d", d=64)
                                fw.op("dve", lambda e: e.tensor_tensor(hd3, tn[:, :, 0:64],
                                                                       dd[:].unsqueeze(2).to_broadcast([128, 4, 64]), ALU.mult),
                                      reads=[rtn, rdd], writes=[rhd])
                                fw.op("dve", lambda e: e.tensor_tensor(hd[:], hd[:], hf[:, ch, :], ALU.add), reads=[rhd, rhf], writes=[rhd])
                                fw.op("dve", lambda e: e.tensor_tensor(sqh[:], hd[:], hd[:], ALU.mult), reads=[rhd], writes=[rsqh])
                                fw.op("dve", lambda e: e.reduce_sum(ss[:], sqh[:].rearrange("p (h d) -> p h d", d=64), AX.X), reads=[rsqh], writes=[rss])
                                fw.op("act", lambda e: e.activation(ss[:], ss[:], AF.Ln, scale=1.0 / 64, bias=self.eps_ap()),
                                      reads=[rss, self.reps], writes=[rss])
                                fw.op("act", lambda e: e.activation(ss[:], ss[:], AF.Exp, scale=-0.5), reads=[rss], writes=[rss])
                                fw.op("dve", lambda e: e.tensor_tensor(hd3, hd3, ss[:].unsqueeze(2).to_broadcast([128, 4, 64]), ALU.mult),
                                      reads=[rhd, rss], writes=[rhd])
                                fw.op("dve", lambda e: e.tensor_tensor(hd[:], hd[:], hng[:], ALU.mult), reads=[rhd, rhng], writes=[rhd])
                                fw.op("dve", lambda e: e.tensor_tensor(yCt[:], hd[:], ot[:, ch, :], ALU.mult), reads=[rhd, rot], writes=[ryCt])
                                for c in range(2):
                                    fw.op("pe", lambda e, c=c: e.transpose(ptr[:, c, :], yCt[:, c * 128:(c + 1) * 128], identb[:]),
                                          reads=[ryCt, ridb], writes=[rptr])
                                fw.op("act", lambda e: e.copy(yco[:, :, cs], ptr[:]), reads=[rptr], writes=[ryco])
                            yield
                        if d == 0:
                            self.stt(S["chf"][tsl, :].rearrange("(c p) f -> p c f", p=128), hf[:], rhf, R["chf"])
                        else:
                            self.stt(S["yall"][512:768, tsl].rearrange("(c p) t -> p c t", p=128), yco[:], ryco, R["yall"])

    def phase_bc(self, l):
        fw = self.fw
        fw.phase_begin()
        with ExitStack() as ps:
            ptr = fw.psum("ptrbc", [128, 2, 128], BF16, ps)
            rptr = fw.res("ptrbc", excl=True)
            ga = self.gen_attn(l, ptr, rptr, ps)
            gm = self.gen_mlstm(l, ptr, rptr, ps)
            live = [[ga, 1], [gm, 2]]
            while live:
                for ent in list(live):
                    g, r = ent
                    try:
                        for _ in range(r):
                            next(g)
                    except StopIteration:
                        live.remove(ent)
            fw.phase_end()

    def phase_conv(self, l):
        fw, I, S, R, C = self.fw, self.I, self.S, self.R, self.C
        UL = self.UL
        onesb, ronesb = C["onesb"]
        link, rlink = C["link"]
        cw, rcw = C["d_conv_wT"]
        dv, rdv = C["d_vec"]
        fw.phase_begin()
        with ExitStack() as ps:
            sb = lambda n, s, dt=F32: fw.sbuf(n, s, dt, ps)
            yp = [(sb("yp", [128, 2, UL + 30]), fw.res("yp", dma=True)) for _ in range(2)]
            acc = sb("acc", [128, 2, 512]); racc2 = [fw.res("acc0"), fw.res("acc1")]
            identf, ridf = C["identf"]
            dg = sb("dg", [128, 2, 31, 128], BF16); rdg = fw.res("dg")
            for cc in range(2):
                for k in range(31):
                    en = "dve" if (k % 2 == 0) else "pool"
                    fw.op(en, lambda e, cc=cc, k=k: e.tensor_scalar(dg[:, cc, k, :], identf[:], cw[:, l, cc, k:k + 1], None, ALU.mult),
                          reads=[ridf, rcw], writes=[rdg])
            ypb = [(sb("ypb", [128, 2, UL + 30], BF16), fw.res("ypb")) for _ in range(2)]
            pcv = [(fw.psum("pcv", [128, 512], F32, ps), fw.res("pcv", excl=True)) for _ in range(2)]
            accb = sb("accb", [128, 2, 512], BF16); raccb = fw.res("accb")
            sqb = sb("sqb", [128, 2, 512], BF16); rsqb = fw.res("sqb")
            m2 = sb("m2", [128, 512]); rm2 = fw.res("m2")
            rs = sb("rs", [128, 512]); rrs = fw.res("rs")
            tt = sb("tt", [128, 512]); rtt = fw.res("tt")
            yo = [(sb("yDo", [128, 2, 512], BF16), fw.res("yDo", dma=True)) for _ in range(2)]
            pM = fw.psum("pM", [128, 512], F32, ps); rpM = fw.res("pM", excl=True)
            pQ = fw.psum("pQ", [128, 512], F32, ps); rpQ = fw.res("pQ", excl=True)
            no = 0
            for u in range(3):
                ypt, ryp = yp[u % 2]
                fw.op("pool", lambda e: e.memset(ypt[:, :, 0:15], 0.0), writes=[ryp])
                fw.op("pool", lambda e: e.memset(ypt[:, :, UL + 15:UL + 30], 0.0), writes=[ryp])
                src = S["dy"].rearrange("(c p) t -> p c t", p=128)
                self.ld(ypt[:, :, 15:15 + UL], src[:, :, u * UL:(u + 1) * UL], ryp, R["dy"])
                if u == 0:
                    self.ld(ypt[:, :, UL + 15:UL + 30], src[:, :, UL:UL + 15], ryp, R["dy"])
                    fw.op("pool", lambda e: e.tensor_scalar(ypt[:, :, UL + 15:UL + 30], ypt[:, :, UL + 15:UL + 30], link[:, 0:1], None, ALU.mult),
                          reads=[ryp, rlink], writes=[ryp])
                elif u == 1:
                    self.ld(ypt[:, :, 0:15], src[:, :, UL - 15:UL], ryp, R["dy"])
                    fw.op("pool", lambda e: e.tensor_scalar(ypt[:, :, 0:15], ypt[:, :, 0:15], link[:, 0:1], None, ALU.mult),
                          reads=[ryp, rlink], writes=[ryp])
                ypbt, rypb = ypb[u % 2]
                fw.op("act", lambda e: e.copy(ypbt[:, 0, :], ypt[:, 0, :]), reads=[ryp], writes=[rypb])
                fw.op("pool", lambda e: e.tensor_copy(ypbt[:, 1, :], ypt[:, 1, :]), reads=[ryp], writes=[rypb])
                for b in range(self.NBK):
                    t0 = b * 512
                    for c in range(2):
                        pct, rpc = pcv[c]
                        for k in range(31):
                            fw.op("pe", lambda e, c=c, k=k: e.matmul(pct[:], dg[:, c, k, :], ypbt[:, c, t0 + k:t0 + k + 512],
                                                                      start=(k == 0), stop=(k == 30)), reads=[rdg, rypb], writes=[rpc])
                        fw.op("act", lambda e, c=c: e.activation(acc[:, c, :], pct[:], AF.Identity, scale=1.0, bias=dv[:, l, 0, c:c + 1]),
                              reads=[rpc, rdv], writes=[racc2[c]])
                    for c in range(2):
                        fw.op("act", lambda e, c=c: e.activation(sqb[:, c, :], acc[:, c, :], AF.Square), reads=[racc2[c]], writes=[rsqb])
                        fw.op("pool", lambda e, c=c: e.tensor_copy(accb[:, c, :], acc[:, c, :]), reads=[racc2[c]], writes=[raccb])
                    for c in range(2):
                        fw.op("pe", lambda e, c=c: e.matmul(pM[:], onesb[:], accb[:, c, :], start=(c == 0), stop=(c == 1)),
                              reads=[ronesb, raccb], writes=[rpM])
                    for c in range(2):
                        fw.op("pe", lambda e, c=c: e.matmul(pQ[:], onesb[:], sqb[:, c, :], start=(c == 0), stop=(c == 1)),
                              reads=[ronesb, rsqb], writes=[rpQ])
                    fw.op("act", lambda e: e.activation(m2[:], pM[:], AF.Square, scale=1.0 / 256), reads=[rpM], writes=[rm2])
                    fw.op("dve", lambda e: e.scalar_tensor_tensor(rs[:], pQ[:], 1.0 / 256, m2[:], ALU.mult, ALU.subtract),
                          reads=[rpQ, rm2], writes=[rrs])
                    fw.op("dve", lambda e: e.tensor_scalar(rs[:], rs[:], 0.0, None, ALU.max), reads=[rrs], writes=[rrs])
                    fw.op("act", lambda e: e.activation(rs[:], rs[:], AF.Sqrt, scale=1.0, bias=self.eps_ap()), reads=[rrs, self.reps], writes=[rrs])
                    fw.op("dve", lambda e: e.reciprocal(rs[:], rs[:]), reads=[rrs], writes=[rrs])
                    yot, ryo = yo[no % 2]
                    no += 1
                    for c in range(2):
                        fw.op("dve", lambda e, c=c: e.scalar_tensor_tensor(tt[:], pM[:], -1.0 / 256, acc[:, c, :], ALU.mult, ALU.add),
                              reads=[rpM, racc2[c]], writes=[rtt])
                        fw.op("dve", lambda e: e.tensor_tensor(tt[:], tt[:], rs[:], ALU.mult), reads=[rtt, rrs], writes=[rtt])
                        fw.op("act", lambda e, c=c: e.activation(yot[:, c, :], tt[:], AF.Silu, scale=dv[:, l, 1, c:c + 1], bias=dv[:, l, 2, c:c + 1]),
                              reads=[rtt, rdv], writes=[ryo])
                    tg = u * UL + t0
                    self.stt(S["yall"][768:1024, tg:tg + 512].rearrange("(c p) t -> p c t", p=128), yot[:], ryo, R["yall"])
            fw.phase_end()

    def phase_p3a(self, l):
        fw, I, S, R, C = self.fw, self.I, self.S, self.R, self.C
        BT = 256
        xsrc, rxsrc = (I["xT"], R["xT"]) if l == 0 else (S["xn"], R["xn"])
        fw.phase_begin()
        with ExitStack() as ps:
            sb = lambda n, s, dt=F32: fw.sbuf(n, s, dt, ps)
            Wg = sb("wg", [128, 8, 4096], BF16); rWg = fw.res("wg")
            Wb = sb("wb", [128, 8, 1024], BF16); rWb = fw.res("wb")
            Wo = sb("wo", [128, 8, 1024], BF16); rWo = fw.res("wo")
            stg = [(sb("stg3", [128, 1024], F32), fw.res("stg3", dma=True)) for _ in range(3)]
            self.stg_i = 0
            self.load_cast(Wg, rWg, 0, 8, I["w_in"][l][:, 2832:6928], 4096, stg, ["pool", "dve", "act"], 1024)
            self.load_cast(Wb, rWb, 0, 8, I["w_branch"][l], 1024, stg, ["pool", "dve", "act"], 1024)
            self.load_cast(Wo, rWo, 0, 8, I["w_out"][l], 1024, stg, ["pool", "dve", "act"], 1024)
            xb = [(sb("xb3", [128, 8, BT]), fw.res("xb3", dma=True)) for _ in range(2)]
            yb = [(sb("yb3", [128, 8, BT], BF16), fw.res("yb3", dma=True)) for _ in range(2)]
            hT = sb("hT3", [128, 8, BT], BF16); rhT = fw.res("hT3")
            hT2 = sb("hT3b", [128, 8, BT], BF16); rhT2 = fw.res("hT3b")
            sq = [sb("sq3", [128, BT]) for _ in range(2)]; rsq = [fw.res("sq3") for _ in range(2)]
            rstd = sb("rstd3", [128, BT]); rrstd = fw.res("rstd3")
            tmp = sb("tmp3", [128, BT]); rtmp = fw.res("tmp3")
            sg = [(sb("sg3", [128, BT]), fw.res("sg3")) for _ in range(3)]
            t2 = [(sb("t23", [128, BT]), fw.res("t23")) for _ in range(2)]
            acc = [(sb("acc3", [128, BT]), fw.res("acc3")) for _ in range(2)]
            mg = sb("mg3", [128, 8, BT], BF16); rmg = fw.res("mg3")
            pg = [(fw.psum("pg3", [128, BT], F32, ps), fw.res("pg3", excl=True)) for _ in range(3)]
            pp = [(fw.psum("pp3", [128, BT], F32, ps), fw.res("pp3", excl=True)) for _ in range(3)]
            po = [(fw.psum("po3", [128, BT], F32, ps), fw.res("po3", excl=True)) for _ in range(2)]
            pst, rpst = po[1]
            n = no = nt2 = 0
            NBLK = min(self.NT // BT, int(os.environ.get('KBLK', '9999')))
            hTs = [(hT, rhT), (hT2, rhT2)]

            def prep(gi):
                t0 = gi * BT
                xt, rx = xb[gi % 2]
                yt, ry = yb[gi % 2]
                self.ld(xt[:], xsrc[:, t0:t0 + BT].rearrange("(k p) t -> p k t", p=128), rx, rxsrc)
                self.ld(yt[:], S["yall"][:, t0:t0 + BT].rearrange("(k p) t -> p k t", p=128), ry, R["yall"])

            def nm(gi):
                xt, rx = xb[gi % 2]
                h_, rh_ = hTs[gi % 2]
                self.norm_mod(xt, rx, h_, rh_, sq, rsq, pst, rpst, rstd, rrstd, l, 0, (gi * BT) // self.UL, tmp, rtmp)

            prep(0)
            nm(0)
            for gi in range(NBLK):
                t0 = gi * BT
                u = t0 // self.UL
                xt, rx = xb[gi % 2]
                yt, ry = yb[gi % 2]
                hT, rhT = hTs[gi % 2]
                if gi + 1 < NBLK:
                    prep(gi + 1)
                for f in range(8):
                    acct, racc = acc[f % 2]
                    for br in range(4):
                        pgt, rpg = pg[n % 3]
                        ppt, rpp = pp[n % 3]
                        sgt, rsg = sg[n % 3]
                        n += 1
                        col = br * 1024 + f * 128
                        for k in range(8):
                            fw.op("pe", lambda e, k=k: e.matmul(pgt[:], Wg[:, k, col:col + 128], hT[:, k, :], start=(k == 0), stop=(k == 7)),
                                  reads=[rWg, rhT], writes=[rpg])
                        for k in range(2):
                            fw.op("pe", lambda e, k=k: e.matmul(ppt[:], Wb[:, br * 2 + k, f * 128:(f + 1) * 128], yt[:, br * 2 + k, :],
                                                                start=(k == 0), stop=(k == 1)), reads=[rWb, ry], writes=[rpp])
                        fw.op("act", lambda e: e.activation(sgt[:], pgt[:], AF.Sigmoid), reads=[rpg], writes=[rsg])
                        if br == 0:
                            fw.op("dve", lambda e: e.tensor_tensor(acct[:], sgt[:], ppt[:], ALU.mult), reads=[rsg, rpp], writes=[racc])
                        else:
                            t2t, rt2 = t2[nt2 % 2]
                            nt2 += 1
                            fw.op("dve", lambda e: e.tensor_tensor(t2t[:], sgt[:], ppt[:], ALU.mult), reads=[rsg, rpp], writes=[rt2])
                            if br < 3:
                                fw.op("pool", lambda e: e.tensor_tensor(acct[:], acct[:], t2t[:], ALU.add), reads=[racc, rt2], writes=[racc])
                            else:
                                fw.op("pool", lambda e, f=f: e.tensor_tensor(mg[:, f, :], acct[:], t2t[:], ALU.add), reads=[racc, rt2], writes=[rmg])
                if gi + 1 < NBLK:
                    nm(gi + 1)
                for f in range(8):
                    pot, rpo = po[no % 2]
                    no += 1
                    for k in range(8):
                        fw.op("pe", lambda e, k=k, f=f: e.matmul(pot[:], Wo[:, k, f * 128:(f + 1) * 128], mg[:, k, :], start=(k == 0), stop=(k == 7)),
                              reads=[rWo, rmg], writes=[rpo])
                    fw.op("dve", lambda e, f=f: e.scalar_tensor_tensor(xt[:, f, :], pot[:], self.mod[:, l, 2, f, u:u + 1], xt[:, f, :],
                                                                       ALU.mult, ALU.add), reads=[rpo, self.rmod, rx], writes=[rx])
                self.stt(S["xm"][:, t0:t0 + BT].rearrange("(k p) t -> p k t", p=128), xt[:], rx, R["xm"])
        fw.phase_end()

    def phase_p3b(self, l):
        fw, I, S, R, C = self.fw, self.I, self.S, self.R, self.C
        BT = 256
        last = (l == self.L - 1)
        fw.phase_begin()
        with ExitStack() as ps:
            sb = lambda n, s, dt=F32: fw.sbuf(n, s, dt, ps)
            W1 = sb("wf1", [128, 8, 2 * DFF], BF16); rW1 = fw.res("wf1")
            W2 = sb("wf2", [128, 22, 1024], BF16); rW2 = fw.res("wf2")
            stg = [(sb("stg4", [128, 1408], F32), fw.res("stg4", dma=True)) for _ in range(2)]
            self.stg_i = 0
            self.load_cast(W1, rW1, 0, 8, I["w_ffn_in"][l], 2 * DFF, stg, ["pool", "dve", "act"], 1408)
            self.load_cast(W2, rW2, 0, 22, I["w_ffn_out"][l], 1024, stg, ["pool", "dve", "act"], 1408)
            xb = [(sb("xb4", [128, 8, BT]), fw.res("xb4", dma=True)) for _ in range(2)]
            hT = sb("hT4", [128, 8, BT], BF16); rhT = fw.res("hT4")
            hT2 = sb("hT4b", [128, 8, BT], BF16); rhT2 = fw.res("hT4b")
            sq = [sb("sq4", [128, BT]) for _ in range(2)]; rsq = [fw.res("sq4") for _ in range(2)]
            rstd = sb("rstd4", [128, BT]); rrstd = fw.res("rstd4")
            tmp = sb("tmp4", [128, BT]); rtmp = fw.res("tmp4")
            sg = [(sb("sg4", [128, BT]), fw.res("sg4")) for _ in range(3)]
            hid = sb("hid4", [128, 22, BT], BF16); rhid = fw.res("hid4")
            pg = [(fw.psum("pg4", [128, BT], F32, ps), fw.res("pg4", excl=True)) for _ in range(3)]
            pu = [(fw.psum("pu4", [128, BT], F32, ps), fw.res("pu4", excl=True)) for _ in range(3)]
            po = [(fw.psum("po4", [128, BT], F32, ps), fw.res("po4", excl=True)) for _ in range(2)]
            pst, rpst = po[1]
            gfin, rgfin = C["g_finalT"]
            onesf, ronesf = C["onesf"]
            n = no = 0
            NBLK = min(self.NT // BT, int(os.environ.get('KBLK', '9999')))
            hTs = [(hT, rhT), (hT2, rhT2)]

            def prep(gi):
                t0 = gi * BT
                xt, rx = xb[gi % 2]
                self.ld(xt[:], S["xm"][:, t0:t0 + BT].rearrange("(k p) t -> p k t", p=128), rx, R["xm"])

            def nm(gi):
                xt, rx = xb[gi % 2]
                h_, rh_ = hTs[gi % 2]
                self.norm_mod(xt, rx, h_, rh_, sq, rsq, pst, rpst, rstd, rrstd, l, 1, (gi * BT) // self.UL, tmp, rtmp)

            prep(0)
            nm(0)
            for gi in range(NBLK):
                t0 = gi * BT
                u = t0 // self.UL
                xt, rx = xb[gi % 2]
                hT, rhT = hTs[gi % 2]
                if gi + 1 < NBLK:
                    prep(gi + 1)
                for j in range(22):
                    pgt, rpg = pg[n % 3]
                    put, rpu = pu[n % 3]
                    sgt, rsg = sg[n % 3]
                    n += 1
                    for k in range(8):
                        fw.op("pe", lambda e, k=k, j=j: e.matmul(pgt[:], W1[:, k, j * 128:(j + 1) * 128], hT[:, k, :], start=(k == 0), stop=(k == 7)),
                              reads=[rW1, rhT], writes=[rpg])
                    for k in range(8):
                        fw.op("pe", lambda e, k=k, j=j: e.matmul(put[:], W1[:, k, DFF + j * 128:DFF + (j + 1) * 128], hT[:, k, :],
                                                                  start=(k == 0), stop=(k == 7)), reads=[rW1, rhT], writes=[rpu])
                    fw.op("act", lambda e: e.activation(sgt[:], pgt[:], AF.Silu), reads=[rpg], writes=[rsg])
                    fw.op("dve", lambda e, j=j: e.tensor_tensor(hid[:, j, :], sgt[:], put[:], ALU.mult), reads=[rsg, rpu], writes=[rhid])
                if gi + 1 < NBLK:
                    nm(gi + 1)
                for f in range(8):
                    pot, rpo = po[no % 2]
                    no += 1
                    for k in range(22):
                        fw.op("pe", lambda e, k=k, f=f: e.matmul(pot[:], W2[:, k, f * 128:(f + 1) * 128], hid[:, k, :], start=(k == 0), stop=(k == 21)),
                              reads=[rW2, rhid], writes=[rpo])
                    fw.op("dve", lambda e, f=f: e.scalar_tensor_tensor(xt[:, f, :], pot[:], self.mod[:, l, 5, f, u:u + 1], xt[:, f, :],
                                                                       ALU.mult, ALU.add), reads=[rpo, self.rmod, rx], writes=[rx])
                if not last:
                    self.stt(S["xn"][:, t0:t0 + BT].rearrange("(k p) t -> p k t", p=128), xt[:], rx, R["xn"])
                else:
                    for k in range(8):
                        sqk, rsqk = sq[k % 2], rsq[k % 2]
                        fw.op("act", lambda e, k=k: e.activation(sqk[:], xt[:, k, :], AF.Square), reads=[rx], writes=[rsqk])
                        fw.op("pe", lambda e, k=k: e.matmul(pst[:], onesf[:], sqk[:], start=(k == 0), stop=(k == 7)),
                              reads=[rsqk, ronesf], writes=[rpst])
                    fw.op("act", lambda e: e.activation(rstd[:], pst[:], AF.Sqrt, scale=1.0 / D, bias=self.eps_ap()),
                          reads=[rpst, self.reps], writes=[rrstd])
                    fw.op("dve", lambda e: e.reciprocal(rstd[:], rstd[:]), reads=[rrstd], writes=[rrstd])
                    for k in range(8):
                        fw.op("dve", lambda e, k=k: e.scalar_tensor_tensor(xt[:, k, :], xt[:, k, :], gfin[:, k:k + 1], rstd[:], ALU.mult, ALU.mult),
                              reads=[rx, rgfin, rrstd], writes=[rx])
                    self.stt(self.yT[:, t0:t0 + BT].rearrange("(k p) t -> p k t", p=128), xt[:], rx, R["yT"])
        fw.phase_end()


def _bias_tile_idx(rows_total, qr0, kr0, q_valid_rows, k_valid_rows):
    kk = np.arange(128)
    qq = np.arange(128)
    krow = kr0 + kk // 64
    kcol = kk % 64
    qrow = qr0 + qq // 64
    qcol = qq % 64
    kr = min(8, rows_total)
    wlo = np.clip(qrow - kr // 2, 0, rows_total - kr)
    clo = np.clip(qcol - 8, 0, 64 - 16)
    vr = (krow[:, None] >= wlo[None, :]) & (krow[:, None] < wlo[None, :] + kr)
    vcol = (kcol[:, None] >= clo[None, :]) & (kcol[:, None] < clo[None, :] + 16)
    valid = vr & vcol
    valid &= (krow[:, None] >= 0) & (krow[:, None] < rows_total) & (qrow[None, :] >= 0) & (qrow[None, :] < rows_total)
    dr = np.clip(krow[:, None] - qrow[None, :] + 7, 0, 14)
    dc = np.clip(kcol[:, None] - qcol[None, :] + 15, 0, 30)
    return valid, dr, dc


def _make_rpbt(rpb_l, UL, link):
    Rr = UL // 64
    NB2 = UL // 128
    out = np.full((128, NSLOT * 4, 128), NEG, np.float32)

    def fill(slot, rows_total, qr0, kr0):
        valid, dr, dc = _bias_tile_idx(rows_total, qr0, kr0, None, None)
        for h in range(4):
            vals = rpb_l[h][dr, dc]
            out[:, slot * 4 + h, :] = np.where(valid, vals, np.float32(NEG))

    big = 64 if Rr >= 16 else Rr
    Rg = max(Rr, 16)
    for cls, bsel in (("INT", 4), ("TOP0", 0), ("TOP1", 1), ("BOT1", Rg // 2 - 2), ("BOT0", Rg // 2 - 1)):
        s0, offs = CLS[cls]
        for i, o in enumerate(offs):
            fill(s0 + i, Rg, 2 * bsel, 2 * (bsel + o))
    for cls, u, b in (("JA1", 0, NB2 - 2), ("JA0", 0, NB2 - 1), ("JB0", 1, 0), ("JB1", 1, 1)):
        s0, offs = CLS[cls]
        for i, o in enumerate(offs):
            if link:
                fill(s0 + i, 2 * Rr, 2 * (u * NB2 + b), 2 * (u * NB2 + b + o))
            else:
                kp = b + o
                if kp < 0 or kp >= NB2:
                    continue
                fill(s0 + i, Rr, 2 * b, 2 * kp)
    return out


def _host_prep(inp, UL, L, units_per_core):
    f32 = np.float32
    shared = {}
    shared["w_ada"] = np.ascontiguousarray(inp["w_ada"][:L])
    shared["b_adaT"] = np.ascontiguousarray(inp["b_ada"][:L].reshape(L, 48, 128).transpose(2, 0, 1))
    gv = np.stack([inp["g_norm_mix"][:L], inp["g_norm_ffn"][:L]], 1)
    shared["gvec"] = np.ascontiguousarray(gv.reshape(L, 2, 8, 128).transpose(3, 0, 1, 2))
    shared["w_in"] = np.ascontiguousarray(inp["w_in"][:L])
    shared["a_ln"] = np.ascontiguousarray(np.concatenate([inp["a_ln_g"][:L], inp["a_ln_b"][:L]], 1))
    shared["a_w_spT"] = np.ascontiguousarray(inp["a_w_sp"][:L].transpose(0, 3, 1, 2))
    shared["a_b_sp"] = np.ascontiguousarray(inp["a_b_sp"][:L].transpose(2, 0, 1))
    shared["c_gate_b"] = np.ascontiguousarray(inp["c_gate_b"][:L])
    shared["c_hnorm"] = np.ascontiguousarray(inp["c_hnorm_g"][:L])
    shared["d_conv_wT"] = np.ascontiguousarray(inp["d_conv_w"][:L].reshape(L, 31, 2, 128).transpose(3, 0, 2, 1))
    dv = np.stack([inp["d_conv_b"][:L], inp["d_ln_g"][:L], inp["d_ln_b"][:L]], 1)
    shared["d_vec"] = np.ascontiguousarray(dv.reshape(L, 3, 2, 128).transpose(3, 0, 1, 2))
    shared["w_branch"] = np.ascontiguousarray(inp["w_branch"][:L].reshape(L, 1024, D))
    shared["w_out"] = np.ascontiguousarray(inp["w_out"][:L])
    shared["w_ffn_in"] = np.ascontiguousarray(inp["w_ffn_in"][:L])
    shared["w_ffn_out"] = np.ascontiguousarray(inp["w_ffn_out"][:L])
    shared["g_finalT"] = np.ascontiguousarray(inp["g_final"].reshape(8, 128).T)
    shared["c_ident"] = np.eye(128, dtype=f32)
    shared["c_triu"] = np.triu(np.ones((128, 128), f32))
    shared["c_tril"] = np.tril(np.ones((128, 128), f32))
    sel = np.zeros((128, 2, 128), f32)
    sel[:, 0, 0:64] = 1.0
    sel[:, 1, 64:128] = 1.0
    shared["c_sel"] = sel
    rp = {}
    for link in (0, 1):
        rp[link] = np.stack([_make_rpbt(inp["b_rpb"][l], UL, link) for l in range(L)], 0)
    in_maps = []
    for link, units in units_per_core:
        m = dict(shared)
        xs, cs = [], []
        for which, si, t0 in units:
            x = inp["x_prompt"] if which == "p" else inp["x_sample"]
            c = inp["c_prompt"] if which == "p" else inp["c_sample"]
            xs.append(x[si, t0:t0 + UL, :])
            cs.append(c[si])
        m["xT"] = np.ascontiguousarray(np.concatenate(xs, 0).T)
        cc = np.stack(cs, 0)
        m["cT"] = np.ascontiguousarray(cc.reshape(3, 8, 128).transpose(2, 1, 0))
        m["link"] = np.full((128, 1), float(link), f32)
        m["rpbt"] = rp[link]
        in_maps.append(m)
    return in_maps


_NC_CACHE = {}


def run_config(inp, UL, L, units_per_core, debug=False):
    key = (UL, L, debug)
    if key not in _NC_CACHE:
        b = Builder(UL, L)
        b.debug = debug
        _NC_CACHE[key] = (b.build(), b)
    nc, b = _NC_CACHE[key]
    in_maps = _host_prep(inp, UL, L, units_per_core)
    res = run_bass_kernel_spmd(nc, in_maps, core_ids=list(range(len(in_maps))))
    if debug:
        return res.results
    return [np.asarray(r["yT"]) for r in res.results]


def kernel(**inputs):
    inp = {k: np.asarray(v) for k, v in inputs.items()}
    UL = 4096
    L = 4
    units = []
    for c in range(4):
        units.append((1, [("p", c, 0), ("p", c, UL), ("s", c, 0)]))
    for c in range(4):
        units.append((0, [("s", 4 + 3 * c + j, 0) for j in range(3)]))
    outs = run_config(inp, UL, L, units)
    yp = np.empty(inp["x_prompt"].shape, np.float32)
    ys = np.empty(inp["x_sample"].shape, np.float32)
    for c, (link, us) in enumerate(units):
        yT = outs[c]
        for j, (which, si, t0) in enumerate(us):
            blk = yT[:, j * UL:(j + 1) * UL].T
            if which == "p":
                yp[si, t0:t0 + UL, :] = blk
            else:
                ys[si, 0:UL, :] = blk
    return (yp, ys)
```

```python
import os
import numpy as np
from contextlib import ExitStack
import concourse.bass as bass
import concourse.mybir as mybir
from concourse.bass_utils import run_bass_kernel_spmd

F32 = mybir.dt.float32
BF16 = mybir.dt.bfloat16
AF = mybir.ActivationFunctionType
ALU = mybir.AluOpType
AX = mybir.AxisListType

D = 1024
NIN = 6928
DFF = 2816
NEG = -30000.0
EPS = 1e-6
NSLOT = 43
CLS = {
    "INT": (0, [-2, -1, 0, 1, 2]),
    "TOP0": (5, [0, 1, 2, 3]),
    "TOP1": (9, [-1, 0, 1, 2]),
    "BOT1": (13, [-2, -1, 0, 1]),
    "BOT0": (17, [-3, -2, -1, 0]),
    "JA1": (21, [-2, -1, 0, 1, 2]),
    "JA0": (26, [-3, -2, -1, 0, 1, 2]),
    "JB0": (32, [-2, -1, 0, 1, 2, 3]),
    "JB1": (38, [-2, -1, 0, 1, 2]),
}


class Res:
    __slots__ = ("name", "w", "r", "dsem", "multi", "excl")

    def __init__(self, name, multi=False, excl=False):
        self.name = name
        self.multi = multi
        self.excl = excl
        self.w = []
        self.r = {}
        self.dsem = None


class Sem:
    __slots__ = ("h", "total", "is_dma", "name")

    def __init__(self, h, is_dma, name):
        self.h = h
        self.total = 0
        self.is_dma = is_dma
        self.name = name


class Eng:
    def __init__(self, name, h, sem):
        self.name = name
        self.h = h
        self.sem = sem
        self.waited = {}


class FW:
    def __init__(self, nc, stack):
        self.nc = nc
        self.stack = stack
        self.eng = {}
        self.sems = []
        for name, h in (("pe", nc.tensor), ("dve", nc.vector), ("act", nc.scalar),
                        ("pool", nc.gpsimd), ("sp", nc.sync)):
            s = Sem(stack.enter_context(nc.semaphore("s_" + name)), False, name)
            self.sems.append(s)
            self.eng[name] = Eng(name, h, s)
        self.ninst = 0
        self.uid = 0
        self.free_dsems = []
        self.phase_dsems = None

    def sbuf(self, name, shape, dt, stack=None):
        self.uid += 1
        return (stack or self.stack).enter_context(
            self.nc.sbuf_tensor("%s_%d" % (name, self.uid), list(shape), dt))

    def psum(self, name, shape, dt=F32, stack=None):
        self.uid += 1
        esz = 4 if dt == F32 else 2
        n = int(np.prod(shape[1:]))
        be = 2048 // esz
        nb = -(-n // be)
        t = (stack or self.stack).enter_context(
            self.nc.psum_tensor("%s_%d" % (name, self.uid), [128, nb * be], dt))
        v = t[:, 0:n]
        if len(shape) == 3:
            v = v.rearrange("p (a b) -> p a b", b=shape[2])
        return v

    def res(self, name, dma=False, multi=False, excl=False):
        r = Res(name, multi, excl)
        if dma:
            if not self.free_dsems:
                self.uid += 1
                h = self.stack.enter_context(self.nc.semaphore("d_%d" % self.uid))
                sm = Sem(h, True, name)
                self.sems.append(sm)
                self.free_dsems.append(sm)
            r.dsem = self.free_dsems.pop()
            if self.phase_dsems is not None:
                self.phase_dsems.append(r.dsem)
        return r

    def phase_begin(self):
        self.phase_dsems = []

    def phase_end(self):
        try:
            print("sbuf remaining", self.nc.sbuf_bytes_remaining, "ninst", self.ninst, flush=True)
        except Exception as ex:
            print("sbuf remaining ?", ex)
        self.barrier()
        self.free_dsems.extend(self.phase_dsems)
        self.phase_dsems = None

    def _need(self, e, deps):
        best = {}
        for s, v in deps:
            if s is e.sem:
                if e.name == "pe" or e.name == "sp":
                    continue
                if e.sem.total - v >= 2:
                    continue
            if s.is_dma:
                v = s.total
            if v > best.get(s, 0):
                best[s] = v
        for s, v in best.items():
            if e.waited.get(s, 0) >= v:
                continue
            e.h.wait_ge(s.h, v)
            e.waited[s] = v
            self.ninst += 1

    def _collect(self, reads, writes):
        deps = []
        for r in reads:
            deps.extend(r.w)
        for w in writes:
            if not w.multi:
                deps.extend(w.w)
            deps.extend(w.r.items())
        return deps

    def _record(self, sem, reads, writes):
        key = (sem, sem.total)
        for r in reads:
            r.r[sem] = sem.total
        for w in writes:
            if w.multi:
                w.w = [k for k in w.w if k[0] is not sem] + [key]
            else:
                w.w = [key]
                w.r = {}

    def op(self, ename, fn, reads=(), writes=()):
        e = self.eng[ename]
        xr = [r for r in reads if r.excl]
        if xr:
            reads = [r for r in reads if not r.excl]
            writes = list(writes) + xr
        self._need(e, self._collect(reads, writes))
        ins = fn(e.h)
        e.sem.total += 1
        ins.then_inc(e.sem.h, 1)
        self.ninst += 1
        self._record(e.sem, reads, writes)
        return ins

    def dma(self, qname, out, in_, reads=(), writes=(), dres=None, **kw):
        e = self.eng[qname]
        self._need(e, self._collect(reads, writes))
        ds = dres.dsem
        ins = e.h.dma_start(out=out, in_=in_, **kw)
        ds.total += 16
        ins.then_inc(ds.h, 16)
        self.ninst += 1
        self._record(ds, reads, writes)
        return ins

    def barrier(self):
        for e in self.eng.values():
            for s in self.sems:
                if s is e.sem or s.total == 0:
                    continue
                if e.waited.get(s, 0) >= s.total:
                    continue
                e.h.wait_ge(s.h, s.total)
                e.waited[s] = s.total
                self.ninst += 1


class Builder:
    def __init__(self, UL, L, last_is_final=True):
        self.UL = UL
        self.L = L
        self.NT = 3 * UL
        self.NBK = UL // 512
        self.NCH = UL // 128
        self.NB2 = UL // 128
        assert self.NB2 >= 8

    def declare(self, nc):
        L, NT = self.L, self.NT
        di = lambda n, s, dt=F32: nc.dram_tensor(n, list(s), dt, kind="ExternalInput").ap()
        dbg = getattr(self, "debug", False)
        dx = lambda n, s, dt=F32: nc.dram_tensor(n, list(s), dt, kind=("ExternalOutput" if dbg else "Internal")).ap()
        I = {}
        I["xT"] = di("xT", [D, NT])
        I["cT"] = di("cT", [128, 8, 3])
        I["link"] = di("link", [128, 1])
        I["w_ada"] = di("w_ada", [L, D, 6 * D])
        I["b_adaT"] = di("b_adaT", [128, L, 48])
        I["gvec"] = di("gvec", [128, L, 2, 8])
        I["w_in"] = di("w_in", [L, D, NIN])
        I["a_ln"] = di("a_ln", [L, 512])
        I["a_w_spT"] = di("a_w_spT", [L, 128, 4, 128])
        I["a_b_sp"] = di("a_b_sp", [128, L, 4])
        I["rpbt"] = di("rpbt", [L, 128, NSLOT * 4, 128])
        I["c_gate_b"] = di("c_gate_b", [L, 16])
        I["c_hnorm"] = di("c_hnorm", [L, 256])
        I["d_conv_wT"] = di("d_conv_wT", [128, L, 2, 31])
        I["d_vec"] = di("d_vec", [128, L, 3, 2])
        I["w_branch"] = di("w_branch", [L, 1024, D])
        I["w_out"] = di("w_out", [L, D, D])
        I["w_ffn_in"] = di("w_ffn_in", [L, D, 2 * DFF])
        I["w_ffn_out"] = di("w_ffn_out", [L, DFF, D])
        I["g_finalT"] = di("g_finalT", [128, 8])
        I["c_ident"] = di("c_ident", [128, 128])
        I["c_triu"] = di("c_triu", [128, 128])
        I["c_tril"] = di("c_tril", [128, 128])
        I["c_sel"] = di("c_sel", [128, 2, 128])
        self.I = I
        self.yT = nc.dram_tensor("yT", [D, NT], F32, kind="ExternalOutput").ap()
        S = {}
        S["xm"] = dx("xm", [D, NT])
        S["xn"] = dx("xn", [D, NT])
        S["yall"] = dx("yall", [D, NT], BF16)
        S["bq"] = dx("bq", [256, NT], BF16)
        S["bk"] = dx("bk", [256, NT], BF16)
        S["bv"] = dx("bv", [NT, 256], BF16)
        S["cq"] = dx("cq", [256, NT], BF16)
        S["ck"] = dx("ck", [256, NT], BF16)
        S["ckt"] = dx("ckt", [NT, 256], BF16)
        S["cvt"] = dx("cvt", [NT, 256])
        S["co"] = dx("co", [NT, 256])
        S["cg"] = dx("cg", [NT, 16])
        S["chf"] = dx("chf", [NT, 256])
        S["dy"] = dx("dy", [256, NT])
        self.S = S

    def build(self):
        nc = bass.Bass("TRN2", target_bir_lowering=False)
        self.nc = nc
        self.declare(nc)
        with ExitStack() as st:
            fw = FW(nc, st)
            self.fw = fw
            self.R = {k: fw.res(k, multi=True) for k in list(self.S) + ["xT", "yT"]}
            self.setup_consts(st)
            self.ensure_eps()
            self.phase_mod()
            import os
            ph = os.environ.get("KPHASES", "p1,bc,conv,p3a,p3b").split(",")
            for l in range(self.L):
                for p in ("p1", "bc", "conv", "p3a", "p3b"):
                    if p in ph:
                        getattr(self, "phase_" + p)(l)
            fw.barrier()
            self.ninst = fw.ninst
        return nc

    def ld(self, tile_ap, dram_ap, res, dram_res=None, q="sp"):
        reads = [dram_res] if dram_res is not None else []
        self.fw.dma(q, tile_ap, dram_ap, reads=reads, writes=[res], dres=res)

    def stt(self, dram_ap, tile_ap, res, dram_res, q=None):
        import os
        q = q or os.environ.get("KSTQ", "pool")
        self.fw.dma(q, dram_ap, tile_ap, reads=[res], writes=[dram_res], dres=res)

    def setup_consts(self, st):
        fw, I = self.fw, self.I
        L = self.L
        C = {}

        def cload(name, shape, src, dt=F32):
            t = fw.sbuf(name, shape, dt)
            r = fw.res(name, dma=True)
            self.ld(t[:], src, r)
            C[name] = (t, r)
            return t, r

        cload("identf", [128, 128], I["c_ident"])
        cload("triu", [128, 128], I["c_triu"])
        cload("tril", [128, 128], I["c_tril"])
        cload("sel", [128, 2, 128], I["c_sel"])
        cload("link", [128, 1], I["link"])
        cload("cT", [128, 8, 3], I["cT"])
        cload("b_adaT", [128, L, 48], I["b_adaT"])
        cload("gvec", [128, L, 2, 8], I["gvec"])
        cload("a_b_sp", [128, L, 4], I["a_b_sp"])
        cload("d_conv_wT", [128, L, 2, 31], I["d_conv_wT"])
        cload("d_vec", [128, L, 3, 2], I["d_vec"])
        cload("g_finalT", [128, 8], I["g_finalT"])
        identb = fw.sbuf("identb", [128, 128], BF16)
        ridb = fw.res("identb")
        fw.op("dve", lambda e: e.tensor_copy(identb[:], C["identf"][0][:]), reads=[C["identf"][1]], writes=[ridb])
        C["identb"] = (identb, ridb)
        onesf = fw.sbuf("onesf", [128, 128], F32)
        ronesf = fw.res("onesf")
        fw.op("dve", lambda e: e.memset(onesf[:], 1.0), writes=[ronesf])
        C["onesf"] = (onesf, ronesf)
        onesb = fw.sbuf("onesb", [128, 128], BF16)
        ronesb = fw.res("onesb")
        fw.op("dve", lambda e: e.memset(onesb[:], 1.0), writes=[ronesb])
        C["onesb"] = (onesb, ronesb)
        self.C = C
        self.mod = fw.sbuf("mod", [128, L, 6, 8, 3], F32)
        self.rmod = fw.res("mod")
        self.gm = fw.sbuf("gm", [128, L, 2, 8, 3], F32)
        self.rgm = fw.res("gm")

    def phase_mod(self):
        fw, I, C = self.fw, self.I, self.C
        L = self.L
        fw.phase_begin()
        with ExitStack() as ps:
            sc = fw.sbuf("silu_c", [128, 8, 3], F32, ps)
            rsc = fw.res("silu_c")
            cT, rcT = C["cT"]
            fw.op("act", lambda e: e.activation(sc[:], cT[:], AF.Silu), reads=[rcT], writes=[rsc])
            wst = [(fw.sbuf("wada", [128, 8, 1024], F32, ps), fw.res("wada", dma=True)) for _ in range(2)]
            pm = [(fw.psum("pmod", [128, 8, 4], F32, ps), fw.res("pmod", excl=True)) for _ in range(2)]
            it = 0
            badaT, rbada = C["b_adaT"]
            for l in range(L):
                for m in range(6):
                    wt, rw = wst[it % 2]
                    pt, rp = pm[it % 2]
                    it += 1
                    src = I["w_ada"][l, :, m * 1024:(m + 1) * 1024].rearrange("(k p) n -> p k n", p=128)
                    for k2 in range(2):
                        self.ld(wt[:, 4 * k2:4 * k2 + 4, :], src[:, 4 * k2:4 * k2 + 4, :], rw)
                    for f in range(8):
                        for k in range(8):
                            fw.op("pe", lambda e, f=f, k=k: e.matmul(pt[:, f, 0:3], wt[:, k, f * 128:(f + 1) * 128], sc[:, k, :],
                                                                     start=(k == 0), stop=(k == 7)),
                                  reads=[rw, rsc], writes=[rp])
                    for f in range(8):
                        fw.op("dve", lambda e, f=f: e.tensor_scalar(self.mod[:, l, m, f, :], pt[:, f, 0:3],
                                                                    badaT[:, l, m * 8 + f:m * 8 + f + 1], None, ALU.add),
                              reads=[rp, rbada], writes=[self.rmod])
            gvec, rg = C["gvec"]
            for l in range(L):
                for j, m in ((0, 1), (1, 4)):
                    for u in range(3):
                        fw.op("dve", lambda e, l=l, j=j, m=m, u=u: e.scalar_tensor_tensor(
                            self.gm[:, l, j, :, u], self.mod[:, l, m, :, u], 1.0, gvec[:, l, j, :], ALU.add, ALU.mult),
                            reads=[self.rmod, rg], writes=[self.rgm])
        fw.phase_end()

    def load_cast(self, dst, rdst, k0, nk, src_rows, ncols, stg, eng_cycle, CW):
        fw = self.fw
        for k in range(nk):
            for c0 in range(0, ncols, CW):
                cw = min(CW, ncols - c0)
                st_t, st_r = stg[self.stg_i % len(stg)]
                self.stg_i += 1
                self.ld(st_t[:, 0:cw], src_rows[k * 128:(k + 1) * 128, c0:c0 + cw], st_r)
                en = eng_cycle[self.stg_i % len(eng_cycle)]
                if en == "act":
                    fw.op("act", lambda e: e.copy(dst[:, k0 + k, c0:c0 + cw], st_t[:, 0:cw]), reads=[st_r], writes=[rdst])
                else:
                    fw.op(en, lambda e: e.tensor_copy(dst[:, k0 + k, c0:c0 + cw], st_t[:, 0:cw]), reads=[st_r], writes=[rdst])

    def norm_mod(self, xb, rxb, hT, rhT, sq, rsq, pst, rpst, rstd, rrstd, l, j, u, tmp, rtmp):
        self.norm_stats(xb, rxb, sq, rsq, pst, rpst, rstd, rrstd)
        self.norm_apply(xb, rxb, hT, rhT, rstd, rrstd, l, j, u, tmp, rtmp)

    def norm_stats(self, xb, rxb, sq, rsq, pst, rpst, rstd, rrstd):
        fw = self.fw
        onesf, ronesf = self.C["onesb"]
        sqe = getattr(self, "sq_eng", "act")
        for k in range(8):
            sqk, rsqk = sq[k % 2], rsq[k % 2]
            if sqe == "act":
                fw.op("act", lambda e, k=k: e.activation(sqk[:], xb[:, k, :], AF.Square), reads=[rxb], writes=[rsqk])
            else:
                fw.op(sqe, lambda e, k=k: e.tensor_tensor(sqk[:], xb[:, k, :], xb[:, k, :], ALU.mult), reads=[rxb], writes=[rsqk])
            fw.op("pe", lambda e, k=k: e.matmul(pst[:], onesf[:], sqk[:], start=(k == 0), stop=(k == 7)),
                  reads=[rsqk, ronesf], writes=[rpst])
        fw.op("act", lambda e: e.activation(rstd[:], pst[:], AF.Sqrt, scale=1.0 / D, bias=self.eps_ap()),
              reads=[rpst, self.reps], writes=[rrstd])
        fw.op("dve", lambda e: e.reciprocal(rstd[:], rstd[:]), reads=[rrstd], writes=[rrstd])

    def norm_apply(self, xb, rxb, hT, rhT, rstd, rrstd, l, j, u, tmp, rtmp):
        fw = self.fw
        m_sh = 0 if j == 0 else 3
        for k in range(8):
            fw.op("dve", lambda e, k=k: e.tensor_tensor(tmp[:], xb[:, k, :], rstd[:], ALU.mult),
                  reads=[rxb, rrstd], writes=[rtmp])
            fw.op("dve", lambda e, k=k: e.tensor_scalar(hT[:, k, :], tmp[:], self.gm[:, l, j, k, u:u + 1],
                                                        self.mod[:, l, m_sh, k, u:u + 1], ALU.mult, ALU.add),
                  reads=[rtmp, self.rgm, self.rmod], writes=[rhT])

    def eps_ap(self):
        return self.epst[:, 0:1]

    def ensure_eps(self):
        if getattr(self, "epst", None) is None:
            fw = self.fw
            self.epst = fw.sbuf("epst", [128, 2], F32)
            self.reps = fw.res("epst")
            fw.op("dve", lambda e: e.memset(self.epst[:, 0:1], EPS), writes=[self.reps])
            fw.op("dve", lambda e: e.memset(self.epst[:, 1:2], 1.0), writes=[self.reps])

    def phase_p1(self, l):
        self.sq_eng = 'dve'
        fw, I, S, R, C = self.fw, self.I, self.S, self.R, self.C
        self.ensure_eps()
        xsrc, rxsrc = (I["xT"], R["xT"]) if l == 0 else (S["xn"], R["xn"])
        fw.phase_begin()
        with ExitStack() as ps:
            sb = lambda n, s, dt=F32: fw.sbuf(n, s, dt, ps)
            W = sb("w1", [128, 8, 2832], BF16)
            rW = fw.res("w1")
            stg = [(sb("stg", [128, 1416], F32), fw.res("stg", dma=True)) for _ in range(3)]
            self.stg_i = 0
            self.load_cast(W, rW, 0, 8, I["w_in"][l], 2832, stg, ["pool", "dve", "act"], 1416)
            wsp = sb("wsp", [128, 4, 128], BF16)
            rwsp = fw.res("wsp")
            wspf = sb("wspf", [128, 4, 128], F32)
            rwspf = fw.res("wspf", dma=True)
            self.ld(wspf[:], I["a_w_spT"][l], rwspf)
            fw.op("pool", lambda e: e.tensor_copy(wsp[:], wspf[:]), reads=[rwspf], writes=[rwsp])
            aln = sb("aln", [128, 512], F32)
            raln = fw.res("aln", dma=True)
            self.ld(aln[:], I["a_ln"][l].partition_broadcast(128), raln)
            absp, rabsp = C["a_b_sp"]
            identb, ridb = C["identb"]
            xb = [(sb("xb", [128, 8, 512]), fw.res("xb", dma=True)) for _ in range(2)]
            hT = sb("hT", [128, 8, 512], BF16); rhT = fw.res("hT")
            sq = [sb("sq", [128, 512], BF16) for _ in range(2)]; rsq = [fw.res("sq") for _ in range(2)]
            rstd = sb("rstd", [128, 512]); rrstd = fw.res("rstd")
            tmp = sb("tmp", [128, 512]); rtmp = fw.res("tmp")
            pst = fw.psum("pst", [128, 512], F32, ps); rpst = fw.res("pst", excl=True)
            pbig = fw.psum("pbig", [128, 4, 512], F32, ps)
            rbig = [fw.res("pbig%d" % i, excl=True) for i in range(4)]
            pfm = [(pbig[:, 0, :], rbig[0]), (pbig[:, 1, :], rbig[1])]
            ptm = [(pbig[:, 2, :], rbig[2]), (pbig[:, 3, :], rbig[3])]
            psA = fw.psum("psA", [128, 4, 256], F32, ps); rpsA = fw.res("psA", excl=True)
            ptr = fw.psum("ptr", [128, 8, 128], BF16, ps); rptr = fw.res("ptr", excl=True)
            ofm = [(sb("ofm", [128, 2, 512], BF16), fw.res("ofm", dma=True)) for _ in range(3)]
            ofd = [(sb("ofd", [128, 2, 512], F32), fw.res("ofd", dma=True)) for _ in range(2)]
            sgd = sb("sgd", [128, 512]); rsgd = fw.res("sgd")
            otm = [(sb("otm", [128, 4, 256], BF16), fw.res("otm", dma=True)) for _ in range(4)]
            oco = [(sb("oco", [128, 4, 256], F32), fw.res("oco", dma=True)) for _ in range(2)]
            ocv = [(sb("ocv", [128, 4, 256], F32), fw.res("ocv", dma=True)) for _ in range(2)]
            ocg = [(sb("ocg", [128, 4, 16], F32), fw.res("ocg", dma=True)) for _ in range(2)]
            yA = [(sb("yA", [128, 2, 512], BF16), fw.res("yA", dma=True)) for _ in range(2)]
            g1 = sb("g1", [128, 4, 512]); rg1 = fw.res("g1")
            gu = sb("gu", [128, 4, 512]); rgu = fw.res("gu")
            vc = sb("vc", [128, 4, 256]); rvc = fw.res("vc")
            vn = sb("vn", [128, 4, 256], BF16); rvn = fw.res("vn")
            st4 = sb("st4", [128, 16]); rst4 = fw.res("st4")
            yAt = sb("yAt", [128, 4, 256], BF16); ryAt = fw.res("yAt")
            nfm = ntm = nofm = 0
            for u in range(3):
                for b in range(self.NBK):
                    t0 = u * self.UL + b * 512
                    gi = u * self.NBK + b
                    xt, rx = xb[gi % 2]
                    self.ld(xt[:], xsrc[:, t0:t0 + 512].rearrange("(k p) t -> p k t", p=128), rx, rxsrc)
                    import os
                    cut = int(os.environ.get("KCUT", "99"))
                    if cut < 1:
                        continue
                    self.norm_mod(xt, rx, hT, rhT, sq, rsq, pst, rpst, rstd, rrstd, l, 0, u, tmp, rtmp)
                    if cut < 2:
                        continue
                    fm_jobs = [(512, "bq"), (768, "bk"), (1280, "cq"), (1536, "ck")]
                    for col0, name in fm_jobs:
                        ot, ro = ofm[nofm % 3]
                        nofm += 1
                        for c in range(2):
                            pt, rp = pfm[nfm % 2]
                            nfm += 1
                            for k in range(8):
                                fw.op("pe", lambda e, k=k, c=c: e.matmul(pt[:], W[:, k, col0 + c * 128:col0 + (c + 1) * 128], hT[:, k, :],
                                                                          start=(k == 0), stop=(k == 7)),
                                      reads=[rW, rhT], writes=[rp])
                            scale = 0.125 if name in ("bq", "cq") else 1.0
                            fw.op("act", lambda e, c=c: e.activation(ot[:, c, :], pt[:], AF.Identity, scale=scale),
                                  reads=[rp], writes=[ro])
                        self.stt(S[name][:, t0:t0 + 512].rearrange("(c p) t -> p c t", p=128), ot[:], ro, R[name])
                    if cut < 3:
                        continue
                    od, rod = ofd[gi % 2]
                    for c in range(2):
                        pa, rpa = pfm[nfm % 2]
                        nfm += 1
                        pg, rpg = pfm[nfm % 2]
                        nfm += 1
                        for k in range(8):
                            fw.op("pe", lambda e, k=k, c=c: e.matmul(pa[:], W[:, k, 2320 + c * 128:2320 + (c + 1) * 128], hT[:, k, :],
                                                                      start=(k == 0), stop=(k == 7)), reads=[rW, rhT], writes=[rpa])
                        for k in range(8):
                            fw.op("pe", lambda e, k=k, c=c: e.matmul(pg[:], W[:, k, 2576 + c * 128:2576 + (c + 1) * 128], hT[:, k, :],
                                                                      start=(k == 0), stop=(k == 7)), reads=[rW, rhT], writes=[rpg])
                        fw.op("act", lambda e: e.activation(sgd[:], pg[:], AF.Sigmoid), reads=[rpg], writes=[rsgd])
                        fw.op("dve", lambda e, c=c: e.tensor_tensor(od[:, c, :], pa[:], sgd[:], ALU.mult),
                              reads=[rpa, rsgd], writes=[rod])
                    self.stt(S["dy"][:, t0:t0 + 512].rearrange("(c p) t -> p c t", p=128), od[:], rod, R["dy"])
                    if cut < 4:
                        continue
                    obv, robv = otm[(2 * gi) % 4]
                    okt, rokt = otm[(2 * gi + 1) % 4]
                    ovt, rovt = ocv[gi % 2]
                    oo, roo = oco[gi % 2]
                    og, rog = ocg[gi % 2]
                    ya, rya = yA[gi % 2]
                    for ch in range(4):
                        ts_ = slice(ch * 128, (ch + 1) * 128)

                        def tm_mm(col0, ncol):
                            nonlocal ntm
                            pt, rp = ptm[ntm % 2]
                            ntm += 1
                            for k in range(8):
                                fw.op("pe", lambda e, k=k: e.matmul(pt[:, 0:ncol], hT[:, k, ts_], W[:, k, col0:col0 + ncol],
                                                                    start=(k == 0), stop=(k == 7)), reads=[rW, rhT], writes=[rp])
                            return pt, rp
                        skip = os.environ.get("KSKIP", "").split(",")
                        if "tm" in skip:
                            continue
                        if "bv" not in skip:
                            pt, rp = tm_mm(1024, 256)
                            if "bve" not in skip:
                                fw.op("act", lambda e: e.copy(obv[:, ch, :], pt[:, 0:256]), reads=[rp], writes=[robv])
                        if "ckv" not in skip:
                            pt, rp = tm_mm(1536, 512)
                            if "e1" not in skip:
                                fw.op("act", lambda e: e.copy(okt[:, ch, :], pt[:, 0:256]), reads=[rp], writes=[rokt])
                            if "e2" not in skip:
                                fw.op("dve", lambda e: e.tensor_copy(ovt[:, ch, :], pt[:, 256:512]), reads=[rp], writes=[rovt])
                        if "og" in skip:
                            continue
                        pt, rp = tm_mm(2048, 272)
                        fw.op("act", lambda e: e.activation(oo[:, ch, :], pt[:, 0:256], AF.Sigmoid), reads=[rp], writes=[roo])
                        fw.op("dve", lambda e: e.tensor_copy(og[:, ch, :], pt[:, 256:272]), reads=[rp], writes=[rog])
                    for ch in range(4):
                        for k in range(8):
                            fw.op("pe", lambda e, k=k, ch=ch: e.matmul(pbig[:, ch, :], hT[:, k, ch * 128:(ch + 1) * 128], W[:, k, 0:512],
                                                                        start=(k == 0), stop=(k == 7)), reads=[rW, rhT], writes=[rbig[ch]])
                    R4 = list(rbig)
                    fw.op("act", lambda e: e.activation(g1[:], pbig[:], AF.Square), reads=R4, writes=[rg1])
                    fw.op("dve", lambda e: e.tensor_scalar(g1[:], g1[:], 0.044715, 1.0, ALU.mult, ALU.add), reads=[rg1], writes=[rg1])
                    fw.op("dve", lambda e: e.tensor_tensor(g1[:], g1[:], pbig[:], ALU.mult), reads=[rg1] + R4, writes=[rg1])
                    fw.op("act", lambda e: e.activation(g1[:], g1[:], AF.Sigmoid, scale=1.5957691216), reads=[rg1], writes=[rg1])
                    fw.op("dve", lambda e: e.tensor_tensor(gu[:], g1[:], pbig[:], ALU.mult), reads=[rg1] + R4, writes=[rgu])
                    guv = gu[:, :, 256:512]
                    g1v = g1[:, :, 0:256]
                    bc4 = lambda ap: ap.unsqueeze(2).to_broadcast([128, 4, 256])
                    fw.op("dve", lambda e: e.reduce_sum(st4[:, 0:4], guv, AX.X), reads=[rgu], writes=[rst4])
                    fw.op("dve", lambda e: e.tensor_scalar(st4[:, 4:8], st4[:, 0:4], 1.0 / 256, None, ALU.mult), reads=[rst4], writes=[rst4])
                    fw.op("dve", lambda e: e.tensor_tensor(vc[:], guv, bc4(st4[:, 4:8]), ALU.subtract), reads=[rgu, rst4], writes=[rvc])
                    fw.op("dve", lambda e: e.tensor_tensor(g1v, vc[:], vc[:], ALU.mult), reads=[rvc], writes=[rg1])
                    fw.op("dve", lambda e: e.reduce_sum(st4[:, 8:12], g1v, AX.X), reads=[rg1], writes=[rst4])
                    fw.op("act", lambda e: e.activation(st4[:, 12:16], st4[:, 8:12], AF.Sqrt, scale=1.0 / 256, bias=self.eps_ap()),
                          reads=[rst4, self.reps], writes=[rst4])
                    fw.op("dve", lambda e: e.reciprocal(st4[:, 12:16], st4[:, 12:16]), reads=[rst4], writes=[rst4])
                    fw.op("dve", lambda e: e.tensor_tensor(vc[:], vc[:], bc4(st4[:, 12:16]), ALU.mult), reads=[rvc, rst4], writes=[rvc])
                    fw.op("dve", lambda e: e.tensor_tensor(vc[:], vc[:], aln[:, 0:256].unsqueeze(1).to_broadcast([128, 4, 256]), ALU.mult),
                          reads=[rvc, raln], writes=[rvc])
                    fw.op("dve", lambda e: e.tensor_tensor(vn[:], vc[:], aln[:, 256:512].unsqueeze(1).to_broadcast([128, 4, 256]), ALU.add),
                          reads=[rvc, raln], writes=[rvn])
                    for ch in range(4):
                        for g in range(4):
                            fw.op("pe", lambda e, g=g, ch=ch: e.matmul(psA[:, ch, g * 64:(g + 1) * 64], wsp[:, g, :], vn[:, ch, g * 64:(g + 1) * 64],
                                                                        start=True, stop=True), reads=[rwsp, rvn], writes=[rpsA])
                    for g in range(4):
                        fw.op("dve", lambda e, g=g: e.scalar_tensor_tensor(yAt[:, :, g * 64:(g + 1) * 64], psA[:, :, g * 64:(g + 1) * 64],
                                                                           absp[:, l, g:g + 1], gu[:, :, g * 64:(g + 1) * 64],
                                                                           ALU.add, ALU.mult),
                              reads=[rpsA, rabsp, rgu], writes=[ryAt])
                    for ch in range(4):
                        for c in range(2):
                            fw.op("pe", lambda e, c=c, ch=ch: e.transpose(ptr[:, ch * 2 + c, :], yAt[:, ch, c * 128:(c + 1) * 128], identb[:]),
                                  reads=[ryAt, ridb], writes=[rptr])
                    fw.op("act", lambda e: e.copy(ya[:].rearrange("p c (h t) -> p h c t", t=128),
                                                  ptr[:].rearrange("p (h c) t -> p h c t", c=2)), reads=[rptr], writes=[rya])
                    tsl = slice(t0, t0 + 512)
                    if "st" in os.environ.get("KSKIP", "").split(","):
                        continue
                    self.stt(S["bv"][tsl, :].rearrange("(c p) f -> p c f", p=128), obv[:], robv, R["bv"])
                    self.stt(S["ckt"][tsl, :].rearrange("(c p) f -> p c f", p=128), okt[:], rokt, R["ckt"])
                    self.stt(S["cvt"][tsl, :].rearrange("(c p) f -> p c f", p=128), ovt[:], rovt, R["cvt"])
                    self.stt(S["co"][tsl, :].rearrange("(c p) f -> p c f", p=128), oo[:], roo, R["co"])
                    self.stt(S["cg"][tsl, :].rearrange("(c p) f -> p c f", p=128), og[:], rog, R["cg"])
                    self.stt(S["yall"][0:256, tsl].rearrange("(c p) t -> p c t", p=128), ya[:], rya, R["yall"])
            fw.phase_end()

    def attn_plan(self):
        NB2 = self.NB2
        plan = []
        for u in range(3):
            for b in range(NB2):
                if u == 2 or (u == 0 and b < NB2 - 2) or (u == 1 and b >= 2):
                    if b == 0 and u != 1:
                        cls = "TOP0"
                    elif b == 1 and u != 1:
                        cls = "TOP1"
                    elif b == NB2 - 2 and u != 0:
                        cls = "BOT1"
                    elif b == NB2 - 1 and u != 0:
                        cls = "BOT0"
                    else:
                        cls = "INT"
                elif u == 0:
                    cls = "JA1" if b == NB2 - 2 else "JA0"
                else:
                    cls = "JB0" if b == 0 else "JB1"
                s0, offs = CLS[cls]
                ents = []
                for i, o in enumerate(offs):
                    kp = b + o
                    ku = u
                    if kp >= NB2:
                        ku, kp = u + 1, kp - NB2
                    elif kp < 0:
                        ku, kp = u - 1, kp + NB2
                    assert 0 <= ku <= 2 and (ku == u or (u, ku) in ((0, 1), (1, 0)))
                    ents.append((ku, kp, s0 + i))
                plan.append((u, b, ents))
        return plan

    def gen_attn(self, l, ptr, rptr, ps):
        fw, I, S, R, C = self.fw, self.I, self.S, self.R, self.C
        UL, NB2 = self.UL, self.NB2
        identb, ridb = C["identb"]
        if True:
            sb = lambda n, s, dt=F32: fw.sbuf(n, s, dt, ps)
            bt = sb("bt", [128, NSLOT * 4, 128], BF16); rbt = fw.res("bt")
            stg = [(sb("bstg", [128, 8, 128], F32), fw.res("bstg", dma=True)) for _ in range(2)]
            n = 0
            for s0 in range(0, NSLOT * 4, 8):
                s1 = min(s0 + 8, NSLOT * 4)
                stt_, rs = stg[n % 2]
                n += 1
                self.ld(stt_[:, 0:s1 - s0, :], I["rpbt"][l, :, s0:s1, :], rs)
                fw.op("pool", lambda e, s0=s0, s1=s1, stt_=stt_: e.tensor_copy(bt[:, s0:s1, :], stt_[:, 0:s1 - s0, :]),
                      reads=[rs], writes=[rbt])
            kT = sb("kT", [128, 2, 2 * UL], BF16); rkT = fw.res("kT", dma=True)
            V = sb("V", [128, 2 * self.NCH, 4, 65], BF16); rV = fw.res("V", dma=True)
            fw.op("pool", lambda e: e.memset(V[:], 1.0), writes=[rV])
            qT = [(sb("qT", [128, 2, 512], BF16), fw.res("qT", dma=True)) for _ in range(2)]
            pS = [(fw.psum("pS", [128, 8, 128], F32, ps), fw.res("pS", excl=True)) for _ in range(1)]
            pO = [(fw.psum("pO", [128, 4, 65], F32, ps), fw.res("pO", excl=True)) for _ in range(1)]
            PT = [(sb("PT", [128, 8, 128], BF16), fw.res("PT")) for _ in range(3)]
            rc = sb("rc", [128, 4]); rrc = fw.res("rc")
            yB = sb("yB", [128, 4, 64], BF16); ryB = fw.res("yB")
            yo = [(sb("yBo", [128, 2, 512], BF16), fw.res("yBo", dma=True)) for _ in range(2)]
            plan = self.attn_plan()
            nps = npt = nq = 0
            for grp in (0, 2):
                units = (0, 1) if grp == 0 else (2,)
                base = grp * UL
                ntok = len(units) * UL
                self.ld(kT[:, :, 0:ntok], S["bk"][:, base:base + ntok].rearrange("(c p) t -> p c t", p=128), rkT, R["bk"])
                for h in range(4):
                    self.ld(V[:, 0:ntok // 128, h, 0:64],
                            S["bv"][base:base + ntok, h * 64:(h + 1) * 64].rearrange("(c p) d -> p c d", p=128),
                            rV, R["bv"])
                for (u, b, ents) in plan:
                    if u not in units:
                        continue
                    if b % 4 == 0:
                        qt, rq = qT[nq % 2]
                        yot, ryo = yo[nq % 2]
                        nq += 1
                        tq0 = u * UL + b * 128
                        self.ld(qt[:], S["bq"][:, tq0:tq0 + 512].rearrange("(c p) t -> p c t", p=128), rq, R["bq"])
                    qs = slice((b % 4) * 128, (b % 4 + 1) * 128)
                    po, rpo = pO[0]
                    ne = len(ents)
                    for h in range(4):
                        hp = slice((h % 2) * 64, (h % 2) * 64 + 64)
                        hc = h // 2
                        pst_, rps = pS[0]
                        nps += 1
                        for i, (ku, kp, slot) in enumerate(ents):
                            k0 = (ku * UL - base) + kp * 128
                            fw.op("pe", lambda e, i=i, k0=k0: e.matmul(pst_[:, i, :], kT[hp, hc, k0:k0 + 128], qt[hp, hc, qs],
                                                                        start=True, stop=False), reads=[rkT, rq], writes=[rps])
                            fw.op("pe", lambda e, i=i, slot=slot: e.matmul(pst_[:, i, :], identb[:], bt[:, slot * 4 + h, :],
                                                                            start=False, stop=True), reads=[ridb, rbt], writes=[rps])
                        pt_, rpt = PT[npt % 3]
                        npt += 1
                        n1 = min(ne, 4)
                        fw.op("act", lambda e: e.activation(pt_[:, 0:n1, :], pst_[:, 0:n1, :], AF.Exp), reads=[rps], writes=[rpt])
                        if ne > 4:
                            fw.op("act", lambda e: e.activation(pt_[:, 4:ne, :], pst_[:, 4:ne, :], AF.Exp), reads=[rps], writes=[rpt])
                        for i, (ku, kp, slot) in enumerate(ents):
                            vc_ = (ku * UL - base) // 128 + kp
                            fw.op("pe", lambda e, i=i, vc_=vc_: e.matmul(po[:, h, :], pt_[:, i, :], V[:, vc_, h, :],
                                                                          start=(i == 0), stop=(i == ne - 1)),
                                  reads=[rpt, rV], writes=[rpo])
                    fw.op("dve", lambda e: e.reciprocal(rc[:], po[:, :, 64]), reads=[rpo], writes=[rrc])
                    fw.op("dve", lambda e: e.tensor_tensor(yB[:], po[:, :, 0:64], rc[:].unsqueeze(2).to_broadcast([128, 4, 64]), ALU.mult),
                          reads=[rpo, rrc], writes=[ryB])
                    for c in range(2):
                        fw.op("pe", lambda e, c=c: e.transpose(ptr[:, c, :], yB[:, 2 * c:2 * c + 2, :], identb[:]),
                              reads=[ryB, ridb], writes=[rptr])
                    fw.op("act", lambda e: e.copy(yot[:, :, qs], ptr[:]), reads=[rptr], writes=[ryo])
                    if b % 4 == 3:
                        self.stt(S["yall"][256:512, tq0:tq0 + 512].rearrange("(c p) t -> p c t", p=128), yot[:], ryo, R["yall"])
                    yield

    def gen_mlstm(self, l, ptr, rptr, ps):
        fw, I, S, R, C = self.fw, self.I, self.S, self.R, self.C
        UL, NCH = self.UL, self.NCH
        identb, ridb = C["identb"]
        onesf, ronesf = C["onesf"]
        triu, rtriu = C["triu"]
        tril, rtril = C["tril"]
        sel, rsel = C["sel"]
        link, rlink = C["link"]
        if True:
            sb = lambda n, s, dt=F32: fw.sbuf(n, s, dt, ps)
            gb = sb("gb", [128, 16]); rgb = fw.res("gb", dma=True)
            self.ld(gb[:], I["c_gate_b"][l].partition_broadcast(128), rgb)
            hng = sb("hng", [128, 256]); rhng = fw.res("hng", dma=True)
            self.ld(hng[:], I["c_hnorm"][l].partition_broadcast(128), rhng)
            NBUF = 2
            qTb = [(sb("cqT", [128, 2, 512], BF16), fw.res("cqT", dma=True)) for _ in range(NBUF)]
            kTb = [(sb("ckT", [128, 2, 512], BF16), fw.res("ckT", dma=True)) for _ in range(NBUF)]
            ktb = [(sb("ckt", [128, 4, 256], BF16), fw.res("ckt", dma=True)) for _ in range(NBUF)]
            vtb = [(sb("cvt", [128, 4, 4, 65], F32), fw.res("cvt", dma=True)) for _ in range(NBUF)]
            for vt_, rv_ in vtb:
                fw.op("pool", lambda e, vt_=vt_: e.memset(vt_[:], 1.0), writes=[rv_])
            gtb = [(sb("cgt", [128, 4, 16]), fw.res("cgt", dma=True)) for _ in range(NBUF)]
            otb = [(sb("cot", [128, 4, 256]), fw.res("cot", dma=True)) for _ in range(NBUF)]
            hfb = [(sb("chf", [128, 4, 256]), fw.res("chf", dma=True)) for _ in range(NBUF)]
            yCo = [(sb("yCo", [128, 2, 512], BF16), fw.res("yCo", dma=True)) for _ in range(2)]
            CT = sb("CT", [128, 2, 65]); rCT = fw.res("CT")
            CTb = sb("CTb", [128, 2, 65], BF16); rCTb = fw.res("CTb")
            klo = sb("klo", [128, 2, 128], BF16); rklo = fw.res("klo")
            khi = sb("khi", [128, 2, 128], BF16); rkhi = fw.res("khi")
            fw.op("pool", lambda e: e.memset(klo[:], 0.0), writes=[rklo])
            fw.op("pool", lambda e: e.memset(khi[:], 0.0), writes=[rkhi])
            g = sb("g", [128, 8]); rg = fw.res("g")
            lf = sb("lf", [128, 4]); rlf = fw.res("lf")
            ex = sb("ex", [128, 12]); rex = fw.res("ex")
            Fm = sb("Fm", [128, 2]); rFm = fw.res("Fm")
            gsb = sb("gsb", [128, 8]); rgsb = fw.res("gsb")
            wm = sb("wm", [128, 4, 128], BF16); rwm = fw.res("wm")
            va = sb("va", [128, 4, 65], BF16); rva = fw.res("va")
            tn = sb("tn", [128, 4, 65]); rtn = fw.res("tn")
            dd = sb("dd", [128, 4]); rdd = fw.res("dd")
            hd = sb("hd", [128, 256]); rhd = fw.res("hd")
            sqh = sb("sqh", [128, 256]); rsqh = fw.res("sqh")
            ss = sb("ss", [128, 4]); rss = fw.res("ss")
            yCt = sb("yCt", [128, 256], BF16); ryCt = fw.res("yCt")
            pGC = fw.psum("pGC", [128, 512], F32, ps); rpG = fw.res("pGC", excl=True)
            pG = pGC[:, 0:16]
            pC = pGC[:, 32:162].rearrange("p (a b) -> p a b", b=65); rpC = rpG
            pQK = fw.psum("pQK", [128, 2, 128], F32, ps); rpQK = fw.res("pQK", excl=True)
            pQK2 = fw.psum("pQK2", [128, 2, 128], F32, ps); rpQK2 = fw.res("pQK2", excl=True)
            pN = fw.psum("pN", [128, 4, 65], F32, ps); rpN = fw.res("pN", excl=True)
            nld = 0
            for d in (0, 1):
                go = 8 * d
                tri, rtri = (triu, rtriu) if d == 0 else (tril, rtril)
                unit_order = (0, 1, 2) if d == 0 else (1, 0, 2)
                for ui, u in enumerate(unit_order):
                    if ui == 1:
                        fw.op("dve", lambda e: e.tensor_scalar(CT[:], CT[:], link[:, 0:1], None, ALU.mult),
                              reads=[rCT, rlink], writes=[rCT])
                    else:
                        fw.op("dve", lambda e: e.memset(CT[:], 0.0), writes=[rCT])
                    blocks = range(self.NBK) if d == 0 else range(self.NBK - 1, -1, -1)
                    for b in blocks:
                        t0 = u * UL + b * 512
                        tsl = slice(t0, t0 + 512)
                        bi = nld % NBUF
                        nld += 1
                        qt, rq = qTb[bi]; kt, rk = kTb[bi]; ktm, rktm = ktb[bi]; vtm, rvtm = vtb[bi]
                        gt, rgt = gtb[bi]; ot, rot = otb[bi]; hf, rhf = hfb[bi]
                        self.ld(qt[:], S["cq"][:, tsl].rearrange("(c p) t -> p c t", p=128), rq, R["cq"])
                        self.ld(kt[:], S["ck"][:, tsl].rearrange("(c p) t -> p c t", p=128), rk, R["ck"])
                        self.ld(ktm[:], S["ckt"][tsl, :].rearrange("(c p) f -> p c f", p=128), rktm, R["ckt"])
                        for h in range(4):
                            self.ld(vtm[:, :, h, 0:64], S["cvt"][tsl, h * 64:(h + 1) * 64].rearrange("(c p) d -> p c d", p=128), rvtm, R["cvt"])
                        self.ld(gt[:], S["cg"][tsl, :].rearrange("(c p) f -> p c f", p=128), rgt, R["cg"])
                        if d == 1:
                            self.ld(ot[:], S["co"][tsl, :].rearrange("(c p) f -> p c f", p=128), rot, R["co"])
                            self.ld(hf[:], S["chf"][tsl, :].rearrange("(c p) f -> p c f", p=128), rhf, R["chf"])
                            yco, ryco = yCo[nld % 2]
                        chunks = range(4) if d == 0 else range(3, -1, -1)
                        for ch in chunks:
                            cs = slice(ch * 128, (ch + 1) * 128)
                            fw.op("dve", lambda e: e.tensor_tensor(g[:], gt[:, ch, go:go + 8], gb[:, go:go + 8], ALU.add),
                                  reads=[rgt, rgb], writes=[rg])
                            if int(os.environ.get('KCUTM', '99')) < 1:
                                continue
                            fw.op("act", lambda e: e.activation(lf[:], g[:, 4:8], AF.Exp, scale=-1.0), reads=[rg], writes=[rlf])
                            fw.op("act", lambda e: e.activation(lf[:], lf[:], AF.Ln, scale=1.0, bias=self.epst[:, 1:2]),
                                  reads=[rlf, self.reps], writes=[rlf])
                            fw.op("dve", lambda e: e.tensor_scalar(lf[:], lf[:], -1.0, None, ALU.mult), reads=[rlf], writes=[rlf])
                            if int(os.environ.get('KCUTM', '99')) < 2:
                                continue
                            fw.op("pe", lambda e: e.matmul(pG[:, 0:4], tri[:], lf[:], start=True, stop=True), reads=[rtri, rlf], writes=[rpG])
                            fw.op("pe", lambda e: e.matmul(pG[:, 4:8], onesf[:], lf[:], start=True, stop=True), reads=[ronesf, rlf], writes=[rpG])
                            fw.op("pe", lambda e: e.matmul(pG[:, 8:10], sel[:, 0, :], lf[:, 0:4:2], start=True, stop=False),
                                  reads=[rsel, rlf], writes=[rpG])
                            fw.op("pe", lambda e: e.matmul(pG[:, 8:10], sel[:, 1, :], lf[:, 1:4:2], start=False, stop=True),
                                  reads=[rsel, rlf], writes=[rpG])
                            if int(os.environ.get('KCUTM', '99')) < 3:
                                continue
                            fw.op("act", lambda e: e.copy(gsb[:], pG[:, 0:8]), reads=[rpG], writes=[rgsb])
                            fw.op("dve", lambda e: e.tensor_tensor(ex[:, 4:8], gsb[:, 0:4], gsb[:, 4:8], ALU.subtract), reads=[rgsb], writes=[rex])
                            fw.op("dve", lambda e: e.tensor_tensor(ex[:, 0:4], g[:, 0:4], ex[:, 4:8], ALU.subtract), reads=[rg, rex], writes=[rex])
                            fw.op("act", lambda e: e.activation(ex[:, 0:8], ex[:, 0:8], AF.Exp), reads=[rex], writes=[rex])
                            fw.op("act", lambda e: e.activation(Fm[:], pG[:, 8:10], AF.Exp), reads=[rpG], writes=[rFm])
                            if int(os.environ.get('KCUTM', '99')) < 4:
                                continue
                            fw.op("dve", lambda e: e.tensor_tensor(va[:], vtm[:, ch, :, :],
                                                                   ex[:, 0:4].unsqueeze(2).to_broadcast([128, 4, 65]), ALU.mult),
                                  reads=[rvtm, rex], writes=[rva])
                            if int(os.environ.get('KCUTM', '99')) < 5:
                                continue
                            for h in range(4):
                                hp = slice((h % 2) * 64, (h % 2) * 64 + 64)
                                pq_, rpq_ = (pQK, rpQK) if h % 2 == 0 else (pQK2, rpQK2)
                                fw.op("pe", lambda e, h=h, hp=hp: e.matmul(pq_[:, h // 2, :], kt[hp, h // 2, cs], qt[hp, h // 2, cs],
                                                                            start=True, stop=True), reads=[rk, rq], writes=[rpq_])
                            wm4 = wm[:].rearrange("p (j q) t -> p j q t", q=2)
                            fw.op("dve", lambda e: e.tensor_tensor(wm4[:, :, 0, :], pQK[:], tri[:].unsqueeze(1).to_broadcast([128, 2, 128]), ALU.mult),
                                  reads=[rpQK, rtri], writes=[rwm])
                            fw.op("dve", lambda e: e.tensor_tensor(wm4[:, :, 1, :], pQK2[:], tri[:].unsqueeze(1).to_broadcast([128, 2, 128]), ALU.mult),
                                  reads=[rpQK2, rtri], writes=[rwm])
                            if int(os.environ.get('KCUTM', '99')) < 6:
                                continue
                            fw.op("dve", lambda e: e.tensor_tensor(CTb[:], CT[:], Fm[:].unsqueeze(2).to_broadcast([128, 2, 65]), ALU.mult),
                                  reads=[rCT, rFm], writes=[rCTb])
                            for h in range(4):
                                hp = slice((h % 2) * 64, (h % 2) * 64 + 64)
                                fw.op("pe", lambda e, h=h: e.matmul(pN[:, h, :], wm[:, h, :], va[:, h, :], start=True, stop=False),
                                      reads=[rwm, rva], writes=[rpN])
                                fw.op("pe", lambda e, h=h, hp=hp: e.matmul(pN[:, h, :], qt[hp, h // 2, cs], CTb[hp, h // 2, :], start=False, stop=True),
                                      reads=[rq, rCTb], writes=[rpN])
                            if int(os.environ.get('KCUTM', '99')) < 7:
                                continue
                            fw.op("pool", lambda e: e.tensor_copy(klo[:, :, 0:64], ktm[:, ch, :].rearrange("p (a b) -> p a b", b=128)[:, :, 0:64]),
                                  reads=[rktm], writes=[rklo])
                            fw.op("pool", lambda e: e.tensor_copy(khi[:, :, 64:128], ktm[:, ch, :].rearrange("p (a b) -> p a b", b=128)[:, :, 64:128]),
                                  reads=[rktm], writes=[rkhi])
                            for p in range(2):
                                fw.op("pe", lambda e, p=p: e.matmul(pC[:, p, :], klo[:, p, :], va[:, 2 * p, :], start=True, stop=False),
                                      reads=[rklo, rva], writes=[rpC])
                                fw.op("pe", lambda e, p=p: e.matmul(pC[:, p, :], khi[:, p, :], va[:, 2 * p + 1, :], start=False, stop=True),
                                      reads=[rkhi, rva], writes=[rpC])
                            for p in range(2):
                                fw.op("dve", lambda e, p=p: e.scalar_tensor_tensor(CT[:, p, :], CT[:, p, :], Fm[:, p:p + 1], pC[:, p, :],
                                                                                   ALU.mult, ALU.add),
                                      reads=[rCT, rFm, rpC], writes=[rCT])
                            if int(os.environ.get('KCUTM', '99')) < 8:
                                continue
                            fw.op("dve", lambda e: e.tensor_tensor(tn[:], pN[:], ex[:, 4:8].unsqueeze(2).to_broadcast([128, 4, 65]), ALU.mult),
                                  reads=[rpN, rex], writes=[rtn])
                            fw.op("dve", lambda e: e.scalar_tensor_tensor(dd[:], tn[:, :, 64], -1.0, tn[:, :, 64], ALU.mult, ALU.max), reads=[rtn], writes=[rdd])
                            fw.op("dve", lambda e: e.tensor_scalar(dd[:], dd[:], 1.0, None, ALU.max), reads=[rdd], writes=[rdd])
                            fw.op("dve", lambda e: e.reciprocal(dd[:], dd[:]), reads=[rdd], writes=[rdd])
                            if d == 0:
                                fw.op("dve", lambda e: e.tensor_tensor(hf[:, ch, :].rearrange("p (h d) -> p h d", d=64), tn[:, :, 0:64],
                                                                       dd[:].unsqueeze(2).to_broadcast([128, 4, 64]), ALU.mult),
                                      reads=[rtn, rdd], writes=[rhf])
                            else:
                                hd3 = hd[:].rearrange("p (h d) -> p h d", d=64)
                                fw.op("dve", lambda e: e.tensor_tensor(hd3, tn[:, :, 0:64],
                                                                       dd[:].unsqueeze(2).to_broadcast([128, 4, 64]), ALU.mult),
                                      reads=[rtn, rdd], writes=[rhd])
                                fw.op("dve", lambda e: e.tensor_tensor(hd[:], hd[:], hf[:, ch, :], ALU.add), reads=[rhd, rhf], writes=[rhd])
                                fw.op("dve", lambda e: e.tensor_tensor(sqh[:], hd[:], hd[:], ALU.mult), reads=[rhd], writes=[rsqh])
                                fw.op("dve", lambda e: e.reduce_sum(ss[:], sqh[:].rearrange("p (h d) -> p h d", d=64), AX.X), reads=[rsqh], writes=[rss])
                                fw.op("act", lambda e: e.activation(ss[:], ss[:], AF.Ln, scale=1.0 / 64, bias=self.eps_ap()),
                                      reads=[rss, self.reps], writes=[rss])
                                fw.op("act", lambda e: e.activation(ss[:], ss[:], AF.Exp, scale=-0.5), reads=[rss], writes=[rss])
                                fw.op("dve", lambda e: e.tensor_tensor(hd3, hd3, ss[:].unsqueeze(2).to_broadcast([128, 4, 64]), ALU.mult),
                                      reads=[rhd, rss], writes=[rhd])
                                fw.op("dve", lambda e: e.tensor_tensor(hd[:], hd[:], hng[:], ALU.mult), reads=[rhd, rhng], writes=[rhd])
                                fw.op("dve", lambda e: e.tensor_tensor(yCt[:], hd[:], ot[:, ch, :], ALU.mult), reads=[rhd, rot], writes=[ryCt])
                                for c in range(2):
                                    fw.op("pe", lambda e, c=c: e.transpose(ptr[:, c, :], yCt[:, c * 128:(c + 1) * 128], identb[:]),
                                          reads=[ryCt, ridb], writes=[rptr])
                                fw.op("act", lambda e: e.copy(yco[:, :, cs], ptr[:]), reads=[rptr], writes=[ryco])
                            yield
                        if d == 0:
                            self.stt(S["chf"][tsl, :].rearrange("(c p) f -> p c f", p=128), hf[:], rhf, R["chf"])
                        else:
                            self.stt(S["yall"][512:768, tsl].rearrange("(c p) t -> p c t", p=128), yco[:], ryco, R["yall"])

    def phase_bc(self, l):
        fw = self.fw
        fw.phase_begin()
        with ExitStack() as ps:
            ptr = fw.psum("ptrbc", [128, 2, 128], BF16, ps)
            rptr = fw.res("ptrbc", excl=True)
            ga = self.gen_attn(l, ptr, rptr, ps)
            gm = self.gen_mlstm(l, ptr, rptr, ps)
            live = [[ga, 1], [gm, 2]]
            while live:
                for ent in list(live):
                    g, r = ent
                    try:
                        for _ in range(r):
                            next(g)
                    except StopIteration:
                        live.remove(ent)
            fw.phase_end()

    def phase_conv(self, l):
        fw, I, S, R, C = self.fw, self.I, self.S, self.R, self.C
        UL = self.UL
        onesb, ronesb = C["onesb"]
        link, rlink = C["link"]
        cw, rcw = C["d_conv_wT"]
        dv, rdv = C["d_vec"]
        fw.phase_begin()
        with ExitStack() as ps:
            sb = lambda n, s, dt=F32: fw.sbuf(n, s, dt, ps)
            yp = [(sb("yp", [128, 2, UL + 30]), fw.res("yp", dma=True)) for _ in range(2)]
            acc = sb("acc", [128, 2, 512]); racc2 = [fw.res("acc0"), fw.res("acc1")]
            identf, ridf = C["identf"]
            dg = sb("dg", [128, 2, 31, 128], BF16); rdg = fw.res("dg")
            for cc in range(2):
                for k in range(31):
                    en = "dve" if (k % 2 == 0) else "pool"
                    fw.op(en, lambda e, cc=cc, k=k: e.tensor_scalar(dg[:, cc, k, :], identf[:], cw[:, l, cc, k:k + 1], None, ALU.mult),
                          reads=[ridf, rcw], writes=[rdg])
            ypb = [(sb("ypb", [128, 2, UL + 30], BF16), fw.res("ypb")) for _ in range(2)]
            pcv = [(fw.psum("pcv", [128, 512], F32, ps), fw.res("pcv", excl=True)) for _ in range(2)]
            accb = sb("accb", [128, 2, 512], BF16); raccb = fw.res("accb")
            sqb = sb("sqb", [128, 2, 512], BF16); rsqb = fw.res("sqb")
            m2 = sb("m2", [128, 512]); rm2 = fw.res("m2")
            rs = sb("rs", [128, 512]); rrs = fw.res("rs")
            tt = sb("tt", [128, 512]); rtt = fw.res("tt")
            yo = [(sb("yDo", [128, 2, 512], BF16), fw.res("yDo", dma=True)) for _ in range(2)]
            pM = fw.psum("pM", [128, 512], F32, ps); rpM = fw.res("pM", excl=True)
            pQ = fw.psum("pQ", [128, 512], F32, ps); rpQ = fw.res("pQ", excl=True)
            no = 0
            for u in range(3):
                ypt, ryp = yp[u % 2]
                fw.op("pool", lambda e: e.memset(ypt[:, :, 0:15], 0.0), writes=[ryp])
                fw.op("pool", lambda e: e.memset(ypt[:, :, UL + 15:UL + 30], 0.0), writes=[ryp])
                src = S["dy"].rearrange("(c p) t -> p c t", p=128)
                self.ld(ypt[:, :, 15:15 + UL], src[:, :, u * UL:(u + 1) * UL], ryp, R["dy"])
                if u == 0:
                    self.ld(ypt[:, :, UL + 15:UL + 30], src[:, :, UL:UL + 15], ryp, R["dy"])
                    fw.op("pool", lambda e: e.tensor_scalar(ypt[:, :, UL + 15:UL + 30], ypt[:, :, UL + 15:UL + 30], link[:, 0:1], None, ALU.mult),
                          reads=[ryp, rlink], writes=[ryp])
                elif u == 1:
                    self.ld(ypt[:, :, 0:15], src[:, :, UL - 15:UL], ryp, R["dy"])
                    fw.op("pool", lambda e: e.tensor_scalar(ypt[:, :, 0:15], ypt[:, :, 0:15], link[:, 0:1], None, ALU.mult),
                          reads=[ryp, rlink], writes=[ryp])
                ypbt, rypb = ypb[u % 2]
                fw.op("act", lambda e: e.copy(ypbt[:, 0, :], ypt[:, 0, :]), reads=[ryp], writes=[rypb])
                fw.op("pool", lambda e: e.tensor_copy(ypbt[:, 1, :], ypt[:, 1, :]), reads=[ryp], writes=[rypb])
                for b in range(self.NBK):
                    t0 = b * 512
                    for c in range(2):
                        pct, rpc = pcv[c]
                        for k in range(31):
                            fw.op("pe", lambda e, c=c, k=k: e.matmul(pct[:], dg[:, c, k, :], ypbt[:, c, t0 + k:t0 + k + 512],
                                                                      start=(k == 0), stop=(k == 30)), reads=[rdg, rypb], writes=[rpc])
                        fw.op("act", lambda e, c=c: e.activation(acc[:, c, :], pct[:], AF.Identity, scale=1.0, bias=dv[:, l, 0, c:c + 1]),
                              reads=[rpc, rdv], writes=[racc2[c]])
                    for c in range(2):
                        fw.op("act", lambda e, c=c: e.activation(sqb[:, c, :], acc[:, c, :], AF.Square), reads=[racc2[c]], writes=[rsqb])
                        fw.op("pool", lambda e, c=c: e.tensor_copy(accb[:, c, :], acc[:, c, :]), reads=[racc2[c]], writes=[raccb])
                    for c in range(2):
                        fw.op("pe", lambda e, c=c: e.matmul(pM[:], onesb[:], accb[:, c, :], start=(c == 0), stop=(c == 1)),
                              reads=[ronesb, raccb], writes=[rpM])
                    for c in range(2):
                        fw.op("pe", lambda e, c=c: e.matmul(pQ[:], onesb[:], sqb[:, c, :], start=(c == 0), stop=(c == 1)),
                              reads=[ronesb, rsqb], writes=[rpQ])
                    fw.op("act", lambda e: e.activation(m2[:], pM[:], AF.Square, scale=1.0 / 256), reads=[rpM], writes=[rm2])
                    fw.op("dve", lambda e: e.scalar_tensor_tensor(rs[:], pQ[:], 1.0 / 256, m2[:], ALU.mult, ALU.subtract),
                          reads=[rpQ, rm2], writes=[rrs])
                    fw.op("dve", lambda e: e.tensor_scalar(rs[:], rs[:], 0.0, None, ALU.max), reads=[rrs], writes=[rrs])
                    fw.op("act", lambda e: e.activation(rs[:], rs[:], AF.Sqrt, scale=1.0, bias=self.eps_ap()), reads=[rrs, self.reps], writes=[rrs])
                    fw.op("dve", lambda e: e.reciprocal(rs[:], rs[:]), reads=[rrs], writes=[rrs])
                    yot, ryo = yo[no % 2]
                    no += 1
                    for c in range(2):
                        fw.op("dve", lambda e, c=c: e.scalar_tensor_tensor(tt[:], pM[:], -1.0 / 256, acc[:, c, :], ALU.mult, ALU.add),
                              reads=[rpM, racc2[c]], writes=[rtt])
                        fw.op("dve", lambda e: e.tensor_tensor(tt[:], tt[:], rs[:], ALU.mult), reads=[rtt, rrs], writes=[rtt])
                        fw.op("act", lambda e, c=c: e.activation(yot[:, c, :], tt[:], AF.Silu, scale=dv[:, l, 1, c:c + 1], bias=dv[:, l, 2, c:c + 1]),
                              reads=[rtt, rdv], writes=[ryo])
                    tg = u * UL + t0
                    self.stt(S["yall"][768:1024, tg:tg + 512].rearrange("(c p) t -> p c t", p=128), yot[:], ryo, R["yall"])
            fw.phase_end()

    def phase_p3a(self, l):
        self.sq_eng = 'act'
        fw, I, S, R, C = self.fw, self.I, self.S, self.R, self.C
        BT = 256
        xsrc, rxsrc = (I["xT"], R["xT"]) if l == 0 else (S["xn"], R["xn"])
        fw.phase_begin()
        with ExitStack() as ps:
            sb = lambda n, s, dt=F32: fw.sbuf(n, s, dt, ps)
            Wg = sb("wg", [128, 8, 4096], BF16); rWg = fw.res("wg")
            Wb = sb("wb", [128, 8, 1024], BF16); rWb = fw.res("wb")
            Wo = sb("wo", [128, 8, 1024], BF16); rWo = fw.res("wo")
            stg = [(sb("stg3", [128, 1024], F32), fw.res("stg3", dma=True)) for _ in range(3)]
            self.stg_i = 0
            self.load_cast(Wg, rWg, 0, 8, I["w_in"][l][:, 2832:6928], 4096, stg, ["pool", "dve", "act"], 1024)
            self.load_cast(Wb, rWb, 0, 8, I["w_branch"][l], 1024, stg, ["pool", "dve", "act"], 1024)
            self.load_cast(Wo, rWo, 0, 8, I["w_out"][l], 1024, stg, ["pool", "dve", "act"], 1024)
            xb = [(sb("xb3", [128, 8, BT]), fw.res("xb3", dma=True)) for _ in range(2)]
            yb = [(sb("yb3", [128, 8, BT], BF16), fw.res("yb3", dma=True)) for _ in range(2)]
            hT = sb("hT3", [128, 8, BT], BF16); rhT = fw.res("hT3")
            hT2 = sb("hT3b", [128, 8, BT], BF16); rhT2 = fw.res("hT3b")
            sq = [sb("sq3", [128, BT], BF16) for _ in range(2)]; rsq = [fw.res("sq3") for _ in range(2)]
            rstd = sb("rstd3", [128, BT]); rrstd = fw.res("rstd3")
            tmp = sb("tmp3", [128, BT]); rtmp = fw.res("tmp3")
            sg = [(sb("sg3", [128, BT]), fw.res("sg3")) for _ in range(3)]
            t2 = [(sb("t23", [128, BT]), fw.res("t23")) for _ in range(2)]
            acc = [(sb("acc3", [128, BT]), fw.res("acc3")) for _ in range(2)]
            mg = sb("mg3", [128, 8, BT], BF16); rmg = fw.res("mg3")
            pg = [(fw.psum("pg3", [128, BT], F32, ps), fw.res("pg3", excl=True)) for _ in range(3)]
            pp = [(fw.psum("pp3", [128, BT], F32, ps), fw.res("pp3", excl=True)) for _ in range(3)]
            po = [(fw.psum("po3", [128, BT], F32, ps), fw.res("po3", excl=True)) for _ in range(2)]
            pst, rpst = po[1]
            n = no = nt2 = 0
            NBLK = min(self.NT // BT, int(os.environ.get('KBLK', '9999')))
            hTs = [(hT, rhT), (hT2, rhT2)]

            def prep(gi):
                t0 = gi * BT
                xt, rx = xb[gi % 2]
                yt, ry = yb[gi % 2]
                self.ld(xt[:], xsrc[:, t0:t0 + BT].rearrange("(k p) t -> p k t", p=128), rx, rxsrc)
                self.ld(yt[:], S["yall"][:, t0:t0 + BT].rearrange("(k p) t -> p k t", p=128), ry, R["yall"])

            def nm_stats(gi):
                xt, rx = xb[gi % 2]
                self.norm_stats(xt, rx, sq, rsq, pst, rpst, rstd, rrstd)

            def nm_apply(gi):
                xt, rx = xb[gi % 2]
                h_, rh_ = hTs[gi % 2]
                self.norm_apply(xt, rx, h_, rh_, rstd, rrstd, l, 0, (gi * BT) // self.UL, tmp, rtmp)

            def nm(gi):
                nm_stats(gi)
                nm_apply(gi)

            prep(0)
            nm(0)
            for gi in range(NBLK):
                t0 = gi * BT
                u = t0 // self.UL
                xt, rx = xb[gi % 2]
                yt, ry = yb[gi % 2]
                hT, rhT = hTs[gi % 2]
                if gi + 1 < NBLK:
                    prep(gi + 1)
                for f in range(8):
                    if f == 6 and gi + 1 < NBLK:
                        nm_stats(gi + 1)
                    acct, racc = acc[f % 2]
                    for br in range(4):
                        pgt, rpg = pg[n % 3]
                        ppt, rpp = pp[n % 3]
                        sgt, rsg = sg[n % 3]
                        n += 1
                        col = br * 1024 + f * 128
                        for k in range(8):
                            fw.op("pe", lambda e, k=k: e.matmul(pgt[:], Wg[:, k, col:col + 128], hT[:, k, :], start=(k == 0), stop=(k == 7)),
                                  reads=[rWg, rhT], writes=[rpg])
                        for k in range(2):
                            fw.op("pe", lambda e, k=k: e.matmul(ppt[:], Wb[:, br * 2 + k, f * 128:(f + 1) * 128], yt[:, br * 2 + k, :],
                                                                start=(k == 0), stop=(k == 1)), reads=[rWb, ry], writes=[rpp])
                        fw.op("act", lambda e: e.activation(sgt[:], pgt[:], AF.Sigmoid), reads=[rpg], writes=[rsg])
                        if br == 0:
                            fw.op("dve", lambda e: e.tensor_tensor(acct[:], sgt[:], ppt[:], ALU.mult), reads=[rsg, rpp], writes=[racc])
                        else:
                            t2t, rt2 = t2[nt2 % 2]
                            nt2 += 1
                            fw.op("dve", lambda e: e.tensor_tensor(t2t[:], sgt[:], ppt[:], ALU.mult), reads=[rsg, rpp], writes=[rt2])
                            if br < 3:
                                fw.op("pool", lambda e: e.tensor_tensor(acct[:], acct[:], t2t[:], ALU.add), reads=[racc, rt2], writes=[racc])
                            else:
                                fw.op("pool", lambda e, f=f: e.tensor_tensor(mg[:, f, :], acct[:], t2t[:], ALU.add), reads=[racc, rt2], writes=[rmg])
                if gi + 1 < NBLK:
                    nm_apply(gi + 1)
                for f in range(8):
                    pot, rpo = po[no % 2]
                    no += 1
                    for k in range(8):
                        fw.op("pe", lambda e, k=k, f=f: e.matmul(pot[:], Wo[:, k, f * 128:(f + 1) * 128], mg[:, k, :], start=(k == 0), stop=(k == 7)),
                              reads=[rWo, rmg], writes=[rpo])
                    fw.op("dve", lambda e, f=f: e.scalar_tensor_tensor(xt[:, f, :], pot[:], self.mod[:, l, 2, f, u:u + 1], xt[:, f, :],
                                                                       ALU.mult, ALU.add), reads=[rpo, self.rmod, rx], writes=[rx])
                self.stt(S["xm"][:, t0:t0 + BT].rearrange("(k p) t -> p k t", p=128), xt[:], rx, R["xm"])
        fw.phase_end()

    def phase_p3b(self, l):
        self.sq_eng = 'act'
        fw, I, S, R, C = self.fw, self.I, self.S, self.R, self.C
        BT = 256
        last = (l == self.L - 1)
        fw.phase_begin()
        with ExitStack() as ps:
            sb = lambda n, s, dt=F32: fw.sbuf(n, s, dt, ps)
            W1 = sb("wf1", [128, 8, 2 * DFF], BF16); rW1 = fw.res("wf1")
            W2 = sb("wf2", [128, 22, 1024], BF16); rW2 = fw.res("wf2")
            stg = [(sb("stg4", [128, 1408], F32), fw.res("stg4", dma=True)) for _ in range(2)]
            self.stg_i = 0
            self.load_cast(W1, rW1, 0, 8, I["w_ffn_in"][l], 2 * DFF, stg, ["pool", "dve", "act"], 1408)
            self.load_cast(W2, rW2, 0, 22, I["w_ffn_out"][l], 1024, stg, ["pool", "dve", "act"], 1408)
            xb = [(sb("xb4", [128, 8, BT]), fw.res("xb4", dma=True)) for _ in range(2)]
            hT = sb("hT4", [128, 8, BT], BF16); rhT = fw.res("hT4")
            hT2 = sb("hT4b", [128, 8, BT], BF16); rhT2 = fw.res("hT4b")
            sq = [sb("sq4", [128, BT], BF16) for _ in range(2)]; rsq = [fw.res("sq4") for _ in range(2)]
            rstd = sb("rstd4", [128, BT]); rrstd = fw.res("rstd4")
            tmp = sb("tmp4", [128, BT]); rtmp = fw.res("tmp4")
            sg = [(sb("sg4", [128, BT]), fw.res("sg4")) for _ in range(3)]
            hid = sb("hid4", [128, 22, BT], BF16); rhid = fw.res("hid4")
            pg = [(fw.psum("pg4", [128, BT], F32, ps), fw.res("pg4", excl=True)) for _ in range(3)]
            pu = [(fw.psum("pu4", [128, BT], F32, ps), fw.res("pu4", excl=True)) for _ in range(3)]
            po = [(fw.psum("po4", [128, BT], F32, ps), fw.res("po4", excl=True)) for _ in range(2)]
            pst, rpst = po[1]
            gfin, rgfin = C["g_finalT"]
            onesf, ronesf = C["onesb"]
            n = no = 0
            NBLK = min(self.NT // BT, int(os.environ.get('KBLK', '9999')))
            hTs = [(hT, rhT), (hT2, rhT2)]

            def prep(gi):
                t0 = gi * BT
                xt, rx = xb[gi % 2]
                self.ld(xt[:], S["xm"][:, t0:t0 + BT].rearrange("(k p) t -> p k t", p=128), rx, R["xm"])

            def nm_stats(gi):
                xt, rx = xb[gi % 2]
                self.norm_stats(xt, rx, sq, rsq, pst, rpst, rstd, rrstd)

            def nm_apply(gi):
                xt, rx = xb[gi % 2]
                h_, rh_ = hTs[gi % 2]
                self.norm_apply(xt, rx, h_, rh_, rstd, rrstd, l, 1, (gi * BT) // self.UL, tmp, rtmp)

            def nm(gi):
                nm_stats(gi)
                nm_apply(gi)

            prep(0)
            nm(0)
            for gi in range(NBLK):
                t0 = gi * BT
                u = t0 // self.UL
                xt, rx = xb[gi % 2]
                hT, rhT = hTs[gi % 2]
                if gi + 1 < NBLK:
                    prep(gi + 1)
                for j in range(22):
                    pgt, rpg = pg[n % 3]
                    put, rpu = pu[n % 3]
                    sgt, rsg = sg[n % 3]
                    n += 1
                    for k in range(8):
                        fw.op("pe", lambda e, k=k, j=j: e.matmul(pgt[:], W1[:, k, j * 128:(j + 1) * 128], hT[:, k, :], start=(k == 0), stop=(k == 7)),
                              reads=[rW1, rhT], writes=[rpg])
                    for k in range(8):
                        fw.op("pe", lambda e, k=k, j=j: e.matmul(put[:], W1[:, k, DFF + j * 128:DFF + (j + 1) * 128], hT[:, k, :],
                                                                  start=(k == 0), stop=(k == 7)), reads=[rW1, rhT], writes=[rpu])
                    fw.op("act", lambda e: e.activation(sgt[:], pgt[:], AF.Silu), reads=[rpg], writes=[rsg])
                    fw.op("dve", lambda e, j=j: e.tensor_tensor(hid[:, j, :], sgt[:], put[:], ALU.mult), reads=[rsg, rpu], writes=[rhid])
                    if j == 17 and gi + 1 < NBLK:
                        nm_stats(gi + 1)
                if gi + 1 < NBLK:
                    nm_apply(gi + 1)
                for f in range(8):
                    pot, rpo = po[no % 2]
                    no += 1
                    for k in range(22):
                        fw.op("pe", lambda e, k=k, f=f: e.matmul(pot[:], W2[:, k, f * 128:(f + 1) * 128], hid[:, k, :], start=(k == 0), stop=(k == 21)),
                              reads=[rW2, rhid], writes=[rpo])
                    fw.op("dve", lambda e, f=f: e.scalar_tensor_tensor(xt[:, f, :], pot[:], self.mod[:, l, 5, f, u:u + 1], xt[:, f, :],
                                                                       ALU.mult, ALU.add), reads=[rpo, self.rmod, rx], writes=[rx])
                if not last:
                    self.stt(S["xn"][:, t0:t0 + BT].rearrange("(k p) t -> p k t", p=128), xt[:], rx, R["xn"])
                else:
                    for k in range(8):
                        sqk, rsqk = sq[k % 2], rsq[k % 2]
                        fw.op("act", lambda e, k=k: e.activation(sqk[:], xt[:, k, :], AF.Square), reads=[rx], writes=[rsqk])
                        fw.op("pe", lambda e, k=k: e.matmul(pst[:], onesf[:], sqk[:], start=(k == 0), stop=(k == 7)),
                              reads=[rsqk, ronesf], writes=[rpst])
                    fw.op("act", lambda e: e.activation(rstd[:], pst[:], AF.Sqrt, scale=1.0 / D, bias=self.eps_ap()),
                          reads=[rpst, self.reps], writes=[rrstd])
                    fw.op("dve", lambda e: e.reciprocal(rstd[:], rstd[:]), reads=[rrstd], writes=[rrstd])
                    for k in range(8):
                        fw.op("dve", lambda e, k=k: e.scalar_tensor_tensor(xt[:, k, :], xt[:, k, :], gfin[:, k:k + 1], rstd[:], ALU.mult, ALU.mult),
                              reads=[rx, rgfin, rrstd], writes=[rx])
                    self.stt(self.yT[:, t0:t0 + BT].rearrange("(k p) t -> p k t", p=128), xt[:], rx, R["yT"])
        fw.phase_end()


def _bias_tile_idx(rows_total, qr0, kr0, q_valid_rows, k_valid_rows):
    kk = np.arange(128)
    qq = np.arange(128)
    krow = kr0 + kk // 64
    kcol = kk % 64
    qrow = qr0 + qq // 64
    qcol = qq % 64
    kr = min(8, rows_total)
    wlo = np.clip(qrow - kr // 2, 0, rows_total - kr)
    clo = np.clip(qcol - 8, 0, 64 - 16)
    vr = (krow[:, None] >= wlo[None, :]) & (krow[:, None] < wlo[None, :] + kr)
    vcol = (kcol[:, None] >= clo[None, :]) & (kcol[:, None] < clo[None, :] + 16)
    valid = vr & vcol
    valid &= (krow[:, None] >= 0) & (krow[:, None] < rows_total) & (qrow[None, :] >= 0) & (qrow[None, :] < rows_total)
    dr = np.clip(krow[:, None] - qrow[None, :] + 7, 0, 14)
    dc = np.clip(kcol[:, None] - qcol[None, :] + 15, 0, 30)
    return valid, dr, dc


def _make_rpbt(rpb_l, UL, link):
    Rr = UL // 64
    NB2 = UL // 128
    out = np.full((128, NSLOT * 4, 128), NEG, np.float32)

    def fill(slot, rows_total, qr0, kr0):
        valid, dr, dc = _bias_tile_idx(rows_total, qr0, kr0, None, None)
        for h in range(4):
            vals = rpb_l[h][dr, dc]
            out[:, slot * 4 + h, :] = np.where(valid, vals, np.float32(NEG))

    big = 64 if Rr >= 16 else Rr
    Rg = max(Rr, 16)
    for cls, bsel in (("INT", 4), ("TOP0", 0), ("TOP1", 1), ("BOT1", Rg // 2 - 2), ("BOT0", Rg // 2 - 1)):
        s0, offs = CLS[cls]
        for i, o in enumerate(offs):
            fill(s0 + i, Rg, 2 * bsel, 2 * (bsel + o))
    for cls, u, b in (("JA1", 0, NB2 - 2), ("JA0", 0, NB2 - 1), ("JB0", 1, 0), ("JB1", 1, 1)):
        s0, offs = CLS[cls]
        for i, o in enumerate(offs):
            if link:
                fill(s0 + i, 2 * Rr, 2 * (u * NB2 + b), 2 * (u * NB2 + b + o))
            else:
                kp = b + o
                if kp < 0 or kp >= NB2:
                    continue
                fill(s0 + i, Rr, 2 * b, 2 * kp)
    return out


def _host_prep(inp, UL, L, units_per_core):
    f32 = np.float32
    shared = {}
    shared["w_ada"] = np.ascontiguousarray(inp["w_ada"][:L])
    shared["b_adaT"] = np.ascontiguousarray(inp["b_ada"][:L].reshape(L, 48, 128).transpose(2, 0, 1))
    gv = np.stack([inp["g_norm_mix"][:L], inp["g_norm_ffn"][:L]], 1)
    shared["gvec"] = np.ascontiguousarray(gv.reshape(L, 2, 8, 128).transpose(3, 0, 1, 2))
    shared["w_in"] = np.ascontiguousarray(inp["w_in"][:L])
    shared["a_ln"] = np.ascontiguousarray(np.concatenate([inp["a_ln_g"][:L], inp["a_ln_b"][:L]], 1))
    shared["a_w_spT"] = np.ascontiguousarray(inp["a_w_sp"][:L].transpose(0, 3, 1, 2))
    shared["a_b_sp"] = np.ascontiguousarray(inp["a_b_sp"][:L].transpose(2, 0, 1))
    shared["c_gate_b"] = np.ascontiguousarray(inp["c_gate_b"][:L])
    shared["c_hnorm"] = np.ascontiguousarray(inp["c_hnorm_g"][:L])
    shared["d_conv_wT"] = np.ascontiguousarray(inp["d_conv_w"][:L].reshape(L, 31, 2, 128).transpose(3, 0, 2, 1))
    dv = np.stack([inp["d_conv_b"][:L], inp["d_ln_g"][:L], inp["d_ln_b"][:L]], 1)
    shared["d_vec"] = np.ascontiguousarray(dv.reshape(L, 3, 2, 128).transpose(3, 0, 1, 2))
    shared["w_branch"] = np.ascontiguousarray(inp["w_branch"][:L].reshape(L, 1024, D))
    shared["w_out"] = np.ascontiguousarray(inp["w_out"][:L])
    shared["w_ffn_in"] = np.ascontiguousarray(inp["w_ffn_in"][:L])
    shared["w_ffn_out"] = np.ascontiguousarray(inp["w_ffn_out"][:L])
    shared["g_finalT"] = np.ascontiguousarray(inp["g_final"].reshape(8, 128).T)
    shared["c_ident"] = np.eye(128, dtype=f32)
    shared["c_triu"] = np.triu(np.ones((128, 128), f32))
    shared["c_tril"] = np.tril(np.ones((128, 128), f32))
    sel = np.zeros((128, 2, 128), f32)
    sel[:, 0, 0:64] = 1.0
    sel[:, 1, 64:128] = 1.0
    shared["c_sel"] = sel
    rp = {}
    for link in (0, 1):
        rp[link] = np.stack([_make_rpbt(inp["b_rpb"][l], UL, link) for l in range(L)], 0)
    in_maps = []
    for link, units in units_per_core:
        m = dict(shared)
        xs, cs = [], []
        for which, si, t0 in units:
            x = inp["x_prompt"] if which == "p" else inp["x_sample"]
            c = inp["c_prompt"] if which == "p" else inp["c_sample"]
            xs.append(x[si, t0:t0 + UL, :])
            cs.append(c[si])
        m["xT"] = np.ascontiguousarray(np.concatenate(xs, 0).T)
        cc = np.stack(cs, 0)
        m["cT"] = np.ascontiguousarray(cc.reshape(3, 8, 128).transpose(2, 1, 0))
        m["link"] = np.full((128, 1), float(link), f32)
        m["rpbt"] = rp[link]
        in_maps.append(m)
    return in_maps


_NC_CACHE = {}


def run_config(inp, UL, L, units_per_core, debug=False):
    key = (UL, L, debug)
    if key not in _NC_CACHE:
        b = Builder(UL, L)
        b.debug = debug
        _NC_CACHE[key] = (b.build(), b)
    nc, b = _NC_CACHE[key]
    in_maps = _host_prep(inp, UL, L, units_per_core)
    res = run_bass_kernel_spmd(nc, in_maps, core_ids=list(range(len(in_maps))))
    if debug:
        return res.results
    return [np.asarray(r["yT"]) for r in res.results]


def kernel(**inputs):
    inp = {k: np.asarray(v) for k, v in inputs.items()}
    UL = 4096
    L = 4
    units = []
    for c in range(4):
        units.append((1, [("p", c, 0), ("p", c, UL), ("s", c, 0)]))
    for c in range(4):
        units.append((0, [("s", 4 + 3 * c + j, 0) for j in range(3)]))
    outs = run_config(inp, UL, L, units)
    yp = np.empty(inp["x_prompt"].shape, np.float32)
    ys = np.empty(inp["x_sample"].shape, np.float32)
    for c, (link, us) in enumerate(units):
        yT = outs[c]
        for j, (which, si, t0) in enumerate(us):
            blk = yT[:, j * UL:(j + 1) * UL].T
            if which == "p":
                yp[si, t0:t0 + UL, :] = blk
            else:
                ys[si, 0:UL, :] = blk
    return (yp, ys)
```

```python
import os
import numpy as np
from contextlib import ExitStack
import concourse.bass as bass
import concourse.mybir as mybir
from concourse.bass_utils import run_bass_kernel_spmd

F32 = mybir.dt.float32
BF16 = mybir.dt.bfloat16
AF = mybir.ActivationFunctionType
ALU = mybir.AluOpType
AX = mybir.AxisListType

D = 1024
NIN = 6928
DFF = 2816
NEG = -30000.0
EPS = 1e-6
NSLOT = 43
CLS = {
    "INT": (0, [-2, -1, 0, 1, 2]),
    "TOP0": (5, [0, 1, 2, 3]),
    "TOP1": (9, [-1, 0, 1, 2]),
    "BOT1": (13, [-2, -1, 0, 1]),
    "BOT0": (17, [-3, -2, -1, 0]),
    "JA1": (21, [-2, -1, 0, 1, 2]),
    "JA0": (26, [-3, -2, -1, 0, 1, 2]),
    "JB0": (32, [-2, -1, 0, 1, 2, 3]),
    "JB1": (38, [-2, -1, 0, 1, 2]),
}


class Res:
    __slots__ = ("name", "w", "r", "dsem", "multi", "excl")

    def __init__(self, name, multi=False, excl=False):
        self.name = name
        self.multi = multi
        self.excl = excl
        self.w = []
        self.r = {}
        self.dsem = None


class Sem:
    __slots__ = ("h", "total", "is_dma", "name")

    def __init__(self, h, is_dma, name):
        self.h = h
        self.total = 0
        self.is_dma = is_dma
        self.name = name


class Eng:
    def __init__(self, name, h, sem):
        self.name = name
        self.h = h
        self.sem = sem
        self.waited = {}


class FW:
    def __init__(self, nc, stack):
        self.nc = nc
        self.stack = stack
        self.eng = {}
        self.sems = []
        for name, h in (("pe", nc.tensor), ("dve", nc.vector), ("act", nc.scalar),
                        ("pool", nc.gpsimd), ("sp", nc.sync)):
            s = Sem(stack.enter_context(nc.semaphore("s_" + name)), False, name)
            self.sems.append(s)
            self.eng[name] = Eng(name, h, s)
        self.ninst = 0
        self.uid = 0
        self.free_dsems = []
        self.phase_dsems = None

    def sbuf(self, name, shape, dt, stack=None):
        self.uid += 1
        return (stack or self.stack).enter_context(
            self.nc.sbuf_tensor("%s_%d" % (name, self.uid), list(shape), dt))

    def psum(self, name, shape, dt=F32, stack=None):
        self.uid += 1
        esz = 4 if dt == F32 else 2
        n = int(np.prod(shape[1:]))
        be = 2048 // esz
        nb = -(-n // be)
        t = (stack or self.stack).enter_context(
            self.nc.psum_tensor("%s_%d" % (name, self.uid), [128, nb * be], dt))
        v = t[:, 0:n]
        if len(shape) == 3:
            v = v.rearrange("p (a b) -> p a b", b=shape[2])
        return v

    def res(self, name, dma=False, multi=False, excl=False):
        r = Res(name, multi, excl)
        if dma:
            if not self.free_dsems:
                self.uid += 1
                h = self.stack.enter_context(self.nc.semaphore("d_%d" % self.uid))
                sm = Sem(h, True, name)
                self.sems.append(sm)
                self.free_dsems.append(sm)
            r.dsem = self.free_dsems.pop()
            if self.phase_dsems is not None:
                self.phase_dsems.append(r.dsem)
        return r

    def phase_begin(self):
        self.phase_dsems = []

    def phase_end(self):
        try:
            print("sbuf remaining", self.nc.sbuf_bytes_remaining, "ninst", self.ninst, flush=True)
        except Exception as ex:
            print("sbuf remaining ?", ex)
        self.barrier()
        self.free_dsems.extend(self.phase_dsems)
        self.phase_dsems = None

    def _need(self, e, deps):
        best = {}
        for s, v in deps:
            if s is e.sem:
                if e.name == "pe" or e.name == "sp":
                    continue
                if e.sem.total - v >= 2:
                    continue
            if s.is_dma:
                v = s.total
            if v > best.get(s, 0):
                best[s] = v
        for s, v in best.items():
            if e.waited.get(s, 0) >= v:
                continue
            e.h.wait_ge(s.h, v)
            e.waited[s] = v
            self.ninst += 1

    def _collect(self, reads, writes):
        deps = []
        for r in reads:
            deps.extend(r.w)
        for w in writes:
            if not w.multi:
                deps.extend(w.w)
            deps.extend(w.r.items())
        return deps

    def _record(self, sem, reads, writes):
        key = (sem, sem.total)
        for r in reads:
            r.r[sem] = sem.total
        for w in writes:
            if w.multi:
                w.w = [k for k in w.w if k[0] is not sem] + [key]
            else:
                w.w = [key]
                w.r = {}

    def op(self, ename, fn, reads=(), writes=()):
        e = self.eng[ename]
        xr = [r for r in reads if r.excl]
        if xr:
            reads = [r for r in reads if not r.excl]
            writes = list(writes) + xr
        self._need(e, self._collect(reads, writes))
        ins = fn(e.h)
        e.sem.total += 1
        ins.then_inc(e.sem.h, 1)
        self.ninst += 1
        self._record(e.sem, reads, writes)
        return ins

    def dma(self, qname, out, in_, reads=(), writes=(), dres=None, **kw):
        e = self.eng[qname]
        self._need(e, self._collect(reads, writes))
        ds = dres.dsem
        ins = e.h.dma_start(out=out, in_=in_, **kw)
        ds.total += 16
        ins.then_inc(ds.h, 16)
        self.ninst += 1
        self._record(ds, reads, writes)
        return ins

    def barrier(self):
        for e in self.eng.values():
            for s in self.sems:
                if s is e.sem or s.total == 0:
                    continue
                if e.waited.get(s, 0) >= s.total:
                    continue
                e.h.wait_ge(s.h, s.total)
                e.waited[s] = s.total
                self.ninst += 1


class Builder:
    def __init__(self, UL, L, last_is_final=True):
        self.UL = UL
        self.L = L
        self.NT = 3 * UL
        self.NBK = UL // 512
        self.NCH = UL // 128
        self.NB2 = UL // 128
        assert self.NB2 >= 8

    def declare(self, nc):
        L, NT = self.L, self.NT
        di = lambda n, s, dt=F32: nc.dram_tensor(n, list(s), dt, kind="ExternalInput").ap()
        dbg = getattr(self, "debug", False)
        dx = lambda n, s, dt=F32: nc.dram_tensor(n, list(s), dt, kind=("ExternalOutput" if dbg else "Internal")).ap()
        I = {}
        I["xT"] = di("xT", [D, NT])
        I["cT"] = di("cT", [128, 8, 3])
        I["link"] = di("link", [128, 1])
        I["w_ada"] = di("w_ada", [L, D, 6 * D])
        I["b_adaT"] = di("b_adaT", [128, L, 48])
        I["gvec"] = di("gvec", [128, L, 2, 8])
        I["w_in"] = di("w_in", [L, D, NIN])
        I["a_ln"] = di("a_ln", [L, 512])
        I["a_w_spT"] = di("a_w_spT", [L, 128, 4, 128])
        I["a_b_sp"] = di("a_b_sp", [128, L, 4])
        I["rpbt"] = di("rpbt", [L, 128, NSLOT * 4, 128])
        I["c_gate_b"] = di("c_gate_b", [L, 16])
        I["c_hnorm"] = di("c_hnorm", [L, 256])
        I["d_conv_wT"] = di("d_conv_wT", [128, L, 2, 31])
        I["d_vec"] = di("d_vec", [128, L, 3, 2])
        I["w_branch"] = di("w_branch", [L, 1024, D])
        I["w_out"] = di("w_out", [L, D, D])
        I["w_ffn_in"] = di("w_ffn_in", [L, D, 2 * DFF])
        I["w_ffn_out"] = di("w_ffn_out", [L, DFF, D])
        I["g_finalT"] = di("g_finalT", [128, 8])
        I["c_ident"] = di("c_ident", [128, 128])
        I["c_triu"] = di("c_triu", [128, 128])
        I["c_tril"] = di("c_tril", [128, 128])
        I["c_sel"] = di("c_sel", [128, 2, 128])
        self.I = I
        self.yT = nc.dram_tensor("yT", [D, NT], F32, kind="ExternalOutput").ap()
        S = {}
        S["xm"] = dx("xm", [D, NT])
        S["xn"] = dx("xn", [D, NT])
        S["yall"] = dx("yall", [D, NT], BF16)
        S["bq"] = dx("bq", [256, NT], BF16)
        S["bk"] = dx("bk", [256, NT], BF16)
        S["bv"] = dx("bv", [NT, 256], BF16)
        S["cq"] = dx("cq", [256, NT], BF16)
        S["ck"] = dx("ck", [256, NT], BF16)
        S["ckt"] = dx("ckt", [NT, 256], BF16)
        S["cvt"] = dx("cvt", [NT, 256])
        S["co"] = dx("co", [NT, 256])
        S["cg"] = dx("cg", [NT, 16])
        S["chf"] = dx("chf", [NT, 256])
        S["dy"] = dx("dy", [256, NT])
        self.S = S

    def build(self):
        nc = bass.Bass("TRN2", target_bir_lowering=False)
        self.nc = nc
        self.declare(nc)
        with ExitStack() as st:
            fw = FW(nc, st)
            self.fw = fw
            self.R = {k: fw.res(k, multi=True) for k in list(self.S) + ["xT", "yT"]}
            self.setup_consts(st)
            self.ensure_eps()
            self.phase_mod()
            import os
            ph = os.environ.get("KPHASES", "p1,bc,conv,p3a,p3b").split(",")
            for l in range(self.L):
                for p in ("p1", "bc", "conv", "p3a", "p3b"):
                    if p in ph:
                        getattr(self, "phase_" + p)(l)
            fw.barrier()
            self.ninst = fw.ninst
        return nc

    def ld(self, tile_ap, dram_ap, res, dram_res=None, q="sp"):
        reads = [dram_res] if dram_res is not None else []
        self.fw.dma(q, tile_ap, dram_ap, reads=reads, writes=[res], dres=res)

    def stt(self, dram_ap, tile_ap, res, dram_res, q=None):
        import os
        q = q or os.environ.get("KSTQ", "pool")
        self.fw.dma(q, dram_ap, tile_ap, reads=[res], writes=[dram_res], dres=res)

    def setup_consts(self, st):
        fw, I = self.fw, self.I
        L = self.L
        C = {}

        def cload(name, shape, src, dt=F32):
            t = fw.sbuf(name, shape, dt)
            r = fw.res(name, dma=True)
            self.ld(t[:], src, r)
            C[name] = (t, r)
            return t, r

        cload("identf", [128, 128], I["c_ident"])
        cload("triu", [128, 128], I["c_triu"])
        cload("tril", [128, 128], I["c_tril"])
        cload("sel", [128, 2, 128], I["c_sel"])
        cload("link", [128, 1], I["link"])
        cload("cT", [128, 8, 3], I["cT"])
        cload("b_adaT", [128, L, 48], I["b_adaT"])
        cload("gvec", [128, L, 2, 8], I["gvec"])
        cload("a_b_sp", [128, L, 4], I["a_b_sp"])
        cload("d_conv_wT", [128, L, 2, 31], I["d_conv_wT"])
        cload("d_vec", [128, L, 3, 2], I["d_vec"])
        cload("g_finalT", [128, 8], I["g_finalT"])
        identb = fw.sbuf("identb", [128, 128], BF16)
        ridb = fw.res("identb")
        fw.op("dve", lambda e: e.tensor_copy(identb[:], C["identf"][0][:]), reads=[C["identf"][1]], writes=[ridb])
        C["identb"] = (identb, ridb)
        onesf = fw.sbuf("onesf", [128, 128], F32)
        ronesf = fw.res("onesf")
        fw.op("dve", lambda e: e.memset(onesf[:], 1.0), writes=[ronesf])
        C["onesf"] = (onesf, ronesf)
        onesb = fw.sbuf("onesb", [128, 128], BF16)
        ronesb = fw.res("onesb")
        fw.op("dve", lambda e: e.memset(onesb[:], 1.0), writes=[ronesb])
        C["onesb"] = (onesb, ronesb)
        self.C = C
        self.mod = fw.sbuf("mod", [128, L, 6, 8, 3], F32)
        self.rmod = fw.res("mod")
        self.gm = fw.sbuf("gm", [128, L, 2, 8, 3], F32)
        self.rgm = fw.res("gm")

    def phase_mod(self):
        fw, I, C = self.fw, self.I, self.C
        L = self.L
        fw.phase_begin()
        with ExitStack() as ps:
            sc = fw.sbuf("silu_c", [128, 8, 3], F32, ps)
            rsc = fw.res("silu_c")
            cT, rcT = C["cT"]
            fw.op("act", lambda e: e.activation(sc[:], cT[:], AF.Silu), reads=[rcT], writes=[rsc])
            wst = [(fw.sbuf("wada", [128, 8, 1024], F32, ps), fw.res("wada", dma=True)) for _ in range(2)]
            pm = [(fw.psum("pmod", [128, 8, 4], F32, ps), fw.res("pmod", excl=True)) for _ in range(2)]
            it = 0
            badaT, rbada = C["b_adaT"]
            for l in range(L):
                for m in range(6):
                    wt, rw = wst[it % 2]
                    pt, rp = pm[it % 2]
                    it += 1
                    src = I["w_ada"][l, :, m * 1024:(m + 1) * 1024].rearrange("(k p) n -> p k n", p=128)
                    for k2 in range(2):
                        self.ld(wt[:, 4 * k2:4 * k2 + 4, :], src[:, 4 * k2:4 * k2 + 4, :], rw)
                    for f in range(8):
                        for k in range(8):
                            fw.op("pe", lambda e, f=f, k=k: e.matmul(pt[:, f, 0:3], wt[:, k, f * 128:(f + 1) * 128], sc[:, k, :],
                                                                     start=(k == 0), stop=(k == 7)),
                                  reads=[rw, rsc], writes=[rp])
                    for f in range(8):
                        fw.op("dve", lambda e, f=f: e.tensor_scalar(self.mod[:, l, m, f, :], pt[:, f, 0:3],
                                                                    badaT[:, l, m * 8 + f:m * 8 + f + 1], None, ALU.add),
                              reads=[rp, rbada], writes=[self.rmod])
            gvec, rg = C["gvec"]
            for l in range(L):
                for j, m in ((0, 1), (1, 4)):
                    for u in range(3):
                        fw.op("dve", lambda e, l=l, j=j, m=m, u=u: e.scalar_tensor_tensor(
                            self.gm[:, l, j, :, u], self.mod[:, l, m, :, u], 1.0, gvec[:, l, j, :], ALU.add, ALU.mult),
                            reads=[self.rmod, rg], writes=[self.rgm])
        fw.phase_end()

    def load_cast(self, dst, rdst, k0, nk, src_rows, ncols, stg, eng_cycle, CW):
        fw = self.fw
        for k in range(nk):
            for c0 in range(0, ncols, CW):
                cw = min(CW, ncols - c0)
                st_t, st_r = stg[self.stg_i % len(stg)]
                self.stg_i += 1
                self.ld(st_t[:, 0:cw], src_rows[k * 128:(k + 1) * 128, c0:c0 + cw], st_r)
                en = eng_cycle[self.stg_i % len(eng_cycle)]
                if en == "act":
                    fw.op("act", lambda e: e.copy(dst[:, k0 + k, c0:c0 + cw], st_t[:, 0:cw]), reads=[st_r], writes=[rdst])
                else:
                    fw.op(en, lambda e: e.tensor_copy(dst[:, k0 + k, c0:c0 + cw], st_t[:, 0:cw]), reads=[st_r], writes=[rdst])

    def norm_mod(self, xb, rxb, hT, rhT, sq, rsq, pst, rpst, rstd, rrstd, l, j, u, tmp, rtmp):
        self.norm_stats(xb, rxb, sq, rsq, pst, rpst, rstd, rrstd)
        self.norm_apply(xb, rxb, hT, rhT, rstd, rrstd, l, j, u, tmp, rtmp)

    def norm_stats(self, xb, rxb, sq, rsq, pst, rpst, rstd, rrstd):
        fw = self.fw
        onesf, ronesf = self.C["onesb"]
        sqe = getattr(self, "sq_eng", "act")
        for k in range(8):
            sqk, rsqk = sq[k % 2], rsq[k % 2]
            if sqe == "act":
                fw.op("act", lambda e, k=k: e.activation(sqk[:], xb[:, k, :], AF.Square), reads=[rxb], writes=[rsqk])
            else:
                fw.op(sqe, lambda e, k=k: e.tensor_tensor(sqk[:], xb[:, k, :], xb[:, k, :], ALU.mult), reads=[rxb], writes=[rsqk])
            fw.op("pe", lambda e, k=k: e.matmul(pst[:], onesf[:], sqk[:], start=(k == 0), stop=(k == 7)),
                  reads=[rsqk, ronesf], writes=[rpst])
        fw.op("act", lambda e: e.activation(rstd[:], pst[:], AF.Sqrt, scale=1.0 / D, bias=self.eps_ap()),
              reads=[rpst, self.reps], writes=[rrstd])
        fw.op("dve", lambda e: e.reciprocal(rstd[:], rstd[:]), reads=[rrstd], writes=[rrstd])

    def norm_apply(self, xb, rxb, hT, rhT, rstd, rrstd, l, j, u, tmp, rtmp):
        fw = self.fw
        m_sh = 0 if j == 0 else 3
        for k in range(8):
            fw.op("dve", lambda e, k=k: e.tensor_tensor(tmp[:], xb[:, k, :], rstd[:], ALU.mult),
                  reads=[rxb, rrstd], writes=[rtmp])
            fw.op("dve", lambda e, k=k: e.tensor_scalar(hT[:, k, :], tmp[:], self.gm[:, l, j, k, u:u + 1],
                                                        self.mod[:, l, m_sh, k, u:u + 1], ALU.mult, ALU.add),
                  reads=[rtmp, self.rgm, self.rmod], writes=[rhT])

    def eps_ap(self):
        return self.epst[:, 0:1]

    def ensure_eps(self):
        if getattr(self, "epst", None) is None:
            fw = self.fw
            self.epst = fw.sbuf("epst", [128, 2], F32)
            self.reps = fw.res("epst")
            fw.op("dve", lambda e: e.memset(self.epst[:, 0:1], EPS), writes=[self.reps])
            fw.op("dve", lambda e: e.memset(self.epst[:, 1:2], 1.0), writes=[self.reps])

    def phase_p1(self, l):
        self.sq_eng = 'dve'
        fw, I, S, R, C = self.fw, self.I, self.S, self.R, self.C
        self.ensure_eps()
        xsrc, rxsrc = (I["xT"], R["xT"]) if l == 0 else (S["xn"], R["xn"])
        fw.phase_begin()
        with ExitStack() as ps:
            sb = lambda n, s, dt=F32: fw.sbuf(n, s, dt, ps)
            W = sb("w1", [128, 8, 2832], BF16)
            rW = fw.res("w1")
            stg = [(sb("stg", [128, 1416], F32), fw.res("stg", dma=True)) for _ in range(3)]
            self.stg_i = 0
            self.load_cast(W, rW, 0, 8, I["w_in"][l], 2832, stg, ["pool", "dve", "act"], 1416)
            wsp = sb("wsp", [128, 4, 128], BF16)
            rwsp = fw.res("wsp")
            wspf = sb("wspf", [128, 4, 128], F32)
            rwspf = fw.res("wspf", dma=True)
            self.ld(wspf[:], I["a_w_spT"][l], rwspf)
            fw.op("pool", lambda e: e.tensor_copy(wsp[:], wspf[:]), reads=[rwspf], writes=[rwsp])
            aln = sb("aln", [128, 512], F32)
            raln = fw.res("aln", dma=True)
            self.ld(aln[:], I["a_ln"][l].partition_broadcast(128), raln)
            absp, rabsp = C["a_b_sp"]
            identb, ridb = C["identb"]
            xb = [(sb("xb", [128, 8, 512]), fw.res("xb", dma=True)) for _ in range(2)]
            hT = sb("hT", [128, 8, 512], BF16); rhT = fw.res("hT")
            sq = [sb("sq", [128, 512], BF16) for _ in range(2)]; rsq = [fw.res("sq") for _ in range(2)]
            rstd = sb("rstd", [128, 512]); rrstd = fw.res("rstd")
            tmp = sb("tmp", [128, 512]); rtmp = fw.res("tmp")
            pst = fw.psum("pst", [128, 512], F32, ps); rpst = fw.res("pst", excl=True)
            pbig = fw.psum("pbig", [128, 4, 512], F32, ps)
            rbig = [fw.res("pbig%d" % i, excl=True) for i in range(4)]
            pfm = [(pbig[:, 0, :], rbig[0]), (pbig[:, 1, :], rbig[1])]
            ptm = [(pbig[:, 2, :], rbig[2]), (pbig[:, 3, :], rbig[3])]
            psA = fw.psum("psA", [128, 4, 256], F32, ps); rpsA = fw.res("psA", excl=True)
            ptr = fw.psum("ptr", [128, 8, 128], BF16, ps); rptr = fw.res("ptr", excl=True)
            ofm = [(sb("ofm", [128, 2, 512], BF16), fw.res("ofm", dma=True)) for _ in range(3)]
            ofd = [(sb("ofd", [128, 2, 512], F32), fw.res("ofd", dma=True)) for _ in range(2)]
            sgd = sb("sgd", [128, 512]); rsgd = fw.res("sgd")
            otm = [(sb("otm", [128, 4, 256], BF16), fw.res("otm", dma=True)) for _ in range(4)]
            oco = [(sb("oco", [128, 4, 256], F32), fw.res("oco", dma=True)) for _ in range(2)]
            ocv = [(sb("ocv", [128, 4, 256], F32), fw.res("ocv", dma=True)) for _ in range(2)]
            ocg = [(sb("ocg", [128, 4, 16], F32), fw.res("ocg", dma=True)) for _ in range(2)]
            yA = [(sb("yA", [128, 2, 512], BF16), fw.res("yA", dma=True)) for _ in range(2)]
            g1 = sb("g1", [128, 4, 512]); rg1 = fw.res("g1")
            gu = sb("gu", [128, 4, 512]); rgu = fw.res("gu")
            vc = sb("vc", [128, 4, 256]); rvc = fw.res("vc")
            vn = sb("vn", [128, 4, 256], BF16); rvn = fw.res("vn")
            st4 = sb("st4", [128, 16]); rst4 = fw.res("st4")
            yAt = sb("yAt", [128, 4, 256], BF16); ryAt = fw.res("yAt")
            nfm = ntm = nofm = 0
            for u in range(3):
                for b in range(self.NBK):
                    t0 = u * self.UL + b * 512
                    gi = u * self.NBK + b
                    xt, rx = xb[gi % 2]
                    self.ld(xt[:], xsrc[:, t0:t0 + 512].rearrange("(k p) t -> p k t", p=128), rx, rxsrc)
                    import os
                    cut = int(os.environ.get("KCUT", "99"))
                    if cut < 1:
                        continue
                    self.norm_mod(xt, rx, hT, rhT, sq, rsq, pst, rpst, rstd, rrstd, l, 0, u, tmp, rtmp)
                    if cut < 2:
                        continue
                    fm_jobs = [(512, "bq"), (768, "bk"), (1280, "cq"), (1536, "ck")]
                    for col0, name in fm_jobs:
                        ot, ro = ofm[nofm % 3]
                        nofm += 1
                        for c in range(2):
                            pt, rp = pfm[nfm % 2]
                            nfm += 1
                            for k in range(8):
                                fw.op("pe", lambda e, k=k, c=c: e.matmul(pt[:], W[:, k, col0 + c * 128:col0 + (c + 1) * 128], hT[:, k, :],
                                                                          start=(k == 0), stop=(k == 7)),
                                      reads=[rW, rhT], writes=[rp])
                            scale = 0.125 if name in ("bq", "cq") else 1.0
                            fw.op("act", lambda e, c=c: e.activation(ot[:, c, :], pt[:], AF.Identity, scale=scale),
                                  reads=[rp], writes=[ro])
                        self.stt(S[name][:, t0:t0 + 512].rearrange("(c p) t -> p c t", p=128), ot[:], ro, R[name])
                    if cut < 3:
                        continue
                    od, rod = ofd[gi % 2]
                    for c in range(2):
                        pa, rpa = pfm[nfm % 2]
                        nfm += 1
                        pg, rpg = pfm[nfm % 2]
                        nfm += 1
                        for k in range(8):
                            fw.op("pe", lambda e, k=k, c=c: e.matmul(pa[:], W[:, k, 2320 + c * 128:2320 + (c + 1) * 128], hT[:, k, :],
                                                                      start=(k == 0), stop=(k == 7)), reads=[rW, rhT], writes=[rpa])
                        for k in range(8):
                            fw.op("pe", lambda e, k=k, c=c: e.matmul(pg[:], W[:, k, 2576 + c * 128:2576 + (c + 1) * 128], hT[:, k, :],
                                                                      start=(k == 0), stop=(k == 7)), reads=[rW, rhT], writes=[rpg])
                        fw.op("act", lambda e: e.activation(sgd[:], pg[:], AF.Sigmoid), reads=[rpg], writes=[rsgd])
                        fw.op("dve", lambda e, c=c: e.tensor_tensor(od[:, c, :], pa[:], sgd[:], ALU.mult),
                              reads=[rpa, rsgd], writes=[rod])
                    self.stt(S["dy"][:, t0:t0 + 512].rearrange("(c p) t -> p c t", p=128), od[:], rod, R["dy"])
                    if cut < 4:
                        continue
                    obv, robv = otm[(2 * gi) % 4]
                    okt, rokt = otm[(2 * gi + 1) % 4]
                    ovt, rovt = ocv[gi % 2]
                    oo, roo = oco[gi % 2]
                    og, rog = ocg[gi % 2]
                    ya, rya = yA[gi % 2]
                    for ch in range(4):
                        ts_ = slice(ch * 128, (ch + 1) * 128)

                        def tm_mm(col0, ncol):
                            nonlocal ntm
                            pt, rp = ptm[ntm % 2]
                            ntm += 1
                            for k in range(8):
                                fw.op("pe", lambda e, k=k: e.matmul(pt[:, 0:ncol], hT[:, k, ts_], W[:, k, col0:col0 + ncol],
                                                                    start=(k == 0), stop=(k == 7)), reads=[rW, rhT], writes=[rp])
                            return pt, rp
                        skip = os.environ.get("KSKIP", "").split(",")
                        if "tm" in skip:
                            continue
                        if "bv" not in skip:
                            pt, rp = tm_mm(1024, 256)
                            if "bve" not in skip:
                                fw.op("act", lambda e: e.copy(obv[:, ch, :], pt[:, 0:256]), reads=[rp], writes=[robv])
                        if "ckv" not in skip:
                            pt, rp = tm_mm(1536, 512)
                            if "e1" not in skip:
                                fw.op("act", lambda e: e.copy(okt[:, ch, :], pt[:, 0:256]), reads=[rp], writes=[rokt])
                            if "e2" not in skip:
                                fw.op("dve", lambda e: e.tensor_copy(ovt[:, ch, :], pt[:, 256:512]), reads=[rp], writes=[rovt])
                        if "og" in skip:
                            continue
                        pt, rp = tm_mm(2048, 272)
                        fw.op("act", lambda e: e.activation(oo[:, ch, :], pt[:, 0:256], AF.Sigmoid), reads=[rp], writes=[roo])
                        fw.op("dve", lambda e: e.tensor_copy(og[:, ch, :], pt[:, 256:272]), reads=[rp], writes=[rog])
                    for ch in range(4):
                        for k in range(8):
                            fw.op("pe", lambda e, k=k, ch=ch: e.matmul(pbig[:, ch, :], hT[:, k, ch * 128:(ch + 1) * 128], W[:, k, 0:512],
                                                                        start=(k == 0), stop=(k == 7)), reads=[rW, rhT], writes=[rbig[ch]])
                    R4 = list(rbig)
                    fw.op("act", lambda e: e.activation(g1[:], pbig[:], AF.Square), reads=R4, writes=[rg1])
                    fw.op("dve", lambda e: e.tensor_scalar(g1[:], g1[:], 0.044715, 1.0, ALU.mult, ALU.add), reads=[rg1], writes=[rg1])
                    fw.op("dve", lambda e: e.tensor_tensor(g1[:], g1[:], pbig[:], ALU.mult), reads=[rg1] + R4, writes=[rg1])
                    fw.op("act", lambda e: e.activation(g1[:], g1[:], AF.Sigmoid, scale=1.5957691216), reads=[rg1], writes=[rg1])
                    fw.op("dve", lambda e: e.tensor_tensor(gu[:], g1[:], pbig[:], ALU.mult), reads=[rg1] + R4, writes=[rgu])
                    guv = gu[:, :, 256:512]
                    g1v = g1[:, :, 0:256]
                    bc4 = lambda ap: ap.unsqueeze(2).to_broadcast([128, 4, 256])
                    fw.op("dve", lambda e: e.reduce_sum(st4[:, 0:4], guv, AX.X), reads=[rgu], writes=[rst4])
                    fw.op("dve", lambda e: e.tensor_scalar(st4[:, 4:8], st4[:, 0:4], 1.0 / 256, None, ALU.mult), reads=[rst4], writes=[rst4])
                    fw.op("dve", lambda e: e.tensor_tensor(vc[:], guv, bc4(st4[:, 4:8]), ALU.subtract), reads=[rgu, rst4], writes=[rvc])
                    fw.op("dve", lambda e: e.tensor_tensor(g1v, vc[:], vc[:], ALU.mult), reads=[rvc], writes=[rg1])
                    fw.op("dve", lambda e: e.reduce_sum(st4[:, 8:12], g1v, AX.X), reads=[rg1], writes=[rst4])
                    fw.op("act", lambda e: e.activation(st4[:, 12:16], st4[:, 8:12], AF.Sqrt, scale=1.0 / 256, bias=self.eps_ap()),
                          reads=[rst4, self.reps], writes=[rst4])
                    fw.op("dve", lambda e: e.reciprocal(st4[:, 12:16], st4[:, 12:16]), reads=[rst4], writes=[rst4])
                    fw.op("dve", lambda e: e.tensor_tensor(vc[:], vc[:], bc4(st4[:, 12:16]), ALU.mult), reads=[rvc, rst4], writes=[rvc])
                    fw.op("dve", lambda e: e.tensor_tensor(vc[:], vc[:], aln[:, 0:256].unsqueeze(1).to_broadcast([128, 4, 256]), ALU.mult),
                          reads=[rvc, raln], writes=[rvc])
                    fw.op("dve", lambda e: e.tensor_tensor(vn[:], vc[:], aln[:, 256:512].unsqueeze(1).to_broadcast([128, 4, 256]), ALU.add),
                          reads=[rvc, raln], writes=[rvn])
                    for ch in range(4):
                        for g in range(4):
                            fw.op("pe", lambda e, g=g, ch=ch: e.matmul(psA[:, ch, g * 64:(g + 1) * 64], wsp[:, g, :], vn[:, ch, g * 64:(g + 1) * 64],
                                                                        start=True, stop=True), reads=[rwsp, rvn], writes=[rpsA])
                    for g in range(4):
                        fw.op("dve", lambda e, g=g: e.scalar_tensor_tensor(yAt[:, :, g * 64:(g + 1) * 64], psA[:, :, g * 64:(g + 1) * 64],
                                                                           absp[:, l, g:g + 1], gu[:, :, g * 64:(g + 1) * 64],
                                                                           ALU.add, ALU.mult),
                              reads=[rpsA, rabsp, rgu], writes=[ryAt])
                    for ch in range(4):
                        for c in range(2):
                            fw.op("pe", lambda e, c=c, ch=ch: e.transpose(ptr[:, ch * 2 + c, :], yAt[:, ch, c * 128:(c + 1) * 128], identb[:]),
                                  reads=[ryAt, ridb], writes=[rptr])
                    fw.op("act", lambda e: e.copy(ya[:].rearrange("p c (h t) -> p h c t", t=128),
                                                  ptr[:].rearrange("p (h c) t -> p h c t", c=2)), reads=[rptr], writes=[rya])
                    tsl = slice(t0, t0 + 512)
                    if "st" in os.environ.get("KSKIP", "").split(","):
                        continue
                    self.stt(S["bv"][tsl, :].rearrange("(c p) f -> p c f", p=128), obv[:], robv, R["bv"])
                    self.stt(S["ckt"][tsl, :].rearrange("(c p) f -> p c f", p=128), okt[:], rokt, R["ckt"])
                    self.stt(S["cvt"][tsl, :].rearrange("(c p) f -> p c f", p=128), ovt[:], rovt, R["cvt"])
                    self.stt(S["co"][tsl, :].rearrange("(c p) f -> p c f", p=128), oo[:], roo, R["co"])
                    self.stt(S["cg"][tsl, :].rearrange("(c p) f -> p c f", p=128), og[:], rog, R["cg"])
                    self.stt(S["yall"][0:256, tsl].rearrange("(c p) t -> p c t", p=128), ya[:], rya, R["yall"])
            fw.phase_end()

    def attn_plan(self):
        NB2 = self.NB2
        plan = []
        for u in range(3):
            for b in range(NB2):
                if u == 2 or (u == 0 and b < NB2 - 2) or (u == 1 and b >= 2):
                    if b == 0 and u != 1:
                        cls = "TOP0"
                    elif b == 1 and u != 1:
                        cls = "TOP1"
                    elif b == NB2 - 2 and u != 0:
                        cls = "BOT1"
                    elif b == NB2 - 1 and u != 0:
                        cls = "BOT0"
                    else:
                        cls = "INT"
                elif u == 0:
                    cls = "JA1" if b == NB2 - 2 else "JA0"
                else:
                    cls = "JB0" if b == 0 else "JB1"
                s0, offs = CLS[cls]
                ents = []
                for i, o in enumerate(offs):
                    kp = b + o
                    ku = u
                    if kp >= NB2:
                        ku, kp = u + 1, kp - NB2
                    elif kp < 0:
                        ku, kp = u - 1, kp + NB2
                    assert 0 <= ku <= 2 and (ku == u or (u, ku) in ((0, 1), (1, 0)))
                    ents.append((ku, kp, s0 + i))
                plan.append((u, b, ents))
        return plan

    def gen_attn(self, l, ptr, rptr, ps):
        fw, I, S, R, C = self.fw, self.I, self.S, self.R, self.C
        UL, NB2 = self.UL, self.NB2
        identb, ridb = C["identb"]
        if True:
            sb = lambda n, s, dt=F32: fw.sbuf(n, s, dt, ps)
            bt = sb("bt", [128, NSLOT * 4, 128], BF16); rbt = fw.res("bt")
            stg = [(sb("bstg", [128, 8, 128], F32), fw.res("bstg", dma=True)) for _ in range(2)]
            n = 0
            for s0 in range(0, NSLOT * 4, 8):
                s1 = min(s0 + 8, NSLOT * 4)
                stt_, rs = stg[n % 2]
                n += 1
                self.ld(stt_[:, 0:s1 - s0, :], I["rpbt"][l, :, s0:s1, :], rs)
                fw.op("pool", lambda e, s0=s0, s1=s1, stt_=stt_: e.tensor_copy(bt[:, s0:s1, :], stt_[:, 0:s1 - s0, :]),
                      reads=[rs], writes=[rbt])
            kT = sb("kT", [128, 2, 2 * UL], BF16); rkT = fw.res("kT", dma=True)
            V = sb("V", [128, 2 * self.NCH, 4, 65], BF16); rV = fw.res("V", dma=True)
            fw.op("pool", lambda e: e.memset(V[:], 1.0), writes=[rV])
            qT = [(sb("qT", [128, 2, 512], BF16), fw.res("qT", dma=True)) for _ in range(2)]
            pS = [(fw.psum("pS", [128, 8, 128], F32, ps), fw.res("pS", excl=True)) for _ in range(1)]
            pO = [(fw.psum("pO", [128, 4, 65], F32, ps), fw.res("pO", excl=True)) for _ in range(1)]
            PT = [(sb("PT", [128, 8, 128], BF16), fw.res("PT")) for _ in range(3)]
            rc = sb("rc", [128, 4]); rrc = fw.res("rc")
            yB = sb("yB", [128, 4, 64], BF16); ryB = fw.res("yB")
            yo = [(sb("yBo", [128, 2, 512], BF16), fw.res("yBo", dma=True)) for _ in range(2)]
            plan = self.attn_plan()
            nps = npt = nq = 0
            for grp in (0, 2):
                units = (0, 1) if grp == 0 else (2,)
                base = grp * UL
                ntok = len(units) * UL
                self.ld(kT[:, :, 0:ntok], S["bk"][:, base:base + ntok].rearrange("(c p) t -> p c t", p=128), rkT, R["bk"])
                for h in range(4):
                    self.ld(V[:, 0:ntok // 128, h, 0:64],
                            S["bv"][base:base + ntok, h * 64:(h + 1) * 64].rearrange("(c p) d -> p c d", p=128),
                            rV, R["bv"])
                for (u, b, ents) in plan:
                    if u not in units:
                        continue
                    if b % 4 == 0:
                        qt, rq = qT[nq % 2]
                        yot, ryo = yo[nq % 2]
                        nq += 1
                        tq0 = u * UL + b * 128
                        self.ld(qt[:], S["bq"][:, tq0:tq0 + 512].rearrange("(c p) t -> p c t", p=128), rq, R["bq"])
                    qs = slice((b % 4) * 128, (b % 4 + 1) * 128)
                    po, rpo = pO[0]
                    ne = len(ents)
                    for h in range(4):
                        hp = slice((h % 2) * 64, (h % 2) * 64 + 64)
                        hc = h // 2
                        pst_, rps = pS[0]
                        nps += 1
                        for i, (ku, kp, slot) in enumerate(ents):
                            k0 = (ku * UL - base) + kp * 128
                            fw.op("pe", lambda e, i=i, k0=k0: e.matmul(pst_[:, i, :], kT[hp, hc, k0:k0 + 128], qt[hp, hc, qs],
                                                                        start=True, stop=False), reads=[rkT, rq], writes=[rps])
                            fw.op("pe", lambda e, i=i, slot=slot: e.matmul(pst_[:, i, :], identb[:], bt[:, slot * 4 + h, :],
                                                                            start=False, stop=True), reads=[ridb, rbt], writes=[rps])
                        pt_, rpt = PT[npt % 3]
                        npt += 1
                        n1 = min(ne, 4)
                        fw.op("act", lambda e: e.activation(pt_[:, 0:n1, :], pst_[:, 0:n1, :], AF.Exp), reads=[rps], writes=[rpt])
                        if ne > 4:
                            fw.op("act", lambda e: e.activation(pt_[:, 4:ne, :], pst_[:, 4:ne, :], AF.Exp), reads=[rps], writes=[rpt])
                        for i, (ku, kp, slot) in enumerate(ents):
                            vc_ = (ku * UL - base) // 128 + kp
                            fw.op("pe", lambda e, i=i, vc_=vc_: e.matmul(po[:, h, :], pt_[:, i, :], V[:, vc_, h, :],
                                                                          start=(i == 0), stop=(i == ne - 1)),
                                  reads=[rpt, rV], writes=[rpo])
                    fw.op("dve", lambda e: e.reciprocal(rc[:], po[:, :, 64]), reads=[rpo], writes=[rrc])
                    fw.op("dve", lambda e: e.tensor_tensor(yB[:], po[:, :, 0:64], rc[:].unsqueeze(2).to_broadcast([128, 4, 64]), ALU.mult),
                          reads=[rpo, rrc], writes=[ryB])
                    for c in range(2):
                        fw.op("pe", lambda e, c=c: e.transpose(ptr[:, c, :], yB[:, 2 * c:2 * c + 2, :], identb[:]),
                              reads=[ryB, ridb], writes=[rptr])
                    fw.op("act", lambda e: e.copy(yot[:, :, qs], ptr[:, 0:2, :]), reads=[rptr], writes=[ryo])
                    if b % 4 == 3:
                        self.stt(S["yall"][256:512, tq0:tq0 + 512].rearrange("(c p) t -> p c t", p=128), yot[:], ryo, R["yall"])
                    yield

    def gen_mlstm(self, l, ptr, rptr, ps):
        fw, I, S, R, C = self.fw, self.I, self.S, self.R, self.C
        UL, NCH = self.UL, self.NCH
        identb, ridb = C["identb"]
        onesf, ronesf = C["onesf"]
        triu, rtriu = C["triu"]
        tril, rtril = C["tril"]
        sel, rsel = C["sel"]
        link, rlink = C["link"]
        if True:
            sb = lambda n, s, dt=F32: fw.sbuf(n, s, dt, ps)
            gb = sb("gb", [128, 16]); rgb = fw.res("gb", dma=True)
            self.ld(gb[:], I["c_gate_b"][l].partition_broadcast(128), rgb)
            hng = sb("hng", [128, 256]); rhng = fw.res("hng", dma=True)
            self.ld(hng[:], I["c_hnorm"][l].partition_broadcast(128), rhng)
            NBUF = 2
            qTb = [(sb("cqT", [128, 2, 512], BF16), fw.res("cqT", dma=True)) for _ in range(NBUF)]
            kTb = [(sb("ckT", [128, 2, 512], BF16), fw.res("ckT", dma=True)) for _ in range(NBUF)]
            ktb = [(sb("ckt", [128, 4, 256], BF16), fw.res("ckt", dma=True)) for _ in range(NBUF)]
            vtb = [(sb("cvt", [128, 4, 4, 65], F32), fw.res("cvt", dma=True)) for _ in range(NBUF)]
            for vt_, rv_ in vtb:
                fw.op("pool", lambda e, vt_=vt_: e.memset(vt_[:], 1.0), writes=[rv_])
            gtb = [(sb("cgt", [128, 4, 16]), fw.res("cgt", dma=True)) for _ in range(NBUF)]
            otb = [(sb("cot", [128, 4, 256]), fw.res("cot", dma=True)) for _ in range(NBUF)]
            hfb = [(sb("chf", [128, 4, 256]), fw.res("chf", dma=True)) for _ in range(NBUF)]
            yCo = [(sb("yCo", [128, 2, 512], BF16), fw.res("yCo", dma=True)) for _ in range(2)]
            CT = sb("CT", [128, 2, 65]); rCT = fw.res("CT")
            CTb = sb("CTb", [128, 2, 65], BF16); rCTb = fw.res("CTb")
            klo = sb("klo", [128, 8, 128], BF16); rklo = fw.res("klo")
            khi = sb("khi", [128, 8, 128], BF16); rkhi = fw.res("khi")
            fw.op("pool", lambda e: e.memset(klo[:], 0.0), writes=[rklo])
            fw.op("pool", lambda e: e.memset(khi[:], 0.0), writes=[rkhi])
            g = sb("g", [128, 8]); rg = fw.res("g")
            lf = sb("lf", [128, 4]); rlf = fw.res("lf")
            ex = sb("ex", [128, 12]); rex = fw.res("ex")
            Fm4 = sb("Fm4", [128, 4, 2]); rFm = fw.res("Fm4")
            gsb4 = sb("gsb4", [128, 4, 8]); rgsb = fw.res("gsb4")
            g4 = sb("g4", [128, 4, 8])
            lf4 = sb("lf4", [128, 4, 4])
            ex4 = sb("ex4", [128, 4, 8])
            wm = sb("wm", [128, 4, 128], BF16); rwm = fw.res("wm")
            va4 = sb("va4", [128, 16, 65], BF16); rva = fw.res("va4")
            tn = sb("tn", [128, 4, 65]); rtn = fw.res("tn")
            dd = sb("dd", [128, 4]); rdd = fw.res("dd")
            hd4 = sb("hd4", [128, 4, 256]); rhd = fw.res("hd4")
            sq4 = sb("sq4", [128, 4, 256]); rsqh = fw.res("sq4")
            ss4 = sb("ss4", [128, 16]); rss = fw.res("ss4")
            yC4 = sb("yC4", [128, 4, 256], BF16); ryCt = fw.res("yC4")
            pGC = fw.psum("pGC", [128, 512], F32, ps); rpG = fw.res("pGC", excl=True)
            pG4 = pGC[:, 0:64].rearrange("p (a b) -> p a b", b=16)
            pC = pGC[:, 64:194].rearrange("p (a b) -> p a b", b=65); rpC = rpG
            pQK = fw.psum("pQK", [128, 2, 128], F32, ps); rpQK = fw.res("pQK", excl=True)
            pQK2 = fw.psum("pQK2", [128, 2, 128], F32, ps); rpQK2 = fw.res("pQK2", excl=True)
            pN = fw.psum("pN", [128, 4, 65], F32, ps); rpN = fw.res("pN", excl=True)
            nld = 0
            for d in (0, 1):
                go = 8 * d
                tri, rtri = (triu, rtriu) if d == 0 else (tril, rtril)
                unit_order = (0, 1, 2) if d == 0 else (1, 0, 2)
                for ui, u in enumerate(unit_order):
                    if ui == 1:
                        fw.op("dve", lambda e: e.tensor_scalar(CT[:], CT[:], link[:, 0:1], None, ALU.mult),
                              reads=[rCT, rlink], writes=[rCT])
                    else:
                        fw.op("dve", lambda e: e.memset(CT[:], 0.0), writes=[rCT])
                    blocks = range(self.NBK) if d == 0 else range(self.NBK - 1, -1, -1)
                    for b in blocks:
                        t0 = u * UL + b * 512
                        tsl = slice(t0, t0 + 512)
                        bi = nld % NBUF
                        nld += 1
                        qt, rq = qTb[bi]; kt, rk = kTb[bi]; ktm, rktm = ktb[bi]; vtm, rvtm = vtb[bi]
                        gt, rgt = gtb[bi]; ot, rot = otb[bi]; hf, rhf = hfb[bi]
                        self.ld(qt[:], S["cq"][:, tsl].rearrange("(c p) t -> p c t", p=128), rq, R["cq"])
                        self.ld(kt[:], S["ck"][:, tsl].rearrange("(c p) t -> p c t", p=128), rk, R["ck"])
                        self.ld(ktm[:], S["ckt"][tsl, :].rearrange("(c p) f -> p c f", p=128), rktm, R["ckt"])
                        for h in range(4):
                            self.ld(vtm[:, :, h, 0:64], S["cvt"][tsl, h * 64:(h + 1) * 64].rearrange("(c p) d -> p c d", p=128), rvtm, R["cvt"])
                        self.ld(gt[:], S["cg"][tsl, :].rearrange("(c p) f -> p c f", p=128), rgt, R["cg"])
                        if d == 1:
                            self.ld(ot[:], S["co"][tsl, :].rearrange("(c p) f -> p c f", p=128), rot, R["co"])
                            self.ld(hf[:], S["chf"][tsl, :].rearrange("(c p) f -> p c f", p=128), rhf, R["chf"])
                            yco, ryco = yCo[nld % 2]
                        fw.op("dve", lambda e: e.tensor_tensor(g4[:], gt[:, :, go:go + 8], gb[:, go:go + 8].unsqueeze(1).to_broadcast([128, 4, 8]), ALU.add),
                              reads=[rgt, rgb], writes=[rg])
                        fw.op("act", lambda e: e.activation(lf4[:], g4[:, :, 4:8], AF.Exp, scale=-1.0), reads=[rg], writes=[rlf])
                        fw.op("act", lambda e: e.activation(lf4[:], lf4[:], AF.Ln, scale=1.0, bias=self.epst[:, 1:2]),
                              reads=[rlf, self.reps], writes=[rlf])
                        fw.op("dve", lambda e: e.tensor_scalar(lf4[:], lf4[:], -1.0, None, ALU.mult), reads=[rlf], writes=[rlf])
                        for c4 in range(4):
                            fw.op("pe", lambda e, c4=c4: e.matmul(pG4[:, c4, 0:4], tri[:], lf4[:, c4, :], start=True, stop=True), reads=[rtri, rlf], writes=[rpG])
                            fw.op("pe", lambda e, c4=c4: e.matmul(pG4[:, c4, 4:8], onesf[:], lf4[:, c4, :], start=True, stop=True), reads=[ronesf, rlf], writes=[rpG])
                            fw.op("pe", lambda e, c4=c4: e.matmul(pG4[:, c4, 8:10], sel[:, 0, :], lf4[:, c4, 0:4:2], start=True, stop=False),
                                  reads=[rsel, rlf], writes=[rpG])
                            fw.op("pe", lambda e, c4=c4: e.matmul(pG4[:, c4, 8:10], sel[:, 1, :], lf4[:, c4, 1:4:2], start=False, stop=True),
                                  reads=[rsel, rlf], writes=[rpG])
                        fw.op("act", lambda e: e.copy(gsb4[:], pG4[:, :, 0:8]), reads=[rpG], writes=[rgsb])
                        fw.op("act", lambda e: e.activation(Fm4[:], pG4[:, :, 8:10], AF.Exp), reads=[rpG], writes=[rFm])
                        fw.op("dve", lambda e: e.tensor_tensor(ex4[:, :, 4:8], gsb4[:, :, 0:4], gsb4[:, :, 4:8], ALU.subtract), reads=[rgsb], writes=[rex])
                        fw.op("dve", lambda e: e.tensor_tensor(ex4[:, :, 0:4], g4[:, :, 0:4], ex4[:, :, 4:8], ALU.subtract), reads=[rg, rex], writes=[rex])
                        fw.op("act", lambda e: e.activation(ex4[:], ex4[:], AF.Exp), reads=[rex], writes=[rex])
                        for c4 in range(4):
                            fw.op("dve", lambda e, c4=c4: e.tensor_tensor(va4[:, c4 * 4:c4 * 4 + 4, :], vtm[:, c4, :, :],
                                                                           ex4[:, c4, 0:4].unsqueeze(2).to_broadcast([128, 4, 65]), ALU.mult),
                                  reads=[rvtm, rex], writes=[rva])
                        k4 = ktm[:].rearrange("p c (a b) -> p (c a) b", b=128)
                        fw.op("pool", lambda e: e.tensor_copy(klo[:, :, 0:64], k4[:, :, 0:64]), reads=[rktm], writes=[rklo])
                        fw.op("pool", lambda e: e.tensor_copy(khi[:, :, 64:128], k4[:, :, 64:128]), reads=[rktm], writes=[rkhi])
                        chunks = list(range(4)) if d == 0 else list(range(3, -1, -1))
                        for ch in chunks:
                            cs = slice(ch * 128, (ch + 1) * 128)
                            if int(os.environ.get('KCUTM', '99')) < 5:
                                continue
                            for h in range(4):
                                hp = slice((h % 2) * 64, (h % 2) * 64 + 64)
                                pq_, rpq_ = (pQK, rpQK) if h % 2 == 0 else (pQK2, rpQK2)
                                fw.op("pe", lambda e, h=h, hp=hp: e.matmul(pq_[:, h // 2, :], kt[hp, h // 2, cs], qt[hp, h // 2, cs],
                                                                            start=True, stop=True), reads=[rk, rq], writes=[rpq_])
                            wm4 = wm[:].rearrange("p (j q) t -> p j q t", q=2)
                            fw.op("dve", lambda e: e.tensor_tensor(wm4[:, :, 0, :], pQK[:], tri[:].unsqueeze(1).to_broadcast([128, 2, 128]), ALU.mult),
                                  reads=[rpQK, rtri], writes=[rwm])
                            fw.op("dve", lambda e: e.tensor_tensor(wm4[:, :, 1, :], pQK2[:], tri[:].unsqueeze(1).to_broadcast([128, 2, 128]), ALU.mult),
                                  reads=[rpQK2, rtri], writes=[rwm])
                            if int(os.environ.get('KCUTM', '99')) < 6:
                                continue
                            fw.op("dve", lambda e: e.tensor_tensor(CTb[:], CT[:], Fm4[:, ch, :].unsqueeze(2).to_broadcast([128, 2, 65]), ALU.mult),
                                  reads=[rCT, rFm], writes=[rCTb])
                            for h in range(4):
                                hp = slice((h % 2) * 64, (h % 2) * 64 + 64)
                                fw.op("pe", lambda e, h=h: e.matmul(pN[:, h, :], wm[:, h, :], va4[:, ch * 4 + h, :], start=True, stop=False),
                                      reads=[rwm, rva], writes=[rpN])
                                fw.op("pe", lambda e, h=h, hp=hp: e.matmul(pN[:, h, :], qt[hp, h // 2, cs], CTb[hp, h // 2, :], start=False, stop=True),
                                      reads=[rq, rCTb], writes=[rpN])
                            if int(os.environ.get('KCUTM', '99')) < 7:
                                continue
                            for p in range(2):
                                fw.op("pe", lambda e, p=p: e.matmul(pC[:, p, :], klo[:, ch * 2 + p, :], va4[:, ch * 4 + 2 * p, :], start=True, stop=False),
                                      reads=[rklo, rva], writes=[rpC])
                                fw.op("pe", lambda e, p=p: e.matmul(pC[:, p, :], khi[:, ch * 2 + p, :], va4[:, ch * 4 + 2 * p + 1, :], start=False, stop=True),
                                      reads=[rkhi, rva], writes=[rpC])
                            for p in range(2):
                                fw.op("dve", lambda e, p=p: e.scalar_tensor_tensor(CT[:, p, :], CT[:, p, :], Fm4[:, ch, p:p + 1], pC[:, p, :],
                                                                                   ALU.mult, ALU.add),
                                      reads=[rCT, rFm, rpC], writes=[rCT])
                            if int(os.environ.get('KCUTM', '99')) < 8:
                                continue
                            fw.op("dve", lambda e: e.tensor_tensor(tn[:], pN[:], ex4[:, ch, 4:8].unsqueeze(2).to_broadcast([128, 4, 65]), ALU.mult),
                                  reads=[rpN, rex], writes=[rtn])
                            fw.op("dve", lambda e: e.scalar_tensor_tensor(dd[:], tn[:, :, 64], -1.0, tn[:, :, 64], ALU.mult, ALU.max), reads=[rtn], writes=[rdd])
                            fw.op("dve", lambda e: e.tensor_scalar(dd[:], dd[:], 1.0, None, ALU.max), reads=[rdd], writes=[rdd])
                            fw.op("dve", lambda e: e.reciprocal(dd[:], dd[:]), reads=[rdd], writes=[rdd])
                            if d == 0:
                                fw.op("dve", lambda e: e.tensor_tensor(hf[:, ch, :].rearrange("p (h d) -> p h d", d=64), tn[:, :, 0:64],
                                                                       dd[:].unsqueeze(2).to_broadcast([128, 4, 64]), ALU.mult),
                                      reads=[rtn, rdd], writes=[rhf])
                            else:
                                fw.op("dve", lambda e: e.tensor_tensor(hd4[:, ch, :].rearrange("p (h d) -> p h d", d=64), tn[:, :, 0:64],
                                                                       dd[:].unsqueeze(2).to_broadcast([128, 4, 64]), ALU.mult),
                                      reads=[rtn, rdd], writes=[rhd])
                                if ch == chunks[-1]:
                                    hdh = hd4[:].rearrange("p c (h d) -> p (c h) d", d=64)
                                    fw.op("dve", lambda e: e.tensor_tensor(hd4[:], hd4[:], hf[:], ALU.add), reads=[rhd, rhf], writes=[rhd])
                                    fw.op("dve", lambda e: e.tensor_tensor(sq4[:], hd4[:], hd4[:], ALU.mult), reads=[rhd], writes=[rsqh])
                                    fw.op("dve", lambda e: e.reduce_sum(ss4[:], sq4[:].rearrange("p c (h d) -> p (c h) d", d=64), AX.X), reads=[rsqh], writes=[rss])
                                    fw.op("act", lambda e: e.activation(ss4[:], ss4[:], AF.Ln, scale=1.0 / 64, bias=self.eps_ap()),
                                          reads=[rss, self.reps], writes=[rss])
                                    fw.op("act", lambda e: e.activation(ss4[:], ss4[:], AF.Exp, scale=-0.5), reads=[rss], writes=[rss])
                                    fw.op("dve", lambda e: e.tensor_tensor(hdh, hdh, ss4[:].unsqueeze(2).to_broadcast([128, 16, 64]), ALU.mult),
                                          reads=[rhd, rss], writes=[rhd])
                                    fw.op("dve", lambda e: e.tensor_tensor(hd4[:], hd4[:], hng[:].unsqueeze(1).to_broadcast([128, 4, 256]), ALU.mult),
                                          reads=[rhd, rhng], writes=[rhd])
                                    fw.op("dve", lambda e: e.tensor_tensor(yC4[:], hd4[:], ot[:], ALU.mult), reads=[rhd, rot], writes=[ryCt])
                                    for c4 in range(4):
                                        for c in range(2):
                                            fw.op("pe", lambda e, c=c, c4=c4: e.transpose(ptr[:, c4 * 2 + c, :], yC4[:, c4, c * 128:(c + 1) * 128], identb[:]),
                                                  reads=[ryCt, ridb], writes=[rptr])
                                    fw.op("act", lambda e: e.copy(yco[:].rearrange("p c (h t) -> p h c t", t=128),
                                                                  ptr[:].rearrange("p (h c) t -> p h c t", c=2)), reads=[rptr], writes=[ryco])
                            yield
                        if d == 0:
                            self.stt(S["chf"][tsl, :].rearrange("(c p) f -> p c f", p=128), hf[:], rhf, R["chf"])
                        else:
                            self.stt(S["yall"][512:768, tsl].rearrange("(c p) t -> p c t", p=128), yco[:], ryco, R["yall"])

    def phase_bc(self, l):
        fw = self.fw
        fw.phase_begin()
        with ExitStack() as ps:
            ptr = fw.psum("ptrbc", [128, 8, 128], BF16, ps)
            rptr = fw.res("ptrbc", excl=True)
            ga = self.gen_attn(l, ptr, rptr, ps)
            gm = self.gen_mlstm(l, ptr, rptr, ps)
            live = [[ga, 1], [gm, 2]]
            while live:
                for ent in list(live):
                    g, r = ent
                    try:
                        for _ in range(r):
                            next(g)
                    except StopIteration:
                        live.remove(ent)
            fw.phase_end()

    def phase_conv(self, l):
        fw, I, S, R, C = self.fw, self.I, self.S, self.R, self.C
        UL = self.UL
        onesb, ronesb = C["onesb"]
        link, rlink = C["link"]
        cw, rcw = C["d_conv_wT"]
        dv, rdv = C["d_vec"]
        fw.phase_begin()
        with ExitStack() as ps:
            sb = lambda n, s, dt=F32: fw.sbuf(n, s, dt, ps)
            yp = [(sb("yp", [128, 2, UL + 30]), fw.res("yp", dma=True)) for _ in range(2)]
            acc = sb("acc", [128, 2, 512]); racc2 = [fw.res("acc0"), fw.res("acc1")]
            identf, ridf = C["identf"]
            dg = sb("dg", [128, 2, 31, 128], BF16); rdg = fw.res("dg")
            for cc in range(2):
                for k in range(31):
                    en = "dve" if (k % 2 == 0) else "pool"
                    fw.op(en, lambda e, cc=cc, k=k: e.tensor_scalar(dg[:, cc, k, :], identf[:], cw[:, l, cc, k:k + 1], None, ALU.mult),
                          reads=[ridf, rcw], writes=[rdg])
            ypb = [(sb("ypb", [128, 2, UL + 30], BF16), fw.res("ypb")) for _ in range(2)]
            pcv = [(fw.psum("pcv", [128, 512], F32, ps), fw.res("pcv", excl=True)) for _ in range(2)]
            accb = sb("accb", [128, 2, 512], BF16); raccb = fw.res("accb")
            sqb = sb("sqb", [128, 2, 512], BF16); rsqb = fw.res("sqb")
            m2 = sb("m2", [128, 512]); rm2 = fw.res("m2")
            rs = sb("rs", [128, 512]); rrs = fw.res("rs")
            tt = sb("tt", [128, 512]); rtt = fw.res("tt")
            yo = [(sb("yDo", [128, 2, 512], BF16), fw.res("yDo", dma=True)) for _ in range(2)]
            pM = fw.psum("pM", [128, 512], F32, ps); rpM = fw.res("pM", excl=True)
            pQ = fw.psum("pQ", [128, 512], F32, ps); rpQ = fw.res("pQ", excl=True)
            no = 0
            for u in range(3):
                ypt, ryp = yp[u % 2]
                fw.op("pool", lambda e: e.memset(ypt[:, :, 0:15], 0.0), writes=[ryp])
                fw.op("pool", lambda e: e.memset(ypt[:, :, UL + 15:UL + 30], 0.0), writes=[ryp])
                src = S["dy"].rearrange("(c p) t -> p c t", p=128)
                self.ld(ypt[:, :, 15:15 + UL], src[:, :, u * UL:(u + 1) * UL], ryp, R["dy"])
                if u == 0:
                    self.ld(ypt[:, :, UL + 15:UL + 30], src[:, :, UL:UL + 15], ryp, R["dy"])
                    fw.op("pool", lambda e: e.tensor_scalar(ypt[:, :, UL + 15:UL + 30], ypt[:, :, UL + 15:UL + 30], link[:, 0:1], None, ALU.mult),
                          reads=[ryp, rlink], writes=[ryp])
                elif u == 1:
                    self.ld(ypt[:, :, 0:15], src[:, :, UL - 15:UL], ryp, R["dy"])
                    fw.op("pool", lambda e: e.tensor_scalar(ypt[:, :, 0:15], ypt[:, :, 0:15], link[:, 0:1], None, ALU.mult),
                          reads=[ryp, rlink], writes=[ryp])
                ypbt, rypb = ypb[u % 2]
                fw.op("act", lambda e: e.copy(ypbt[:, 0, :], ypt[:, 0, :]), reads=[ryp], writes=[rypb])
                fw.op("pool", lambda e: e.tensor_copy(ypbt[:, 1, :], ypt[:, 1, :]), reads=[ryp], writes=[rypb])
                for b in range(self.NBK):
                    t0 = b * 512
                    for c in range(2):
                        pct, rpc = pcv[c]
                        for k in range(31):
                            fw.op("pe", lambda e, c=c, k=k: e.matmul(pct[:], dg[:, c, k, :], ypbt[:, c, t0 + k:t0 + k + 512],
                                                                      start=(k == 0), stop=(k == 30)), reads=[rdg, rypb], writes=[rpc])
                        fw.op("act", lambda e, c=c: e.activation(acc[:, c, :], pct[:], AF.Identity, scale=1.0, bias=dv[:, l, 0, c:c + 1]),
                              reads=[rpc, rdv], writes=[racc2[c]])
                    for c in range(2):
                        fw.op("act", lambda e, c=c: e.activation(sqb[:, c, :], acc[:, c, :], AF.Square), reads=[racc2[c]], writes=[rsqb])
                        fw.op("pool", lambda e, c=c: e.tensor_copy(accb[:, c, :], acc[:, c, :]), reads=[racc2[c]], writes=[raccb])
                    for c in range(2):
                        fw.op("pe", lambda e, c=c: e.matmul(pM[:], onesb[:], accb[:, c, :], start=(c == 0), stop=(c == 1)),
                              reads=[ronesb, raccb], writes=[rpM])
                    for c in range(2):
                        fw.op("pe", lambda e, c=c: e.matmul(pQ[:], onesb[:], sqb[:, c, :], start=(c == 0), stop=(c == 1)),
                              reads=[ronesb, rsqb], writes=[rpQ])
                    fw.op("act", lambda e: e.activation(m2[:], pM[:], AF.Square, scale=1.0 / 256), reads=[rpM], writes=[rm2])
                    fw.op("dve", lambda e: e.scalar_tensor_tensor(rs[:], pQ[:], 1.0 / 256, m2[:], ALU.mult, ALU.subtract),
                          reads=[rpQ, rm2], writes=[rrs])
                    fw.op("dve", lambda e: e.tensor_scalar(rs[:], rs[:], 0.0, None, ALU.max), reads=[rrs], writes=[rrs])
                    fw.op("act", lambda e: e.activation(rs[:], rs[:], AF.Sqrt, scale=1.0, bias=self.eps_ap()), reads=[rrs, self.reps], writes=[rrs])
                    fw.op("dve", lambda e: e.reciprocal(rs[:], rs[:]), reads=[rrs], writes=[rrs])
                    yot, ryo = yo[no % 2]
                    no += 1
                    for c in range(2):
                        fw.op("dve", lambda e, c=c: e.scalar_tensor_tensor(tt[:], pM[:], -1.0 / 256, acc[:, c, :], ALU.mult, ALU.add),
                              reads=[rpM, racc2[c]], writes=[rtt])
                        fw.op("dve", lambda e: e.tensor_tensor(tt[:], tt[:], rs[:], ALU.mult), reads=[rtt, rrs], writes=[rtt])
                        fw.op("act", lambda e, c=c: e.activation(yot[:, c, :], tt[:], AF.Silu, scale=dv[:, l, 1, c:c + 1], bias=dv[:, l, 2, c:c + 1]),
                              reads=[rtt, rdv], writes=[ryo])
                    tg = u * UL + t0
                    self.stt(S["yall"][768:1024, tg:tg + 512].rearrange("(c p) t -> p c t", p=128), yot[:], ryo, R["yall"])
            fw.phase_end()

    def phase_p3a(self, l):
        self.sq_eng = 'act'
        fw, I, S, R, C = self.fw, self.I, self.S, self.R, self.C
        BT = 256
        xsrc, rxsrc = (I["xT"], R["xT"]) if l == 0 else (S["xn"], R["xn"])
        fw.phase_begin()
        with ExitStack() as ps:
            sb = lambda n, s, dt=F32: fw.sbuf(n, s, dt, ps)
            Wg = sb("wg", [128, 8, 4096], BF16); rWg = fw.res("wg")
            Wb = sb("wb", [128, 8, 1024], BF16); rWb = fw.res("wb")
            Wo = sb("wo", [128, 8, 1024], BF16); rWo = fw.res("wo")
            stg = [(sb("stg3", [128, 1024], F32), fw.res("stg3", dma=True)) for _ in range(3)]
            self.stg_i = 0
            self.load_cast(Wg, rWg, 0, 8, I["w_in"][l][:, 2832:6928], 4096, stg, ["pool", "dve", "act"], 1024)
            self.load_cast(Wb, rWb, 0, 8, I["w_branch"][l], 1024, stg, ["pool", "dve", "act"], 1024)
            self.load_cast(Wo, rWo, 0, 8, I["w_out"][l], 1024, stg, ["pool", "dve", "act"], 1024)
            xb = [(sb("xb3", [128, 8, BT]), fw.res("xb3", dma=True)) for _ in range(2)]
            yb = [(sb("yb3", [128, 8, BT], BF16), fw.res("yb3", dma=True)) for _ in range(2)]
            hT = sb("hT3", [128, 8, BT], BF16); rhT = fw.res("hT3")
            hT2 = sb("hT3b", [128, 8, BT], BF16); rhT2 = fw.res("hT3b")
            sq = [sb("sq3", [128, BT], BF16) for _ in range(2)]; rsq = [fw.res("sq3") for _ in range(2)]
            rstd = sb("rstd3", [128, BT]); rrstd = fw.res("rstd3")
            tmp = sb("tmp3", [128, BT]); rtmp = fw.res("tmp3")
            sg = [(sb("sg3", [128, BT]), fw.res("sg3")) for _ in range(3)]
            t2 = [(sb("t23", [128, BT]), fw.res("t23")) for _ in range(2)]
            acc = [(sb("acc3", [128, BT]), fw.res("acc3")) for _ in range(2)]
            mg = sb("mg3", [128, 8, BT], BF16); rmg = fw.res("mg3")
            pg = [(fw.psum("pg3", [128, BT], F32, ps), fw.res("pg3", excl=True)) for _ in range(3)]
            pp = [(fw.psum("pp3", [128, BT], F32, ps), fw.res("pp3", excl=True)) for _ in range(3)]
            po = [(fw.psum("po3", [128, BT], F32, ps), fw.res("po3", excl=True)) for _ in range(2)]
            pst, rpst = po[1]
            n = no = nt2 = 0
            NBLK = min(self.NT // BT, int(os.environ.get('KBLK', '9999')))
            hTs = [(hT, rhT), (hT2, rhT2)]

            def prep(gi):
                t0 = gi * BT
                xt, rx = xb[gi % 2]
                yt, ry = yb[gi % 2]
                self.ld(xt[:], xsrc[:, t0:t0 + BT].rearrange("(k p) t -> p k t", p=128), rx, rxsrc)
                self.ld(yt[:], S["yall"][:, t0:t0 + BT].rearrange("(k p) t -> p k t", p=128), ry, R["yall"])

            def nm_stats(gi):
                xt, rx = xb[gi % 2]
                self.norm_stats(xt, rx, sq, rsq, pst, rpst, rstd, rrstd)

            def nm_apply(gi):
                xt, rx = xb[gi % 2]
                h_, rh_ = hTs[gi % 2]
                self.norm_apply(xt, rx, h_, rh_, rstd, rrstd, l, 0, (gi * BT) // self.UL, tmp, rtmp)

            def nm(gi):
                nm_stats(gi)
                nm_apply(gi)

            prep(0)
            nm(0)
            for gi in range(NBLK):
                t0 = gi * BT
                u = t0 // self.UL
                xt, rx = xb[gi % 2]
                yt, ry = yb[gi % 2]
                hT, rhT = hTs[gi % 2]
                if gi + 1 < NBLK:
                    prep(gi + 1)
                for f in range(8):
                    if f == 6 and gi + 1 < NBLK:
                        nm_stats(gi + 1)
                    acct, racc = acc[f % 2]
                    for br in range(4):
                        pgt, rpg = pg[n % 3]
                        ppt, rpp = pp[n % 3]
                        sgt, rsg = sg[n % 3]
                        n += 1
                        col = br * 1024 + f * 128
                        for k in range(8):
                            fw.op("pe", lambda e, k=k: e.matmul(pgt[:], Wg[:, k, col:col + 128], hT[:, k, :], start=(k == 0), stop=(k == 7)),
                                  reads=[rWg, rhT], writes=[rpg])
                        for k in range(2):
                            fw.op("pe", lambda e, k=k: e.matmul(ppt[:], Wb[:, br * 2 + k, f * 128:(f + 1) * 128], yt[:, br * 2 + k, :],
                                                                start=(k == 0), stop=(k == 1)), reads=[rWb, ry], writes=[rpp])
                        fw.op("act", lambda e: e.activation(sgt[:], pgt[:], AF.Sigmoid), reads=[rpg], writes=[rsg])
                        if br == 0:
                            fw.op("dve", lambda e: e.tensor_tensor(acct[:], sgt[:], ppt[:], ALU.mult), reads=[rsg, rpp], writes=[racc])
                        else:
                            t2t, rt2 = t2[nt2 % 2]
                            nt2 += 1
                            fw.op("dve", lambda e: e.tensor_tensor(t2t[:], sgt[:], ppt[:], ALU.mult), reads=[rsg, rpp], writes=[rt2])
                            if br < 3:
                                fw.op("pool", lambda e: e.tensor_tensor(acct[:], acct[:], t2t[:], ALU.add), reads=[racc, rt2], writes=[racc])
                            else:
                                fw.op("pool", lambda e, f=f: e.tensor_tensor(mg[:, f, :], acct[:], t2t[:], ALU.add), reads=[racc, rt2], writes=[rmg])
                if gi + 1 < NBLK:
                    nm_apply(gi + 1)
                for f in range(8):
                    pot, rpo = po[no % 2]
                    no += 1
                    for k in range(8):
                        fw.op("pe", lambda e, k=k, f=f: e.matmul(pot[:], Wo[:, k, f * 128:(f + 1) * 128], mg[:, k, :], start=(k == 0), stop=(k == 7)),
                              reads=[rWo, rmg], writes=[rpo])
                    fw.op("dve", lambda e, f=f: e.scalar_tensor_tensor(xt[:, f, :], pot[:], self.mod[:, l, 2, f, u:u + 1], xt[:, f, :],
                                                                       ALU.mult, ALU.add), reads=[rpo, self.rmod, rx], writes=[rx])
                self.stt(S["xm"][:, t0:t0 + BT].rearrange("(k p) t -> p k t", p=128), xt[:], rx, R["xm"])
        fw.phase_end()

    def phase_p3b(self, l):
        self.sq_eng = 'act'
        fw, I, S, R, C = self.fw, self.I, self.S, self.R, self.C
        BT = 256
        last = (l == self.L - 1)
        fw.phase_begin()
        with ExitStack() as ps:
            sb = lambda n, s, dt=F32: fw.sbuf(n, s, dt, ps)
            W1 = sb("wf1", [128, 8, 2 * DFF], BF16); rW1 = fw.res("wf1")
            W2 = sb("wf2", [128, 22, 1024], BF16); rW2 = fw.res("wf2")
            stg = [(sb("stg4", [128, 1408], F32), fw.res("stg4", dma=True)) for _ in range(2)]
            self.stg_i = 0
            self.load_cast(W1, rW1, 0, 8, I["w_ffn_in"][l], 2 * DFF, stg, ["pool", "dve", "act"], 1408)
            self.load_cast(W2, rW2, 0, 22, I["w_ffn_out"][l], 1024, stg, ["pool", "dve", "act"], 1408)
            xb = [(sb("xb4", [128, 8, BT]), fw.res("xb4", dma=True)) for _ in range(2)]
            hT = sb("hT4", [128, 8, BT], BF16); rhT = fw.res("hT4")
            hT2 = sb("hT4b", [128, 8, BT], BF16); rhT2 = fw.res("hT4b")
            sq = [sb("sq4", [128, BT], BF16) for _ in range(2)]; rsq = [fw.res("sq4") for _ in range(2)]
            rstd = sb("rstd4", [128, BT]); rrstd = fw.res("rstd4")
            tmp = sb("tmp4", [128, BT]); rtmp = fw.res("tmp4")
            sg = [(sb("sg4", [128, BT]), fw.res("sg4")) for _ in range(3)]
            hid = sb("hid4", [128, 22, BT], BF16); rhid = fw.res("hid4")
            pg = [(fw.psum("pg4", [128, BT], F32, ps), fw.res("pg4", excl=True)) for _ in range(3)]
            pu = [(fw.psum("pu4", [128, BT], F32, ps), fw.res("pu4", excl=True)) for _ in range(3)]
            po = [(fw.psum("po4", [128, BT], F32, ps), fw.res("po4", excl=True)) for _ in range(2)]
            pst, rpst = po[1]
            gfin, rgfin = C["g_finalT"]
            onesf, ronesf = C["onesb"]
            n = no = 0
            NBLK = min(self.NT // BT, int(os.environ.get('KBLK', '9999')))
            hTs = [(hT, rhT), (hT2, rhT2)]

            def prep(gi):
                t0 = gi * BT
                xt, rx = xb[gi % 2]
                self.ld(xt[:], S["xm"][:, t0:t0 + BT].rearrange("(k p) t -> p k t", p=128), rx, R["xm"])

            def nm_stats(gi):
                xt, rx = xb[gi % 2]
                self.norm_stats(xt, rx, sq, rsq, pst, rpst, rstd, rrstd)

            def nm_apply(gi):
                xt, rx = xb[gi % 2]
                h_, rh_ = hTs[gi % 2]
                self.norm_apply(xt, rx, h_, rh_, rstd, rrstd, l, 1, (gi * BT) // self.UL, tmp, rtmp)

            def nm(gi):
                nm_stats(gi)
                nm_apply(gi)

            prep(0)
            nm(0)
            for gi in range(NBLK):
                t0 = gi * BT
                u = t0 // self.UL
                xt, rx = xb[gi % 2]
                hT, rhT = hTs[gi % 2]
                if gi + 1 < NBLK:
                    prep(gi + 1)
                for j in range(22):
                    pgt, rpg = pg[n % 3]
                    put, rpu = pu[n % 3]
                    sgt, rsg = sg[n % 3]
                    n += 1
                    for k in range(8):
                        fw.op("pe", lambda e, k=k, j=j: e.matmul(pgt[:], W1[:, k, j * 128:(j + 1) * 128], hT[:, k, :], start=(k == 0), stop=(k == 7)),
                              reads=[rW1, rhT], writes=[rpg])
                    for k in range(8):
                        fw.op("pe", lambda e, k=k, j=j: e.matmul(put[:], W1[:, k, DFF + j * 128:DFF + (j + 1) * 128], hT[:, k, :],
                                                                  start=(k == 0), stop=(k == 7)), reads=[rW1, rhT], writes=[rpu])
                    fw.op("act", lambda e: e.activation(sgt[:], pgt[:], AF.Silu), reads=[rpg], writes=[rsg])
                    fw.op("dve", lambda e, j=j: e.tensor_tensor(hid[:, j, :], sgt[:], put[:], ALU.mult), reads=[rsg, rpu], writes=[rhid])
                    if j == 17 and gi + 1 < NBLK:
                        nm_stats(gi + 1)
                if gi + 1 < NBLK:
                    nm_apply(gi + 1)
                for f in range(8):
                    pot, rpo = po[no % 2]
                    no += 1
                    for k in range(22):
                        fw.op("pe", lambda e, k=k, f=f: e.matmul(pot[:], W2[:, k, f * 128:(f + 1) * 128], hid[:, k, :], start=(k == 0), stop=(k == 21)),
                              reads=[rW2, rhid], writes=[rpo])
                    fw.op("dve", lambda e, f=f: e.scalar_tensor_tensor(xt[:, f, :], pot[:], self.mod[:, l, 5, f, u:u + 1], xt[:, f, :],
                                                                       ALU.mult, ALU.add), reads=[rpo, self.rmod, rx], writes=[rx])
                if not last:
                    self.stt(S["xn"][:, t0:t0 + BT].rearrange("(k p) t -> p k t", p=128), xt[:], rx, R["xn"])
                else:
                    for k in range(8):
                        sqk, rsqk = sq[k % 2], rsq[k % 2]
                        fw.op("act", lambda e, k=k: e.activation(sqk[:], xt[:, k, :], AF.Square), reads=[rx], writes=[rsqk])
                        fw.op("pe", lambda e, k=k: e.matmul(pst[:], onesf[:], sqk[:], start=(k == 0), stop=(k == 7)),
                              reads=[rsqk, ronesf], writes=[rpst])
                    fw.op("act", lambda e: e.activation(rstd[:], pst[:], AF.Sqrt, scale=1.0 / D, bias=self.eps_ap()),
                          reads=[rpst, self.reps], writes=[rrstd])
                    fw.op("dve", lambda e: e.reciprocal(rstd[:], rstd[:]), reads=[rrstd], writes=[rrstd])
                    for k in range(8):
                        fw.op("dve", lambda e, k=k: e.scalar_tensor_tensor(xt[:, k, :], xt[:, k, :], gfin[:, k:k + 1], rstd[:], ALU.mult, ALU.mult),
                              reads=[rx, rgfin, rrstd], writes=[rx])
                    self.stt(self.yT[:, t0:t0 + BT].rearrange("(k p) t -> p k t", p=128), xt[:], rx, R["yT"])
        fw.phase_end()


def _bias_tile_idx(rows_total, qr0, kr0, q_valid_rows, k_valid_rows):
    kk = np.arange(128)
    qq = np.arange(128)
    krow = kr0 + kk // 64
    kcol = kk % 64
    qrow = qr0 + qq // 64
    qcol = qq % 64
    kr = min(8, rows_total)
    wlo = np.clip(qrow - kr // 2, 0, rows_total - kr)
    clo = np.clip(qcol - 8, 0, 64 - 16)
    vr = (krow[:, None] >= wlo[None, :]) & (krow[:, None] < wlo[None, :] + kr)
    vcol = (kcol[:, None] >= clo[None, :]) & (kcol[:, None] < clo[None, :] + 16)
    valid = vr & vcol
    valid &= (krow[:, None] >= 0) & (krow[:, None] < rows_total) & (qrow[None, :] >= 0) & (qrow[None, :] < rows_total)
    dr = np.clip(krow[:, None] - qrow[None, :] + 7, 0, 14)
    dc = np.clip(kcol[:, None] - qcol[None, :] + 15, 0, 30)
    return valid, dr, dc


def _make_rpbt(rpb_l, UL, link):
    Rr = UL // 64
    NB2 = UL // 128
    out = np.full((128, NSLOT * 4, 128), NEG, np.float32)

    def fill(slot, rows_total, qr0, kr0):
        valid, dr, dc = _bias_tile_idx(rows_total, qr0, kr0, None, None)
        for h in range(4):
            vals = rpb_l[h][dr, dc]
            out[:, slot * 4 + h, :] = np.where(valid, vals, np.float32(NEG))

    big = 64 if Rr >= 16 else Rr
    Rg = max(Rr, 16)
    for cls, bsel in (("INT", 4), ("TOP0", 0), ("TOP1", 1), ("BOT1", Rg // 2 - 2), ("BOT0", Rg // 2 - 1)):
        s0, offs = CLS[cls]
        for i, o in enumerate(offs):
            fill(s0 + i, Rg, 2 * bsel, 2 * (bsel + o))
    for cls, u, b in (("JA1", 0, NB2 - 2), ("JA0", 0, NB2 - 1), ("JB0", 1, 0), ("JB1", 1, 1)):
        s0, offs = CLS[cls]
        for i, o in enumerate(offs):
            if link:
                fill(s0 + i, 2 * Rr, 2 * (u * NB2 + b), 2 * (u * NB2 + b + o))
            else:
                kp = b + o
                if kp < 0 or kp >= NB2:
                    continue
                fill(s0 + i, Rr, 2 * b, 2 * kp)
    return out


def _host_prep(inp, UL, L, units_per_core):
    f32 = np.float32
    shared = {}
    shared["w_ada"] = np.ascontiguousarray(inp["w_ada"][:L])
    shared["b_adaT"] = np.ascontiguousarray(inp["b_ada"][:L].reshape(L, 48, 128).transpose(2, 0, 1))
    gv = np.stack([inp["g_norm_mix"][:L], inp["g_norm_ffn"][:L]], 1)
    shared["gvec"] = np.ascontiguousarray(gv.reshape(L, 2, 8, 128).transpose(3, 0, 1, 2))
    shared["w_in"] = np.ascontiguousarray(inp["w_in"][:L])
    shared["a_ln"] = np.ascontiguousarray(np.concatenate([inp["a_ln_g"][:L], inp["a_ln_b"][:L]], 1))
    shared["a_w_spT"] = np.ascontiguousarray(inp["a_w_sp"][:L].transpose(0, 3, 1, 2))
    shared["a_b_sp"] = np.ascontiguousarray(inp["a_b_sp"][:L].transpose(2, 0, 1))
    shared["c_gate_b"] = np.ascontiguousarray(inp["c_gate_b"][:L])
    shared["c_hnorm"] = np.ascontiguousarray(inp["c_hnorm_g"][:L])
    shared["d_conv_wT"] = np.ascontiguousarray(inp["d_conv_w"][:L].reshape(L, 31, 2, 128).transpose(3, 0, 2, 1))
    dv = np.stack([inp["d_conv_b"][:L], inp["d_ln_g"][:L], inp["d_ln_b"][:L]], 1)
    shared["d_vec"] = np.ascontiguousarray(dv.reshape(L, 3, 2, 128).transpose(3, 0, 1, 2))
    shared["w_branch"] = np.ascontiguousarray(inp["w_branch"][:L].reshape(L, 1024, D))
    shared["w_out"] = np.ascontiguousarray(inp["w_out"][:L])
    shared["w_ffn_in"] = np.ascontiguousarray(inp["w_ffn_in"][:L])
    shared["w_ffn_out"] = np.ascontiguousarray(inp["w_ffn_out"][:L])
    shared["g_finalT"] = np.ascontiguousarray(inp["g_final"].reshape(8, 128).T)
    shared["c_ident"] = np.eye(128, dtype=f32)
    shared["c_triu"] = np.triu(np.ones((128, 128), f32))
    shared["c_tril"] = np.tril(np.ones((128, 128), f32))
    sel = np.zeros((128, 2, 128), f32)
    sel[:, 0, 0:64] = 1.0
    sel[:, 1, 64:128] = 1.0
    shared["c_sel"] = sel
    rp = {}
    for link in (0, 1):
        rp[link] = np.stack([_make_rpbt(inp["b_rpb"][l], UL, link) for l in range(L)], 0)
    in_maps = []
    for link, units in units_per_core:
        m = dict(shared)
        xs, cs = [], []
        for which, si, t0 in units:
            x = inp["x_prompt"] if which == "p" else inp["x_sample"]
            c = inp["c_prompt"] if which == "p" else inp["c_sample"]
            xs.append(x[si, t0:t0 + UL, :])
            cs.append(c[si])
        m["xT"] = np.ascontiguousarray(np.concatenate(xs, 0).T)
        cc = np.stack(cs, 0)
        m["cT"] = np.ascontiguousarray(cc.reshape(3, 8, 128).transpose(2, 1, 0))
        m["link"] = np.full((128, 1), float(link), f32)
        m["rpbt"] = rp[link]
        in_maps.append(m)
    return in_maps


_NC_CACHE = {}


def run_config(inp, UL, L, units_per_core, debug=False):
    key = (UL, L, debug)
    if key not in _NC_CACHE:
        b = Builder(UL, L)
        b.debug = debug
        _NC_CACHE[key] = (b.build(), b)
    nc, b = _NC_CACHE[key]
    in_maps = _host_prep(inp, UL, L, units_per_core)
    res = run_bass_kernel_spmd(nc, in_maps, core_ids=list(range(len(in_maps))))
    if debug:
        return res.results
    return [np.asarray(r["yT"]) for r in res.results]


def kernel(**inputs):
    inp = {k: np.asarray(v) for k, v in inputs.items()}
    UL = 4096
    L = 4
    units = []
    for c in range(4):
        units.append((1, [("p", c, 0), ("p", c, UL), ("s", c, 0)]))
    for c in range(4):
        units.append((0, [("s", 4 + 3 * c + j, 0) for j in range(3)]))
    outs = run_config(inp, UL, L, units)
    yp = np.empty(inp["x_prompt"].shape, np.float32)
    ys = np.empty(inp["x_sample"].shape, np.float32)
    for c, (link, us) in enumerate(units):
        yT = outs[c]
        for j, (which, si, t0) in enumerate(us):
            blk = yT[:, j * UL:(j + 1) * UL].T
            if which == "p":
                yp[si, t0:t0 + UL, :] = blk
            else:
                ys[si, 0:UL, :] = blk
    return (yp, ys)
```

```python
import os
import numpy as np
from contextlib import ExitStack
import concourse.bass as bass
import concourse.mybir as mybir
from concourse.bass_utils import run_bass_kernel_spmd

F32 = mybir.dt.float32
BF16 = mybir.dt.bfloat16
AF = mybir.ActivationFunctionType
ALU = mybir.AluOpType
AX = mybir.AxisListType

D = 1024
NIN = 6928
DFF = 2816
NEG = -30000.0
EPS = 1e-6
NSLOT = 43
CLS = {
    "INT": (0, [-2, -1, 0, 1, 2]),
    "TOP0": (5, [0, 1, 2, 3]),
    "TOP1": (9, [-1, 0, 1, 2]),
    "BOT1": (13, [-2, -1, 0, 1]),
    "BOT0": (17, [-3, -2, -1, 0]),
    "JA1": (21, [-2, -1, 0, 1, 2]),
    "JA0": (26, [-3, -2, -1, 0, 1, 2]),
    "JB0": (32, [-2, -1, 0, 1, 2, 3]),
    "JB1": (38, [-2, -1, 0, 1, 2]),
}


class Res:
    __slots__ = ("name", "w", "r", "dsem", "multi", "excl")

    def __init__(self, name, multi=False, excl=False):
        self.name = name
        self.multi = multi
        self.excl = excl
        self.w = []
        self.r = {}
        self.dsem = None


class Sem:
    __slots__ = ("h", "total", "is_dma", "name")

    def __init__(self, h, is_dma, name):
        self.h = h
        self.total = 0
        self.is_dma = is_dma
        self.name = name


class Eng:
    def __init__(self, name, h, sem):
        self.name = name
        self.h = h
        self.sem = sem
        self.waited = {}


class FW:
    def __init__(self, nc, stack):
        self.nc = nc
        self.stack = stack
        self.eng = {}
        self.sems = []
        for name, h in (("pe", nc.tensor), ("dve", nc.vector), ("act", nc.scalar),
                        ("pool", nc.gpsimd), ("sp", nc.sync)):
            s = Sem(stack.enter_context(nc.semaphore("s_" + name)), False, name)
            self.sems.append(s)
            self.eng[name] = Eng(name, h, s)
        self.ninst = 0
        self.uid = 0
        self.free_dsems = []
        self.phase_dsems = None

    def sbuf(self, name, shape, dt, stack=None):
        self.uid += 1
        return (stack or self.stack).enter_context(
            self.nc.sbuf_tensor("%s_%d" % (name, self.uid), list(shape), dt))

    def psum(self, name, shape, dt=F32, stack=None):
        self.uid += 1
        esz = 4 if dt == F32 else 2
        n = int(np.prod(shape[1:]))
        be = 2048 // esz
        nb = -(-n // be)
        t = (stack or self.stack).enter_context(
            self.nc.psum_tensor("%s_%d" % (name, self.uid), [128, nb * be], dt))
        v = t[:, 0:n]
        if len(shape) == 3:
            v = v.rearrange("p (a b) -> p a b", b=shape[2])
        return v

    def res(self, name, dma=False, multi=False, excl=False):
        r = Res(name, multi, excl)
        if dma:
            if not self.free_dsems:
                self.uid += 1
                h = self.stack.enter_context(self.nc.semaphore("d_%d" % self.uid))
                sm = Sem(h, True, name)
                self.sems.append(sm)
                self.free_dsems.append(sm)
            r.dsem = self.free_dsems.pop()
            if self.phase_dsems is not None:
                self.phase_dsems.append(r.dsem)
        return r

    def phase_begin(self):
        self.phase_dsems = []

    def phase_end(self):
        try:
            print("sbuf remaining", self.nc.sbuf_bytes_remaining, "ninst", self.ninst, flush=True)
        except Exception as ex:
            print("sbuf remaining ?", ex)
        self.barrier()
        self.free_dsems.extend(self.phase_dsems)
        self.phase_dsems = None

    def _need(self, e, deps):
        best = {}
        for s, v in deps:
            if s is e.sem:
                if e.name == "pe" or e.name == "sp":
                    continue
                if e.sem.total - v >= 2:
                    continue
            if s.is_dma:
                v = s.total
            if v > best.get(s, 0):
                best[s] = v
        for s, v in best.items():
            if e.waited.get(s, 0) >= v:
                continue
            e.h.wait_ge(s.h, v)
            e.waited[s] = v
            self.ninst += 1

    def _collect(self, reads, writes):
        deps = []
        for r in reads:
            deps.extend(r.w)
        for w in writes:
            if not w.multi:
                deps.extend(w.w)
            deps.extend(w.r.items())
        return deps

    def _record(self, sem, reads, writes):
        key = (sem, sem.total)
        for r in reads:
            r.r[sem] = sem.total
        for w in writes:
            if w.multi:
                w.w = [k for k in w.w if k[0] is not sem] + [key]
            else:
                w.w = [key]
                w.r = {}

    def op(self, ename, fn, reads=(), writes=()):
        e = self.eng[ename]
        xr = [r for r in reads if r.excl]
        if xr:
            reads = [r for r in reads if not r.excl]
            writes = list(writes) + xr
        self._need(e, self._collect(reads, writes))
        ins = fn(e.h)
        e.sem.total += 1
        ins.then_inc(e.sem.h, 1)
        self.ninst += 1
        self._record(e.sem, reads, writes)
        return ins

    def dma(self, qname, out, in_, reads=(), writes=(), dres=None, **kw):
        e = self.eng[qname]
        self._need(e, self._collect(reads, writes))
        ds = dres.dsem
        ins = e.h.dma_start(out=out, in_=in_, **kw)
        ds.total += 16
        ins.then_inc(ds.h, 16)
        self.ninst += 1
        self._record(ds, reads, writes)
        return ins

    def barrier(self):
        for e in self.eng.values():
            for s in self.sems:
                if s is e.sem or s.total == 0:
                    continue
                if e.waited.get(s, 0) >= s.total:
                    continue
                e.h.wait_ge(s.h, s.total)
                e.waited[s] = s.total
                self.ninst += 1


class Builder:
    def __init__(self, UL, L, last_is_final=True):
        self.UL = UL
        self.L = L
        self.NT = 3 * UL
        self.NBK = UL // 512
        self.NCH = UL // 128
        self.NB2 = UL // 128
        assert self.NB2 >= 8

    def declare(self, nc):
        L, NT = self.L, self.NT
        di = lambda n, s, dt=F32: nc.dram_tensor(n, list(s), dt, kind="ExternalInput").ap()
        dbg = getattr(self, "debug", False)
        dx = lambda n, s, dt=F32: nc.dram_tensor(n, list(s), dt, kind=("ExternalOutput" if dbg else "Internal")).ap()
        I = {}
        I["xT"] = di("xT", [D, NT])
        I["cT"] = di("cT", [128, 8, 3])
        I["link"] = di("link", [128, 1])
        I["w_ada"] = di("w_ada", [L, D, 6 * D])
        I["b_adaT"] = di("b_adaT", [128, L, 48])
        I["gvec"] = di("gvec", [128, L, 2, 8])
        I["w_in"] = di("w_in", [L, D, NIN])
        I["a_ln"] = di("a_ln", [L, 512])
        I["a_w_spT"] = di("a_w_spT", [L, 128, 4, 128])
        I["a_b_sp"] = di("a_b_sp", [128, L, 4])
        I["rpbt"] = di("rpbt", [L, 128, NSLOT * 4, 128])
        I["c_gate_b"] = di("c_gate_b", [L, 16])
        I["c_hnorm"] = di("c_hnorm", [L, 256])
        I["d_conv_wT"] = di("d_conv_wT", [128, L, 2, 31])
        I["d_vec"] = di("d_vec", [128, L, 3, 2])
        I["w_branch"] = di("w_branch", [L, 1024, D])
        I["w_out"] = di("w_out", [L, D, D])
        I["w_ffn_in"] = di("w_ffn_in", [L, D, 2 * DFF])
        I["w_ffn_out"] = di("w_ffn_out", [L, DFF, D])
        I["g_finalT"] = di("g_finalT", [128, 8])
        I["c_ident"] = di("c_ident", [128, 128])
        I["c_triu"] = di("c_triu", [128, 128])
        I["c_tril"] = di("c_tril", [128, 128])
        I["c_sel"] = di("c_sel", [128, 2, 128])
        self.I = I
        self.yT = nc.dram_tensor("yT", [D, NT], F32, kind="ExternalOutput").ap()
        S = {}
        S["xm"] = dx("xm", [D, NT])
        S["xn"] = dx("xn", [D, NT])
        S["yall"] = dx("yall", [D, NT], BF16)
        S["bq"] = dx("bq", [256, NT], BF16)
        S["bk"] = dx("bk", [256, NT], BF16)
        S["bv"] = dx("bv", [NT, 256], BF16)
        S["cq"] = dx("cq", [256, NT], BF16)
        S["ck"] = dx("ck", [256, NT], BF16)
        S["ckt"] = dx("ckt", [NT, 256], BF16)
        S["cvt"] = dx("cvt", [NT, 256])
        S["co"] = dx("co", [NT, 256])
        S["cg"] = dx("cg", [NT, 16])
        S["chf"] = dx("chf", [NT, 256])
        S["dy"] = dx("dy", [256, NT])
        self.S = S

    def build(self):
        nc = bass.Bass("TRN2", target_bir_lowering=False)
        self.nc = nc
        self.declare(nc)
        with ExitStack() as st:
            fw = FW(nc, st)
            self.fw = fw
            self.R = {k: fw.res(k, multi=True) for k in list(self.S) + ["xT", "yT"]}
            self.setup_consts(st)
            self.ensure_eps()
            self.phase_mod()
            import os
            ph = os.environ.get("KPHASES", "p1,bc,conv,p3a,p3b").split(",")
            for l in range(self.L):
                for p in ("p1", "bc", "conv", "p3a", "p3b"):
                    if p in ph:
                        getattr(self, "phase_" + p)(l)
            fw.barrier()
            self.ninst = fw.ninst
        return nc

    def ld(self, tile_ap, dram_ap, res, dram_res=None, q="sp"):
        reads = [dram_res] if dram_res is not None else []
        self.fw.dma(q, tile_ap, dram_ap, reads=reads, writes=[res], dres=res)

    def stt(self, dram_ap, tile_ap, res, dram_res, q=None):
        import os
        q = q or os.environ.get("KSTQ", "pool")
        self.fw.dma(q, dram_ap, tile_ap, reads=[res], writes=[dram_res], dres=res)

    def setup_consts(self, st):
        fw, I = self.fw, self.I
        L = self.L
        C = {}

        def cload(name, shape, src, dt=F32):
            t = fw.sbuf(name, shape, dt)
            r = fw.res(name, dma=True)
            self.ld(t[:], src, r)
            C[name] = (t, r)
            return t, r

        cload("identf", [128, 128], I["c_ident"])
        cload("triu", [128, 128], I["c_triu"])
        cload("tril", [128, 128], I["c_tril"])
        cload("sel", [128, 2, 128], I["c_sel"])
        cload("link", [128, 1], I["link"])
        cload("cT", [128, 8, 3], I["cT"])
        cload("b_adaT", [128, L, 48], I["b_adaT"])
        cload("gvec", [128, L, 2, 8], I["gvec"])
        cload("a_b_sp", [128, L, 4], I["a_b_sp"])
        cload("d_conv_wT", [128, L, 2, 31], I["d_conv_wT"])
        cload("d_vec", [128, L, 3, 2], I["d_vec"])
        cload("g_finalT", [128, 8], I["g_finalT"])
        identb = fw.sbuf("identb", [128, 128], BF16)
        ridb = fw.res("identb")
        fw.op("dve", lambda e: e.tensor_copy(identb[:], C["identf"][0][:]), reads=[C["identf"][1]], writes=[ridb])
        C["identb"] = (identb, ridb)
        onesf = fw.sbuf("onesf", [128, 128], F32)
        ronesf = fw.res("onesf")
        fw.op("dve", lambda e: e.memset(onesf[:], 1.0), writes=[ronesf])
        C["onesf"] = (onesf, ronesf)
        onesb = fw.sbuf("onesb", [128, 128], BF16)
        ronesb = fw.res("onesb")
        fw.op("dve", lambda e: e.memset(onesb[:], 1.0), writes=[ronesb])
        C["onesb"] = (onesb, ronesb)
        self.C = C
        self.mod = fw.sbuf("mod", [128, L, 6, 8, 3], F32)
        self.rmod = fw.res("mod")
        self.gm = fw.sbuf("gm", [128, L, 2, 8, 3], F32)
        self.rgm = fw.res("gm")

    def phase_mod(self):
        fw, I, C = self.fw, self.I, self.C
        L = self.L
        fw.phase_begin()
        with ExitStack() as ps:
            sc = fw.sbuf("silu_c", [128, 8, 3], F32, ps)
            rsc = fw.res("silu_c")
            cT, rcT = C["cT"]
            fw.op("act", lambda e: e.activation(sc[:], cT[:], AF.Silu), reads=[rcT], writes=[rsc])
            wst = [(fw.sbuf("wada", [128, 8, 1024], F32, ps), fw.res("wada", dma=True)) for _ in range(2)]
            pm = [(fw.psum("pmod", [128, 8, 4], F32, ps), fw.res("pmod", excl=True)) for _ in range(2)]
            it = 0
            badaT, rbada = C["b_adaT"]
            for l in range(L):
                for m in range(6):
                    wt, rw = wst[it % 2]
                    pt, rp = pm[it % 2]
                    it += 1
                    src = I["w_ada"][l, :, m * 1024:(m + 1) * 1024].rearrange("(k p) n -> p k n", p=128)
                    for k2 in range(2):
                        self.ld(wt[:, 4 * k2:4 * k2 + 4, :], src[:, 4 * k2:4 * k2 + 4, :], rw)
                    for f in range(8):
                        for k in range(8):
                            fw.op("pe", lambda e, f=f, k=k: e.matmul(pt[:, f, 0:3], wt[:, k, f * 128:(f + 1) * 128], sc[:, k, :],
                                                                     start=(k == 0), stop=(k == 7)),
                                  reads=[rw, rsc], writes=[rp])
                    for f in range(8):
                        fw.op("dve", lambda e, f=f: e.tensor_scalar(self.mod[:, l, m, f, :], pt[:, f, 0:3],
                                                                    badaT[:, l, m * 8 + f:m * 8 + f + 1], None, ALU.add),
                              reads=[rp, rbada], writes=[self.rmod])
            gvec, rg = C["gvec"]
            for l in range(L):
                for j, m in ((0, 1), (1, 4)):
                    for u in range(3):
                        fw.op("dve", lambda e, l=l, j=j, m=m, u=u: e.scalar_tensor_tensor(
                            self.gm[:, l, j, :, u], self.mod[:, l, m, :, u], 1.0, gvec[:, l, j, :], ALU.add, ALU.mult),
                            reads=[self.rmod, rg], writes=[self.rgm])
        fw.phase_end()

    def load_cast(self, dst, rdst, k0, nk, src_rows, ncols, stg, eng_cycle, CW):
        fw = self.fw
        for k in range(nk):
            for c0 in range(0, ncols, CW):
                cw = min(CW, ncols - c0)
                st_t, st_r = stg[self.stg_i % len(stg)]
                self.stg_i += 1
                self.ld(st_t[:, 0:cw], src_rows[k * 128:(k + 1) * 128, c0:c0 + cw], st_r)
                en = eng_cycle[self.stg_i % len(eng_cycle)]
                if en == "act":
                    fw.op("act", lambda e: e.copy(dst[:, k0 + k, c0:c0 + cw], st_t[:, 0:cw]), reads=[st_r], writes=[rdst])
                else:
                    fw.op(en, lambda e: e.tensor_copy(dst[:, k0 + k, c0:c0 + cw], st_t[:, 0:cw]), reads=[st_r], writes=[rdst])

    def norm_mod(self, xb, rxb, hT, rhT, sq, rsq, pst, rpst, rstd, rrstd, l, j, u, tmp, rtmp):
        self.norm_stats(xb, rxb, sq, rsq, pst, rpst, rstd, rrstd)
        self.norm_apply(xb, rxb, hT, rhT, rstd, rrstd, l, j, u, tmp, rtmp)

    def norm_stats(self, xb, rxb, sq, rsq, pst, rpst, rstd, rrstd):
        fw = self.fw
        onesf, ronesf = self.C["onesb"]
        sqe = getattr(self, "sq_eng", "act")
        for k in range(8):
            sqk, rsqk = sq[k % 2], rsq[k % 2]
            if sqe == "act":
                fw.op("act", lambda e, k=k: e.activation(sqk[:], xb[:, k, :], AF.Square), reads=[rxb], writes=[rsqk])
            else:
                fw.op(sqe, lambda e, k=k: e.tensor_tensor(sqk[:], xb[:, k, :], xb[:, k, :], ALU.mult), reads=[rxb], writes=[rsqk])
            fw.op("pe", lambda e, k=k: e.matmul(pst[:], onesf[:], sqk[:], start=(k == 0), stop=(k == 7)),
                  reads=[rsqk, ronesf], writes=[rpst])
        fw.op("act", lambda e: e.activation(rstd[:], pst[:], AF.Sqrt, scale=1.0 / D, bias=self.eps_ap()),
              reads=[rpst, self.reps], writes=[rrstd])
        fw.op("dve", lambda e: e.reciprocal(rstd[:], rstd[:]), reads=[rrstd], writes=[rrstd])

    def norm_apply(self, xb, rxb, hT, rhT, rstd, rrstd, l, j, u, tmp, rtmp):
        fw = self.fw
        m_sh = 0 if j == 0 else 3
        for k in range(8):
            fw.op("dve", lambda e, k=k: e.tensor_tensor(tmp[:], xb[:, k, :], rstd[:], ALU.mult),
                  reads=[rxb, rrstd], writes=[rtmp])
            fw.op("dve", lambda e, k=k: e.tensor_scalar(hT[:, k, :], tmp[:], self.gm[:, l, j, k, u:u + 1],
                                                        self.mod[:, l, m_sh, k, u:u + 1], ALU.mult, ALU.add),
                  reads=[rtmp, self.rgm, self.rmod], writes=[rhT])

    def eps_ap(self):
        return self.epst[:, 0:1]

    def ensure_eps(self):
        if getattr(self, "epst", None) is None:
            fw = self.fw
            self.epst = fw.sbuf("epst", [128, 2], F32)
            self.reps = fw.res("epst")
            fw.op("dve", lambda e: e.memset(self.epst[:, 0:1], EPS), writes=[self.reps])
            fw.op("dve", lambda e: e.memset(self.epst[:, 1:2], 1.0), writes=[self.reps])

    def phase_p1(self, l):
        self.sq_eng = 'dve'
        fw, I, S, R, C = self.fw, self.I, self.S, self.R, self.C
        self.ensure_eps()
        xsrc, rxsrc = (I["xT"], R["xT"]) if l == 0 else (S["xn"], R["xn"])
        fw.phase_begin()
        with ExitStack() as ps:
            sb = lambda n, s, dt=F32: fw.sbuf(n, s, dt, ps)
            W = sb("w1", [128, 8, 2832], BF16)
            rW = fw.res("w1")
            stg = [(sb("stg", [128, 1416], F32), fw.res("stg", dma=True)) for _ in range(3)]
            self.stg_i = 0
            self.load_cast(W, rW, 0, 8, I["w_in"][l], 2832, stg, ["pool", "dve", "act"], 1416)
            wsp = sb("wsp", [128, 4, 128], BF16)
            rwsp = fw.res("wsp")
            wspf = sb("wspf", [128, 4, 128], F32)
            rwspf = fw.res("wspf", dma=True)
            self.ld(wspf[:], I["a_w_spT"][l], rwspf)
            fw.op("pool", lambda e: e.tensor_copy(wsp[:], wspf[:]), reads=[rwspf], writes=[rwsp])
            aln = sb("aln", [128, 512], F32)
            raln = fw.res("aln", dma=True)
            self.ld(aln[:], I["a_ln"][l].partition_broadcast(128), raln)
            absp, rabsp = C["a_b_sp"]
            identb, ridb = C["identb"]
            xb = [(sb("xb", [128, 8, 512]), fw.res("xb", dma=True)) for _ in range(2)]
            hT = sb("hT", [128, 8, 512], BF16); rhT = fw.res("hT")
            sq = [sb("sq", [128, 512], BF16) for _ in range(2)]; rsq = [fw.res("sq") for _ in range(2)]
            rstd = sb("rstd", [128, 512]); rrstd = fw.res("rstd")
            tmp = sb("tmp", [128, 512]); rtmp = fw.res("tmp")
            pst = fw.psum("pst", [128, 512], F32, ps); rpst = fw.res("pst", excl=True)
            pbig = fw.psum("pbig", [128, 4, 512], F32, ps)
            rbig = [fw.res("pbig%d" % i, excl=True) for i in range(4)]
            pfm = [(pbig[:, 0, :], rbig[0]), (pbig[:, 1, :], rbig[1])]
            ptm = [(pbig[:, 2, :], rbig[2]), (pbig[:, 3, :], rbig[3])]
            psA = fw.psum("psA", [128, 4, 256], F32, ps); rpsA = fw.res("psA", excl=True)
            ptr = fw.psum("ptr", [128, 8, 128], BF16, ps); rptr = fw.res("ptr", excl=True)
            ofm = [(sb("ofm", [128, 2, 512], BF16), fw.res("ofm", dma=True)) for _ in range(3)]
            ofd = [(sb("ofd", [128, 2, 512], F32), fw.res("ofd", dma=True)) for _ in range(2)]
            sgd = sb("sgd", [128, 512]); rsgd = fw.res("sgd")
            otm = [(sb("otm", [128, 4, 256], BF16), fw.res("otm", dma=True)) for _ in range(4)]
            oco = [(sb("oco", [128, 4, 256], F32), fw.res("oco", dma=True)) for _ in range(2)]
            ocv = [(sb("ocv", [128, 4, 256], F32), fw.res("ocv", dma=True)) for _ in range(2)]
            ocg = [(sb("ocg", [128, 4, 16], F32), fw.res("ocg", dma=True)) for _ in range(2)]
            yA = [(sb("yA", [128, 2, 512], BF16), fw.res("yA", dma=True)) for _ in range(2)]
            g1 = sb("g1", [128, 4, 512]); rg1 = fw.res("g1")
            gu = sb("gu", [128, 4, 512]); rgu = fw.res("gu")
            vc = sb("vc", [128, 4, 256]); rvc = fw.res("vc")
            vn = sb("vn", [128, 4, 256], BF16); rvn = fw.res("vn")
            st4 = sb("st4", [128, 16]); rst4 = fw.res("st4")
            yAt = sb("yAt", [128, 4, 256], BF16); ryAt = fw.res("yAt")
            nfm = ntm = nofm = 0
            for u in range(3):
                for b in range(self.NBK):
                    t0 = u * self.UL + b * 512
                    gi = u * self.NBK + b
                    xt, rx = xb[gi % 2]
                    self.ld(xt[:], xsrc[:, t0:t0 + 512].rearrange("(k p) t -> p k t", p=128), rx, rxsrc)
                    import os
                    cut = int(os.environ.get("KCUT", "99"))
                    if cut < 1:
                        continue
                    self.norm_mod(xt, rx, hT, rhT, sq, rsq, pst, rpst, rstd, rrstd, l, 0, u, tmp, rtmp)
                    if cut < 2:
                        continue
                    fm_jobs = [(512, "bq"), (768, "bk"), (1280, "cq"), (1536, "ck")]
                    for col0, name in fm_jobs:
                        ot, ro = ofm[nofm % 3]
                        nofm += 1
                        for c in range(2):
                            pt, rp = pfm[nfm % 2]
                            nfm += 1
                            for k in range(8):
                                fw.op("pe", lambda e, k=k, c=c: e.matmul(pt[:], W[:, k, col0 + c * 128:col0 + (c + 1) * 128], hT[:, k, :],
                                                                          start=(k == 0), stop=(k == 7)),
                                      reads=[rW, rhT], writes=[rp])
                            scale = 0.125 if name in ("bq", "cq") else 1.0
                            fw.op("act", lambda e, c=c: e.activation(ot[:, c, :], pt[:], AF.Identity, scale=scale),
                                  reads=[rp], writes=[ro])
                        self.stt(S[name][:, t0:t0 + 512].rearrange("(c p) t -> p c t", p=128), ot[:], ro, R[name])
                    if cut < 3:
                        continue
                    od, rod = ofd[gi % 2]
                    for c in range(2):
                        pa, rpa = pfm[nfm % 2]
                        nfm += 1
                        pg, rpg = pfm[nfm % 2]
                        nfm += 1
                        for k in range(8):
                            fw.op("pe", lambda e, k=k, c=c: e.matmul(pa[:], W[:, k, 2320 + c * 128:2320 + (c + 1) * 128], hT[:, k, :],
                                                                      start=(k == 0), stop=(k == 7)), reads=[rW, rhT], writes=[rpa])
                        for k in range(8):
                            fw.op("pe", lambda e, k=k, c=c: e.matmul(pg[:], W[:, k, 2576 + c * 128:2576 + (c + 1) * 128], hT[:, k, :],
                                                                      start=(k == 0), stop=(k == 7)), reads=[rW, rhT], writes=[rpg])
                        fw.op("act", lambda e: e.activation(sgd[:], pg[:], AF.Sigmoid), reads=[rpg], writes=[rsgd])
                        fw.op("dve", lambda e, c=c: e.tensor_tensor(od[:, c, :], pa[:], sgd[:], ALU.mult),
                              reads=[rpa, rsgd], writes=[rod])
                    self.stt(S["dy"][:, t0:t0 + 512].rearrange("(c p) t -> p c t", p=128), od[:], rod, R["dy"])
                    if cut < 4:
                        continue
                    obv, robv = otm[(2 * gi) % 4]
                    okt, rokt = otm[(2 * gi + 1) % 4]
                    ovt, rovt = ocv[gi % 2]
                    oo, roo = oco[gi % 2]
                    og, rog = ocg[gi % 2]
                    ya, rya = yA[gi % 2]
                    for ch in range(4):
                        ts_ = slice(ch * 128, (ch + 1) * 128)

                        def tm_mm(col0, ncol):
                            nonlocal ntm
                            pt, rp = ptm[ntm % 2]
                            ntm += 1
                            for k in range(8):
                                fw.op("pe", lambda e, k=k: e.matmul(pt[:, 0:ncol], hT[:, k, ts_], W[:, k, col0:col0 + ncol],
                                                                    start=(k == 0), stop=(k == 7)), reads=[rW, rhT], writes=[rp])
                            return pt, rp
                        skip = os.environ.get("KSKIP", "").split(",")
                        if "tm" in skip:
                            continue
                        if "bv" not in skip:
                            pt, rp = tm_mm(1024, 256)
                            if "bve" not in skip:
                                fw.op("act", lambda e: e.copy(obv[:, ch, :], pt[:, 0:256]), reads=[rp], writes=[robv])
                        if "ckv" not in skip:
                            pt, rp = tm_mm(1536, 512)
                            if "e1" not in skip:
                                fw.op("act", lambda e: e.copy(okt[:, ch, :], pt[:, 0:256]), reads=[rp], writes=[rokt])
                            if "e2" not in skip:
                                fw.op("dve", lambda e: e.tensor_copy(ovt[:, ch, :], pt[:, 256:512]), reads=[rp], writes=[rovt])
                        if "og" in skip:
                            continue
                        pt, rp = tm_mm(2048, 272)
                        fw.op("act", lambda e: e.activation(oo[:, ch, :], pt[:, 0:256], AF.Sigmoid), reads=[rp], writes=[roo])
                        fw.op("dve", lambda e: e.tensor_copy(og[:, ch, :], pt[:, 256:272]), reads=[rp], writes=[rog])
                    for ch in range(4):
                        for k in range(8):
                            fw.op("pe", lambda e, k=k, ch=ch: e.matmul(pbig[:, ch, :], hT[:, k, ch * 128:(ch + 1) * 128], W[:, k, 0:512],
                                                                        start=(k == 0), stop=(k == 7)), reads=[rW, rhT], writes=[rbig[ch]])
                    R4 = list(rbig)
                    fw.op("act", lambda e: e.activation(g1[:], pbig[:], AF.Square), reads=R4, writes=[rg1])
                    fw.op("dve", lambda e: e.tensor_scalar(g1[:], g1[:], 0.044715, 1.0, ALU.mult, ALU.add), reads=[rg1], writes=[rg1])
                    fw.op("dve", lambda e: e.tensor_tensor(g1[:], g1[:], pbig[:], ALU.mult), reads=[rg1] + R4, writes=[rg1])
                    fw.op("act", lambda e: e.activation(g1[:], g1[:], AF.Sigmoid, scale=1.5957691216), reads=[rg1], writes=[rg1])
                    fw.op("dve", lambda e: e.tensor_tensor(gu[:], g1[:], pbig[:], ALU.mult), reads=[rg1] + R4, writes=[rgu])
                    guv = gu[:, :, 256:512]
                    g1v = g1[:, :, 0:256]
                    bc4 = lambda ap: ap.unsqueeze(2).to_broadcast([128, 4, 256])
                    fw.op("dve", lambda e: e.reduce_sum(st4[:, 0:4], guv, AX.X), reads=[rgu], writes=[rst4])
                    fw.op("dve", lambda e: e.tensor_scalar(st4[:, 4:8], st4[:, 0:4], 1.0 / 256, None, ALU.mult), reads=[rst4], writes=[rst4])
                    fw.op("dve", lambda e: e.tensor_tensor(vc[:], guv, bc4(st4[:, 4:8]), ALU.subtract), reads=[rgu, rst4], writes=[rvc])
                    fw.op("dve", lambda e: e.tensor_tensor(g1v, vc[:], vc[:], ALU.mult), reads=[rvc], writes=[rg1])
                    fw.op("dve", lambda e: e.reduce_sum(st4[:, 8:12], g1v, AX.X), reads=[rg1], writes=[rst4])
                    fw.op("act", lambda e: e.activation(st4[:, 12:16], st4[:, 8:12], AF.Sqrt, scale=1.0 / 256, bias=self.eps_ap()),
                          reads=[rst4, self.reps], writes=[rst4])
                    fw.op("dve", lambda e: e.reciprocal(st4[:, 12:16], st4[:, 12:16]), reads=[rst4], writes=[rst4])
                    fw.op("dve", lambda e: e.tensor_tensor(vc[:], vc[:], bc4(st4[:, 12:16]), ALU.mult), reads=[rvc, rst4], writes=[rvc])
                    fw.op("dve", lambda e: e.tensor_tensor(vc[:], vc[:], aln[:, 0:256].unsqueeze(1).to_broadcast([128, 4, 256]), ALU.mult),
                          reads=[rvc, raln], writes=[rvc])
                    fw.op("dve", lambda e: e.tensor_tensor(vn[:], vc[:], aln[:, 256:512].unsqueeze(1).to_broadcast([128, 4, 256]), ALU.add),
                          reads=[rvc, raln], writes=[rvn])
                    for ch in range(4):
                        for g in range(4):
                            fw.op("pe", lambda e, g=g, ch=ch: e.matmul(psA[:, ch, g * 64:(g + 1) * 64], wsp[:, g, :], vn[:, ch, g * 64:(g + 1) * 64],
                                                                        start=True, stop=True), reads=[rwsp, rvn], writes=[rpsA])
                    for g in range(4):
                        fw.op("dve", lambda e, g=g: e.scalar_tensor_tensor(yAt[:, :, g * 64:(g + 1) * 64], psA[:, :, g * 64:(g + 1) * 64],
                                                                           absp[:, l, g:g + 1], gu[:, :, g * 64:(g + 1) * 64],
                                                                           ALU.add, ALU.mult),
                              reads=[rpsA, rabsp, rgu], writes=[ryAt])
                    for ch in range(4):
                        for c in range(2):
                            fw.op("pe", lambda e, c=c, ch=ch: e.transpose(ptr[:, ch * 2 + c, :], yAt[:, ch, c * 128:(c + 1) * 128], identb[:]),
                                  reads=[ryAt, ridb], writes=[rptr])
                    fw.op("act", lambda e: e.copy(ya[:].rearrange("p c (h t) -> p h c t", t=128),
                                                  ptr[:].rearrange("p (h c) t -> p h c t", c=2)), reads=[rptr], writes=[rya])
                    tsl = slice(t0, t0 + 512)
                    if "st" in os.environ.get("KSKIP", "").split(","):
                        continue
                    self.stt(S["bv"][tsl, :].rearrange("(c p) f -> p c f", p=128), obv[:], robv, R["bv"])
                    self.stt(S["ckt"][tsl, :].rearrange("(c p) f -> p c f", p=128), okt[:], rokt, R["ckt"])
                    self.stt(S["cvt"][tsl, :].rearrange("(c p) f -> p c f", p=128), ovt[:], rovt, R["cvt"])
                    self.stt(S["co"][tsl, :].rearrange("(c p) f -> p c f", p=128), oo[:], roo, R["co"])
                    self.stt(S["cg"][tsl, :].rearrange("(c p) f -> p c f", p=128), og[:], rog, R["cg"])
                    self.stt(S["yall"][0:256, tsl].rearrange("(c p) t -> p c t", p=128), ya[:], rya, R["yall"])
            fw.phase_end()

    def attn_plan(self):
        NB2 = self.NB2
        plan = []
        for u in range(3):
            for b in range(NB2):
                if u == 2 or (u == 0 and b < NB2 - 2) or (u == 1 and b >= 2):
                    if b == 0 and u != 1:
                        cls = "TOP0"
                    elif b == 1 and u != 1:
                        cls = "TOP1"
                    elif b == NB2 - 2 and u != 0:
                        cls = "BOT1"
                    elif b == NB2 - 1 and u != 0:
                        cls = "BOT0"
                    else:
                        cls = "INT"
                elif u == 0:
                    cls = "JA1" if b == NB2 - 2 else "JA0"
                else:
                    cls = "JB0" if b == 0 else "JB1"
                s0, offs = CLS[cls]
                ents = []
                for i, o in enumerate(offs):
                    kp = b + o
                    ku = u
                    if kp >= NB2:
                        ku, kp = u + 1, kp - NB2
                    elif kp < 0:
                        ku, kp = u - 1, kp + NB2
                    assert 0 <= ku <= 2 and (ku == u or (u, ku) in ((0, 1), (1, 0)))
                    ents.append((ku, kp, s0 + i))
                plan.append((u, b, ents))
        return plan

    def gen_attn(self, l, ptr, rptr, ps):
        fw, I, S, R, C = self.fw, self.I, self.S, self.R, self.C
        UL, NB2 = self.UL, self.NB2
        identb, ridb = C["identb"]
        if True:
            sb = lambda n, s, dt=F32: fw.sbuf(n, s, dt, ps)
            bt = sb("bt", [128, NSLOT * 4, 128], BF16); rbt = fw.res("bt")
            stg = [(sb("bstg", [128, 8, 128], F32), fw.res("bstg", dma=True)) for _ in range(2)]
            n = 0
            for s0 in range(0, NSLOT * 4, 8):
                s1 = min(s0 + 8, NSLOT * 4)
                stt_, rs = stg[n % 2]
                n += 1
                self.ld(stt_[:, 0:s1 - s0, :], I["rpbt"][l, :, s0:s1, :], rs)
                fw.op("pool", lambda e, s0=s0, s1=s1, stt_=stt_: e.tensor_copy(bt[:, s0:s1, :], stt_[:, 0:s1 - s0, :]),
                      reads=[rs], writes=[rbt])
            kT = sb("kT", [128, 2, 2 * UL], BF16); rkT = fw.res("kT", dma=True)
            V = sb("V", [128, 2 * self.NCH, 4, 65], BF16); rV = fw.res("V", dma=True)
            fw.op("pool", lambda e: e.memset(V[:], 1.0), writes=[rV])
            qT = [(sb("qT", [128, 2, 512], BF16), fw.res("qT", dma=True)) for _ in range(2)]
            pS = [(fw.psum("pS", [128, 8, 128], F32, ps), fw.res("pS", excl=True)) for _ in range(1)]
            pO = [(fw.psum("pO", [128, 4, 65], F32, ps), fw.res("pO", excl=True)) for _ in range(1)]
            PT = [(sb("PT", [128, 8, 128], BF16), fw.res("PT")) for _ in range(3)]
            rc = sb("rc", [128, 4]); rrc = fw.res("rc")
            yB = sb("yB", [128, 4, 64], BF16); ryB = fw.res("yB")
            yo = [(sb("yBo", [128, 2, 512], BF16), fw.res("yBo", dma=True)) for _ in range(2)]
            plan = self.attn_plan()
            nps = npt = nq = 0
            for grp in (0, 2):
                units = (0, 1) if grp == 0 else (2,)
                base = grp * UL
                ntok = len(units) * UL
                self.ld(kT[:, :, 0:ntok], S["bk"][:, base:base + ntok].rearrange("(c p) t -> p c t", p=128), rkT, R["bk"])
                for h in range(4):
                    self.ld(V[:, 0:ntok // 128, h, 0:64],
                            S["bv"][base:base + ntok, h * 64:(h + 1) * 64].rearrange("(c p) d -> p c d", p=128),
                            rV, R["bv"])
                for (u, b, ents) in plan:
                    if u not in units:
                        continue
                    if b % 4 == 0:
                        qt, rq = qT[nq % 2]
                        yot, ryo = yo[nq % 2]
                        nq += 1
                        tq0 = u * UL + b * 128
                        self.ld(qt[:], S["bq"][:, tq0:tq0 + 512].rearrange("(c p) t -> p c t", p=128), rq, R["bq"])
                    qs = slice((b % 4) * 128, (b % 4 + 1) * 128)
                    po, rpo = pO[0]
                    ne = len(ents)
                    for h in range(4):
                        hp = slice((h % 2) * 64, (h % 2) * 64 + 64)
                        hc = h // 2
                        pst_, rps = pS[0]
                        nps += 1
                        slot0 = ents[0][2]
                        pflat = pst_[:].rearrange("p a b -> p (a b)")
                        bflat = bt[:].rearrange("p s q -> p (s q)")
                        n1_ = min(ne, 4)
                        c0 = (h * NSLOT + slot0) * 128
                        fw.op("pe", lambda e: e.matmul(pflat[:, 0:n1_ * 128], identb[:], bflat[:, c0:c0 + n1_ * 128],
                                                       start=True, stop=False, skip_group_check=True), reads=[ridb, rbt], writes=[rps])
                        if ne > 4:
                            fw.op("pe", lambda e: e.matmul(pflat[:, 512:ne * 128], identb[:], bflat[:, c0 + 512:c0 + ne * 128],
                                                           start=True, stop=False, skip_group_check=True), reads=[ridb, rbt], writes=[rps])
                        for i, (ku, kp, slot) in enumerate(ents):
                            assert slot == slot0 + i
                            k0 = (ku * UL - base) + kp * 128
                            fw.op("pe", lambda e, i=i, k0=k0: e.matmul(pst_[:, i, :], kT[hp, hc, k0:k0 + 128], qt[hp, hc, qs],
                                                                        start=False, stop=True, skip_group_check=True), reads=[rkT, rq], writes=[rps])
                        pt_, rpt = PT[npt % 3]
                        npt += 1
                        n1 = min(ne, 4)
                        fw.op("act", lambda e: e.activation(pt_[:, 0:n1, :], pst_[:, 0:n1, :], AF.Exp), reads=[rps], writes=[rpt])
                        if ne > 4:
                            fw.op("act", lambda e: e.activation(pt_[:, 4:ne, :], pst_[:, 4:ne, :], AF.Exp), reads=[rps], writes=[rpt])
                        for i, (ku, kp, slot) in enumerate(ents):
                            vc_ = (ku * UL - base) // 128 + kp
                            fw.op("pe", lambda e, i=i, vc_=vc_: e.matmul(po[:, h, :], pt_[:, i, :], V[:, vc_, h, :],
                                                                          start=(i == 0), stop=(i == ne - 1)),
                                  reads=[rpt, rV], writes=[rpo])
                    fw.op("dve", lambda e: e.reciprocal(rc[:], po[:, :, 64]), reads=[rpo], writes=[rrc])
                    fw.op("dve", lambda e: e.tensor_tensor(yB[:], po[:, :, 0:64], rc[:].unsqueeze(2).to_broadcast([128, 4, 64]), ALU.mult),
                          reads=[rpo, rrc], writes=[ryB])
                    for c in range(2):
                        fw.op("pe", lambda e, c=c: e.transpose(ptr[:, c, :], yB[:, 2 * c:2 * c + 2, :], identb[:]),
                              reads=[ryB, ridb], writes=[rptr])
                    fw.op("act", lambda e: e.copy(yot[:, :, qs], ptr[:, 0:2, :]), reads=[rptr], writes=[ryo])
                    if b % 4 == 3:
                        self.stt(S["yall"][256:512, tq0:tq0 + 512].rearrange("(c p) t -> p c t", p=128), yot[:], ryo, R["yall"])
                    yield

    def gen_mlstm(self, l, ptr, rptr, ps):
        fw, I, S, R, C = self.fw, self.I, self.S, self.R, self.C
        UL, NCH = self.UL, self.NCH
        identb, ridb = C["identb"]
        onesf, ronesf = C["onesf"]
        triu, rtriu = C["triu"]
        tril, rtril = C["tril"]
        sel, rsel = C["sel"]
        link, rlink = C["link"]
        if True:
            sb = lambda n, s, dt=F32: fw.sbuf(n, s, dt, ps)
            gb = sb("gb", [128, 16]); rgb = fw.res("gb", dma=True)
            self.ld(gb[:], I["c_gate_b"][l].partition_broadcast(128), rgb)
            hng = sb("hng", [128, 256]); rhng = fw.res("hng", dma=True)
            self.ld(hng[:], I["c_hnorm"][l].partition_broadcast(128), rhng)
            NBUF = 2
            qTb = [(sb("cqT", [128, 2, 512], BF16), fw.res("cqT", dma=True)) for _ in range(NBUF)]
            kTb = [(sb("ckT", [128, 2, 512], BF16), fw.res("ckT", dma=True)) for _ in range(NBUF)]
            ktb = [(sb("ckt", [128, 4, 256], BF16), fw.res("ckt", dma=True)) for _ in range(NBUF)]
            vtb = [(sb("cvt", [128, 4, 4, 65], F32), fw.res("cvt", dma=True)) for _ in range(NBUF)]
            for vt_, rv_ in vtb:
                fw.op("pool", lambda e, vt_=vt_: e.memset(vt_[:], 1.0), writes=[rv_])
            gtb = [(sb("cgt", [128, 4, 16]), fw.res("cgt", dma=True)) for _ in range(NBUF)]
            otb = [(sb("cot", [128, 4, 256]), fw.res("cot", dma=True)) for _ in range(NBUF)]
            hfb = [(sb("chf", [128, 4, 256]), fw.res("chf", dma=True)) for _ in range(NBUF)]
            yCo = [(sb("yCo", [128, 2, 512], BF16), fw.res("yCo", dma=True)) for _ in range(2)]
            CT = sb("CT", [128, 2, 65]); rCT = fw.res("CT")
            CTb = sb("CTb", [128, 2, 65], BF16); rCTb = fw.res("CTb")
            klo = sb("klo", [128, 8, 128], BF16); rklo = fw.res("klo")
            khi = sb("khi", [128, 8, 128], BF16); rkhi = fw.res("khi")
            fw.op("pool", lambda e: e.memset(klo[:], 0.0), writes=[rklo])
            fw.op("pool", lambda e: e.memset(khi[:], 0.0), writes=[rkhi])
            g = sb("g", [128, 8]); rg = fw.res("g")
            lf = sb("lf", [128, 4]); rlf = fw.res("lf")
            ex = sb("ex", [128, 12]); rex = fw.res("ex")
            Fm4 = sb("Fm4", [128, 4, 2]); rFm = fw.res("Fm4")
            gsb4 = sb("gsb4", [128, 4, 8]); rgsb = fw.res("gsb4")
            g4 = sb("g4", [128, 4, 8])
            lf4 = sb("lf4", [128, 4, 4])
            ex4 = sb("ex4", [128, 4, 8])
            wm = sb("wm", [128, 4, 128], BF16); rwm = fw.res("wm")
            va4 = sb("va4", [128, 16, 65], BF16); rva = fw.res("va4")
            tn = sb("tn", [128, 4, 65]); rtn = fw.res("tn")
            dd = sb("dd", [128, 4]); rdd = fw.res("dd")
            hd4 = sb("hd4", [128, 4, 256]); rhd = fw.res("hd4")
            sq4 = sb("sq4", [128, 4, 256]); rsqh = fw.res("sq4")
            ss4 = sb("ss4", [128, 16]); rss = fw.res("ss4")
            yC4 = sb("yC4", [128, 4, 256], BF16); ryCt = fw.res("yC4")
            pGC = fw.psum("pGC", [128, 512], F32, ps); rpG = fw.res("pGC", excl=True)
            pG4 = pGC[:, 0:64].rearrange("p (a b) -> p a b", b=16)
            pC = pGC[:, 64:194].rearrange("p (a b) -> p a b", b=65); rpC = rpG
            pQK = fw.psum("pQK", [128, 2, 128], F32, ps); rpQK = fw.res("pQK", excl=True)
            pQK2 = fw.psum("pQK2", [128, 2, 128], F32, ps); rpQK2 = fw.res("pQK2", excl=True)
            pN = fw.psum("pN", [128, 4, 65], F32, ps); rpN = fw.res("pN", excl=True)
            nld = 0
            for d in (0, 1):
                go = 8 * d
                tri, rtri = (triu, rtriu) if d == 0 else (tril, rtril)
                unit_order = (0, 1, 2) if d == 0 else (1, 0, 2)
                for ui, u in enumerate(unit_order):
                    if ui == 1:
                        fw.op("dve", lambda e: e.tensor_scalar(CT[:], CT[:], link[:, 0:1], None, ALU.mult),
                              reads=[rCT, rlink], writes=[rCT])
                    else:
                        fw.op("dve", lambda e: e.memset(CT[:], 0.0), writes=[rCT])
                    blocks = range(self.NBK) if d == 0 else range(self.NBK - 1, -1, -1)
                    for b in blocks:
                        t0 = u * UL + b * 512
                        tsl = slice(t0, t0 + 512)
                        bi = nld % NBUF
                        nld += 1
                        qt, rq = qTb[bi]; kt, rk = kTb[bi]; ktm, rktm = ktb[bi]; vtm, rvtm = vtb[bi]
                        gt, rgt = gtb[bi]; ot, rot = otb[bi]; hf, rhf = hfb[bi]
                        self.ld(qt[:], S["cq"][:, tsl].rearrange("(c p) t -> p c t", p=128), rq, R["cq"])
                        self.ld(kt[:], S["ck"][:, tsl].rearrange("(c p) t -> p c t", p=128), rk, R["ck"])
                        self.ld(ktm[:], S["ckt"][tsl, :].rearrange("(c p) f -> p c f", p=128), rktm, R["ckt"])
                        for h in range(4):
                            self.ld(vtm[:, :, h, 0:64], S["cvt"][tsl, h * 64:(h + 1) * 64].rearrange("(c p) d -> p c d", p=128), rvtm, R["cvt"])
                        self.ld(gt[:], S["cg"][tsl, :].rearrange("(c p) f -> p c f", p=128), rgt, R["cg"])
                        if d == 1:
                            self.ld(ot[:], S["co"][tsl, :].rearrange("(c p) f -> p c f", p=128), rot, R["co"])
                            self.ld(hf[:], S["chf"][tsl, :].rearrange("(c p) f -> p c f", p=128), rhf, R["chf"])
                            yco, ryco = yCo[nld % 2]
                        fw.op("dve", lambda e: e.tensor_tensor(g4[:], gt[:, :, go:go + 8], gb[:, go:go + 8].unsqueeze(1).to_broadcast([128, 4, 8]), ALU.add),
                              reads=[rgt, rgb], writes=[rg])
                        fw.op("act", lambda e: e.activation(lf4[:], g4[:, :, 4:8], AF.Exp, scale=-1.0), reads=[rg], writes=[rlf])
                        fw.op("act", lambda e: e.activation(lf4[:], lf4[:], AF.Ln, scale=1.0, bias=self.epst[:, 1:2]),
                              reads=[rlf, self.reps], writes=[rlf])
                        fw.op("dve", lambda e: e.tensor_scalar(lf4[:], lf4[:], -1.0, None, ALU.mult), reads=[rlf], writes=[rlf])
                        for c4 in range(4):
                            fw.op("pe", lambda e, c4=c4: e.matmul(pG4[:, c4, 0:4], tri[:], lf4[:, c4, :], start=True, stop=True), reads=[rtri, rlf], writes=[rpG])
                            fw.op("pe", lambda e, c4=c4: e.matmul(pG4[:, c4, 4:8], onesf[:], lf4[:, c4, :], start=True, stop=True), reads=[ronesf, rlf], writes=[rpG])
                            fw.op("pe", lambda e, c4=c4: e.matmul(pG4[:, c4, 8:10], sel[:, 0, :], lf4[:, c4, 0:4:2], start=True, stop=False),
                                  reads=[rsel, rlf], writes=[rpG])
                            fw.op("pe", lambda e, c4=c4: e.matmul(pG4[:, c4, 8:10], sel[:, 1, :], lf4[:, c4, 1:4:2], start=False, stop=True),
                                  reads=[rsel, rlf], writes=[rpG])
                        fw.op("act", lambda e: e.copy(gsb4[:], pG4[:, :, 0:8]), reads=[rpG], writes=[rgsb])
                        fw.op("act", lambda e: e.activation(Fm4[:], pG4[:, :, 8:10], AF.Exp), reads=[rpG], writes=[rFm])
                        fw.op("dve", lambda e: e.tensor_tensor(ex4[:, :, 4:8], gsb4[:, :, 0:4], gsb4[:, :, 4:8], ALU.subtract), reads=[rgsb], writes=[rex])
                        fw.op("dve", lambda e: e.tensor_tensor(ex4[:, :, 0:4], g4[:, :, 0:4], ex4[:, :, 4:8], ALU.subtract), reads=[rg, rex], writes=[rex])
                        fw.op("act", lambda e: e.activation(ex4[:], ex4[:], AF.Exp), reads=[rex], writes=[rex])
                        for c4 in range(4):
                            fw.op("dve", lambda e, c4=c4: e.tensor_tensor(va4[:, c4 * 4:c4 * 4 + 4, :], vtm[:, c4, :, :],
                                                                           ex4[:, c4, 0:4].unsqueeze(2).to_broadcast([128, 4, 65]), ALU.mult),
                                  reads=[rvtm, rex], writes=[rva])
                        k4 = ktm[:].rearrange("p c (a b) -> p (c a) b", b=128)
                        fw.op("pool", lambda e: e.tensor_copy(klo[:, :, 0:64], k4[:, :, 0:64]), reads=[rktm], writes=[rklo])
                        fw.op("pool", lambda e: e.tensor_copy(khi[:, :, 64:128], k4[:, :, 64:128]), reads=[rktm], writes=[rkhi])
                        chunks = list(range(4)) if d == 0 else list(range(3, -1, -1))
                        for ch in chunks:
                            cs = slice(ch * 128, (ch + 1) * 128)
                            if int(os.environ.get('KCUTM', '99')) < 5:
                                continue
                            for h in range(4):
                                hp = slice((h % 2) * 64, (h % 2) * 64 + 64)
                                pq_, rpq_ = (pQK, rpQK) if h % 2 == 0 else (pQK2, rpQK2)
                                fw.op("pe", lambda e, h=h, hp=hp: e.matmul(pq_[:, h // 2, :], kt[hp, h // 2, cs], qt[hp, h // 2, cs],
                                                                            start=True, stop=True), reads=[rk, rq], writes=[rpq_])
                            wm4 = wm[:].rearrange("p (j q) t -> p j q t", q=2)
                            fw.op("dve", lambda e: e.tensor_tensor(wm4[:, :, 0, :], pQK[:], tri[:].unsqueeze(1).to_broadcast([128, 2, 128]), ALU.mult),
                                  reads=[rpQK, rtri], writes=[rwm])
                            fw.op("dve", lambda e: e.tensor_tensor(wm4[:, :, 1, :], pQK2[:], tri[:].unsqueeze(1).to_broadcast([128, 2, 128]), ALU.mult),
                                  reads=[rpQK2, rtri], writes=[rwm])
                            if int(os.environ.get('KCUTM', '99')) < 6:
                                continue
                            fw.op("dve", lambda e: e.tensor_tensor(CTb[:], CT[:], Fm4[:, ch, :].unsqueeze(2).to_broadcast([128, 2, 65]), ALU.mult),
                                  reads=[rCT, rFm], writes=[rCTb])
                            for h in range(4):
                                hp = slice((h % 2) * 64, (h % 2) * 64 + 64)
                                fw.op("pe", lambda e, h=h: e.matmul(pN[:, h, :], wm[:, h, :], va4[:, ch * 4 + h, :], start=True, stop=False),
                                      reads=[rwm, rva], writes=[rpN])
                                fw.op("pe", lambda e, h=h, hp=hp: e.matmul(pN[:, h, :], qt[hp, h // 2, cs], CTb[hp, h // 2, :], start=False, stop=True),
                                      reads=[rq, rCTb], writes=[rpN])
                            if int(os.environ.get('KCUTM', '99')) < 7:
                                continue
                            for p in range(2):
                                fw.op("pe", lambda e, p=p: e.matmul(pC[:, p, :], klo[:, ch * 2 + p, :], va4[:, ch * 4 + 2 * p, :], start=True, stop=False),
                                      reads=[rklo, rva], writes=[rpC])
                                fw.op("pe", lambda e, p=p: e.matmul(pC[:, p, :], khi[:, ch * 2 + p, :], va4[:, ch * 4 + 2 * p + 1, :], start=False, stop=True),
                                      reads=[rkhi, rva], writes=[rpC])
                            for p in range(2):
                                fw.op("dve", lambda e, p=p: e.scalar_tensor_tensor(CT[:, p, :], CT[:, p, :], Fm4[:, ch, p:p + 1], pC[:, p, :],
                                                                                   ALU.mult, ALU.add),
                                      reads=[rCT, rFm, rpC], writes=[rCT])
                            if int(os.environ.get('KCUTM', '99')) < 8:
                                continue
                            fw.op("dve", lambda e: e.tensor_tensor(tn[:], pN[:], ex4[:, ch, 4:8].unsqueeze(2).to_broadcast([128, 4, 65]), ALU.mult),
                                  reads=[rpN, rex], writes=[rtn])
                            fw.op("dve", lambda e: e.scalar_tensor_tensor(dd[:], tn[:, :, 64], -1.0, tn[:, :, 64], ALU.mult, ALU.max), reads=[rtn], writes=[rdd])
                            fw.op("dve", lambda e: e.tensor_scalar(dd[:], dd[:], 1.0, None, ALU.max), reads=[rdd], writes=[rdd])
                            fw.op("dve", lambda e: e.reciprocal(dd[:], dd[:]), reads=[rdd], writes=[rdd])
                            if d == 0:
                                fw.op("dve", lambda e: e.tensor_tensor(hf[:, ch, :].rearrange("p (h d) -> p h d", d=64), tn[:, :, 0:64],
                                                                       dd[:].unsqueeze(2).to_broadcast([128, 4, 64]), ALU.mult),
                                      reads=[rtn, rdd], writes=[rhf])
                            else:
                                fw.op("dve", lambda e: e.tensor_tensor(hd4[:, ch, :].rearrange("p (h d) -> p h d", d=64), tn[:, :, 0:64],
                                                                       dd[:].unsqueeze(2).to_broadcast([128, 4, 64]), ALU.mult),
                                      reads=[rtn, rdd], writes=[rhd])
                                if ch == chunks[-1]:
                                    hdh = hd4[:].rearrange("p c (h d) -> p (c h) d", d=64)
                                    fw.op("dve", lambda e: e.tensor_tensor(hd4[:], hd4[:], hf[:], ALU.add), reads=[rhd, rhf], writes=[rhd])
                                    fw.op("dve", lambda e: e.tensor_tensor(sq4[:], hd4[:], hd4[:], ALU.mult), reads=[rhd], writes=[rsqh])
                                    fw.op("dve", lambda e: e.reduce_sum(ss4[:], sq4[:].rearrange("p c (h d) -> p (c h) d", d=64), AX.X), reads=[rsqh], writes=[rss])
                                    fw.op("act", lambda e: e.activation(ss4[:], ss4[:], AF.Ln, scale=1.0 / 64, bias=self.eps_ap()),
                                          reads=[rss, self.reps], writes=[rss])
                                    fw.op("act", lambda e: e.activation(ss4[:], ss4[:], AF.Exp, scale=-0.5), reads=[rss], writes=[rss])
                                    fw.op("dve", lambda e: e.tensor_tensor(hdh, hdh, ss4[:].unsqueeze(2).to_broadcast([128, 16, 64]), ALU.mult),
                                          reads=[rhd, rss], writes=[rhd])
                                    fw.op("dve", lambda e: e.tensor_tensor(hd4[:], hd4[:], hng[:].unsqueeze(1).to_broadcast([128, 4, 256]), ALU.mult),
                                          reads=[rhd, rhng], writes=[rhd])
                                    fw.op("dve", lambda e: e.tensor_tensor(yC4[:], hd4[:], ot[:], ALU.mult), reads=[rhd, rot], writes=[ryCt])
                                    for c4 in range(4):
                                        for c in range(2):
                                            fw.op("pe", lambda e, c=c, c4=c4: e.transpose(ptr[:, c4 * 2 + c, :], yC4[:, c4, c * 128:(c + 1) * 128], identb[:]),
                                                  reads=[ryCt, ridb], writes=[rptr])
                                    fw.op("act", lambda e: e.copy(yco[:].rearrange("p c (h t) -> p h c t", t=128),
                                                                  ptr[:].rearrange("p (h c) t -> p h c t", c=2)), reads=[rptr], writes=[ryco])
                            yield
                        if d == 0:
                            self.stt(S["chf"][tsl, :].rearrange("(c p) f -> p c f", p=128), hf[:], rhf, R["chf"])
                        else:
                            self.stt(S["yall"][512:768, tsl].rearrange("(c p) t -> p c t", p=128), yco[:], ryco, R["yall"])

    def phase_bc(self, l):
        fw = self.fw
        fw.phase_begin()
        with ExitStack() as ps:
            ptr = fw.psum("ptrbc", [128, 8, 128], BF16, ps)
            rptr = fw.res("ptrbc", excl=True)
            ga = self.gen_attn(l, ptr, rptr, ps)
            gm = self.gen_mlstm(l, ptr, rptr, ps)
            live = [[ga, 1], [gm, 2]]
            while live:
                for ent in list(live):
                    g, r = ent
                    try:
                        for _ in range(r):
                            next(g)
                    except StopIteration:
                        live.remove(ent)
            fw.phase_end()

    def phase_conv(self, l):
        fw, I, S, R, C = self.fw, self.I, self.S, self.R, self.C
        UL = self.UL
        onesb, ronesb = C["onesb"]
        link, rlink = C["link"]
        cw, rcw = C["d_conv_wT"]
        dv, rdv = C["d_vec"]
        fw.phase_begin()
        with ExitStack() as ps:
            sb = lambda n, s, dt=F32: fw.sbuf(n, s, dt, ps)
            yp = [(sb("yp", [128, 2, UL + 30]), fw.res("yp", dma=True)) for _ in range(2)]
            acc = sb("acc", [128, 2, 512]); racc2 = [fw.res("acc0"), fw.res("acc1")]
            identf, ridf = C["identf"]
            dg = sb("dg", [128, 2, 31, 128], BF16); rdg = fw.res("dg")
            for cc in range(2):
                for k in range(31):
                    en = "dve" if (k % 2 == 0) else "pool"
                    fw.op(en, lambda e, cc=cc, k=k: e.tensor_scalar(dg[:, cc, k, :], identf[:], cw[:, l, cc, k:k + 1], None, ALU.mult),
                          reads=[ridf, rcw], writes=[rdg])
            ypb = [(sb("ypb", [128, 2, UL + 30], BF16), fw.res("ypb")) for _ in range(2)]
            pcv = [(fw.psum("pcv", [128, 512], F32, ps), fw.res("pcv", excl=True)) for _ in range(2)]
            accb = sb("accb", [128, 2, 512], BF16); raccb = fw.res("accb")
            sqb = sb("sqb", [128, 2, 512], BF16); rsqb = fw.res("sqb")
            m2 = sb("m2", [128, 512]); rm2 = fw.res("m2")
            rs = sb("rs", [128, 512]); rrs = fw.res("rs")
            tt = sb("tt", [128, 512]); rtt = fw.res("tt")
            yo = [(sb("yDo", [128, 2, 512], BF16), fw.res("yDo", dma=True)) for _ in range(2)]
            pM = fw.psum("pM", [128, 512], F32, ps); rpM = fw.res("pM", excl=True)
            pQ = fw.psum("pQ", [128, 512], F32, ps); rpQ = fw.res("pQ", excl=True)
            no = 0
            for u in range(3):
                ypt, ryp = yp[u % 2]
                fw.op("pool", lambda e: e.memset(ypt[:, :, 0:15], 0.0), writes=[ryp])
                fw.op("pool", lambda e: e.memset(ypt[:, :, UL + 15:UL + 30], 0.0), writes=[ryp])
                src = S["dy"].rearrange("(c p) t -> p c t", p=128)
                self.ld(ypt[:, :, 15:15 + UL], src[:, :, u * UL:(u + 1) * UL], ryp, R["dy"])
                if u == 0:
                    self.ld(ypt[:, :, UL + 15:UL + 30], src[:, :, UL:UL + 15], ryp, R["dy"])
                    fw.op("pool", lambda e: e.tensor_scalar(ypt[:, :, UL + 15:UL + 30], ypt[:, :, UL + 15:UL + 30], link[:, 0:1], None, ALU.mult),
                          reads=[ryp, rlink], writes=[ryp])
                elif u == 1:
                    self.ld(ypt[:, :, 0:15], src[:, :, UL - 15:UL], ryp, R["dy"])
                    fw.op("pool", lambda e: e.tensor_scalar(ypt[:, :, 0:15], ypt[:, :, 0:15], link[:, 0:1], None, ALU.mult),
                          reads=[ryp, rlink], writes=[ryp])
                ypbt, rypb = ypb[u % 2]
                fw.op("act", lambda e: e.copy(ypbt[:, 0, :], ypt[:, 0, :]), reads=[ryp], writes=[rypb])
                fw.op("pool", lambda e: e.tensor_copy(ypbt[:, 1, :], ypt[:, 1, :]), reads=[ryp], writes=[rypb])
                for b in range(self.NBK):
                    t0 = b * 512
                    for c in range(2):
                        pct, rpc = pcv[c]
                        for k in range(31):
                            fw.op("pe", lambda e, c=c, k=k: e.matmul(pct[:], dg[:, c, k, :], ypbt[:, c, t0 + k:t0 + k + 512],
                                                                      start=(k == 0), stop=(k == 30)), reads=[rdg, rypb], writes=[rpc])
                        fw.op("act", lambda e, c=c: e.activation(acc[:, c, :], pct[:], AF.Identity, scale=1.0, bias=dv[:, l, 0, c:c + 1]),
                              reads=[rpc, rdv], writes=[racc2[c]])
                    for c in range(2):
                        fw.op("act", lambda e, c=c: e.activation(sqb[:, c, :], acc[:, c, :], AF.Square), reads=[racc2[c]], writes=[rsqb])
                        fw.op("pool", lambda e, c=c: e.tensor_copy(accb[:, c, :], acc[:, c, :]), reads=[racc2[c]], writes=[raccb])
                    for c in range(2):
                        fw.op("pe", lambda e, c=c: e.matmul(pM[:], onesb[:], accb[:, c, :], start=(c == 0), stop=(c == 1)),
                              reads=[ronesb, raccb], writes=[rpM])
                    for c in range(2):
                        fw.op("pe", lambda e, c=c: e.matmul(pQ[:], onesb[:], sqb[:, c, :], start=(c == 0), stop=(c == 1)),
                              reads=[ronesb, rsqb], writes=[rpQ])
                    fw.op("act", lambda e: e.activation(m2[:], pM[:], AF.Square, scale=1.0 / 256), reads=[rpM], writes=[rm2])
                    fw.op("dve", lambda e: e.scalar_tensor_tensor(rs[:], pQ[:], 1.0 / 256, m2[:], ALU.mult, ALU.subtract),
                          reads=[rpQ, rm2], writes=[rrs])
                    fw.op("dve", lambda e: e.tensor_scalar(rs[:], rs[:], 0.0, None, ALU.max), reads=[rrs], writes=[rrs])
                    fw.op("act", lambda e: e.activation(rs[:], rs[:], AF.Sqrt, scale=1.0, bias=self.eps_ap()), reads=[rrs, self.reps], writes=[rrs])
                    fw.op("dve", lambda e: e.reciprocal(rs[:], rs[:]), reads=[rrs], writes=[rrs])
                    yot, ryo = yo[no % 2]
                    no += 1
                    for c in range(2):
                        fw.op("dve", lambda e, c=c: e.scalar_tensor_tensor(tt[:], pM[:], -1.0 / 256, acc[:, c, :], ALU.mult, ALU.add),
                              reads=[rpM, racc2[c]], writes=[rtt])
                        fw.op("dve", lambda e: e.tensor_tensor(tt[:], tt[:], rs[:], ALU.mult), reads=[rtt, rrs], writes=[rtt])
                        fw.op("act", lambda e, c=c: e.activation(yot[:, c, :], tt[:], AF.Silu, scale=dv[:, l, 1, c:c + 1], bias=dv[:, l, 2, c:c + 1]),
                              reads=[rtt, rdv], writes=[ryo])
                    tg = u * UL + t0
                    self.stt(S["yall"][768:1024, tg:tg + 512].rearrange("(c p) t -> p c t", p=128), yot[:], ryo, R["yall"])
            fw.phase_end()

    def phase_p3a(self, l):
        self.sq_eng = 'act'
        fw, I, S, R, C = self.fw, self.I, self.S, self.R, self.C
        BT = 256
        xsrc, rxsrc = (I["xT"], R["xT"]) if l == 0 else (S["xn"], R["xn"])
        fw.phase_begin()
        with ExitStack() as ps:
            sb = lambda n, s, dt=F32: fw.sbuf(n, s, dt, ps)
            Wg = sb("wg", [128, 8, 4096], BF16); rWg = fw.res("wg")
            Wb = sb("wb", [128, 8, 1024], BF16); rWb = fw.res("wb")
            Wo = sb("wo", [128, 8, 1024], BF16); rWo = fw.res("wo")
            stg = [(sb("stg3", [128, 1024], F32), fw.res("stg3", dma=True)) for _ in range(3)]
            self.stg_i = 0
            self.load_cast(Wg, rWg, 0, 8, I["w_in"][l][:, 2832:6928], 4096, stg, ["pool", "dve", "act"], 1024)
            self.load_cast(Wb, rWb, 0, 8, I["w_branch"][l], 1024, stg, ["pool", "dve", "act"], 1024)
            self.load_cast(Wo, rWo, 0, 8, I["w_out"][l], 1024, stg, ["pool", "dve", "act"], 1024)
            xb = [(sb("xb3", [128, 8, BT]), fw.res("xb3", dma=True)) for _ in range(2)]
            yb = [(sb("yb3", [128, 8, BT], BF16), fw.res("yb3", dma=True)) for _ in range(2)]
            hT = sb("hT3", [128, 8, BT], BF16); rhT = fw.res("hT3")
            hT2 = sb("hT3b", [128, 8, BT], BF16); rhT2 = fw.res("hT3b")
            sq = [sb("sq3", [128, BT], BF16) for _ in range(2)]; rsq = [fw.res("sq3") for _ in range(2)]
            rstd = sb("rstd3", [128, BT]); rrstd = fw.res("rstd3")
            tmp = sb("tmp3", [128, BT]); rtmp = fw.res("tmp3")
            sg = [(sb("sg3", [128, BT]), fw.res("sg3")) for _ in range(3)]
            t2 = [(sb("t23", [128, BT]), fw.res("t23")) for _ in range(2)]
            acc = [(sb("acc3", [128, BT]), fw.res("acc3")) for _ in range(2)]
            mg = sb("mg3", [128, 8, BT], BF16); rmg = fw.res("mg3")
            pg = [(fw.psum("pg3", [128, BT], F32, ps), fw.res("pg3", excl=True)) for _ in range(3)]
            pp = [(fw.psum("pp3", [128, BT], F32, ps), fw.res("pp3", excl=True)) for _ in range(3)]
            po = [(fw.psum("po3", [128, BT], F32, ps), fw.res("po3", excl=True)) for _ in range(2)]
            pst, rpst = po[1]
            n = no = nt2 = 0
            NBLK = min(self.NT // BT, int(os.environ.get('KBLK', '9999')))
            hTs = [(hT, rhT), (hT2, rhT2)]

            def prep(gi):
                t0 = gi * BT
                xt, rx = xb[gi % 2]
                yt, ry = yb[gi % 2]
                self.ld(xt[:], xsrc[:, t0:t0 + BT].rearrange("(k p) t -> p k t", p=128), rx, rxsrc)
                self.ld(yt[:], S["yall"][:, t0:t0 + BT].rearrange("(k p) t -> p k t", p=128), ry, R["yall"])

            def nm_stats(gi):
                xt, rx = xb[gi % 2]
                self.norm_stats(xt, rx, sq, rsq, pst, rpst, rstd, rrstd)

            def nm_apply(gi):
                xt, rx = xb[gi % 2]
                h_, rh_ = hTs[gi % 2]
                self.norm_apply(xt, rx, h_, rh_, rstd, rrstd, l, 0, (gi * BT) // self.UL, tmp, rtmp)

            def nm(gi):
                nm_stats(gi)
                nm_apply(gi)

            prep(0)
            nm(0)
            for gi in range(NBLK):
                t0 = gi * BT
                u = t0 // self.UL
                xt, rx = xb[gi % 2]
                yt, ry = yb[gi % 2]
                hT, rhT = hTs[gi % 2]
                if gi + 1 < NBLK:
                    prep(gi + 1)
                for f in range(8):
                    if f == 6 and gi + 1 < NBLK:
                        nm_stats(gi + 1)
                    acct, racc = acc[f % 2]
                    for br in range(4):
                        pgt, rpg = pg[n % 3]
                        ppt, rpp = pp[n % 3]
                        sgt, rsg = sg[n % 3]
                        n += 1
                        col = br * 1024 + f * 128
                        for k in range(8):
                            fw.op("pe", lambda e, k=k: e.matmul(pgt[:], Wg[:, k, col:col + 128], hT[:, k, :], start=(k == 0), stop=(k == 7)),
                                  reads=[rWg, rhT], writes=[rpg])
                        for k in range(2):
                            fw.op("pe", lambda e, k=k: e.matmul(ppt[:], Wb[:, br * 2 + k, f * 128:(f + 1) * 128], yt[:, br * 2 + k, :],
                                                                start=(k == 0), stop=(k == 1)), reads=[rWb, ry], writes=[rpp])
                        fw.op("act", lambda e: e.activation(sgt[:], pgt[:], AF.Sigmoid), reads=[rpg], writes=[rsg])
                        if br == 0:
                            fw.op("dve", lambda e: e.tensor_tensor(acct[:], sgt[:], ppt[:], ALU.mult), reads=[rsg, rpp], writes=[racc])
                        else:
                            t2t, rt2 = t2[nt2 % 2]
                            nt2 += 1
                            fw.op("dve", lambda e: e.tensor_tensor(t2t[:], sgt[:], ppt[:], ALU.mult), reads=[rsg, rpp], writes=[rt2])
                            if br < 3:
                                fw.op("pool", lambda e: e.tensor_tensor(acct[:], acct[:], t2t[:], ALU.add), reads=[racc, rt2], writes=[racc])
                            else:
                                fw.op("pool", lambda e, f=f: e.tensor_tensor(mg[:, f, :], acct[:], t2t[:], ALU.add), reads=[racc, rt2], writes=[rmg])
                if gi + 1 < NBLK:
                    nm_apply(gi + 1)
                for f in range(8):
                    pot, rpo = po[no % 2]
                    no += 1
                    for k in range(8):
                        fw.op("pe", lambda e, k=k, f=f: e.matmul(pot[:], Wo[:, k, f * 128:(f + 1) * 128], mg[:, k, :], start=(k == 0), stop=(k == 7)),
                              reads=[rWo, rmg], writes=[rpo])
                    fw.op("dve", lambda e, f=f: e.scalar_tensor_tensor(xt[:, f, :], pot[:], self.mod[:, l, 2, f, u:u + 1], xt[:, f, :],
                                                                       ALU.mult, ALU.add), reads=[rpo, self.rmod, rx], writes=[rx])
                self.stt(S["xm"][:, t0:t0 + BT].rearrange("(k p) t -> p k t", p=128), xt[:], rx, R["xm"])
        fw.phase_end()

    def phase_p3b(self, l):
        self.sq_eng = 'act'
        fw, I, S, R, C = self.fw, self.I, self.S, self.R, self.C
        BT = 256
        last = (l == self.L - 1)
        fw.phase_begin()
        with ExitStack() as ps:
            sb = lambda n, s, dt=F32: fw.sbuf(n, s, dt, ps)
            W1 = sb("wf1", [128, 8, 2 * DFF], BF16); rW1 = fw.res("wf1")
            W2 = sb("wf2", [128, 22, 1024], BF16); rW2 = fw.res("wf2")
            stg = [(sb("stg4", [128, 1408], F32), fw.res("stg4", dma=True)) for _ in range(2)]
            self.stg_i = 0
            self.load_cast(W1, rW1, 0, 8, I["w_ffn_in"][l], 2 * DFF, stg, ["pool", "dve", "act"], 1408)
            self.load_cast(W2, rW2, 0, 22, I["w_ffn_out"][l], 1024, stg, ["pool", "dve", "act"], 1408)
            xb = [(sb("xb4", [128, 8, BT]), fw.res("xb4", dma=True)) for _ in range(2)]
            hT = sb("hT4", [128, 8, BT], BF16); rhT = fw.res("hT4")
            hT2 = sb("hT4b", [128, 8, BT], BF16); rhT2 = fw.res("hT4b")
            sq = [sb("sq4", [128, BT], BF16) for _ in range(2)]; rsq = [fw.res("sq4") for _ in range(2)]
            rstd = sb("rstd4", [128, BT]); rrstd = fw.res("rstd4")
            tmp = sb("tmp4", [128, BT]); rtmp = fw.res("tmp4")
            sg = [(sb("sg4", [128, BT]), fw.res("sg4")) for _ in range(3)]
            hid = sb("hid4", [128, 22, BT], BF16); rhid = fw.res("hid4")
            pg = [(fw.psum("pg4", [128, BT], F32, ps), fw.res("pg4", excl=True)) for _ in range(3)]
            pu = [(fw.psum("pu4", [128, BT], F32, ps), fw.res("pu4", excl=True)) for _ in range(3)]
            po = [(fw.psum("po4", [128, BT], F32, ps), fw.res("po4", excl=True)) for _ in range(2)]
            pst, rpst = po[1]
            gfin, rgfin = C["g_finalT"]
            onesf, ronesf = C["onesb"]
            n = no = 0
            NBLK = min(self.NT // BT, int(os.environ.get('KBLK', '9999')))
            hTs = [(hT, rhT), (hT2, rhT2)]

            def prep(gi):
                t0 = gi * BT
                xt, rx = xb[gi % 2]
                self.ld(xt[:], S["xm"][:, t0:t0 + BT].rearrange("(k p) t -> p k t", p=128), rx, R["xm"])

            def nm_stats(gi):
                xt, rx = xb[gi % 2]
                self.norm_stats(xt, rx, sq, rsq, pst, rpst, rstd, rrstd)

            def nm_apply(gi):
                xt, rx = xb[gi % 2]
                h_, rh_ = hTs[gi % 2]
                self.norm_apply(xt, rx, h_, rh_, rstd, rrstd, l, 1, (gi * BT) // self.UL, tmp, rtmp)

            def nm(gi):
                nm_stats(gi)
                nm_apply(gi)

            prep(0)
            nm(0)
            for gi in range(NBLK):
                t0 = gi * BT
                u = t0 // self.UL
                xt, rx = xb[gi % 2]
                hT, rhT = hTs[gi % 2]
                if gi + 1 < NBLK:
                    prep(gi + 1)
                for j in range(22):
                    pgt, rpg = pg[n % 3]
                    put, rpu = pu[n % 3]
                    sgt, rsg = sg[n % 3]
                    n += 1
                    for k in range(8):
                        fw.op("pe", lambda e, k=k, j=j: e.matmul(pgt[:], W1[:, k, j * 128:(j + 1) * 128], hT[:, k, :], start=(k == 0), stop=(k == 7)),
                              reads=[rW1, rhT], writes=[rpg])
                    for k in range(8):
                        fw.op("pe", lambda e, k=k, j=j: e.matmul(put[:], W1[:, k, DFF + j * 128:DFF + (j + 1) * 128], hT[:, k, :],
                                                                  start=(k == 0), stop=(k == 7)), reads=[rW1, rhT], writes=[rpu])
                    fw.op("act", lambda e: e.activation(sgt[:], pgt[:], AF.Silu), reads=[rpg], writes=[rsg])
                    fw.op("dve", lambda e, j=j: e.tensor_tensor(hid[:, j, :], sgt[:], put[:], ALU.mult), reads=[rsg, rpu], writes=[rhid])
                    if j == 17 and gi + 1 < NBLK:
                        nm_stats(gi + 1)
                if gi + 1 < NBLK:
                    nm_apply(gi + 1)
                for f in range(8):
                    pot, rpo = po[no % 2]
                    no += 1
                    for k in range(22):
                        fw.op("pe", lambda e, k=k, f=f: e.matmul(pot[:], W2[:, k, f * 128:(f + 1) * 128], hid[:, k, :], start=(k == 0), stop=(k == 21)),
                              reads=[rW2, rhid], writes=[rpo])
                    fw.op("dve", lambda e, f=f: e.scalar_tensor_tensor(xt[:, f, :], pot[:], self.mod[:, l, 5, f, u:u + 1], xt[:, f, :],
                                                                       ALU.mult, ALU.add), reads=[rpo, self.rmod, rx], writes=[rx])
                if not last:
                    self.stt(S["xn"][:, t0:t0 + BT].rearrange("(k p) t -> p k t", p=128), xt[:], rx, R["xn"])
                else:
                    for k in range(8):
                        sqk, rsqk = sq[k % 2], rsq[k % 2]
                        fw.op("act", lambda e, k=k: e.activation(sqk[:], xt[:, k, :], AF.Square), reads=[rx], writes=[rsqk])
                        fw.op("pe", lambda e, k=k: e.matmul(pst[:], onesf[:], sqk[:], start=(k == 0), stop=(k == 7)),
                              reads=[rsqk, ronesf], writes=[rpst])
                    fw.op("act", lambda e: e.activation(rstd[:], pst[:], AF.Sqrt, scale=1.0 / D, bias=self.eps_ap()),
                          reads=[rpst, self.reps], writes=[rrstd])
                    fw.op("dve", lambda e: e.reciprocal(rstd[:], rstd[:]), reads=[rrstd], writes=[rrstd])
                    for k in range(8):
                        fw.op("dve", lambda e, k=k: e.scalar_tensor_tensor(xt[:, k, :], xt[:, k, :], gfin[:, k:k + 1], rstd[:], ALU.mult, ALU.mult),
                              reads=[rx, rgfin, rrstd], writes=[rx])
                    self.stt(self.yT[:, t0:t0 + BT].rearrange("(k p) t -> p k t", p=128), xt[:], rx, R["yT"])
        fw.phase_end()


def _bias_tile_idx(rows_total, qr0, kr0, q_valid_rows, k_valid_rows):
    kk = np.arange(128)
    qq = np.arange(128)
    krow = kr0 + kk // 64
    kcol = kk % 64
    qrow = qr0 + qq // 64
    qcol = qq % 64
    kr = min(8, rows_total)
    wlo = np.clip(qrow - kr // 2, 0, rows_total - kr)
    clo = np.clip(qcol - 8, 0, 64 - 16)
    vr = (krow[:, None] >= wlo[None, :]) & (krow[:, None] < wlo[None, :] + kr)
    vcol = (kcol[:, None] >= clo[None, :]) & (kcol[:, None] < clo[None, :] + 16)
    valid = vr & vcol
    valid &= (krow[:, None] >= 0) & (krow[:, None] < rows_total) & (qrow[None, :] >= 0) & (qrow[None, :] < rows_total)
    dr = np.clip(krow[:, None] - qrow[None, :] + 7, 0, 14)
    dc = np.clip(kcol[:, None] - qcol[None, :] + 15, 0, 30)
    return valid, dr, dc


def _make_rpbt(rpb_l, UL, link):
    Rr = UL // 64
    NB2 = UL // 128
    out = np.full((128, NSLOT * 4, 128), NEG, np.float32)

    def fill(slot, rows_total, qr0, kr0):
        valid, dr, dc = _bias_tile_idx(rows_total, qr0, kr0, None, None)
        for h in range(4):
            vals = rpb_l[h][dr, dc]
            out[:, h * NSLOT + slot, :] = np.where(valid, vals, np.float32(NEG))

    big = 64 if Rr >= 16 else Rr
    Rg = max(Rr, 16)
    for cls, bsel in (("INT", 4), ("TOP0", 0), ("TOP1", 1), ("BOT1", Rg // 2 - 2), ("BOT0", Rg // 2 - 1)):
        s0, offs = CLS[cls]
        for i, o in enumerate(offs):
            fill(s0 + i, Rg, 2 * bsel, 2 * (bsel + o))
    for cls, u, b in (("JA1", 0, NB2 - 2), ("JA0", 0, NB2 - 1), ("JB0", 1, 0), ("JB1", 1, 1)):
        s0, offs = CLS[cls]
        for i, o in enumerate(offs):
            if link:
                fill(s0 + i, 2 * Rr, 2 * (u * NB2 + b), 2 * (u * NB2 + b + o))
            else:
                kp = b + o
                if kp < 0 or kp >= NB2:
                    continue
                fill(s0 + i, Rr, 2 * b, 2 * kp)
    return out


def _host_prep(inp, UL, L, units_per_core):
    f32 = np.float32
    shared = {}
    shared["w_ada"] = np.ascontiguousarray(inp["w_ada"][:L])
    shared["b_adaT"] = np.ascontiguousarray(inp["b_ada"][:L].reshape(L, 48, 128).transpose(2, 0, 1))
    gv = np.stack([inp["g_norm_mix"][:L], inp["g_norm_ffn"][:L]], 1)
    shared["gvec"] = np.ascontiguousarray(gv.reshape(L, 2, 8, 128).transpose(3, 0, 1, 2))
    shared["w_in"] = np.ascontiguousarray(inp["w_in"][:L])
    shared["a_ln"] = np.ascontiguousarray(np.concatenate([inp["a_ln_g"][:L], inp["a_ln_b"][:L]], 1))
    shared["a_w_spT"] = np.ascontiguousarray(inp["a_w_sp"][:L].transpose(0, 3, 1, 2))
    shared["a_b_sp"] = np.ascontiguousarray(inp["a_b_sp"][:L].transpose(2, 0, 1))
    shared["c_gate_b"] = np.ascontiguousarray(inp["c_gate_b"][:L])
    shared["c_hnorm"] = np.ascontiguousarray(inp["c_hnorm_g"][:L])
    shared["d_conv_wT"] = np.ascontiguousarray(inp["d_conv_w"][:L].reshape(L, 31, 2, 128).transpose(3, 0, 2, 1))
    dv = np.stack([inp["d_conv_b"][:L], inp["d_ln_g"][:L], inp["d_ln_b"][:L]], 1)
    shared["d_vec"] = np.ascontiguousarray(dv.reshape(L, 3, 2, 128).transpose(3, 0, 1, 2))
    shared["w_branch"] = np.ascontiguousarray(inp["w_branch"][:L].reshape(L, 1024, D))
    shared["w_out"] = np.ascontiguousarray(inp["w_out"][:L])
    shared["w_ffn_in"] = np.ascontiguousarray(inp["w_ffn_in"][:L])
    shared["w_ffn_out"] = np.ascontiguousarray(inp["w_ffn_out"][:L])
    shared["g_finalT"] = np.ascontiguousarray(inp["g_final"].reshape(8, 128).T)
    shared["c_ident"] = np.eye(128, dtype=f32)
    shared["c_triu"] = np.triu(np.ones((128, 128), f32))
    shared["c_tril"] = np.tril(np.ones((128, 128), f32))
    sel = np.zeros((128, 2, 128), f32)
    sel[:, 0, 0:64] = 1.0
    sel[:, 1, 64:128] = 1.0
    shared["c_sel"] = sel
    rp = {}
    for link in (0, 1):
        rp[link] = np.stack([_make_rpbt(inp["b_rpb"][l], UL, link) for l in range(L)], 0)
    in_maps = []
    for link, units in units_per_core:
        m = dict(shared)
        xs, cs = [], []
        for which, si, t0 in units:
            x = inp["x_prompt"] if which == "p" else inp["x_sample"]
            c = inp["c_prompt"] if which == "p" else inp["c_sample"]
            xs.append(x[si, t0:t0 + UL, :])
            cs.append(c[si])
        m["xT"] = np.ascontiguousarray(np.concatenate(xs, 0).T)
        cc = np.stack(cs, 0)
        m["cT"] = np.ascontiguousarray(cc.reshape(3, 8, 128).transpose(2, 1, 0))
        m["link"] = np.full((128, 1), float(link), f32)
        m["rpbt"] = rp[link]
        in_maps.append(m)
    return in_maps


_NC_CACHE = {}


def run_config(inp, UL, L, units_per_core, debug=False):
    key = (UL, L, debug)
    if key not in _NC_CACHE:
        b = Builder(UL, L)
        b.debug = debug
        _NC_CACHE[key] = (b.build(), b)
    nc, b = _NC_CACHE[key]
    in_maps = _host_prep(inp, UL, L, units_per_core)
    res = run_bass_kernel_spmd(nc, in_maps, core_ids=list(range(len(in_maps))))
    if debug:
        return res.results
    return [np.asarray(r["yT"]) for r in res.results]


def kernel(**inputs):
    inp = {k: np.asarray(v) for k, v in inputs.items()}
    UL = 4096
    L = 4
    units = []
    for c in range(4):
        units.append((1, [("p", c, 0), ("p", c, UL), ("s", c, 0)]))
    for c in range(4):
        units.append((0, [("s", 4 + 3 * c + j, 0) for j in range(3)]))
    outs = run_config(inp, UL, L, units)
    yp = np.empty(inp["x_prompt"].shape, np.float32)
    ys = np.empty(inp["x_sample"].shape, np.float32)
    for c, (link, us) in enumerate(units):
        yT = outs[c]
        for j, (which, si, t0) in enumerate(us):
            blk = yT[:, j * UL:(j + 1) * UL].T
            if which == "p":
                yp[si, t0:t0 + UL, :] = blk
            else:
                ys[si, 0:UL, :] = blk
    return (yp, ys)
```
